# Optimizing a Trainium2 kernel written in Bass

```python
import math
import jax, jax.numpy as jnp
from jax import lax
import numpy as np

D_MODEL = 1024
BATCH = 8
SEQ = 4096
DEPTH = 2
DEC_BATCH = 8
DEC_SEQ = 2048
PAST_LEN = 128

GRID_W = 64
HEAD_DIM = 64
MIX_WIDTH = D_MODEL
Q_BLOCK = 128
NORM_EPS = 1e-6
A_HEADS = 4
A_KV_HEADS = 2
A_THETA = 10000.0
B_HEADS = 4
B_SUB = HEAD_DIM // 2
B_ROT = B_SUB // 4
PARTIAL_THETA = 500000.0
C_HEADS = 4
RET_THETA = 10000.0
RET_CHUNK = 128
D_HEADS = 4
W_LORA = 64
A_LORA = 64
G_LORA = 128
WKV_GN_EPS = 64e-5
D_FF = ((-(-8 * D_MODEL // 3) + 255) // 256) * 256

A_W = A_HEADS * HEAD_DIM
A_KV_W = A_KV_HEADS * HEAD_DIM
B_W = B_HEADS * HEAD_DIM
C_W = C_HEADS * HEAD_DIM
D_W = D_HEADS * HEAD_DIM
A_COLS = A_W + 2 * A_KV_W
B_COLS = 3 * B_W
C_COLS = 4 * C_W
D_SPLITS = (D_W, D_W, D_W, W_LORA, W_LORA, A_LORA, G_LORA)
D_COLS = 3 * D_W + 2 * W_LORA + A_LORA + G_LORA
IN_COLS = A_COLS + B_COLS + C_COLS + D_COLS
GROUP_OFFS = (A_COLS, A_COLS + B_COLS, A_COLS + B_COLS + C_COLS)

kernel_name = 'hybrid_bidir_parallel_heads_encoder'


def _rms(x, gain, eps=NORM_EPS):
    xf = x.astype(jnp.float32)
    y = xf * lax.rsqrt(jnp.mean(xf * xf, axis=-1, keepdims=True) + eps)
    return (y * gain.astype(jnp.float32)).astype(x.dtype)


def _split(z, sizes):
    offs = [int(o) for o in np.cumsum(sizes)[:-1]]
    return jnp.split(z, offs, axis=-1)


def _angles(pos, rot_dim, theta):
    inv = theta ** (-jnp.arange(0, rot_dim, 2, dtype=jnp.float32) / rot_dim)
    return pos.astype(jnp.float32)[:, None] * inv[None, :]


def _rotate(x, ang):
    half = x.shape[-1] // 2
    x1, x2 = x[..., :half], x[..., half:]
    c = jnp.cos(ang)[None, :, None, :].astype(x.dtype)
    s = jnp.sin(ang)[None, :, None, :].astype(x.dtype)
    return jnp.concatenate([x1 * c - x2 * s, x2 * c + x1 * s], axis=-1)


def _mixer_gqa(z, q_gain, k_gain, row_idx, col_idx):
    bn, t, _ = z.shape
    q, k, v = _split(z, (A_W, A_KV_W, A_KV_W))
    q = _rms(q.reshape(bn, t, A_HEADS, HEAD_DIM), q_gain)
    k = _rms(k.reshape(bn, t, A_KV_HEADS, HEAD_DIM), k_gain)
    v = v.reshape(bn, t, A_KV_HEADS, HEAD_DIM)
    half = HEAD_DIM // 2
    ang_r = _angles(row_idx, half, A_THETA)
    ang_c = _angles(col_idx, half, A_THETA)

    def axial(x):
        return jnp.concatenate([_rotate(x[..., :half], ang_r), _rotate(x[..., half:], ang_c)], axis=-1)

    q, k = axial(q), axial(k)
    g = A_HEADS // A_KV_HEADS
    nb = t // Q_BLOCK
    qb = q.reshape(bn, nb, Q_BLOCK, A_KV_HEADS, g, HEAD_DIM).transpose(1, 0, 3, 4, 2, 5)
    kt = k.transpose(0, 2, 1, 3)
    vt = v.transpose(0, 2, 1, 3)
    scale = HEAD_DIM ** -0.5

    def block(qblk):
        s = jnp.einsum('bhgqd,bhkd->bhgqk', qblk, kt).astype(jnp.float32) * scale
        p = jax.nn.softmax(s, axis=-1).astype(vt.dtype)
        return jnp.einsum('bhgqk,bhkd->bhgqd', p, vt)

    o = lax.map(block, qb)
    return o.transpose(1, 0, 4, 2, 3, 5).reshape(bn, t, A_W)


def _mixer_diff(z, lam_params, subln_gain, pos, lam_init):
    bn, t, _ = z.shape
    q, k, v = _split(z, (B_W, B_W, B_W))
    q = q.reshape(bn, t, 2 * B_HEADS, B_SUB)
    k = k.reshape(bn, t, 2 * B_HEADS, B_SUB)
    v = v.reshape(bn, t, B_HEADS, HEAD_DIM)
    ang = _angles(pos, B_ROT, PARTIAL_THETA)

    def partial_rope(x):
        return jnp.concatenate([_rotate(x[..., :B_ROT], ang), x[..., B_ROT:]], axis=-1)

    q, k = partial_rope(q), partial_rope(k)
    lp = lam_params.astype(jnp.float32)
    lam = jnp.exp(jnp.sum(lp[0] * lp[1])) - jnp.exp(jnp.sum(lp[2] * lp[3])) + lam_init
    nb = t // Q_BLOCK
    qb = q.reshape(bn, nb, Q_BLOCK, 2 * B_HEADS, B_SUB).transpose(1, 0, 3, 2, 4)
    kt = k.transpose(0, 2, 1, 3)
    vt = v.transpose(0, 2, 1, 3)
    scale = B_SUB ** -0.5

    def block(qblk):
        s = jnp.einsum('bhqd,bhkd->bhqk', qblk, kt).astype(jnp.float32) * scale
        p = jax.nn.softmax(s, axis=-1)
        p = p.reshape(p.shape[0], B_HEADS, 2, p.shape[2], p.shape[3])
        a = p[:, :, 0] - lam * p[:, :, 1]
        return jnp.einsum('bhqk,bhkd->bhqd', a.astype(vt.dtype), vt)

    o = lax.map(block, qb)
    o = o.transpose(1, 0, 3, 2, 4).reshape(bn, t, B_HEADS, HEAD_DIM)
    o = _rms(o, subln_gain) * (1.0 - lam_init)
    return o.reshape(bn, t, B_W)


def _retention_dir(q, k, v, log_gamma):
    bn, h, t, d = q.shape
    c = RET_CHUNK
    n = t // c
    qc = q.reshape(bn, h, n, c, d)
    kc = k.reshape(bn, h, n, c, d)
    vc = v.reshape(bn, h, n, c, d)
    i = jnp.arange(c, dtype=jnp.float32)
    lg = log_gamma.astype(jnp.float32)[:, None]
    diff = i[:, None] - i[None, :]
    dmask = jnp.where(diff >= 0, jnp.exp(lg[:, :, None] * jnp.maximum(diff, 0.0)), 0.0)
    inner = jnp.einsum('bhnid,bhnjd->bhnij', qc, kc) * dmask[None, :, None].astype(q.dtype)
    o_inner = jnp.einsum('bhnij,bhnje->bhnie', inner, vc)
    k_dec = jnp.exp(lg * (c - 1 - i))
    kv = jnp.einsum('bhnjd,bhnje->bhnde', kc * k_dec[None, :, None, :, None].astype(q.dtype), vc)
    g_chunk = jnp.exp(log_gamma.astype(jnp.float32) * c)[None, :, None, None]

    def step(state, kv_c):
        return g_chunk * state + kv_c, state

    s0 = jnp.zeros((bn, h, d, d), jnp.float32)
    _, states = lax.scan(step, s0, kv.transpose(2, 0, 1, 3, 4).astype(jnp.float32))
    states = states.transpose(1, 2, 0, 3, 4).astype(q.dtype)
    q_dec = jnp.exp(lg * (i + 1.0))
    cross = jnp.einsum('bhnid,bhnde->bhnie', qc * q_dec[None, :, None, :, None].astype(q.dtype), states)
    return (o_inner + cross).reshape(bn, h, t, d)


def _mixer_retention(z, gn_gain, pos, lg_fwd, lg_bwd):
    bn, t, _ = z.shape
    q, k, v, g = _split(z, (C_W, C_W, C_W, C_W))
    ang = _angles(pos, HEAD_DIM, RET_THETA)
    q = _rotate(q.reshape(bn, t, C_HEADS, HEAD_DIM), ang).transpose(0, 2, 1, 3)
    k = (_rotate(k.reshape(bn, t, C_HEADS, HEAD_DIM), ang) * (HEAD_DIM ** -0.5)).transpose(0, 2, 1, 3)
    v = v.reshape(bn, t, C_HEADS, HEAD_DIM).transpose(0, 2, 1, 3)
    fwd = _retention_dir(q, k, v, lg_fwd)
    bwd = jnp.flip(_retention_dir(jnp.flip(q, 2), jnp.flip(k, 2), jnp.flip(v, 2), lg_bwd), 2)
    o = (fwd + bwd).transpose(0, 2, 1, 3)
    o = _rms(o, gn_gain.reshape(C_HEADS, HEAD_DIM))
    return o.reshape(bn, t, C_W) * jax.nn.silu(g)


def _wkv7(r, w, k, v, kk, b, reverse):
    bn, t, h, n = r.shape
    xs = (r.transpose(1, 0, 2, 3), w.transpose(1, 0, 2, 3), k.transpose(1, 0, 2, 3),
          v.transpose(1, 0, 2, 3), kk.transpose(1, 0, 2, 3), b.transpose(1, 0, 2, 3))

    def step(state, xs_t):
        r_t, w_t, k_t, v_t, kk_t, b_t = xs_t
        sa = jnp.einsum('bhvk,bhk->bhv', state, -kk_t)
        state = state * w_t[:, :, None, :] + sa[..., None] * b_t[:, :, None, :] + v_t[..., None] * k_t[:, :, None, :]
        return state, jnp.einsum('bhvk,bhk->bhv', state, r_t)

    _, y = lax.scan(step, jnp.zeros((bn, h, n, n), jnp.float32), xs, reverse=reverse)
    return y.transpose(1, 0, 2, 3)


def _mixer_rwkv7(z, mu_prev, mu_next, w0, w_up, a0, a_up, g_up, k_k, k_a, r_k, gn_w, gn_b):
    bn, t, _ = z.shape
    z_prev = jnp.pad(z[:, :-1], ((0, 0), (1, 0), (0, 0)))
    z_next = jnp.pad(z[:, 1:], ((0, 0), (0, 1), (0, 0)))
    u = z + mu_prev * (z_prev - z) + mu_next * (z_next - z)
    r, k, v, wdf, wdb, ad, gd = _split(u, D_SPLITS)
    f32 = jnp.float32

    def decay(wd, w0_d, w_up_d):
        w = -jax.nn.softplus(-(w0_d + jnp.tanh(wd) @ w_up_d).astype(f32)) - 0.5
        return jnp.exp(-jnp.exp(w)).reshape(bn, t, D_HEADS, HEAD_DIM)

    dec_f = decay(wdf, w0[0], w_up[0])
    dec_b = decay(wdb, w0[1], w_up[1])
    a = jax.nn.sigmoid((a0 + ad @ a_up).astype(f32)).reshape(bn, t, D_HEADS, HEAD_DIM)
    g = (jax.nn.sigmoid(gd) @ g_up).astype(f32)
    rh = r.astype(f32).reshape(bn, t, D_HEADS, HEAD_DIM)
    kh = k.astype(f32).reshape(bn, t, D_HEADS, HEAD_DIM)
    vh = v.astype(f32).reshape(bn, t, D_HEADS, HEAD_DIM)
    kk = kh * k_k.astype(f32).reshape(D_HEADS, HEAD_DIM)
    kk = kk / jnp.maximum(jnp.sqrt(jnp.sum(kk * kk, axis=-1, keepdims=True)), 1e-12)
    kh = kh * (1.0 + (a - 1.0) * k_a.astype(f32).reshape(D_HEADS, HEAD_DIM))
    b = kk * a
    y = _wkv7(rh, dec_f, kh, vh, kk, b, False) + _wkv7(rh, dec_b, kh, vh, kk, b, True)
    mean = jnp.mean(y, axis=-1, keepdims=True)
    var = jnp.mean((y - mean) ** 2, axis=-1, keepdims=True)
    yn = (y - mean) * lax.rsqrt(var + WKV_GN_EPS) * gn_w.astype(f32).reshape(D_HEADS, HEAD_DIM) \
        + gn_b.astype(f32).reshape(D_HEADS, HEAD_DIM)
    yn = yn + jnp.sum(rh * kh * r_k.astype(f32), axis=-1, keepdims=True) * vh
    return (yn.reshape(bn, t, D_W) * g).astype(z.dtype)


def _trunk(x, p):
    bn, t, _ = x.shape
    rows = t // GRID_W
    row_idx = jnp.repeat(jnp.arange(rows), GRID_W)
    col_idx = jnp.tile(jnp.arange(GRID_W), rows)
    pos = jnp.arange(t)
    lg_fwd = jnp.log1p(-jnp.exp2(-5.0 - jnp.arange(C_HEADS, dtype=jnp.float32)))
    lg_bwd = lg_fwd[::-1]
    for l in range(DEPTH):
        h = _rms(x, p['norm_mix_pre'][l])
        z = h @ p['w_in'][l]
        z_a, z_b, z_c, z_d = jnp.split(z, list(GROUP_OFFS), axis=-1)
        lam_init = 0.8 - 0.6 * math.exp(-0.3 * l)
        o_a = _mixer_gqa(z_a, p['a_q_gain'][l], p['a_k_gain'][l], row_idx, col_idx)
        o_b = _mixer_diff(z_b, p['b_lambda'][l], p['b_subln_gain'][l], pos, lam_init)
        o_c = _mixer_retention(z_c, p['c_gn_gain'][l], pos, lg_fwd, lg_bwd)
        o_d = _mixer_rwkv7(z_d, p['d_mu_prev'][l], p['d_mu_next'][l], p['d_w0'][l], p['d_w_up'][l],
                           p['d_a0'][l], p['d_a_up'][l], p['d_g_up'][l], p['d_k_k'][l], p['d_k_a'][l],
                           p['d_r_k'][l], p['d_gn_w'][l], p['d_gn_b'][l])
        mix = jnp.concatenate([o_a, o_b, o_c, o_d], axis=-1) @ p['w_out'][l]
        x = x + _rms(mix, p['norm_mix_post'][l])
        h = _rms(x, p['norm_ffn_pre'][l])
        f = (jax.nn.silu(h @ p['ffn_w_gate'][l]) * (h @ p['ffn_w_up'][l])) @ p['ffn_w_down'][l]
        x = x + _rms(f, p['norm_ffn_post'][l])
    return x


def setup_inputs(seed: int = 0) -> dict:
    key = jax.random.key(seed)
    ks = iter(jax.random.split(key, 40))
    L = DEPTH

    def nrm(shape, scale):
        return jax.random.normal(next(ks), shape, jnp.float32) * scale

    return {
        'x_prompt': nrm((BATCH, SEQ, D_MODEL), 1.0),
        'x_sample': nrm((DEC_BATCH, DEC_SEQ, D_MODEL), 1.0),
        'norm_mix_pre': 1.0 + nrm((L, D_MODEL), 0.02),
        'norm_mix_post': 1.0 + nrm((L, D_MODEL), 0.02),
        'norm_ffn_pre': 1.0 + nrm((L, D_MODEL), 0.02),
        'norm_ffn_post': 1.0 + nrm((L, D_MODEL), 0.02),
        'w_in': nrm((L, D_MODEL, IN_COLS), D_MODEL ** -0.5),
        'w_out': nrm((L, MIX_WIDTH, D_MODEL), MIX_WIDTH ** -0.5),
        'a_q_gain': 1.0 + nrm((L, HEAD_DIM), 0.02),
        'a_k_gain': 1.0 + nrm((L, HEAD_DIM), 0.02),
        'b_lambda': nrm((L, 4, B_SUB), 0.1),
        'b_subln_gain': 1.0 + nrm((L, HEAD_DIM), 0.02),
        'c_gn_gain': 1.0 + nrm((L, C_W), 0.02),
        'd_mu_prev': 0.3 + nrm((L, D_COLS), 0.1),
        'd_mu_next': 0.3 + nrm((L, D_COLS), 0.1),
        'd_w0': -1.0 + nrm((L, 2, D_W), 0.5),
        'd_w_up': nrm((L, 2, W_LORA, D_W), 0.1 * W_LORA ** -0.5),
        'd_a0': nrm((L, D_W), 0.1),
        'd_a_up': nrm((L, A_LORA, D_W), 0.5 * A_LORA ** -0.5),
        'd_g_up': nrm((L, G_LORA, D_W), G_LORA ** -0.5),
        'd_k_k': 0.85 + nrm((L, D_W), 0.05),
        'd_k_a': 1.0 + nrm((L, D_W), 0.05),
        'd_r_k': nrm((L, D_HEADS, HEAD_DIM), 0.1),
        'd_gn_w': 1.0 + nrm((L, D_W), 0.02),
        'd_gn_b': nrm((L, D_W), 0.02),
        'ffn_w_gate': nrm((L, D_MODEL, D_FF), D_MODEL ** -0.5),
        'ffn_w_up': nrm((L, D_MODEL, D_FF), D_MODEL ** -0.5),
        'ffn_w_down': nrm((L, D_FF, D_MODEL), D_FF ** -0.5),
    }


def reference(x_prompt, x_sample, norm_mix_pre, norm_mix_post, norm_ffn_pre, norm_ffn_post,
              w_in, w_out, a_q_gain, a_k_gain, b_lambda, b_subln_gain, c_gn_gain,
              d_mu_prev, d_mu_next, d_w0, d_w_up, d_a0, d_a_up, d_g_up, d_k_k, d_k_a, d_r_k,
              d_gn_w, d_gn_b, ffn_w_gate, ffn_w_up, ffn_w_down):
    params = {
        'norm_mix_pre': norm_mix_pre, 'norm_mix_post': norm_mix_post,
        'norm_ffn_pre': norm_ffn_pre, 'norm_ffn_post': norm_ffn_post,
        'w_in': w_in, 'w_out': w_out,
        'a_q_gain': a_q_gain, 'a_k_gain': a_k_gain,
        'b_lambda': b_lambda, 'b_subln_gain': b_subln_gain,
        'c_gn_gain': c_gn_gain,
        'd_mu_prev': d_mu_prev, 'd_mu_next': d_mu_next, 'd_w0': d_w0, 'd_w_up': d_w_up,
        'd_a0': d_a0, 'd_a_up': d_a_up, 'd_g_up': d_g_up, 'd_k_k': d_k_k, 'd_k_a': d_k_a,
        'd_r_k': d_r_k, 'd_gn_w': d_gn_w, 'd_gn_b': d_gn_b,
        'ffn_w_gate': ffn_w_gate, 'ffn_w_up': ffn_w_up, 'ffn_w_down': ffn_w_down,
    }
    y_prompt = _trunk(x_prompt, params)
    y_sample = _trunk(x_sample, params)
    return (y_prompt, y_sample)
```

```python
import math
from contextlib import ExitStack
import numpy as np
import concourse.bass as bass
import concourse.mybir as mybir
from concourse.bass_utils import run_bass_kernel_spmd

F32 = mybir.dt.float32
BF16 = mybir.dt.bfloat16
AF = mybir.ActivationFunctionType
ALU = mybir.AluOpType
AX = mybir.AxisListType

D_MODEL = 1024
IN_COLS = 3392
D_FF = 2816
NFF = D_FF // 128
EPS = 1e-6
GN_EPS = 64e-5
TMAX = 4096


class View:
    __slots__ = ("t", "ap")

    def __init__(self, t, ap):
        self.t = t
        self.ap = ap


class T:
    __slots__ = ("ap", "w", "r", "name")

    def __init__(self, ap, name=""):
        self.ap = ap
        self.w = {}
        self.r = {}
        self.name = name

    def __getitem__(self, idx):
        return View(self, self.ap[idx])

    def bc(self, idx, axis, shape):
        return View(self, self.ap[idx].unsqueeze(axis).to_broadcast(list(shape)))


class Eng:
    def __init__(self, name, handle, sem):
        self.name = name
        self.h = handle
        self.sem = sem
        self.count = 0
        self.seen = {}


class Pool:
    def __init__(self, items):
        self.items = items
        self.i = 0

    def next(self):
        t = self.items[self.i]
        self.i = (self.i + 1) % len(self.items)
        return t


class Phase:
    def __init__(self, fw):
        self.fw = fw
        self.es = ExitStack()
        self.n = 0

    def sb(self, shape, dt, name="t"):
        self.n += 1
        t = self.es.enter_context(self.fw.nc.sbuf_tensor("%s_%d_%d" % (name, self.fw.uid(), self.n), list(shape), dt))
        return T(t, name)

    def ps(self, shape, dt=F32, name="p"):
        self.n += 1
        t = self.es.enter_context(self.fw.nc.psum_tensor("%s_%d_%d" % (name, self.fw.uid(), self.n), list(shape), dt))
        return T(t, name)

    def sbpool(self, n, shape, dt, name="t"):
        return Pool([self.sb(shape, dt, name) for _ in range(n)])

    def pspool(self, n, shape, dt=F32, name="p"):
        return Pool([self.ps(shape, dt, name) for _ in range(n)])

    def close(self):
        self.fw.barrier()
        self.es.close()


class FW:
    NDMA = 32

    def __init__(self, nc, es):
        self.nc = nc
        self.es = es
        self._uid = 0
        self.E = {}
        for nm, h in (("pe", nc.tensor), ("act", nc.scalar), ("dve", nc.vector), ("pool", nc.gpsimd), ("sp", nc.sync)):
            sem = es.enter_context(nc.semaphore("s_" + nm))
            self.E[nm] = Eng(nm, h, sem)
        self.dsem = [es.enter_context(nc.semaphore("d%d" % i)) for i in range(self.NDMA)]
        self.dval = [0] * self.NDMA
        self.dnext = {"sp": 0, "pool": 0, "act": 0}
        self.drange = {"sp": (0, 14), "pool": (14, 28), "act": (28, 32)}
        self.semobj = {}
        for e in self.E.values():
            self.semobj[("e", e.name)] = e.sem
        for i, s in enumerate(self.dsem):
            self.semobj[("d", i)] = s
        self.nins = 0

    def uid(self):
        self._uid += 1
        return self._uid

    def phase(self):
        return Phase(self)

    def _needs(self, reads, writes):
        needs = {}
        for t in reads:
            for k, v in t.w.items():
                if needs.get(k, 0) < v:
                    needs[k] = v
        for t in writes:
            for d in (t.w, t.r):
                for k, v in d.items():
                    if needs.get(k, 0) < v:
                        needs[k] = v
        return needs

    def _emit_waits(self, e, needs, skip_self=False):
        for k, v in needs.items():
            if skip_self and k == ("e", e.name):
                continue
            if e.seen.get(k, 0) >= v:
                continue
            if k[0] == "e":
                assert self.E[k[1]].count >= v, "wait on future event %s %d" % (k, v)
            e.seen[k] = v
            e.h.wait_ge(self.semobj[k], v)
            self.nins += 1

    def op(self, eng, fn, reads=(), writes=(), signal=True):
        e = self.E[eng]
        reads = [v.t for v in reads]
        writes = [v.t for v in writes]
        needs = self._needs(reads, writes)
        self._emit_waits(e, needs, skip_self=(eng == "pe"))
        ins = fn(e.h)
        self.nins += 1
        if signal:
            e.count += 1
            ins.then_inc(e.sem, 1)
            ev = e.count
        else:
            ev = e.count + 1
        key = ("e", eng)
        for t in writes:
            t.w = {key: ev}
            t.r = {}
        for t in reads:
            if t.r.get(key, 0) < ev:
                t.r[key] = ev

    def dma(self, q, out, in_, **kw):
        e = self.E[q]
        reads = [in_.t] if isinstance(in_, View) else []
        writes = [out.t] if isinstance(out, View) else []
        needs = self._needs(reads, writes)
        lo_, hi_ = self.drange[q]
        i = lo_ + self.dnext[q]
        self.dnext[q] = (self.dnext[q] + 1) % (hi_ - lo_)
        key = ("d", i)
        if self.dval[i] > 0 and needs.get(key, 0) < self.dval[i]:
            needs[key] = self.dval[i]
        self._emit_waits(e, needs)
        self.dval[i] += 16
        v = self.dval[i]
        oap = out.ap if isinstance(out, View) else out
        iap = in_.ap if isinstance(in_, View) else in_
        e.h.dma_start(out=oap, in_=iap, **kw).then_inc(self.dsem[i], 16)
        self.nins += 1
        for t in writes:
            t.w = {key: v}
            t.r = {}
        for t in reads:
            t.r[key] = v

    def barrier(self):
        allev = {}
        for e in self.E.values():
            if e.count > 0:
                allev[("e", e.name)] = e.count
        for i in range(self.NDMA):
            if self.dval[i] > 0:
                allev[("d", i)] = self.dval[i]
        for e in self.E.values():
            self._emit_waits(e, dict(allev))

    def ld(self, dst, src, q="sp", **kw):
        self.dma(q, dst, src, **kw)

    def ldn(self, dst, src, q="sp"):
        self.dma(q, dst, src, allow_slow_non_contiguous=True)

    def st(self, dst, src, q="pool"):
        self.dma(q, dst, src)

    def mm(self, out, lhsT, rhs, start=True, stop=True, sig=None):
        if sig is None:
            sig = stop
        self.op("pe", lambda h: h.matmul(out.ap, lhsT=lhsT.ap, rhs=rhs.ap, start=start, stop=stop),
                reads=[lhsT, rhs], writes=[out], signal=sig)

    def tr(self, out, in_, ident, sig=True):
        self.op("pe", lambda h: h.transpose(out=out.ap, in_=in_.ap, identity=ident.ap),
                reads=[in_, ident], writes=[out], signal=sig)

    def act(self, out, in_, func, bias=None, scale=None, accum=None):
        reads = [in_]
        kw = {}
        if bias is not None:
            if isinstance(bias, View):
                reads.append(bias)
                kw["bias"] = bias.ap
            else:
                kw["bias"] = float(bias)
        if scale is not None:
            if isinstance(scale, View):
                reads.append(scale)
                kw["scale"] = scale.ap
            else:
                kw["scale"] = float(scale)
        writes = [out]
        if accum is not None:
            kw["accum_out"] = accum.ap
            writes.append(accum)
        self.op("act", lambda h: h.activation(out=out.ap, in_=in_.ap, func=func, **kw), reads=reads, writes=writes)

    def tt(self, eng, out, in0, in1, op):
        self.op(eng, lambda h: h.tensor_tensor(out=out.ap, in0=in0.ap, in1=in1.ap, op=op), reads=[in0, in1], writes=[out])

    def ts(self, eng, out, in0, s1, s2, op0, op1=None):
        reads = [in0]
        a1 = s1.ap if isinstance(s1, View) else s1
        a2 = s2.ap if isinstance(s2, View) else s2
        if isinstance(s1, View):
            reads.append(s1)
        if isinstance(s2, View):
            reads.append(s2)
        if op1 is None:
            self.op(eng, lambda h: h.tensor_scalar(out=out.ap, in0=in0.ap, scalar1=a1, scalar2=None, op0=op0), reads=reads, writes=[out])
        else:
            self.op(eng, lambda h: h.tensor_scalar(out=out.ap, in0=in0.ap, scalar1=a1, scalar2=a2, op0=op0, op1=op1), reads=reads, writes=[out])

    def stt(self, out, in0, scalar, in1, op0, op1):
        reads = [in0, in1]
        sa = scalar.ap if isinstance(scalar, View) else scalar
        if isinstance(scalar, View):
            reads.append(scalar)
        self.op("dve", lambda h: h.scalar_tensor_tensor(out=out.ap, in0=in0.ap, scalar=sa, in1=in1.ap, op0=op0, op1=op1), reads=reads, writes=[out])

    def copy(self, eng, out, in_):
        if eng == "act":
            self.op("act", lambda h: h.activation(out=out.ap, in_=in_.ap, func=AF.Identity), reads=[in_], writes=[out])
        else:
            self.op(eng, lambda h: h.tensor_copy(out=out.ap, in_=in_.ap), reads=[in_], writes=[out])

    def recip(self, out, in_):
        self.op("dve", lambda h: h.reciprocal(out=out.ap, in_=in_.ap), reads=[in_], writes=[out])

    def memset(self, eng, out, val):
        self.op(eng, lambda h: h.memset(out.ap, val), writes=[out])

    def reduce(self, out, in_, op, axis=AX.X):
        self.op("dve", lambda h: h.tensor_reduce(out=out.ap, in_=in_.ap, axis=axis, op=op), reads=[in_], writes=[out])

    def scan(self, out, d0, d1, init, op0, op1):
        self.op("dve", lambda h: h.tensor_tensor_scan(out=out.ap, data0=d0.ap, data1=d1.ap, initial=init, op0=op0, op1=op1), reads=[d0, d1], writes=[out])


def _angles(pos, rot_dim, theta):
    inv = (theta ** (-np.arange(0, rot_dim, 2, dtype=np.float32) / rot_dim)).astype(np.float32)
    return pos.astype(np.float32)[:, None] * inv[None, :]


def make_consts():
    c = {}
    c["c_ident"] = np.eye(128, dtype=np.float32)
    T = TMAX
    pos = np.arange(T)
    rope = np.zeros((64, 6, T), np.float32)
    ar = _angles(pos // 64, 32, 10000.0)
    ac = _angles(pos % 64, 32, 10000.0)
    for d in range(64):
        a = ar if d < 32 else ac
        idx = (d % 32) % 16
        rope[d, 0] = np.cos(a[:, idx])
        rope[d, 1] = np.sin(a[:, idx])
    ab = _angles(pos, 8, 500000.0)
    for d in range(64):
        dd = d % 32
        if dd < 8:
            rope[d, 2] = np.cos(ab[:, dd % 4])
            rope[d, 3] = np.sin(ab[:, dd % 4])
        else:
            rope[d, 2] = 1.0
            rope[d, 3] = 0.0
    acc = _angles(pos, 64, 10000.0)
    for d in range(64):
        rope[d, 4] = np.cos(acc[:, d % 32])
        rope[d, 5] = np.sin(acc[:, d % 32])
    c["c_rope"] = rope
    pm = np.zeros((64, 3, 64), np.float32)
    for m in range(64):
        mm_ = m % 32
        base = m - mm_
        if mm_ < 16:
            pm[base + mm_ + 16, 0, m] = -1.0
        else:
            pm[base + mm_ - 16, 0, m] = 1.0
        if mm_ < 4:
            pm[base + mm_ + 4, 1, m] = -1.0
        elif mm_ < 8:
            pm[base + mm_ - 4, 1, m] = 1.0
        if m < 32:
            pm[m + 32, 2, m] = -1.0
        else:
            pm[m - 32, 2, m] = 1.0
    c["c_pm"] = pm
    c["c_ones"] = np.ones((128, 128), np.float32)
    h = np.arange(4, dtype=np.float32)
    lgf = np.log1p(-np.exp2(-5.0 - h)).astype(np.float32)
    lgb = lgf[::-1].copy()
    i = np.arange(128, dtype=np.float32)
    diff = i[None, :] - i[:, None]
    rm = np.zeros((128, 4, 128), np.float32)
    for hh in range(4):
        rm[:, hh, :] = np.where(diff >= 0, np.exp(lgf[hh] * np.maximum(diff, 0)), 0.0) + \
            np.where(diff <= 0, np.exp(lgb[hh] * np.maximum(-diff, 0)), 0.0)
    c["c_retmask"] = rm
    qd = np.zeros((64, 2, 4, 128), np.float32)
    kd = np.zeros((128, 2, 4), np.float32)
    for hh in range(4):
        qd[:, 0, hh, :] = np.exp(lgf[hh] * (i + 1.0))[None, :]
        qd[:, 1, hh, :] = np.exp(lgb[hh] * (128.0 - i))[None, :]
        kd[:, 0, hh] = np.exp(lgf[hh] * (127.0 - i))
        kd[:, 1, hh] = np.exp(lgb[hh] * i)
    c["c_qdec"] = qd
    c["c_kdec"] = kd
    dm = np.zeros((128, 2, 5, 128), np.float32)
    p = np.arange(128)[:, None]
    f = np.arange(128)[None, :]
    dm[:, 0, 0, :] = -1.0 * (p > f)
    dm[:, 0, 1, :] = -1.0 * (f > p)
    dm[:, 0, 2, :] = (f > p)
    dm[:, 0, 3, :] = (f >= p)
    dm[:, 0, 4, :] = -1.0 * (f >= p)
    dm[:, 1, 0, :] = -1.0 * (p < f)
    dm[:, 1, 1, :] = -1.0 * (f < p)
    dm[:, 1, 2, :] = (f < p)
    dm[:, 1, 3, :] = (f <= p)
    dm[:, 1, 4, :] = -1.0 * (f <= p)
    c["c_dmask"] = dm.astype(np.float32)
    seg = np.ones((64, 512), np.float32)
    seg[:, ::128] = 0.0
    c["c_seg"] = seg
    return c


RET_G = [1.0 - 2.0 ** (-5 - h) for h in range(4)]

WEIGHT_SPECS = [
    ("norm_mix_pre", (2, 1024)), ("norm_mix_post", (2, 1024)), ("norm_ffn_pre", (2, 1024)), ("norm_ffn_post", (2, 1024)),
    ("w_in", (2, 1024, 3392)), ("w_out", (2, 1024, 1024)), ("a_q_gain", (2, 64)), ("a_k_gain", (2, 64)),
    ("b_lambda", (2, 4, 32)), ("b_subln_gain", (2, 64)), ("c_gn_gain", (2, 256)), ("d_mu_prev", (2, 1088)),
    ("d_mu_next", (2, 1088)), ("d_w0", (2, 2, 256)), ("d_w_up", (2, 2, 64, 256)), ("d_a0", (2, 256)),
    ("d_a_up", (2, 64, 256)), ("d_g_up", (2, 128, 256)), ("d_k_k", (2, 256)), ("d_k_a", (2, 256)),
    ("d_r_k", (2, 4, 64)), ("d_gn_w", (2, 256)), ("d_gn_b", (2, 256)),
    ("ffn_w_gate", (2, 1024, 2816)), ("ffn_w_up", (2, 1024, 2816)), ("ffn_w_down", (2, 2816, 1024)),
]


class Ctx:
    pass


def load_consts(fw, C, ph):
    K = Ctx()
    K.ident = ph.sb([128, 128], BF16, "ident")
    fw.ld(K.ident[:], C.d["c_ident"], q="pool")
    K.identf = ph.sb([128, 128], F32, "identf")
    fw.ld(K.identf[:], C.d["c_ident"])
    K.ones_bf = ph.sb([128, 128], BF16, "ones_bf")
    fw.ld(K.ones_bf[:], C.d["c_ones"], q="pool")
    K.ones_f = ph.sb([128, 128], F32, "ones_f")
    fw.ld(K.ones_f[:], C.d["c_ones"])
    K.pm = []
    for i in range(3):
        t = ph.sb([64, 64], BF16, "pm%d" % i)
        fw.ld(t[:], C.d["c_pm"][:, i, :], q="pool")
        K.pm.append(t)
    return K


def rms_rows(fw, xt, gain, hb, tmp, ss, rstd):
    fw.act(tmp[:], xt[:], AF.Square, accum=ss[:])
    fw.act(rstd[:], ss[:], AF.Sqrt, bias=EPS, scale=1.0 / D_MODEL)
    fw.recip(rstd[:], rstd[:])
    fw.stt(hb[:], xt[:], rstd[:, 0:1], gain[:], ALU.mult, ALU.mult)


def make_hT(fw, K, src_rows, gain, hT, j, P):
    xt = P.xt.next()
    fw.ld(xt[:], src_rows)
    hb = P.hb.next()
    ss = P.ss.next()
    rstd = P.rstd.next()
    tmp = P.tmp.next()
    rms_rows(fw, xt, gain, hb, tmp, ss, rstd)
    tp = P.tp.next()
    for k in range(8):
        fw.tr(tp[:, k, :], hb[:, k * 128:(k + 1) * 128], K.ident[:], sig=(k == 7))
    fw.copy("dve", hT[:, :, j * 128:(j + 1) * 128], tp[:])
    return xt


def phase_inproj(fw, C, K, l, si):
    T_ = C.seqs[si]
    src = C.xin[si] if l == 0 else C.y[si]
    d = C.d
    ph = fw.phase()
    W = ph.sb([128, 8, IN_COLS], BF16, "W")
    for k in range(8):
        fw.ld(W[:, k, :], d["w_in"][l, k * 128:(k + 1) * 128, :], q="pool")
    gain = ph.sb([128, 1024], F32, "gain")
    fw.ld(gain[:], d["norm_mix_pre"][l].partition_broadcast(128))
    gqk = ph.sb([64, 2], F32, "gqk")
    fw.ld(gqk[:, 0:1], d["a_q_gain"][l].rearrange("(p o) -> p o", o=1))
    fw.ld(gqk[:, 1:2], d["a_k_gain"][l].rearrange("(p o) -> p o", o=1))
    P = Ctx()
    P.xt = ph.sbpool(2, [128, 1024], F32, "xt")
    P.hb = ph.sbpool(2, [128, 1024], BF16, "hb")
    P.tmp = ph.sbpool(1, [128, 1024], F32, "tmp")
    P.ss = ph.sbpool(2, [128, 1], F32, "ss")
    P.rstd = ph.sbpool(2, [128, 1], F32, "rstd")
    P.tp = ph.pspool(1, [128, 8, 128], BF16, "tp")
    hTs = ph.sbpool(2, [128, 8, 512], BF16, "hT")
    tabs = ph.sbpool(2, [64, 6, 512], F32, "tab")
    zps = ph.pspool(3, [128, 512], F32, "zp")
    aps = ph.pspool(2, [128, 512], F32, "ap")
    tps = ph.pspool(2, [128, 512], F32, "tmps")
    f32p = ph.sbpool(6, [64, 512], F32, "f32p")
    bfp = ph.sbpool(6, [64, 512], BF16, "bfp")
    f32w = ph.sbpool(3, [128, 512], F32, "f32w")
    bfw = ph.sbpool(3, [128, 512], BF16, "bfw")
    chunks = []
    for h in range(4):
        chunks.append(("A", 0 + 64 * h, C.QA, h, 0))
    for h in range(2):
        chunks.append(("A", 256 + 64 * h, C.KA, h, 1))
    for h in range(4):
        chunks.append(("B", 512 + 64 * h, C.QB, h, 0))
    for h in range(4):
        chunks.append(("B", 768 + 64 * h, C.KB, h, 0))
    for h in range(4):
        chunks.append(("C", 1280 + 64 * h, C.QC, h, 0))
    for h in range(4):
        chunks.append(("C", 1536 + 64 * h, C.KC, h, 0))
    for h in range(4):
        chunks.append(("G", 2048 + 64 * h, C.GC, h, 0))
    for j in range(15):
        chunks.append(("D", 2304 + 64 * j, C.ZD, j, 0))
    import os
    STOP = int(os.environ.get("DBG_STOP", "99"))
    if STOP <= 1:
        ph.close()
        return
    for b in range(T_ // 512):
        t0 = b * 512
        tab = tabs.next()
        fw.ld(tab[:], d["c_rope"][:, :, t0:t0 + 512])
        hT = hTs.next()
        if STOP <= 2:
            continue
        for j in range(4):
            make_hT(fw, K, src[t0 + j * 128:t0 + (j + 1) * 128, :], gain, hT, j, P)
        if STOP <= 3:
            continue
        kinds_ok = os.environ.get("DBG_KINDS", "ABCGD")
        for (kind, col, dst, slot, gi) in chunks:
            if kind not in kinds_ok:
                continue
            zp = zps.next()
            for k in range(8):
                fw.mm(zp[0:64, :], W[:, k, col:col + 64], hT[:, k, :], start=(k == 0), stop=(k == 7))
            z = zp[0:64, :]
            if kind in ("A", "B", "C"):
                ti = {"A": 0, "B": 2, "C": 4}[kind]
                pmi = {"A": 0, "B": 1, "C": 2}[kind]
                qn = bfp.next()
                if kind == "A":
                    sq = bfp.next()
                    fw.act(sq[:], z, AF.Square)
                    ms = aps.next()
                    fw.mm(ms[0:64, :], K.ones_bf[0:64, 0:64], sq[:])
                    rs = f32p.next()
                    fw.act(rs[:], ms[0:64, :], AF.Sqrt, bias=EPS, scale=1.0 / 64)
                    fw.recip(rs[:], rs[:])
                    qf = f32p.next()
                    fw.stt(qf[:], z, gqk[:, gi:gi + 1], rs[:], ALU.mult, ALU.mult)
                    fw.copy("act", qn[:], qf[:])
                    src_f = qf[:]
                else:
                    qf = f32p.next()
                    fw.copy("act", qf[:], z)
                    fw.copy("act", qn[:], qf[:])
                    src_f = qf[:]
                pq = aps.next()
                fw.mm(pq[0:64, :], K.pm[pmi][:], qn[:])
                t1 = f32p.next()
                fw.tt("dve", t1[:], src_f, tab[:, ti, :], ALU.mult)
                t2 = f32p.next()
                fw.tt("dve", t2[:], pq[0:64, :], tab[:, ti + 1, :], ALU.mult)
                ob = bfp.next()
                fw.tt("pool", ob[:], t1[:], t2[:], ALU.add)
                fw.st(dst[:, slot, t0:t0 + 512], ob[:])
            elif kind == "G":
                ob = bfp.next()
                fw.act(ob[:], z, AF.Silu)
                fw.st(dst[:, slot, t0:t0 + 512], ob[:])
            else:
                of = f32p.next()
                fw.copy("act", of[:], z)
                fw.st(dst[:, slot, t0:t0 + 512], of[:])
        if STOP <= 8:
            continue
        zp = zps.next()
        for k in range(8):
            fw.mm(zp[:, :], W[:, k, 3264:3392], hT[:, k, :], start=(k == 0), stop=(k == 7))
        of = f32w.next()
        fw.copy("act", of[:], zp[:])
        fw.st(C.ZG[:, t0:t0 + 512], of[:])
        for j in range(4):
            tp_ = tps.next()
            for k in range(8):
                fw.mm(tp_[:, 0:128], hT[:, k, j * 128:(j + 1) * 128], W[:, k, 384:512], start=(k == 0), stop=(k == 7), sig=False)
            for k in range(8):
                fw.mm(tp_[:, 128:384], hT[:, k, j * 128:(j + 1) * 128], W[:, k, 1024:1280], start=(k == 0), stop=(k == 7))
            ob = bfw.next()
            fw.copy("act", ob[:, 0:384], tp_[:, 0:384])
            r0 = t0 + j * 128
            fw.st(C.VA[r0:r0 + 128, :], ob[:, 0:128])
            fw.st(C.VB[r0:r0 + 128, :], ob[:, 128:384])
            tp2 = tps.next()
            for k in range(8):
                fw.mm(tp2[:, 0:256], hT[:, k, j * 128:(j + 1) * 128], W[:, k, 1792:2048], start=(k == 0), stop=(k == 7))
            ob2 = bfw.next()
            fw.copy("dve", ob2[:, 0:256], tp2[:, 0:256])
            fw.st(C.VC[r0:r0 + 128, :], ob2[:, 0:256])
    ph.close()


def attn_core(fw, ph, K, T_, QT, KT, V, heads, negM, scale, emit):
    LOOK = 2
    sps = ph.pspool(4, [128, 512], F32, "sps")
    ops_ = ph.pspool(2, [128, 512], F32, "ops")
    pts = ph.sbpool(4, [128, 512], BF16, "pt")
    osbs = ph.sbpool(4, [65, 512], F32, "osb")
    nkt = T_ // 128
    iters = []
    for qb in range(T_ // 512):
        for hd in heads:
            for kt in range(nkt):
                iters.append((qb, hd, kt))
    pend = []
    state = {"o_ps": None}

    def do_pv(item):
        (qb, hd, kt, pt) = item
        (qs, lo, hi, ks, vs, tag) = hd
        if kt == 0:
            state["o_ps"] = ops_.next()
        o_ps = state["o_ps"]
        fw.mm(o_ps[0:65, :], V[:, kt, vs, :], pt[:], start=(kt == 0), stop=(kt == nkt - 1))
        if kt == nkt - 1:
            osb = osbs.next()
            fw.copy("dve", osb[:], o_ps[0:65, :])
            emit(qb, tag, osb)

    for (qb, hd, kt) in iters:
        (qs, lo, hi, ks, vs, tag) = hd
        q0 = qb * 512
        st = sps.next()
        fw.mm(st[:, :], KT[lo:hi, ks, kt * 128:(kt + 1) * 128], QT[lo:hi, qs, q0:q0 + 512])
        pt = pts.next()
        if negM is None:
            fw.act(pt[:], st[:], AF.Exp, scale=scale)
        else:
            fw.act(pt[:], st[:], AF.Exp, scale=scale, bias=negM[:, 0:1])
        pend.append((qb, hd, kt, pt))
        if len(pend) > LOOK:
            do_pv(pend.pop(0))
    while pend:
        do_pv(pend.pop(0))


def attn_core2(fw, ph, T_, groups, V, negM, scale, emit, bcs):
    LOOK = 2
    sps = ph.pspool(3, [128, 1024], F32, "sps")
    ops_ = bcs
    pts = ph.sbpool(4, [128, 1024], BF16, "pt")
    osbs = ph.sbpool(4, [65, 512], F32, "osb")
    nkt = T_ // 128
    iters = []
    for qb in range(T_ // 512):
        for g in groups:
            for kt in range(nkt):
                iters.append((qb, g, kt))
    pend = []
    state = {}
    import os
    NW = int(os.environ.get("DBG_WARM", "0"))
    if NW:
        wz = pts.next()
        fw.memset("pool", wz[:], 1.0)
        st0 = sps.next()
        for i in range(NW):
            fw.mm(st0[:, 0:512], wz[:, 0:128], wz[:, 512:1024], sig=(i == NW - 1))

    def do_pv(item):
        (qb, g, kt, pt) = item
        (kt_ap, q_ap, vs, tag) = g
        if kt == 0:
            state["o"] = [ops_.next(), ops_.next()]
        o = state["o"]
        for j in range(2):
            fw.mm(o[j][0:65, :], V[:, kt, vs[j], :], pt[:, j * 512:(j + 1) * 512], start=(kt == 0), stop=(kt == nkt - 1))
        if kt == nkt - 1:
            osb = [osbs.next(), osbs.next()]
            for j in range(2):
                fw.copy("dve", osb[j][:], o[j][0:65, :])
            emit(qb, tag, osb)

    for (qb, g, kt) in iters:
        (kt_ap, q_ap, vs, tag) = g
        st = sps.next()
        for j in range(2):
            fw.mm(st[:, j * 512:(j + 1) * 512], kt_ap(j, kt), q_ap(j, qb), sig=(j == 1))
        pt = pts.next()
        if negM is None:
            fw.act(pt[:], st[:], AF.Exp, scale=scale)
        else:
            fw.act(pt[:], st[:], AF.Exp, scale=scale, bias=negM[:, 0:1])
        pend.append((qb, g, kt, pt))
        if len(pend) > LOOK:
            do_pv(pend.pop(0))
    while pend:
        do_pv(pend.pop(0))


def bcast_recip(fw, K, osb, recs, hls, bcs, mulv):
    rec = recs.next()
    fw.act(rec[64:65, :], osb[64:65, :], AF.Ln)
    fw.act(rec[64:65, :], rec[64:65, :], AF.Exp, scale=-1.0)
    if mulv is not None:
        fw.ts("dve", rec[64:65, :], rec[64:65, :], mulv[64:65, 0:1], None, ALU.mult)
    hl = hls.next()
    fw.copy("dve", hl[64:65, 0, :], rec[64:65, :])
    fw.tt("dve", hl[64:65, 1, :], rec[64:65, :], hl[64:65, 0, :], ALU.subtract)
    bc = bcs.next()
    fw.mm(bc[0:64, :], K.ones_bf[64:65, 0:64], hl[64:65, 0, :], start=True, stop=False, sig=False)
    fw.mm(bc[0:64, :], K.ones_bf[64:65, 0:64], hl[64:65, 1, :], start=False, stop=True)
    return bc


def phase_attnA(fw, C, K, l, si):
    T_ = C.seqs[si]
    d = C.d
    ph = fw.phase()
    QT = ph.sb([128, 2, T_], BF16, "QT")
    KT = ph.sb([128, T_], BF16, "KT")
    V = ph.sb([128, T_ // 128, 2, 65], BF16, "V")
    for p in range(2):
        fw.ld(QT[0:64, p, :], C.QA[:, p, 0:T_])
        fw.ld(QT[64:128, p, :], C.QA[:, p + 2, 0:T_])
    for h in range(2):
        fw.ld(KT[64 * h:64 * h + 64, :], C.KA[:, h, 0:T_])
    fw.memset("pool", V[:, :, :, 64:65], 1.0)
    for h in range(2):
        fw.ld(V[:, :, h, 0:64], C.VA[0:T_, h * 64:(h + 1) * 64].rearrange("(n p) c -> p n c", p=128))
    g2 = ph.sb([128, 2, 64], F32, "g2")
    fw.ld(g2[:, 0, :], d["a_q_gain"][l].partition_broadcast(128))
    fw.ld(g2[:, 1, :], d["a_k_gain"][l].partition_broadcast(128))
    fw.tt("dve", g2[:], g2[:], g2[:], ALU.mult)
    mx = ph.sb([128, 2], F32, "mx")
    fw.reduce(mx[:], g2[:], ALU.max)
    negM = ph.sb([128, 1], F32, "negM")
    fw.tt("dve", negM[:], mx[:, 0:1], mx[:, 1:2], ALU.mult)
    fw.act(negM[:], negM[:], AF.Sqrt, scale=64.0)
    fw.ts("dve", negM[:], negM[:], -1.0, None, ALU.mult)
    recs = ph.sbpool(2, [65, 512], F32, "rec")
    hls = ph.sbpool(2, [65, 2, 512], BF16, "hl")
    bcs = ph.pspool(2, [128, 512], F32, "bc")
    outs = ph.sbpool(3, [64, 512], BF16, "ob")

    def emit(qb, p, osbs_):
        for j in range(2):
            h = p + 2 * j
            osb = osbs_[j]
            bc = bcast_recip(fw, K, osb, recs, hls, bcs, None)
            ob = outs.next()
            fw.tt("dve", ob[:], osb[0:64, :], bc[0:64, :], ALU.mult)
            fw.st(C.OT[h, :, qb * 512:(qb + 1) * 512], ob[:])

    groups = []
    for p in range(2):
        groups.append((lambda j, kt: KT[64 * j:64 * j + 64, kt * 128:(kt + 1) * 128],
                       (lambda p_: (lambda j, qb: QT[64 * j:64 * j + 64, p_, qb * 512:(qb + 1) * 512]))(p),
                       (0, 1), p))
    attn_core2(fw, ph, T_, groups, V, negM, 0.125, emit, bcs)
    ph.close()


def phase_attnB(fw, C, K, l, si):
    T_ = C.seqs[si]
    d = C.d
    lam_init = 0.8 - 0.6 * math.exp(-0.3 * l)
    ph = fw.phase()
    QT = ph.sb([64, 4, T_], BF16, "QT")
    KT = ph.sb([64, 4, T_], BF16, "KT")
    V = ph.sb([128, T_ // 128, 4, 65], BF16, "V")
    for h in range(4):
        fw.ld(QT[:, h, :], C.QB[:, h, 0:T_])
        fw.ld(KT[:, h, :], C.KB[:, h, 0:T_])
    fw.memset("pool", V[:, :, :, 64:65], 1.0)
    for h in range(4):
        fw.ld(V[:, :, h, 0:64], C.VB[0:T_, h * 64:(h + 1) * 64].rearrange("(n p) c -> p n c", p=128))
    lp = ph.sb([128, 4, 32], F32, "lp")
    fw.ld(lp[:], d["b_lambda"][l].partition_broadcast(128))
    pr = ph.sb([128, 2, 32], F32, "pr")
    fw.tt("dve", pr[:, 0, :], lp[:, 0, :], lp[:, 1, :], ALU.mult)
    fw.tt("dve", pr[:, 1, :], lp[:, 2, :], lp[:, 3, :], ALU.mult)
    sm = ph.sb([128, 2], F32, "sm")
    fw.reduce(sm[:], pr[:], ALU.add)
    fw.act(sm[:], sm[:], AF.Exp)
    nlam = ph.sb([128, 1], F32, "nlam")
    fw.tt("dve", nlam[:], sm[:, 1:2], sm[:, 0:1], ALU.subtract)
    fw.ts("dve", nlam[:], nlam[:], -lam_init, None, ALU.add)
    gsub = ph.sb([64, 1], F32, "gsub")
    fw.ld(gsub[:], d["b_subln_gain"][l].rearrange("(p o) -> p o", o=1))
    fw.ts("dve", gsub[:], gsub[:], 1.0 - lam_init, None, ALU.mult)
    recs = ph.sbpool(2, [65, 512], F32, "rec")
    hls = ph.sbpool(2, [65, 2, 512], BF16, "hl")
    bcs = ph.pspool(2, [128, 512], F32, "bc")
    o1s = ph.sbpool(4, [64, 512], F32, "o1")
    rss = ph.sbpool(2, [64, 512], F32, "rsb")
    sqs = ph.sbpool(2, [64, 512], BF16, "sq")
    outs = ph.sbpool(3, [64, 512], BF16, "ob")

    def emit(qb, h, osbs_):
        oo = []
        for s_ in range(2):
            osb = osbs_[s_]
            bc = bcast_recip(fw, K, osb, recs, hls, bcs, nlam if s_ == 1 else None)
            o_ = o1s.next()
            fw.tt("dve", o_[:], osb[0:64, :], bc[0:64, :], ALU.mult)
            oo.append(o_)
        o1, o2 = oo
        fw.tt("pool", o1[:], o1[:], o2[:], ALU.add)
        sq = sqs.next()
        fw.act(sq[:], o1[:], AF.Square)
        ms = bcs.next()
        fw.mm(ms[0:64, :], K.ones_bf[0:64, 0:64], sq[:])
        rs = rss.next()
        fw.ts("dve", rs[:], ms[0:64, :], 1.0 / 64, EPS, ALU.mult, ALU.add)
        fw.act(rs[:], rs[:], AF.Ln)
        fw.act(rs[:], rs[:], AF.Exp, scale=-0.5)
        ob = outs.next()
        fw.stt(ob[:], o1[:], gsub[:, 0:1], rs[:], ALU.mult, ALU.mult)
        fw.st(C.OT[4 + h, :, qb * 512:(qb + 1) * 512], ob[:])

    groups = []
    for h in range(4):
        groups.append(((lambda h_: (lambda j, kt: KT[32 * j:32 * j + 32, h_, kt * 128:(kt + 1) * 128]))(h),
                       (lambda h_: (lambda j, qb: QT[32 * j:32 * j + 32, h_, qb * 512:(qb + 1) * 512]))(h),
                       (h, h), h))
    attn_core2(fw, ph, T_, groups, V, None, 32 ** -0.5, emit, bcs)
    ph.close()


def phase_ret(fw, C, K, l, si):
    T_ = C.seqs[si]
    d = C.d
    n = T_ // 128
    ph = fw.phase()
    QT = ph.sb([64, 4, T_], BF16, "QT")
    KT = ph.sb([64, 4, T_], BF16, "KT")
    G = ph.sb([64, 4, T_], BF16, "G")
    V = ph.sb([128, n, 256], BF16, "V")
    for h in range(4):
        fw.ld(QT[:, h, :], C.QC[:, h, 0:T_])
        fw.ld(KT[:, h, :], C.KC[:, h, 0:T_])
        fw.ld(G[:, h, :], C.GC[:, h, 0:T_])
    fw.ld(V[:], C.VC[0:T_, :].rearrange("(n p) c -> p n c", p=128))
    mask = ph.sb([128, 4, 128], F32, "mask")
    fw.ld(mask[:], d["c_retmask"])
    qdec = ph.sb([64, 2, 4, 128], F32, "qdec")
    fw.ld(qdec[:], d["c_qdec"])
    kdec = ph.sb([128, 2, 4], F32, "kdec")
    fw.ld(kdec[:], d["c_kdec"])
    gng = ph.sb([64, 4], F32, "gng")
    fw.ldn(gng[:], d["c_gn_gain"][l].rearrange("(h p) -> p h", p=64))
    Sf = ph.sb([64, n, 4, 64], BF16, "Sf")
    Sb = ph.sb([64, n, 4, 64], BF16, "Sb")
    cur = [ph.sb([64, 4, 64], F32, "curf"), ph.sb([64, 4, 64], F32, "curb")]
    ktp = ph.pspool(1, [128, 4, 64], BF16, "ktp")
    ktm = ph.sbpool(2, [128, 4, 64], BF16, "ktm")
    kds = ph.sbpool(2, [128, 4, 64], BF16, "kds")
    kvp = ph.pspool(2, [128, 512], F32, "kvp")
    gam = [[RET_G[h] ** 128 for h in range(4)], [RET_G[3 - h] ** 128 for h in range(4)]]
    for dr in range(2):
        fw.memset("dve", cur[dr][:], 0.0)
        order = range(n) if dr == 0 else range(n - 1, -1, -1)
        Sx = Sf if dr == 0 else Sb
        for c in order:
            fw.copy("act", Sx[:, c, :, :], cur[dr][:])
            if (dr == 0 and c == n - 1) or (dr == 1 and c == 0):
                break
            tp = ktp.next()
            for h in range(4):
                fw.tr(tp[:, h, :], KT[:, h, c * 128:(c + 1) * 128], K.ident[0:64, 0:64], sig=(h == 3))
            kd = kds.next()
            fw.tt("dve", kd[:], tp[:], kdec.bc((slice(None), dr, slice(None)), 2, [128, 4, 64]), ALU.mult)
            kv = kvp.next()
            for h in range(4):
                fw.mm(kv[0:64, h * 64:(h + 1) * 64], kd[:, h, :], V[:, c, h * 64:(h + 1) * 64], sig=(h == 3))
            for h in range(4):
                fw.stt(cur[dr][:, h, :], cur[dr][:, h, :], gam[dr][h], kv[0:64, h * 64:(h + 1) * 64], ALU.mult, ALU.add)
    inp = ph.pspool(2, [128, 512], F32, "inp")
    ins_ = ph.sbpool(2, [128, 4, 128], BF16, "ins")
    qds = ph.sbpool(2, [64, 2, 4, 128], BF16, "qds")
    otp = ph.pspool(2, [128, 512], F32, "otp")
    osb = ph.sbpool(2, [64, 4, 128], F32, "osb")
    sqs = ph.sbpool(2, [64, 4, 128], BF16, "sq")
    msp = ph.pspool(1, [128, 512], F32, "msp")
    rss = ph.sbpool(2, [64, 4, 128], F32, "rs")
    obs = ph.sbpool(2, [64, 4, 128], BF16, "ob")
    for c in range(n):
        cs = slice(c * 128, (c + 1) * 128)
        ip = inp.next()
        for h in range(4):
            fw.mm(ip[:, h * 128:(h + 1) * 128], KT[:, h, cs], QT[:, h, cs], sig=(h == 3))
        it = ins_.next()
        fw.tt("dve", it[:], View(ip, ip.ap[:, :].rearrange("p (h i) -> p h i", h=4)), mask[:], ALU.mult)
        qd = qds.next()
        for dr in range(2):
            fw.tt("pool", qd[:, dr, :, :], QT[:, :, cs], qdec[:, dr, :, :], ALU.mult)
        op_ = otp.next()
        for h in range(4):
            o_h = op_[0:64, h * 128:(h + 1) * 128]
            fw.mm(o_h, V[:, c, h * 64:(h + 1) * 64], it[:, h, :], start=True, stop=False, sig=False)
            fw.mm(o_h, Sf[:, c, h, :], qd[:, 0, h, :], start=False, stop=False, sig=False)
            fw.mm(o_h, Sb[:, c, h, :], qd[:, 1, h, :], start=False, stop=True, sig=(h == 3))
        o = osb.next()
        fw.ts("dve", o[:], View(op_, op_.ap[0:64, :].rearrange("p (h i) -> p h i", h=4)), 0.125, None, ALU.mult)
        sq = sqs.next()
        fw.act(sq[:], o[:], AF.Square)
        ms = msp.next()
        fw.mm(ms[0:64, :], K.ones_bf[0:64, 0:64], View(sq, sq.ap[:].rearrange("p h i -> p (h i)")))
        rs = rss.next()
        fw.act(View(rs, rs.ap[:].rearrange("p h i -> p (h i)")), ms[0:64, :], AF.Sqrt, bias=EPS, scale=1.0 / 64)
        fw.recip(rs[:], rs[:])
        fw.tt("dve", o[:], o[:], rs[:], ALU.mult)
        fw.tt("pool", o[:], o[:], gng.bc((slice(None), slice(None)), 2, [64, 4, 128]), ALU.mult)
        ob = obs.next()
        fw.tt("dve", ob[:], o[:], G[:, :, cs], ALU.mult)
        for h in range(4):
            fw.st(C.OT[8 + h, :, cs], ob[:, h, :])
    ph.close()


def phase_outproj(fw, C, K, l, si):
    T_ = C.seqs[si]
    d = C.d
    src = C.xin[si] if l == 0 else C.y[si]
    ph = fw.phase()
    W = ph.sb([128, 8, 1024], BF16, "Wo")
    for k in range(8):
        fw.ld(W[:, k, :], d["w_out"][l, k * 128:(k + 1) * 128, :], q="pool")
    gain = ph.sb([128, 1024], F32, "gain")
    fw.ld(gain[:], d["norm_mix_post"][l].partition_broadcast(128))
    oTs = ph.sbpool(2, [128, 8, 512], BF16, "oT")
    xts = ph.sbpool(2, [128, 1024], F32, "xt")
    mps = ph.pspool(4, [128, 512], F32, "mp")
    tmps = ph.sbpool(2, [128, 1024], F32, "tmp")
    sss = ph.sbpool(2, [128, 2], F32, "ss")
    rstds = ph.sbpool(2, [128, 1], F32, "rstd")
    for b in range(T_ // 512):
        t0 = b * 512
        oT = oTs.next()
        for k in range(8):
            fw.ld(oT[0:64, k, :], C.OT[2 * k, :, t0:t0 + 512])
            fw.ld(oT[64:128, k, :], C.OT[2 * k + 1, :, t0:t0 + 512])
        for j in range(4):
            r0 = t0 + j * 128
            xt = xts.next()
            fw.ld(xt[:], src[r0:r0 + 128, :])
            m = [mps.next(), mps.next()]
            for half in range(2):
                for k in range(8):
                    fw.mm(m[half][:, :], oT[:, k, j * 128:(j + 1) * 128], W[:, k, half * 512:(half + 1) * 512], start=(k == 0), stop=(k == 7))
            tmp = tmps.next()
            ss = sss.next()
            for half in range(2):
                fw.act(tmp[:, half * 512:(half + 1) * 512], m[half][:, :], AF.Square, accum=ss[:, half:half + 1])
            rstd = rstds.next()
            fw.tt("dve", rstd[:], ss[:, 0:1], ss[:, 1:2], ALU.add)
            fw.act(rstd[:], rstd[:], AF.Sqrt, bias=EPS, scale=1.0 / D_MODEL)
            fw.recip(rstd[:], rstd[:])
            for half in range(2):
                hs = slice(half * 512, (half + 1) * 512)
                fw.stt(tmp[:, hs], m[half][:, :], rstd[:, 0:1], gain[:, hs], ALU.mult, ALU.mult)
            fw.tt("pool", tmp[:], tmp[:], xt[:], ALU.add)
            fw.st(C.y[si][r0:r0 + 128, :], tmp[:])
    ph.close()


def phase_ffn(fw, C, K, l, si):
    T_ = C.seqs[si]
    d = C.d
    TB = 1024
    ph = fw.phase()
    gain = ph.sb([128, 1024], F32, "gain")
    fw.ld(gain[:], d["norm_ffn_pre"][l].partition_broadcast(128))
    gpost = ph.sb([128, 1024], F32, "gpost")
    fw.ld(gpost[:], d["norm_ffn_post"][l].partition_broadcast(128))
    P = Ctx()
    P.xt = ph.sbpool(2, [128, 1024], F32, "xt")
    P.hb = ph.sbpool(2, [128, 1024], BF16, "hb")
    P.tmp = ph.sbpool(2, [128, 1024], F32, "tmp")
    P.ss = ph.sbpool(2, [128, 1], F32, "ss")
    P.rstd = ph.sbpool(2, [128, 1], F32, "rstd")
    P.tp = ph.pspool(1, [128, 8, 128], BF16, "tp")
    hT = ph.sb([128, 8, TB], BF16, "hT")
    act = ph.sb([128, NFF, TB], BF16, "act")
    wgs = ph.sbpool(3, [128, 8, 128], BF16, "wg")
    wus = ph.sbpool(3, [128, 8, 128], BF16, "wu")
    Wd = ph.sb([128, NFF, 1024], BF16, "Wd")
    for f in range(NFF):
        fw.ld(Wd[:, f, :], d["ffn_w_down"][l, f * 128:(f + 1) * 128, :], q="pool")
    gps = ph.pspool(2, [128, 512], F32, "gp")
    ups = ph.pspool(2, [128, 512], F32, "up")
    dps = ph.pspool(2, [128, 512], F32, "dp")
    sgs = ph.sbpool(2, [128, 512], F32, "sg")
    sss = ph.sbpool(2, [128, 2], F32, "ss2")
    wg_d = d["ffn_w_gate"][l].rearrange("(k p) n -> p k n", p=128)
    wu_d = d["ffn_w_up"][l].rearrange("(k p) n -> p k n", p=128)
    for b in range(T_ // TB):
        t0 = b * TB
        for j in range(TB // 128):
            make_hT(fw, K, C.y[si][t0 + j * 128:t0 + (j + 1) * 128, :], gain, hT, j, P)
        for f in range(NFF):
            wg = wgs.next()
            fw.ld(wg[:], wg_d[:, :, f * 128:(f + 1) * 128], q="pool")
            wu = wus.next()
            fw.ld(wu[:], wu_d[:, :, f * 128:(f + 1) * 128], q="pool")
            for tb in range(TB // 512):
                ts_ = slice(tb * 512, (tb + 1) * 512)
                gp = gps.next()
                up = ups.next()
                for k in range(8):
                    fw.mm(gp[:, :], wg[:, k, :], hT[:, k, ts_], start=(k == 0), stop=(k == 7))
                for k in range(8):
                    fw.mm(up[:, :], wu[:, k, :], hT[:, k, ts_], start=(k == 0), stop=(k == 7))
                sg = sgs.next()
                fw.act(sg[:], gp[:, :], AF.Silu)
                fw.tt("dve", act[:, f, ts_], sg[:], up[:, :], ALU.mult)
        ph_all = Pool(dps.items + gps.items + ups.items)
        for g0 in range(0, TB // 128, 3):
            subs = list(range(g0, min(g0 + 3, TB // 128)))
            banks = {(j, half): ph_all.next() for j in subs for half in range(2)}
            for f in range(NFF):
                for j in subs:
                    for half in range(2):
                        fw.mm(banks[(j, half)][:, :], act[:, f, j * 128:(j + 1) * 128], Wd[:, f, half * 512:(half + 1) * 512],
                              start=(f == 0), stop=(f == NFF - 1))
            for j in subs:
                r0 = t0 + j * 128
                xt = P.xt.next()
                fw.ld(xt[:], C.y[si][r0:r0 + 128, :])
                tmp = P.tmp.next()
                ss = sss.next()
                for half in range(2):
                    fw.act(tmp[:, half * 512:(half + 1) * 512], banks[(j, half)][:, :], AF.Square, accum=ss[:, half:half + 1])
                rstd = P.rstd.next()
                fw.tt("dve", rstd[:], ss[:, 0:1], ss[:, 1:2], ALU.add)
                fw.act(rstd[:], rstd[:], AF.Sqrt, bias=EPS, scale=1.0 / D_MODEL)
                fw.recip(rstd[:], rstd[:])
                for half in range(2):
                    hs = slice(half * 512, (half + 1) * 512)
                    fw.stt(tmp[:, hs], banks[(j, half)][:, :], rstd[:, 0:1], gpost[:, hs], ALU.mult, ALU.mult)
                fw.tt("pool", tmp[:], tmp[:], xt[:], ALU.add)
                fw.st(C.y[si][r0:r0 + 128, :], tmp[:])
    ph.close()


def phase_rwkv(fw, C, K, l, si):
    T_ = C.seqs[si]
    d = C.d
    n = T_ // 128
    CW = math.exp(-0.5)
    ph = fw.phase()
    mu = ph.sb([64, 3, 15], F32, "mu")
    fw.ldn(mu[:, 1, :], d["d_mu_prev"][l, 0:960].rearrange("(j p) -> p j", p=64))
    fw.ldn(mu[:, 2, :], d["d_mu_next"][l, 0:960].rearrange("(j p) -> p j", p=64))
    fw.tt("dve", mu[:, 0, :], mu[:, 1, :], mu[:, 2, :], ALU.add)
    fw.ts("dve", mu[:, 0, :], mu[:, 0, :], -1.0, 1.0, ALU.mult, ALU.add)
    mug = ph.sb([128, 3], F32, "mug")
    fw.ld(mug[:, 1:2], d["d_mu_prev"][l, 960:1088].rearrange("(p o) -> p o", o=1))
    fw.ld(mug[:, 2:3], d["d_mu_next"][l, 960:1088].rearrange("(p o) -> p o", o=1))
    fw.tt("dve", mug[:, 0:1], mug[:, 1:2], mug[:, 2:3], ALU.add)
    fw.ts("dve", mug[:, 0:1], mug[:, 0:1], -1.0, 1.0, ALU.mult, ALU.add)
    prm = ph.sb([64, 9, 4], F32, "prm")
    fw.ldn(prm[:, 0, :], d["d_w0"][l, 0].rearrange("(h p) -> p h", p=64))
    fw.ldn(prm[:, 1, :], d["d_w0"][l, 1].rearrange("(h p) -> p h", p=64))
    fw.ldn(prm[:, 2, :], d["d_a0"][l].rearrange("(h p) -> p h", p=64))
    fw.ldn(prm[:, 3, :], d["d_k_k"][l].rearrange("(h p) -> p h", p=64))
    fw.ldn(prm[:, 4, :], d["d_k_a"][l].rearrange("(h p) -> p h", p=64))
    fw.ldn(prm[:, 6, :], d["d_r_k"][l].rearrange("h p -> p h"))
    fw.ldn(prm[:, 7, :], d["d_gn_w"][l].rearrange("(h p) -> p h", p=64))
    fw.ldn(prm[:, 8, :], d["d_gn_b"][l].rearrange("(h p) -> p h", p=64))
    fw.ts("dve", prm[:, 5, :], prm[:, 4, :], -1.0, 1.0, ALU.mult, ALU.add)
    wup = ph.sb([64, 2, 256], BF16, "wup")
    fw.ld(wup[:, 0, :], d["d_w_up"][l, 0], q="pool")
    fw.ld(wup[:, 1, :], d["d_w_up"][l, 1], q="pool")
    aup = ph.sb([64, 256], BF16, "aup")
    fw.ld(aup[:], d["d_a_up"][l], q="pool")
    gup = ph.sb([128, 256], BF16, "gup")
    fw.ld(gup[:], d["d_g_up"][l], q="pool")
    dmask = ph.sb([128, 2, 5, 128], BF16, "dmask")
    fw.ld(dmask[:], d["c_dmask"], q="pool")
    seg = ph.sb([64, 512], F32, "seg")
    fw.ld(seg[:], d["c_seg"])

    def pbc(i):
        return prm.bc((slice(None), i, slice(None)), 2, [64, 4, 128])

    ytok = [T(None, "ytok%d" % c) for c in range(n)]
    H = [ph.sb([64, 4, 64], F32, "Hf"), ph.sb([64, 4, 64], F32, "Hb")]
    Hb = [ph.sb([64, 4, 64], BF16, "Hfb"), ph.sb([64, 4, 64], BF16, "Hbb")]
    for dr in range(2):
        fw.memset("dve", H[dr][:], 0.0)
        fw.memset("pool", Hb[dr][:], 0.0)
    PST = ph.pspool(1, [128, 4, 4, 64], BF16, "pst")
    PP = []
    for dr in range(2):
        Q = Ctx()
        Q.PS = ph.pspool(4 if dr == 0 else 3, [128, 512], F32, "ps")
        Q.zts = ph.sbpool(1, [64, 15, 130], F32, "zt")
        Q.zgs = ph.sbpool(2, [128, 130], F32, "zg")
        Q.us = ph.sbpool(1, [64, 15, 128], F32, "u")
        Q.u2s = ph.sbpool(1, [64, 15, 128], F32, "u2")
        Q.ugs = ph.sbpool(1, [128, 128], F32, "ug")
        Q.F4 = ph.sbpool(10, [64, 4, 128], F32, "f4")
        Q.FL = ph.sbpool(3, [64, 4, 128], F32, "fl")
        Q.B4 = ph.sbpool(12, [64, 4, 128], BF16, "b4")
        Q.M4 = ph.sbpool(9, [128, 4, 128], BF16, "m4")
        Q.MK = ph.sbpool(4, [128, 4, 128], BF16, "mk")
        Q.TM = ph.sbpool(2, [128, 4, 4, 64], BF16, "tm")
        Q.SM = ph.sbpool(4, [128, 128], BF16, "sm")
        Q.S4 = ph.sbpool(6, [64, 4], F32, "s4")
        Q.UV = ph.sbpool(2, [128, 4, 64], F32, "uv")
        Q.UB = ph.sbpool(3, [128, 4, 64], BF16, "ub")
        PP.append(Q)

    def v3(t, npart=64):
        return View(t, t.ap[0:npart, :].rearrange("p (h i) -> p h i", h=4))

    def v3s(t):
        return View(t, t.ap[:, 0:256].rearrange("p (h i) -> p h i", h=4))

    def flat(t):
        return View(t, t.ap[:].rearrange("p h i -> p (h i)"))

    def unit(c, dr, first):
        Q = PP[dr]
        PS, F4, FL, B4, M4, MK, SM, S4 = Q.PS, Q.F4, Q.FL, Q.B4, Q.M4, Q.MK, Q.SM, Q.S4
        need_post = not first
        c0 = c * 128
        zt = Q.zts.next()
        zg = Q.zgs.next()
        lo = 1 if c == 0 else 0
        hi = 129 if c == n - 1 else 130
        if c == 0:
            fw.memset("pool", zt[:, :, 0:1], 0.0)
            fw.memset("pool", zg[:, 0:1], 0.0)
        if c == n - 1:
            fw.memset("pool", zt[:, :, 129:130], 0.0)
            fw.memset("pool", zg[:, 129:130], 0.0)
        fw.ld(zt[:, :, lo:hi], C.ZD[:, :, c0 - 1 + lo:c0 - 1 + hi])
        fw.ld(zg[:, lo:hi], C.ZG[:, c0 - 1 + lo:c0 - 1 + hi])
        if need_post:
            yprev = FL.next()
            fw.ld(yprev[:], View(ytok[c], C.YD[c]))
        yield
        u = Q.us.next()
        u2 = Q.u2s.next()
        fw.tt("dve", u[:], zt[:, :, 1:129], mu.bc((slice(None), 0, slice(None)), 2, [64, 15, 128]), ALU.mult)
        fw.tt("pool", u2[:], zt[:, :, 0:128], mu.bc((slice(None), 1, slice(None)), 2, [64, 15, 128]), ALU.mult)
        yield
        fw.tt("dve", u[:], u[:], u2[:], ALU.add)
        fw.tt("pool", u2[:], zt[:, :, 2:130], mu.bc((slice(None), 2, slice(None)), 2, [64, 15, 128]), ALU.mult)
        yield
        fw.tt("dve", u[:], u[:], u2[:], ALU.add)
        ug = Q.ugs.next()
        fw.ts("dve", ug[:], zg[:, 1:129], mug[:, 0:1], None, ALU.mult)
        fw.stt(ug[:], zg[:, 0:128], mug[:, 1:2], ug[:], ALU.mult, ALU.add)
        fw.stt(ug[:], zg[:, 2:130], mug[:, 2:3], ug[:], ALU.mult, ALU.add)
        yield
        tw = SM.next()
        fw.act(tw[0:64, :], u[:, 12 + dr, :], AF.Tanh)
        adb = SM.next()
        fw.copy("act", adb[0:64, :], u[:, 14, :])
        yw = PS.next()
        ya = PS.next()
        for h in range(4):
            fw.mm(yw[0:64, h * 128:(h + 1) * 128], wup[:, dr, h * 64:(h + 1) * 64], tw[0:64, :], sig=(h == 3))
        for h in range(4):
            fw.mm(ya[0:64, h * 128:(h + 1) * 128], aup[:, h * 64:(h + 1) * 64], adb[0:64, :], sig=(h == 3))
        yield
        sg = F4.next()
        fw.tt("dve", sg[:], v3(yw), pbc(dr), ALU.add)
        fw.act(sg[:], sg[:], AF.Tanh, scale=0.5)
        fw.ts("pool", sg[:], sg[:], 0.5, 0.5, ALU.mult, ALU.add)
        a = F4.next()
        fw.tt("dve", a[:], v3(ya), pbc(2), ALU.add)
        fw.act(a[:], a[:], AF.Tanh, scale=0.5)
        fw.ts("pool", a[:], a[:], 0.5, 0.5, ALU.mult, ALU.add)
        yield
        if need_post:
            sgd = SM.next()
            fw.act(sgd[:], ug[:], AF.Tanh, scale=0.5)
            fw.ts("pool", sgd[:], sgd[:], 0.5, 0.5, ALU.mult, ALU.add)
            gps_ = PS.next()
            for h in range(4):
                fw.mm(gps_[0:64, h * 128:(h + 1) * 128], gup[:, h * 64:(h + 1) * 64], sgd[:], sig=(h == 3))
            g_ = FL.next()
            fw.copy("act", g_[:], v3(gps_))
            yield
        r = u[:, 0:4, :]
        k = u[:, 4:8, :]
        v = u[:, 8:12, :]
        kk = F4.next()
        fw.tt("dve", kk[:], k, pbc(3), ALU.mult)
        sq = B4.next()
        fw.tt("pool", sq[:], kk[:], kk[:], ALU.mult)
        ssp = PS.next()
        fw.mm(ssp[0:64, :], K.ones_bf[0:64, 0:64], flat(sq))
        yield
        nr = F4.next()
        fw.ts("dve", nr[:], v3(ssp), 1e-24, None, ALU.max)
        fw.act(nr[:], nr[:], AF.Ln)
        fw.act(nr[:], nr[:], AF.Exp, scale=-0.5)
        kap = F4.next()
        fw.tt("dve", kap[:], kk[:], nr[:], ALU.mult)
        yield
        tk = F4.next()
        fw.tt("pool", tk[:], a[:], pbc(4), ALU.mult)
        fw.tt("pool", tk[:], tk[:], pbc(5), ALU.add)
        kh = FL.next()
        fw.tt("dve", kh[:], k, tk[:], ALU.mult)
        bb = F4.next()
        fw.tt("pool", bb[:], kap[:], a[:], ALU.mult)
        yield
        Pp = F4.next()
        fw.scan(flat(Pp), seg[:], flat(sg), 0.0, ALU.mult, ALU.add)
        Qp = F4.next()
        fw.tt("pool", Qp[:], Pp[:], sg[:], ALU.subtract)
        nT = S4.next()
        pT = S4.next()
        fw.ts("dve", nT[:], Pp[:, :, 127], -CW, None, ALU.mult)
        fw.ts("dve", pT[:], Pp[:, :, 127], CW, None, ALU.mult)
        gC = S4.next()
        fw.act(gC[:], nT[:], AF.Exp)
        yield
        rt = B4.next()
        kp = B4.next()
        kt = B4.next()
        bt = B4.next()
        Khf = B4.next()
        Bhf = B4.next()
        E1 = F4.next()
        E2 = F4.next()
        if dr == 0:
            fw.act(E1[:], Pp[:], AF.Exp, scale=-CW)
            fw.tt("dve", rt[:], r, E1[:], ALU.mult)
            fw.act(E2[:], Qp[:], AF.Exp, scale=-CW)
            fw.tt("pool", kp[:], kap[:], E2[:], ALU.mult)
            yield
            E3 = F4.next()
            fw.act(E3[:], Pp[:], AF.Exp, scale=CW)
            fw.tt("dve", kt[:], kh[:], E3[:], ALU.mult)
            fw.tt("pool", bt[:], bb[:], E3[:], ALU.mult)
            E4 = F4.next()
            for h in range(4):
                fw.act(E4[:, h, :], Pp[:, h, :], AF.Exp, scale=CW, bias=nT[:, h:h + 1])
        else:
            for h in range(4):
                fw.act(E1[:, h, :], Qp[:, h, :], AF.Exp, scale=CW, bias=nT[:, h:h + 1])
            fw.tt("dve", rt[:], r, E1[:], ALU.mult)
            for h in range(4):
                fw.act(E2[:, h, :], Pp[:, h, :], AF.Exp, scale=CW, bias=nT[:, h:h + 1])
            fw.tt("pool", kp[:], kap[:], E2[:], ALU.mult)
            yield
            E3 = F4.next()
            for h in range(4):
                fw.act(E3[:, h, :], Qp[:, h, :], AF.Exp, scale=-CW, bias=pT[:, h:h + 1])
            fw.tt("dve", kt[:], kh[:], E3[:], ALU.mult)
            fw.tt("pool", bt[:], bb[:], E3[:], ALU.mult)
            E4 = F4.next()
            fw.act(E4[:], Qp[:], AF.Exp, scale=-CW)
        yield
        fw.tt("dve", Khf[:], kh[:], E4[:], ALU.mult)
        fw.stt(Bhf[:], bb[:], -1.0, E4[:], ALU.mult, ALU.mult)
        vb = B4.next()
        fw.copy("act", vb[:], v)
        pst = PST.next()
        for ki, src_ in enumerate((kp, Khf, Bhf, vb)):
            for h in range(4):
                fw.tr(pst[:, ki, h, :], src_[:, h, :], K.ident[0:64, 0:64], sig=(ki == 3 and h == 3))
        tm = Q.TM.next()
        fw.copy("dve", tm[:], pst[:])
        yield
        mats = []
        pairs = [(kp, bt), (bt, kp), (kt, kp), (kt, rt), (bt, rt)]
        for ki, (lt, rt_) in enumerate(pairs):
            p_ = PS.next()
            for h in range(4):
                fw.mm(p_[:, h * 128:(h + 1) * 128], lt[:, h, :], rt_[:, h, :], sig=(h == 3))
            m_ = M4.next() if ki < 2 else MK.next()
            fw.tt("dve", m_[:], v3(p_, 128), dmask.bc((slice(None), dr, ki, slice(None)), 1, [128, 4, 128]), ALU.mult)
            mats.append(m_)
            yield
        X, XT, AkkT, ArkT, ArbT = mats
        PT = M4.next()
        fw.tt("pool", PT[:], XT[:], K.ident.bc((slice(None), slice(None)), 1, [128, 4, 128]), ALU.add)
        for lvl in range(6):
            p1 = PS.next()
            for h in range(4):
                fw.mm(p1[:, h * 128:(h + 1) * 128], XT[:, h, :], X[:, h, :], sig=(h == 3))
            if lvl < 5:
                p2 = PS.next()
                for h in range(4):
                    fw.mm(p2[:, h * 128:(h + 1) * 128], X[:, h, :], XT[:, h, :], sig=(h == 3))
            yield
            X2 = M4.next()
            fw.copy("act", X2[:], v3(p1, 128))
            if lvl < 5:
                XT2 = M4.next()
                fw.copy("dve", XT2[:], v3(p2, 128))
            p3 = PS.next()
            for h in range(4):
                fw.mm(p3[:, h * 128:(h + 1) * 128], X2[:, h, :], PT[:, h, :], sig=(h == 3))
            yield
            PT2 = M4.next()
            fw.tt("dve", PT2[:], v3(p3, 128), PT[:], ALU.add)
            PT = PT2
            X = X2
            if lvl < 5:
                XT = XT2
        pw = PS.next()
        for h in range(4):
            fw.mm(pw[0:64, h * 128:(h + 1) * 128], tm[:, 0, h, :], PT[:, h, :], sig=(h == 3))
        pa = PS.next()
        for h in range(4):
            fw.mm(pa[:, h * 64:(h + 1) * 64], AkkT[:, h, :], tm[:, 3, h, :], sig=(h == 3))
        yield
        WkT = B4.next()
        fw.copy("act", WkT[:], v3(pw))
        AV = Q.UB.next()
        fw.copy("dve", AV[:], v3s(pa))
        pu = PS.next()
        for h in range(4):
            fw.mm(pu[:, h * 64:(h + 1) * 64], PT[:, h, :], AV[:, h, :], sig=(h == 3))
        yield
        Uv = Q.UV.next()
        fw.copy("act", Uv[:], v3s(pu))
        pU = PS.next()
        for h in range(4):
            fw.mm(pU[:, h * 64:(h + 1) * 64], WkT[:, h, :], Hb[dr][:, h, :], sig=(h == 3))
        yield
        U = Q.UB.next()
        fw.tt("dve", U[:], v3s(pU), Uv[:], ALU.add)
        pY = PS.next()
        for h in range(4):
            yh = pY[0:64, h * 128:(h + 1) * 128]
            fw.mm(yh, Hb[dr][:, h, :], rt[:, h, :], start=True, stop=False, sig=False)
            fw.mm(yh, tm[:, 3, h, :], ArkT[:, h, :], start=False, stop=False, sig=False)
            fw.mm(yh, U[:, h, :], ArbT[:, h, :], start=False, stop=True, sig=(h == 3))
        pH = PS.next()
        for h in range(4):
            hh = pH[0:64, h * 64:(h + 1) * 64]
            fw.mm(hh, tm[:, 1, h, :], tm[:, 3, h, :], start=True, stop=False, sig=False)
            fw.mm(hh, tm[:, 2, h, :], U[:, h, :], start=False, stop=True, sig=(h == 3))
        fw.tt("pool", H[dr][:], H[dr][:], gC.bc((slice(None), slice(None)), 2, [64, 4, 64]), ALU.mult)
        yield
        fw.tt("dve", H[dr][:], H[dr][:], View(pH, pH.ap[0:64, 0:256].rearrange("p (h i) -> p h i", h=4)), ALU.add)
        fw.copy("act", Hb[dr][:], H[dr][:])
        if first:
            ysb = F4.next()
            fw.copy("act", ysb[:], v3(pY))
            fw.st(View(ytok[c], C.YD[c]), ysb[:])
            return
        y = F4.next()
        fw.tt("dve", y[:], v3(pY), yprev[:], ALU.add)
        pm_ = PS.next()
        fw.mm(pm_[0:64, :], K.ones_f[0:64, 0:64], flat(y))
        rk = F4.next()
        fw.tt("pool", rk[:], r, kh[:], ALU.mult)
        fw.tt("pool", rk[:], rk[:], pbc(6), ALU.mult)
        yield
        yc = F4.next()
        fw.stt(yc[:], v3(pm_), -1.0 / 64, y[:], ALU.mult, ALU.add)
        sq2 = F4.next()
        fw.act(sq2[:], yc[:], AF.Square)
        pv = PS.next()
        fw.mm(pv[0:64, :], K.ones_f[0:64, 0:64], flat(sq2))
        pr_ = PS.next()
        fw.mm(pr_[0:64, :], K.ones_f[0:64, 0:64], flat(rk))
        yield
        rs = F4.next()
        fw.ts("dve", rs[:], v3(pv), 1.0 / 64, GN_EPS, ALU.mult, ALU.add)
        fw.act(rs[:], rs[:], AF.Ln)
        fw.act(rs[:], rs[:], AF.Exp, scale=-0.5)
        fw.tt("dve", yc[:], yc[:], rs[:], ALU.mult)
        fw.tt("pool", yc[:], yc[:], pbc(7), ALU.mult)
        yield
        fw.tt("pool", yc[:], yc[:], pbc(8), ALU.add)
        t_ = F4.next()
        fw.tt("dve", t_[:], v3(pr_), v, ALU.mult)
        fw.tt("pool", yc[:], yc[:], t_[:], ALU.add)
        ob = B4.next()
        fw.tt("dve", ob[:], yc[:], g_[:], ALU.mult)
        for h in range(4):
            fw.st(C.OT[12 + h, :, c * 128:(c + 1) * 128], ob[:, h, :])

    for i in range(n):
        first = i < n // 2
        gens = [unit(i, 0, first), unit(n - 1 - i, 1, first)]
        alive = [True, True]
        while any(alive):
            for gi in range(2):
                if alive[gi]:
                    try:
                        next(gens[gi])
                    except StopIteration:
                        alive[gi] = False
    ph.close()


def build(seqs=(4096, 2048), depth=2, phases=None, debug=False):
    nc = bass.Bass("TRN2", target_bir_lowering=False)
    C = Ctx()
    C.seqs = list(seqs)
    C.d = {}
    C.xin = [nc.dram_tensor("x%d" % i, [t, D_MODEL], F32, kind="ExternalInput").ap() for i, t in enumerate(seqs)]
    C.y = [nc.dram_tensor("y%d" % i, [t, D_MODEL], F32, kind="ExternalOutput").ap() for i, t in enumerate(seqs)]
    for nm, shp in WEIGHT_SPECS:
        C.d[nm] = nc.dram_tensor(nm, list(shp), F32, kind="ExternalInput").ap()
    for nm, arr in make_consts().items():
        C.d[nm] = nc.dram_tensor(nm, list(arr.shape), F32, kind="ExternalInput").ap()
    TM = max(seqs)
    sk = "ExternalOutput" if debug else "Internal"

    def scratch(nm, shp, dt):
        return nc.dram_tensor(nm, list(shp), dt, kind=sk).ap()
    C.QA = scratch("QA", [64, 4, TM], BF16)
    C.KA = scratch("KA", [64, 2, TM], BF16)
    C.VA = scratch("VA", [TM, 128], BF16)
    C.QB = scratch("QB", [64, 4, TM], BF16)
    C.KB = scratch("KB", [64, 4, TM], BF16)
    C.VB = scratch("VB", [TM, 256], BF16)
    C.QC = scratch("QC", [64, 4, TM], BF16)
    C.KC = scratch("KC", [64, 4, TM], BF16)
    C.GC = scratch("GC", [64, 4, TM], BF16)
    C.VC = scratch("VC", [TM, 256], BF16)
    C.ZD = scratch("ZD", [64, 15, TM], F32)
    C.ZG = scratch("ZG", [128, TM], F32)
    C.OT = scratch("OT", [16, 64, TM], BF16)
    C.YD = scratch("YD", [TM // 128, 64, 4, 128], F32)
    allph = ["inproj", "A", "B", "C", "D", "outproj", "ffn"]
    phases = allph if phases is None else phases
    with ExitStack() as es:
        fw = FW(nc, es)
        ph0 = fw.phase()
        K = load_consts(fw, C, ph0)
        fw.barrier()
        for l in range(depth):
            for si in range(len(seqs)):
                if "inproj" in phases:
                    phase_inproj(fw, C, K, l, si)
                if "A" in phases:
                    phase_attnA(fw, C, K, l, si)
                if "B" in phases:
                    phase_attnB(fw, C, K, l, si)
                if "C" in phases:
                    phase_ret(fw, C, K, l, si)
                if "D" in phases:
                    phase_rwkv(fw, C, K, l, si)
                if "outproj" in phases:
                    phase_outproj(fw, C, K, l, si)
                if "ffn" in phases:
                    phase_ffn(fw, C, K, l, si)
        fw.barrier()
        ph0.es.close()
        C.nins = fw.nins
    return nc, C


_CACHE = {}


def kernel(**inputs):
    xp = np.ascontiguousarray(inputs["x_prompt"], dtype=np.float32)
    xs = np.ascontiguousarray(inputs["x_sample"], dtype=np.float32)
    nc, C = build()
    consts = make_consts()
    in_maps = []
    for i in range(8):
        m = {"x0": xp[i], "x1": xs[i]}
        for nm, _ in WEIGHT_SPECS:
            m[nm] = np.ascontiguousarray(inputs[nm], dtype=np.float32)
        m.update(consts)
        in_maps.append(m)
    res = run_bass_kernel_spmd(nc, in_maps, core_ids=list(range(8)))
    yp = np.stack([r["y0"] for r in res.results], 0).astype(np.float32)
    ys = np.stack([r["y1"] for r in res.results], 0).astype(np.float32)
    return (yp, ys)
```

```python
import math
from contextlib import ExitStack
import numpy as np
import concourse.bass as bass
import concourse.mybir as mybir
from concourse.bass_utils import run_bass_kernel_spmd

F32 = mybir.dt.float32
BF16 = mybir.dt.bfloat16
AF = mybir.ActivationFunctionType
ALU = mybir.AluOpType
AX = mybir.AxisListType

D_MODEL = 1024
IN_COLS = 3392
D_FF = 2816
NFF = D_FF // 128
EPS = 1e-6
GN_EPS = 64e-5
TMAX = 4096


class View:
    __slots__ = ("t", "ap")

    def __init__(self, t, ap):
        self.t = t
        self.ap = ap


class T:
    __slots__ = ("ap", "w", "r", "name")

    def __init__(self, ap, name=""):
        self.ap = ap
        self.w = {}
        self.r = {}
        self.name = name

    def __getitem__(self, idx):
        return View(self, self.ap[idx])

    def bc(self, idx, axis, shape):
        return View(self, self.ap[idx].unsqueeze(axis).to_broadcast(list(shape)))


class Eng:
    def __init__(self, name, handle, sem):
        self.name = name
        self.h = handle
        self.sem = sem
        self.count = 0
        self.seen = {}


class Pool:
    def __init__(self, items):
        self.items = items
        self.i = 0

    def next(self):
        t = self.items[self.i]
        self.i = (self.i + 1) % len(self.items)
        return t


class Phase:
    def __init__(self, fw):
        self.fw = fw
        self.es = ExitStack()
        self.n = 0

    def sb(self, shape, dt, name="t"):
        self.n += 1
        t = self.es.enter_context(self.fw.nc.sbuf_tensor("%s_%d_%d" % (name, self.fw.uid(), self.n), list(shape), dt))
        return T(t, name)

    def ps(self, shape, dt=F32, name="p"):
        self.n += 1
        t = self.es.enter_context(self.fw.nc.psum_tensor("%s_%d_%d" % (name, self.fw.uid(), self.n), list(shape), dt))
        return T(t, name)

    def sbpool(self, n, shape, dt, name="t"):
        return Pool([self.sb(shape, dt, name) for _ in range(n)])

    def pspool(self, n, shape, dt=F32, name="p"):
        return Pool([self.ps(shape, dt, name) for _ in range(n)])

    def close(self):
        self.fw.barrier()
        self.es.close()


class FW:
    NDMA = 32

    def __init__(self, nc, es):
        self.nc = nc
        self.es = es
        self._uid = 0
        self.E = {}
        for nm, h in (("pe", nc.tensor), ("act", nc.scalar), ("dve", nc.vector), ("pool", nc.gpsimd), ("sp", nc.sync)):
            sem = es.enter_context(nc.semaphore("s_" + nm))
            self.E[nm] = Eng(nm, h, sem)
        self.dsem = [es.enter_context(nc.semaphore("d%d" % i)) for i in range(self.NDMA)]
        self.dval = [0] * self.NDMA
        self.dnext = {"sp": 0, "pool": 0, "act": 0}
        self.drange = {"sp": (0, 14), "pool": (14, 28), "act": (28, 32)}
        self.semobj = {}
        for e in self.E.values():
            self.semobj[("e", e.name)] = e.sem
        for i, s in enumerate(self.dsem):
            self.semobj[("d", i)] = s
        self.nins = 0

    def uid(self):
        self._uid += 1
        return self._uid

    def phase(self):
        return Phase(self)

    def _needs(self, reads, writes):
        needs = {}
        for t in reads:
            for k, v in t.w.items():
                if needs.get(k, 0) < v:
                    needs[k] = v
        for t in writes:
            for d in (t.w, t.r):
                for k, v in d.items():
                    if needs.get(k, 0) < v:
                        needs[k] = v
        return needs

    def _emit_waits(self, e, needs, skip_self=False):
        for k, v in needs.items():
            if skip_self and k == ("e", e.name):
                continue
            if e.seen.get(k, 0) >= v:
                continue
            if k[0] == "e":
                assert self.E[k[1]].count >= v, "wait on future event %s %d" % (k, v)
            e.seen[k] = v
            e.h.wait_ge(self.semobj[k], v)
            self.nins += 1

    def op(self, eng, fn, reads=(), writes=(), signal=True):
        e = self.E[eng]
        reads = [v.t for v in reads]
        writes = [v.t for v in writes]
        needs = self._needs(reads, writes)
        self._emit_waits(e, needs, skip_self=(eng == "pe"))
        ins = fn(e.h)
        self.nins += 1
        if signal:
            e.count += 1
            ins.then_inc(e.sem, 1)
            ev = e.count
        else:
            ev = e.count + 1
        key = ("e", eng)
        for t in writes:
            t.w = {key: ev}
            t.r = {}
        for t in reads:
            if t.r.get(key, 0) < ev:
                t.r[key] = ev

    def dma(self, q, out, in_, **kw):
        e = self.E[q]
        reads = [in_.t] if isinstance(in_, View) else []
        writes = [out.t] if isinstance(out, View) else []
        needs = self._needs(reads, writes)
        lo_, hi_ = self.drange[q]
        i = lo_ + self.dnext[q]
        self.dnext[q] = (self.dnext[q] + 1) % (hi_ - lo_)
        key = ("d", i)
        if self.dval[i] > 0 and needs.get(key, 0) < self.dval[i]:
            needs[key] = self.dval[i]
        self._emit_waits(e, needs)
        self.dval[i] += 16
        v = self.dval[i]
        oap = out.ap if isinstance(out, View) else out
        iap = in_.ap if isinstance(in_, View) else in_
        e.h.dma_start(out=oap, in_=iap, **kw).then_inc(self.dsem[i], 16)
        self.nins += 1
        for t in writes:
            t.w = {key: v}
            t.r = {}
        for t in reads:
            t.r[key] = v

    def barrier(self):
        allev = {}
        for e in self.E.values():
            if e.count > 0:
                allev[("e", e.name)] = e.count
        for i in range(self.NDMA):
            if self.dval[i] > 0:
                allev[("d", i)] = self.dval[i]
        for e in self.E.values():
            self._emit_waits(e, dict(allev))

    def ld(self, dst, src, q="sp", **kw):
        self.dma(q, dst, src, **kw)

    def ldn(self, dst, src, q="sp"):
        self.dma(q, dst, src, allow_slow_non_contiguous=True)

    def st(self, dst, src, q="pool"):
        self.dma(q, dst, src)

    def mm(self, out, lhsT, rhs, start=True, stop=True, sig=None):
        if sig is None:
            sig = stop
        self.op("pe", lambda h: h.matmul(out.ap, lhsT=lhsT.ap, rhs=rhs.ap, start=start, stop=stop),
                reads=[lhsT, rhs], writes=[out], signal=sig)

    def tr(self, out, in_, ident, sig=True):
        self.op("pe", lambda h: h.transpose(out=out.ap, in_=in_.ap, identity=ident.ap),
                reads=[in_, ident], writes=[out], signal=sig)

    def act(self, out, in_, func, bias=None, scale=None, accum=None):
        reads = [in_]
        kw = {}
        if bias is not None:
            if isinstance(bias, View):
                reads.append(bias)
                kw["bias"] = bias.ap
            else:
                kw["bias"] = float(bias)
        if scale is not None:
            if isinstance(scale, View):
                reads.append(scale)
                kw["scale"] = scale.ap
            else:
                kw["scale"] = float(scale)
        writes = [out]
        if accum is not None:
            kw["accum_out"] = accum.ap
            writes.append(accum)
        self.op("act", lambda h: h.activation(out=out.ap, in_=in_.ap, func=func, **kw), reads=reads, writes=writes)

    def tt(self, eng, out, in0, in1, op):
        self.op(eng, lambda h: h.tensor_tensor(out=out.ap, in0=in0.ap, in1=in1.ap, op=op), reads=[in0, in1], writes=[out])

    def ts(self, eng, out, in0, s1, s2, op0, op1=None):
        reads = [in0]
        a1 = s1.ap if isinstance(s1, View) else s1
        a2 = s2.ap if isinstance(s2, View) else s2
        if isinstance(s1, View):
            reads.append(s1)
        if isinstance(s2, View):
            reads.append(s2)
        if op1 is None:
            self.op(eng, lambda h: h.tensor_scalar(out=out.ap, in0=in0.ap, scalar1=a1, scalar2=None, op0=op0), reads=reads, writes=[out])
        else:
            self.op(eng, lambda h: h.tensor_scalar(out=out.ap, in0=in0.ap, scalar1=a1, scalar2=a2, op0=op0, op1=op1), reads=reads, writes=[out])

    def stt(self, out, in0, scalar, in1, op0, op1):
        reads = [in0, in1]
        sa = scalar.ap if isinstance(scalar, View) else scalar
        if isinstance(scalar, View):
            reads.append(scalar)
        self.op("dve", lambda h: h.scalar_tensor_tensor(out=out.ap, in0=in0.ap, scalar=sa, in1=in1.ap, op0=op0, op1=op1), reads=reads, writes=[out])

    def copy(self, eng, out, in_):
        if eng == "act":
            self.op("act", lambda h: h.activation(out=out.ap, in_=in_.ap, func=AF.Identity), reads=[in_], writes=[out])
        else:
            self.op(eng, lambda h: h.tensor_copy(out=out.ap, in_=in_.ap), reads=[in_], writes=[out])

    def recip(self, out, in_):
        self.op("dve", lambda h: h.reciprocal(out=out.ap, in_=in_.ap), reads=[in_], writes=[out])

    def memset(self, eng, out, val):
        self.op(eng, lambda h: h.memset(out.ap, val), writes=[out])

    def reduce(self, out, in_, op, axis=AX.X):
        self.op("dve", lambda h: h.tensor_reduce(out=out.ap, in_=in_.ap, axis=axis, op=op), reads=[in_], writes=[out])

    def scan(self, out, d0, d1, init, op0, op1):
        self.op("dve", lambda h: h.tensor_tensor_scan(out=out.ap, data0=d0.ap, data1=d1.ap, initial=init, op0=op0, op1=op1), reads=[d0, d1], writes=[out])


def _angles(pos, rot_dim, theta):
    inv = (theta ** (-np.arange(0, rot_dim, 2, dtype=np.float32) / rot_dim)).astype(np.float32)
    return pos.astype(np.float32)[:, None] * inv[None, :]


def make_consts():
    c = {}
    c["c_ident"] = np.eye(128, dtype=np.float32)
    T = TMAX
    pos = np.arange(T)
    rope = np.zeros((64, 6, T), np.float32)
    ar = _angles(pos // 64, 32, 10000.0)
    ac = _angles(pos % 64, 32, 10000.0)
    for d in range(64):
        a = ar if d < 32 else ac
        idx = (d % 32) % 16
        rope[d, 0] = np.cos(a[:, idx])
        rope[d, 1] = np.sin(a[:, idx])
    ab = _angles(pos, 8, 500000.0)
    for d in range(64):
        dd = d % 32
        if dd < 8:
            rope[d, 2] = np.cos(ab[:, dd % 4])
            rope[d, 3] = np.sin(ab[:, dd % 4])
        else:
            rope[d, 2] = 1.0
            rope[d, 3] = 0.0
    acc = _angles(pos, 64, 10000.0)
    for d in range(64):
        rope[d, 4] = np.cos(acc[:, d % 32])
        rope[d, 5] = np.sin(acc[:, d % 32])
    c["c_rope"] = rope
    pm = np.zeros((64, 3, 64), np.float32)
    for m in range(64):
        mm_ = m % 32
        base = m - mm_
        if mm_ < 16:
            pm[base + mm_ + 16, 0, m] = -1.0
        else:
            pm[base + mm_ - 16, 0, m] = 1.0
        if mm_ < 4:
            pm[base + mm_ + 4, 1, m] = -1.0
        elif mm_ < 8:
            pm[base + mm_ - 4, 1, m] = 1.0
        if m < 32:
            pm[m + 32, 2, m] = -1.0
        else:
            pm[m - 32, 2, m] = 1.0
    c["c_pm"] = pm
    c["c_ones"] = np.ones((128, 128), np.float32)
    h = np.arange(4, dtype=np.float32)
    lgf = np.log1p(-np.exp2(-5.0 - h)).astype(np.float32)
    lgb = lgf[::-1].copy()
    i = np.arange(128, dtype=np.float32)
    diff = i[None, :] - i[:, None]
    rm = np.zeros((128, 4, 128), np.float32)
    for hh in range(4):
        rm[:, hh, :] = np.where(diff >= 0, np.exp(lgf[hh] * np.maximum(diff, 0)), 0.0) + \
            np.where(diff <= 0, np.exp(lgb[hh] * np.maximum(-diff, 0)), 0.0)
    c["c_retmask"] = rm
    qd = np.zeros((64, 2, 4, 128), np.float32)
    kd = np.zeros((128, 2, 4), np.float32)
    for hh in range(4):
        qd[:, 0, hh, :] = np.exp(lgf[hh] * (i + 1.0))[None, :]
        qd[:, 1, hh, :] = np.exp(lgb[hh] * (128.0 - i))[None, :]
        kd[:, 0, hh] = np.exp(lgf[hh] * (127.0 - i))
        kd[:, 1, hh] = np.exp(lgb[hh] * i)
    c["c_qdec"] = qd
    c["c_kdec"] = kd
    dm = np.zeros((128, 2, 5, 128), np.float32)
    p = np.arange(128)[:, None]
    f = np.arange(128)[None, :]
    dm[:, 0, 0, :] = -1.0 * (p > f)
    dm[:, 0, 1, :] = -1.0 * (f > p)
    dm[:, 0, 2, :] = (f > p)
    dm[:, 0, 3, :] = (f >= p)
    dm[:, 0, 4, :] = -1.0 * (f >= p)
    dm[:, 1, 0, :] = -1.0 * (p < f)
    dm[:, 1, 1, :] = -1.0 * (f < p)
    dm[:, 1, 2, :] = (f < p)
    dm[:, 1, 3, :] = (f <= p)
    dm[:, 1, 4, :] = -1.0 * (f <= p)
    c["c_dmask"] = dm.astype(np.float32)
    seg = np.ones((64, 512), np.float32)
    seg[:, ::128] = 0.0
    c["c_seg"] = seg
    return c


RET_G = [1.0 - 2.0 ** (-5 - h) for h in range(4)]

WEIGHT_SPECS = [
    ("norm_mix_pre", (2, 1024)), ("norm_mix_post", (2, 1024)), ("norm_ffn_pre", (2, 1024)), ("norm_ffn_post", (2, 1024)),
    ("w_in", (2, 1024, 3392)), ("w_out", (2, 1024, 1024)), ("a_q_gain", (2, 64)), ("a_k_gain", (2, 64)),
    ("b_lambda", (2, 4, 32)), ("b_subln_gain", (2, 64)), ("c_gn_gain", (2, 256)), ("d_mu_prev", (2, 1088)),
    ("d_mu_next", (2, 1088)), ("d_w0", (2, 2, 256)), ("d_w_up", (2, 2, 64, 256)), ("d_a0", (2, 256)),
    ("d_a_up", (2, 64, 256)), ("d_g_up", (2, 128, 256)), ("d_k_k", (2, 256)), ("d_k_a", (2, 256)),
    ("d_r_k", (2, 4, 64)), ("d_gn_w", (2, 256)), ("d_gn_b", (2, 256)),
    ("ffn_w_gate", (2, 1024, 2816)), ("ffn_w_up", (2, 1024, 2816)), ("ffn_w_down", (2, 2816, 1024)),
]


class Ctx:
    pass


def load_consts(fw, C, ph):
    K = Ctx()
    K.ident = ph.sb([128, 128], BF16, "ident")
    fw.ld(K.ident[:], C.d["c_ident"], q="pool")
    K.identf = ph.sb([128, 128], F32, "identf")
    fw.ld(K.identf[:], C.d["c_ident"])
    K.ones_bf = ph.sb([128, 128], BF16, "ones_bf")
    fw.ld(K.ones_bf[:], C.d["c_ones"], q="pool")
    K.ones_f = ph.sb([128, 128], F32, "ones_f")
    fw.ld(K.ones_f[:], C.d["c_ones"])
    K.pm = []
    for i in range(3):
        t = ph.sb([64, 64], BF16, "pm%d" % i)
        fw.ld(t[:], C.d["c_pm"][:, i, :], q="pool")
        K.pm.append(t)
    return K


def rms_rows(fw, xt, gain, hb, tmp, ss, rstd):
    fw.act(tmp[:], xt[:], AF.Square, accum=ss[:])
    fw.act(rstd[:], ss[:], AF.Sqrt, bias=EPS, scale=1.0 / D_MODEL)
    fw.recip(rstd[:], rstd[:])
    fw.stt(hb[:], xt[:], rstd[:, 0:1], gain[:], ALU.mult, ALU.mult)


def make_hT(fw, K, src_rows, gain, hT, j, P):
    xt = P.xt.next()
    fw.ld(xt[:], src_rows)
    hb = P.hb.next()
    ss = P.ss.next()
    rstd = P.rstd.next()
    tmp = P.tmp.next()
    rms_rows(fw, xt, gain, hb, tmp, ss, rstd)
    tp = P.tp.next()
    for k in range(8):
        fw.tr(tp[:, k, :], hb[:, k * 128:(k + 1) * 128], K.ident[:], sig=(k == 7))
    fw.copy("dve", hT[:, :, j * 128:(j + 1) * 128], tp[:])
    return xt


def phase_inproj(fw, C, K, l, si):
    T_ = C.seqs[si]
    src = C.xin[si] if l == 0 else C.y[si]
    d = C.d
    ph = fw.phase()
    W = ph.sb([128, 8, IN_COLS], BF16, "W")
    for k in range(8):
        fw.ld(W[:, k, :], d["w_in"][l, k * 128:(k + 1) * 128, :], q="pool")
    gain = ph.sb([128, 1024], F32, "gain")
    fw.ld(gain[:], d["norm_mix_pre"][l].partition_broadcast(128))
    gqk = ph.sb([64, 2], F32, "gqk")
    fw.ld(gqk[:, 0:1], d["a_q_gain"][l].rearrange("(p o) -> p o", o=1))
    fw.ld(gqk[:, 1:2], d["a_k_gain"][l].rearrange("(p o) -> p o", o=1))
    P = Ctx()
    P.xt = ph.sbpool(2, [128, 1024], F32, "xt")
    P.hb = ph.sbpool(2, [128, 1024], BF16, "hb")
    P.tmp = ph.sbpool(1, [128, 1024], F32, "tmp")
    P.ss = ph.sbpool(2, [128, 1], F32, "ss")
    P.rstd = ph.sbpool(2, [128, 1], F32, "rstd")
    P.tp = ph.pspool(1, [128, 8, 128], BF16, "tp")
    hTs = ph.sbpool(2, [128, 8, 512], BF16, "hT")
    tabs = ph.sbpool(2, [64, 6, 512], F32, "tab")
    zps = ph.pspool(4, [128, 512], F32, "zp")
    aps = ph.pspool(2, [128, 512], F32, "ap")
    tps = ph.pspool(1, [128, 512], F32, "tmps")
    f32p = ph.sbpool(24, [64, 512], F32, "f32p")
    bfp = ph.sbpool(16, [64, 512], BF16, "bfp")
    f32w = ph.sbpool(3, [128, 512], F32, "f32w")
    bfw = ph.sbpool(3, [128, 512], BF16, "bfw")
    chunks = []
    for h in range(4):
        chunks.append(("A", 0 + 64 * h, C.QA, h, 0))
    for h in range(2):
        chunks.append(("A", 256 + 64 * h, C.KA, h, 1))
    for h in range(4):
        chunks.append(("B", 512 + 64 * h, C.QB, h, 0))
    for h in range(4):
        chunks.append(("B", 768 + 64 * h, C.KB, h, 0))
    for h in range(4):
        chunks.append(("C", 1280 + 64 * h, C.QC, h, 0))
    for h in range(4):
        chunks.append(("C", 1536 + 64 * h, C.KC, h, 0))
    for h in range(4):
        chunks.append(("G", 2048 + 64 * h, C.GC, h, 0))
    for j in range(15):
        chunks.append(("D", 2304 + 64 * j, C.ZD, j, 0))
    DEPTHP = 4

    def chunk_gen(kind, col, dst, slot, gi, hT, tab, t0):
        zp = zps.next()
        for k in range(8):
            fw.mm(zp[0:64, :], W[:, k, col:col + 64], hT[:, k, :], start=(k == 0), stop=(k == 7))
        z = zp[0:64, :]
        yield
        if kind in ("A", "B", "C"):
            ti = {"A": 0, "B": 2, "C": 4}[kind]
            pmi = {"A": 0, "B": 1, "C": 2}[kind]
            qn = bfp.next()
            qf = f32p.next()
            fw.copy("act", qf[:], z)
            if kind == "A":
                sq = bfp.next()
                fw.act(sq[:], z, AF.Square)
                yield
                ms = aps.next()
                fw.mm(ms[0:64, :], K.ones_bf[0:64, 0:64], sq[:])
                rs = f32p.next()
                fw.ts("dve", rs[:], ms[0:64, :], 1.0 / 64, EPS, ALU.mult, ALU.add)
                fw.act(rs[:], rs[:], AF.Ln)
                fw.act(rs[:], rs[:], AF.Exp, scale=-0.5)
                qf2 = f32p.next()
                fw.stt(qf2[:], qf[:], gqk[:, gi:gi + 1], rs[:], ALU.mult, ALU.mult)
                qf = qf2
            fw.copy("act", qn[:], qf[:])
            yield
            pq = aps.next()
            fw.mm(pq[0:64, :], K.pm[pmi][:], qn[:])
            t2 = f32p.next()
            fw.tt("dve", t2[:], pq[0:64, :], tab[:, ti + 1, :], ALU.mult)
            t1 = f32p.next()
            fw.tt("pool", t1[:], qf[:], tab[:, ti, :], ALU.mult)
            ob = bfp.next()
            fw.tt("pool", ob[:], t1[:], t2[:], ALU.add)
            fw.st(dst[:, slot, t0:t0 + 512], ob[:])
        elif kind == "G":
            ob = bfp.next()
            fw.act(ob[:], z, AF.Silu)
            fw.st(dst[:, slot, t0:t0 + 512], ob[:])
        else:
            of = f32p.next()
            fw.copy("act", of[:], z)
            fw.st(dst[:, slot, t0:t0 + 512], of[:])

    def tail_gen(hT, t0):
        zp = zps.next()
        for k in range(8):
            fw.mm(zp[:, :], W[:, k, 3264:3392], hT[:, k, :], start=(k == 0), stop=(k == 7))
        of = f32w.next()
        fw.copy("act", of[:], zp[:])
        fw.st(C.ZG[:, t0:t0 + 512], of[:])
        yield
        for j in range(4):
            tp_ = tps.next()
            for k in range(8):
                fw.mm(tp_[:, 0:128], hT[:, k, j * 128:(j + 1) * 128], W[:, k, 384:512], start=(k == 0), stop=(k == 7), sig=False)
            for k in range(8):
                fw.mm(tp_[:, 128:384], hT[:, k, j * 128:(j + 1) * 128], W[:, k, 1024:1280], start=(k == 0), stop=(k == 7))
            ob = bfw.next()
            fw.copy("act", ob[:, 0:384], tp_[:, 0:384])
            r0 = t0 + j * 128
            fw.st(C.VA[r0:r0 + 128, :], ob[:, 0:128])
            fw.st(C.VB[r0:r0 + 128, :], ob[:, 128:384])
            yield
            tp2 = tps.next()
            for k in range(8):
                fw.mm(tp2[:, 0:256], hT[:, k, j * 128:(j + 1) * 128], W[:, k, 1792:2048], start=(k == 0), stop=(k == 7))
            ob2 = bfw.next()
            fw.copy("dve", ob2[:, 0:256], tp2[:, 0:256])
            fw.st(C.VC[r0:r0 + 128, :], ob2[:, 0:256])
            yield

    def hT_gen(b, hT):
        t0 = b * 512
        for j in range(4):
            make_hT(fw, K, src[t0 + j * 128:t0 + (j + 1) * 128, :], gain, hT, j, P)
            yield

    def run_pipe(gens, depth, side=None):
        gens = list(gens)
        active = []
        tick = 0
        while gens or active:
            while gens and len(active) < depth:
                active.append(gens.pop(0))
            for g in list(active):
                try:
                    next(g)
                except StopIteration:
                    active.remove(g)
            tick += 1
            if side is not None and tick % 8 == 0:
                try:
                    next(side)
                except StopIteration:
                    side = None
        if side is not None:
            for _ in side:
                pass

    nb = T_ // 512
    hT_cur = hTs.next()
    for _ in hT_gen(0, hT_cur):
        pass
    for b in range(nb):
        t0 = b * 512
        tab = tabs.next()
        fw.ld(tab[:], d["c_rope"][:, :, t0:t0 + 512])
        hT = hT_cur
        side = None
        if b + 1 < nb:
            hT_cur = hTs.next()
            side = hT_gen(b + 1, hT_cur)
        gens = [chunk_gen(kind, col, dst, slot, gi, hT, tab, t0) for (kind, col, dst, slot, gi) in chunks]
        gens.append(tail_gen(hT, t0))
        run_pipe(gens, DEPTHP, side)
    ph.close()


def attn_core(fw, ph, K, T_, QT, KT, V, heads, negM, scale, emit):
    LOOK = 2
    sps = ph.pspool(4, [128, 512], F32, "sps")
    ops_ = ph.pspool(2, [128, 512], F32, "ops")
    pts = ph.sbpool(4, [128, 512], BF16, "pt")
    osbs = ph.sbpool(4, [65, 512], F32, "osb")
    nkt = T_ // 128
    iters = []
    for qb in range(T_ // 512):
        for hd in heads:
            for kt in range(nkt):
                iters.append((qb, hd, kt))
    pend = []
    state = {"o_ps": None}

    def do_pv(item):
        (qb, hd, kt, pt) = item
        (qs, lo, hi, ks, vs, tag) = hd
        if kt == 0:
            state["o_ps"] = ops_.next()
        o_ps = state["o_ps"]
        fw.mm(o_ps[0:65, :], V[:, kt, vs, :], pt[:], start=(kt == 0), stop=(kt == nkt - 1))
        if kt == nkt - 1:
            osb = osbs.next()
            fw.copy("dve", osb[:], o_ps[0:65, :])
            emit(qb, tag, osb)

    for (qb, hd, kt) in iters:
        (qs, lo, hi, ks, vs, tag) = hd
        q0 = qb * 512
        st = sps.next()
        fw.mm(st[:, :], KT[lo:hi, ks, kt * 128:(kt + 1) * 128], QT[lo:hi, qs, q0:q0 + 512])
        pt = pts.next()
        if negM is None:
            fw.act(pt[:], st[:], AF.Exp, scale=scale)
        else:
            fw.act(pt[:], st[:], AF.Exp, scale=scale, bias=negM[:, 0:1])
        pend.append((qb, hd, kt, pt))
        if len(pend) > LOOK:
            do_pv(pend.pop(0))
    while pend:
        do_pv(pend.pop(0))


def attn_core2(fw, ph, T_, groups, V, negM, scale, emit, bcs):
    LOOK = 2
    sps = ph.pspool(3, [128, 1024], F32, "sps")
    ops_ = bcs
    pts = ph.sbpool(4, [128, 1024], BF16, "pt")
    osbs = ph.sbpool(4, [65, 512], F32, "osb")
    nkt = T_ // 128
    iters = []
    for qb in range(T_ // 512):
        for g in groups:
            for kt in range(nkt):
                iters.append((qb, g, kt))
    pend = []
    state = {}
    import os
    NW = int(os.environ.get("DBG_WARM", "0"))
    if NW:
        wz = pts.next()
        fw.memset("pool", wz[:], 1.0)
        st0 = sps.next()
        for i in range(NW):
            fw.mm(st0[:, 0:512], wz[:, 0:128], wz[:, 512:1024], sig=(i == NW - 1))

    def do_pv(item):
        (qb, g, kt, pt) = item
        (kt_ap, q_ap, vs, tag) = g
        if kt == 0:
            state["o"] = [ops_.next(), ops_.next()]
        o = state["o"]
        for j in range(2):
            fw.mm(o[j][0:65, :], V[:, kt, vs[j], :], pt[:, j * 512:(j + 1) * 512], start=(kt == 0), stop=(kt == nkt - 1))
        if kt == nkt - 1:
            osb = [osbs.next(), osbs.next()]
            for j in range(2):
                fw.copy("dve", osb[j][:], o[j][0:65, :])
            emit(qb, tag, osb)

    for (qb, g, kt) in iters:
        (kt_ap, q_ap, vs, tag) = g
        st = sps.next()
        for j in range(2):
            fw.mm(st[:, j * 512:(j + 1) * 512], kt_ap(j, kt), q_ap(j, qb), sig=(j == 1))
        pt = pts.next()
        if negM is None:
            fw.act(pt[:], st[:], AF.Exp, scale=scale)
        else:
            fw.act(pt[:], st[:], AF.Exp, scale=scale, bias=negM[:, 0:1])
        pend.append((qb, g, kt, pt))
        if len(pend) > LOOK:
            do_pv(pend.pop(0))
    while pend:
        do_pv(pend.pop(0))


def bcast_recip(fw, K, osb, recs, hls, bcs, mulv):
    rec = recs.next()
    fw.act(rec[64:65, :], osb[64:65, :], AF.Ln)
    fw.act(rec[64:65, :], rec[64:65, :], AF.Exp, scale=-1.0)
    if mulv is not None:
        fw.ts("dve", rec[64:65, :], rec[64:65, :], mulv[64:65, 0:1], None, ALU.mult)
    hl = hls.next()
    fw.copy("dve", hl[64:65, 0, :], rec[64:65, :])
    fw.tt("dve", hl[64:65, 1, :], rec[64:65, :], hl[64:65, 0, :], ALU.subtract)
    bc = bcs.next()
    fw.mm(bc[0:64, :], K.ones_bf[64:65, 0:64], hl[64:65, 0, :], start=True, stop=False, sig=False)
    fw.mm(bc[0:64, :], K.ones_bf[64:65, 0:64], hl[64:65, 1, :], start=False, stop=True)
    return bc


def phase_attnA(fw, C, K, l, si):
    T_ = C.seqs[si]
    d = C.d
    ph = fw.phase()
    QT = ph.sb([128, 2, T_], BF16, "QT")
    KT = ph.sb([128, T_], BF16, "KT")
    V = ph.sb([128, T_ // 128, 2, 65], BF16, "V")
    for p in range(2):
        fw.ld(QT[0:64, p, :], C.QA[:, p, 0:T_])
        fw.ld(QT[64:128, p, :], C.QA[:, p + 2, 0:T_])
    for h in range(2):
        fw.ld(KT[64 * h:64 * h + 64, :], C.KA[:, h, 0:T_])
    fw.memset("pool", V[:, :, :, 64:65], 1.0)
    for h in range(2):
        fw.ld(V[:, :, h, 0:64], C.VA[0:T_, h * 64:(h + 1) * 64].rearrange("(n p) c -> p n c", p=128))
    g2 = ph.sb([128, 2, 64], F32, "g2")
    fw.ld(g2[:, 0, :], d["a_q_gain"][l].partition_broadcast(128))
    fw.ld(g2[:, 1, :], d["a_k_gain"][l].partition_broadcast(128))
    fw.tt("dve", g2[:], g2[:], g2[:], ALU.mult)
    mx = ph.sb([128, 2], F32, "mx")
    fw.reduce(mx[:], g2[:], ALU.max)
    negM = ph.sb([128, 1], F32, "negM")
    fw.tt("dve", negM[:], mx[:, 0:1], mx[:, 1:2], ALU.mult)
    fw.act(negM[:], negM[:], AF.Sqrt, scale=64.0)
    fw.ts("dve", negM[:], negM[:], -1.0, None, ALU.mult)
    recs = ph.sbpool(2, [65, 512], F32, "rec")
    hls = ph.sbpool(2, [65, 2, 512], BF16, "hl")
    bcs = ph.pspool(2, [128, 512], F32, "bc")
    outs = ph.sbpool(3, [64, 512], BF16, "ob")

    def emit(qb, p, osbs_):
        for j in range(2):
            h = p + 2 * j
            osb = osbs_[j]
            bc = bcast_recip(fw, K, osb, recs, hls, bcs, None)
            ob = outs.next()
            fw.tt("dve", ob[:], osb[0:64, :], bc[0:64, :], ALU.mult)
            fw.st(C.OT[h, :, qb * 512:(qb + 1) * 512], ob[:])

    groups = []
    for p in range(2):
        groups.append((lambda j, kt: KT[64 * j:64 * j + 64, kt * 128:(kt + 1) * 128],
                       (lambda p_: (lambda j, qb: QT[64 * j:64 * j + 64, p_, qb * 512:(qb + 1) * 512]))(p),
                       (0, 1), p))
    attn_core2(fw, ph, T_, groups, V, negM, 0.125, emit, bcs)
    ph.close()


def phase_attnB(fw, C, K, l, si):
    T_ = C.seqs[si]
    d = C.d
    lam_init = 0.8 - 0.6 * math.exp(-0.3 * l)
    ph = fw.phase()
    QT = ph.sb([64, 4, T_], BF16, "QT")
    KT = ph.sb([64, 4, T_], BF16, "KT")
    V = ph.sb([128, T_ // 128, 4, 65], BF16, "V")
    for h in range(4):
        fw.ld(QT[:, h, :], C.QB[:, h, 0:T_])
        fw.ld(KT[:, h, :], C.KB[:, h, 0:T_])
    fw.memset("pool", V[:, :, :, 64:65], 1.0)
    for h in range(4):
        fw.ld(V[:, :, h, 0:64], C.VB[0:T_, h * 64:(h + 1) * 64].rearrange("(n p) c -> p n c", p=128))
    lp = ph.sb([128, 4, 32], F32, "lp")
    fw.ld(lp[:], d["b_lambda"][l].partition_broadcast(128))
    pr = ph.sb([128, 2, 32], F32, "pr")
    fw.tt("dve", pr[:, 0, :], lp[:, 0, :], lp[:, 1, :], ALU.mult)
    fw.tt("dve", pr[:, 1, :], lp[:, 2, :], lp[:, 3, :], ALU.mult)
    sm = ph.sb([128, 2], F32, "sm")
    fw.reduce(sm[:], pr[:], ALU.add)
    fw.act(sm[:], sm[:], AF.Exp)
    nlam = ph.sb([128, 1], F32, "nlam")
    fw.tt("dve", nlam[:], sm[:, 1:2], sm[:, 0:1], ALU.subtract)
    fw.ts("dve", nlam[:], nlam[:], -lam_init, None, ALU.add)
    gsub = ph.sb([64, 1], F32, "gsub")
    fw.ld(gsub[:], d["b_subln_gain"][l].rearrange("(p o) -> p o", o=1))
    fw.ts("dve", gsub[:], gsub[:], 1.0 - lam_init, None, ALU.mult)
    recs = ph.sbpool(2, [65, 512], F32, "rec")
    hls = ph.sbpool(2, [65, 2, 512], BF16, "hl")
    bcs = ph.pspool(2, [128, 512], F32, "bc")
    o1s = ph.sbpool(4, [64, 512], F32, "o1")
    rss = ph.sbpool(2, [64, 512], F32, "rsb")
    sqs = ph.sbpool(2, [64, 512], BF16, "sq")
    outs = ph.sbpool(3, [64, 512], BF16, "ob")

    def emit(qb, h, osbs_):
        oo = []
        for s_ in range(2):
            osb = osbs_[s_]
            bc = bcast_recip(fw, K, osb, recs, hls, bcs, nlam if s_ == 1 else None)
            o_ = o1s.next()
            fw.tt("dve", o_[:], osb[0:64, :], bc[0:64, :], ALU.mult)
            oo.append(o_)
        o1, o2 = oo
        fw.tt("pool", o1[:], o1[:], o2[:], ALU.add)
        sq = sqs.next()
        fw.act(sq[:], o1[:], AF.Square)
        ms = bcs.next()
        fw.mm(ms[0:64, :], K.ones_bf[0:64, 0:64], sq[:])
        rs = rss.next()
        fw.ts("dve", rs[:], ms[0:64, :], 1.0 / 64, EPS, ALU.mult, ALU.add)
        fw.act(rs[:], rs[:], AF.Ln)
        fw.act(rs[:], rs[:], AF.Exp, scale=-0.5)
        ob = outs.next()
        fw.stt(ob[:], o1[:], gsub[:, 0:1], rs[:], ALU.mult, ALU.mult)
        fw.st(C.OT[4 + h, :, qb * 512:(qb + 1) * 512], ob[:])

    groups = []
    for h in range(4):
        groups.append(((lambda h_: (lambda j, kt: KT[32 * j:32 * j + 32, h_, kt * 128:(kt + 1) * 128]))(h),
                       (lambda h_: (lambda j, qb: QT[32 * j:32 * j + 32, h_, qb * 512:(qb + 1) * 512]))(h),
                       (h, h), h))
    attn_core2(fw, ph, T_, groups, V, None, 32 ** -0.5, emit, bcs)
    ph.close()


def phase_ret(fw, C, K, l, si):
    T_ = C.seqs[si]
    d = C.d
    n = T_ // 128
    ph = fw.phase()
    QT = ph.sb([64, 4, T_], BF16, "QT")
    KT = ph.sb([64, 4, T_], BF16, "KT")
    G = ph.sb([64, 4, T_], BF16, "G")
    V = ph.sb([128, n, 256], BF16, "V")
    for h in range(4):
        fw.ld(QT[:, h, :], C.QC[:, h, 0:T_])
        fw.ld(KT[:, h, :], C.KC[:, h, 0:T_])
        fw.ld(G[:, h, :], C.GC[:, h, 0:T_])
    fw.ld(V[:], C.VC[0:T_, :].rearrange("(n p) c -> p n c", p=128))
    mask = ph.sb([128, 4, 128], F32, "mask")
    fw.ld(mask[:], d["c_retmask"])
    qdec = ph.sb([64, 2, 4, 128], F32, "qdec")
    fw.ld(qdec[:], d["c_qdec"])
    kdec = ph.sb([128, 2, 4], F32, "kdec")
    fw.ld(kdec[:], d["c_kdec"])
    gng = ph.sb([64, 4], F32, "gng")
    fw.ldn(gng[:], d["c_gn_gain"][l].rearrange("(h p) -> p h", p=64))
    Sf = ph.sb([64, n, 4, 64], BF16, "Sf")
    Sb = ph.sb([64, n, 4, 64], BF16, "Sb")
    cur = [ph.sb([64, 4, 64], F32, "curf"), ph.sb([64, 4, 64], F32, "curb")]
    ktp = ph.pspool(1, [128, 4, 64], BF16, "ktp")
    ktm = ph.sbpool(2, [128, 4, 64], BF16, "ktm")
    kds = ph.sbpool(2, [128, 4, 64], BF16, "kds")
    kvp = ph.pspool(2, [128, 512], F32, "kvp")
    gam = [[RET_G[h] ** 128 for h in range(4)], [RET_G[3 - h] ** 128 for h in range(4)]]
    for dr in range(2):
        fw.memset("dve", cur[dr][:], 0.0)
        order = range(n) if dr == 0 else range(n - 1, -1, -1)
        Sx = Sf if dr == 0 else Sb
        for c in order:
            fw.copy("act", Sx[:, c, :, :], cur[dr][:])
            if (dr == 0 and c == n - 1) or (dr == 1 and c == 0):
                break
            tp = ktp.next()
            for h in range(4):
                fw.tr(tp[:, h, :], KT[:, h, c * 128:(c + 1) * 128], K.ident[0:64, 0:64], sig=(h == 3))
            kd = kds.next()
            fw.tt("dve", kd[:], tp[:], kdec.bc((slice(None), dr, slice(None)), 2, [128, 4, 64]), ALU.mult)
            kv = kvp.next()
            for h in range(4):
                fw.mm(kv[0:64, h * 64:(h + 1) * 64], kd[:, h, :], V[:, c, h * 64:(h + 1) * 64], sig=(h == 3))
            for h in range(4):
                fw.stt(cur[dr][:, h, :], cur[dr][:, h, :], gam[dr][h], kv[0:64, h * 64:(h + 1) * 64], ALU.mult, ALU.add)
    inp = ph.pspool(2, [128, 512], F32, "inp")
    ins_ = ph.sbpool(2, [128, 4, 128], BF16, "ins")
    qds = ph.sbpool(2, [64, 2, 4, 128], BF16, "qds")
    otp = ph.pspool(2, [128, 512], F32, "otp")
    osb = ph.sbpool(2, [64, 4, 128], F32, "osb")
    sqs = ph.sbpool(2, [64, 4, 128], BF16, "sq")
    msp = ph.pspool(1, [128, 512], F32, "msp")
    rss = ph.sbpool(2, [64, 4, 128], F32, "rs")
    obs = ph.sbpool(2, [64, 4, 128], BF16, "ob")
    for c in range(n):
        cs = slice(c * 128, (c + 1) * 128)
        ip = inp.next()
        for h in range(4):
            fw.mm(ip[:, h * 128:(h + 1) * 128], KT[:, h, cs], QT[:, h, cs], sig=(h == 3))
        it = ins_.next()
        fw.tt("dve", it[:], View(ip, ip.ap[:, :].rearrange("p (h i) -> p h i", h=4)), mask[:], ALU.mult)
        qd = qds.next()
        for dr in range(2):
            fw.tt("pool", qd[:, dr, :, :], QT[:, :, cs], qdec[:, dr, :, :], ALU.mult)
        op_ = otp.next()
        for h in range(4):
            o_h = op_[0:64, h * 128:(h + 1) * 128]
            fw.mm(o_h, V[:, c, h * 64:(h + 1) * 64], it[:, h, :], start=True, stop=False, sig=False)
            fw.mm(o_h, Sf[:, c, h, :], qd[:, 0, h, :], start=False, stop=False, sig=False)
            fw.mm(o_h, Sb[:, c, h, :], qd[:, 1, h, :], start=False, stop=True, sig=(h == 3))
        o = osb.next()
        fw.ts("dve", o[:], View(op_, op_.ap[0:64, :].rearrange("p (h i) -> p h i", h=4)), 0.125, None, ALU.mult)
        sq = sqs.next()
        fw.act(sq[:], o[:], AF.Square)
        ms = msp.next()
        fw.mm(ms[0:64, :], K.ones_bf[0:64, 0:64], View(sq, sq.ap[:].rearrange("p h i -> p (h i)")))
        rs = rss.next()
        fw.act(View(rs, rs.ap[:].rearrange("p h i -> p (h i)")), ms[0:64, :], AF.Sqrt, bias=EPS, scale=1.0 / 64)
        fw.recip(rs[:], rs[:])
        fw.tt("dve", o[:], o[:], rs[:], ALU.mult)
        fw.tt("pool", o[:], o[:], gng.bc((slice(None), slice(None)), 2, [64, 4, 128]), ALU.mult)
        ob = obs.next()
        fw.tt("dve", ob[:], o[:], G[:, :, cs], ALU.mult)
        for h in range(4):
            fw.st(C.OT[8 + h, :, cs], ob[:, h, :])
    ph.close()


def phase_outproj(fw, C, K, l, si):
    T_ = C.seqs[si]
    d = C.d
    src = C.xin[si] if l == 0 else C.y[si]
    ph = fw.phase()
    W = ph.sb([128, 8, 1024], BF16, "Wo")
    for k in range(8):
        fw.ld(W[:, k, :], d["w_out"][l, k * 128:(k + 1) * 128, :], q="pool")
    gain = ph.sb([128, 1024], F32, "gain")
    fw.ld(gain[:], d["norm_mix_post"][l].partition_broadcast(128))
    oTs = ph.sbpool(2, [128, 8, 512], BF16, "oT")
    xts = ph.sbpool(2, [128, 1024], F32, "xt")
    mps = ph.pspool(4, [128, 512], F32, "mp")
    tmps = ph.sbpool(2, [128, 1024], F32, "tmp")
    sss = ph.sbpool(2, [128, 2], F32, "ss")
    rstds = ph.sbpool(2, [128, 1], F32, "rstd")
    for b in range(T_ // 512):
        t0 = b * 512
        oT = oTs.next()
        for k in range(8):
            fw.ld(oT[0:64, k, :], C.OT[2 * k, :, t0:t0 + 512])
            fw.ld(oT[64:128, k, :], C.OT[2 * k + 1, :, t0:t0 + 512])
        for j in range(4):
            r0 = t0 + j * 128
            xt = xts.next()
            fw.ld(xt[:], src[r0:r0 + 128, :])
            m = [mps.next(), mps.next()]
            for half in range(2):
                for k in range(8):
                    fw.mm(m[half][:, :], oT[:, k, j * 128:(j + 1) * 128], W[:, k, half * 512:(half + 1) * 512], start=(k == 0), stop=(k == 7))
            tmp = tmps.next()
            ss = sss.next()
            for half in range(2):
                fw.act(tmp[:, half * 512:(half + 1) * 512], m[half][:, :], AF.Square, accum=ss[:, half:half + 1])
            rstd = rstds.next()
            fw.tt("dve", rstd[:], ss[:, 0:1], ss[:, 1:2], ALU.add)
            fw.act(rstd[:], rstd[:], AF.Sqrt, bias=EPS, scale=1.0 / D_MODEL)
            fw.recip(rstd[:], rstd[:])
            for half in range(2):
                hs = slice(half * 512, (half + 1) * 512)
                fw.stt(tmp[:, hs], m[half][:, :], rstd[:, 0:1], gain[:, hs], ALU.mult, ALU.mult)
            fw.tt("pool", tmp[:], tmp[:], xt[:], ALU.add)
            fw.st(C.y[si][r0:r0 + 128, :], tmp[:])
    ph.close()


def phase_ffn(fw, C, K, l, si):
    T_ = C.seqs[si]
    d = C.d
    TB = 1024
    ph = fw.phase()
    gain = ph.sb([128, 1024], F32, "gain")
    fw.ld(gain[:], d["norm_ffn_pre"][l].partition_broadcast(128))
    gpost = ph.sb([128, 1024], F32, "gpost")
    fw.ld(gpost[:], d["norm_ffn_post"][l].partition_broadcast(128))
    P = Ctx()
    P.xt = ph.sbpool(2, [128, 1024], F32, "xt")
    P.hb = ph.sbpool(2, [128, 1024], BF16, "hb")
    P.tmp = ph.sbpool(2, [128, 1024], F32, "tmp")
    P.ss = ph.sbpool(2, [128, 1], F32, "ss")
    P.rstd = ph.sbpool(2, [128, 1], F32, "rstd")
    P.tp = ph.pspool(1, [128, 8, 128], BF16, "tp")
    hT = ph.sb([128, 8, TB], BF16, "hT")
    act = ph.sb([128, NFF, TB], BF16, "act")
    wgs = ph.sbpool(3, [128, 8, 128], BF16, "wg")
    wus = ph.sbpool(3, [128, 8, 128], BF16, "wu")
    Wd = ph.sb([128, NFF, 1024], BF16, "Wd")
    for f in range(NFF):
        fw.ld(Wd[:, f, :], d["ffn_w_down"][l, f * 128:(f + 1) * 128, :], q="pool")
    gps = ph.pspool(2, [128, 512], F32, "gp")
    ups = ph.pspool(2, [128, 512], F32, "up")
    dps = ph.pspool(2, [128, 512], F32, "dp")
    sgs = ph.sbpool(2, [128, 512], F32, "sg")
    sss = ph.sbpool(2, [128, 2], F32, "ss2")
    wg_d = d["ffn_w_gate"][l].rearrange("(k p) n -> p k n", p=128)
    wu_d = d["ffn_w_up"][l].rearrange("(k p) n -> p k n", p=128)
    for b in range(T_ // TB):
        t0 = b * TB
        for j in range(TB // 128):
            make_hT(fw, K, C.y[si][t0 + j * 128:t0 + (j + 1) * 128, :], gain, hT, j, P)
        for f in range(NFF):
            wg = wgs.next()
            fw.ld(wg[:], wg_d[:, :, f * 128:(f + 1) * 128], q="pool")
            wu = wus.next()
            fw.ld(wu[:], wu_d[:, :, f * 128:(f + 1) * 128], q="pool")
            for tb in range(TB // 512):
                ts_ = slice(tb * 512, (tb + 1) * 512)
                gp = gps.next()
                up = ups.next()
                for k in range(8):
                    fw.mm(gp[:, :], wg[:, k, :], hT[:, k, ts_], start=(k == 0), stop=(k == 7))
                for k in range(8):
                    fw.mm(up[:, :], wu[:, k, :], hT[:, k, ts_], start=(k == 0), stop=(k == 7))
                sg = sgs.next()
                fw.act(sg[:], gp[:, :], AF.Silu)
                fw.tt("dve", act[:, f, ts_], sg[:], up[:, :], ALU.mult)
        ph_all = Pool(dps.items + gps.items + ups.items)
        for g0 in range(0, TB // 128, 3):
            subs = list(range(g0, min(g0 + 3, TB // 128)))
            banks = {(j, half): ph_all.next() for j in subs for half in range(2)}
            for f in range(NFF):
                for j in subs:
                    for half in range(2):
                        fw.mm(banks[(j, half)][:, :], act[:, f, j * 128:(j + 1) * 128], Wd[:, f, half * 512:(half + 1) * 512],
                              start=(f == 0), stop=(f == NFF - 1))
            for j in subs:
                r0 = t0 + j * 128
                xt = P.xt.next()
                fw.ld(xt[:], C.y[si][r0:r0 + 128, :])
                tmp = P.tmp.next()
                ss = sss.next()
                for half in range(2):
                    fw.act(tmp[:, half * 512:(half + 1) * 512], banks[(j, half)][:, :], AF.Square, accum=ss[:, half:half + 1])
                rstd = P.rstd.next()
                fw.tt("dve", rstd[:], ss[:, 0:1], ss[:, 1:2], ALU.add)
                fw.act(rstd[:], rstd[:], AF.Sqrt, bias=EPS, scale=1.0 / D_MODEL)
                fw.recip(rstd[:], rstd[:])
                for half in range(2):
                    hs = slice(half * 512, (half + 1) * 512)
                    fw.stt(tmp[:, hs], banks[(j, half)][:, :], rstd[:, 0:1], gpost[:, hs], ALU.mult, ALU.mult)
                fw.tt("pool", tmp[:], tmp[:], xt[:], ALU.add)
                fw.st(C.y[si][r0:r0 + 128, :], tmp[:])
    ph.close()


def phase_rwkv(fw, C, K, l, si):
    T_ = C.seqs[si]
    d = C.d
    n = T_ // 128
    CW = math.exp(-0.5)
    ph = fw.phase()
    mu = ph.sb([64, 3, 15], F32, "mu")
    fw.ldn(mu[:, 1, :], d["d_mu_prev"][l, 0:960].rearrange("(j p) -> p j", p=64))
    fw.ldn(mu[:, 2, :], d["d_mu_next"][l, 0:960].rearrange("(j p) -> p j", p=64))
    fw.tt("dve", mu[:, 0, :], mu[:, 1, :], mu[:, 2, :], ALU.add)
    fw.ts("dve", mu[:, 0, :], mu[:, 0, :], -1.0, 1.0, ALU.mult, ALU.add)
    mug = ph.sb([128, 3], F32, "mug")
    fw.ld(mug[:, 1:2], d["d_mu_prev"][l, 960:1088].rearrange("(p o) -> p o", o=1))
    fw.ld(mug[:, 2:3], d["d_mu_next"][l, 960:1088].rearrange("(p o) -> p o", o=1))
    fw.tt("dve", mug[:, 0:1], mug[:, 1:2], mug[:, 2:3], ALU.add)
    fw.ts("dve", mug[:, 0:1], mug[:, 0:1], -1.0, 1.0, ALU.mult, ALU.add)
    prm = ph.sb([64, 9, 4], F32, "prm")
    fw.ldn(prm[:, 0, :], d["d_w0"][l, 0].rearrange("(h p) -> p h", p=64))
    fw.ldn(prm[:, 1, :], d["d_w0"][l, 1].rearrange("(h p) -> p h", p=64))
    fw.ldn(prm[:, 2, :], d["d_a0"][l].rearrange("(h p) -> p h", p=64))
    fw.ldn(prm[:, 3, :], d["d_k_k"][l].rearrange("(h p) -> p h", p=64))
    fw.ldn(prm[:, 4, :], d["d_k_a"][l].rearrange("(h p) -> p h", p=64))
    fw.ldn(prm[:, 6, :], d["d_r_k"][l].rearrange("h p -> p h"))
    fw.ldn(prm[:, 7, :], d["d_gn_w"][l].rearrange("(h p) -> p h", p=64))
    fw.ldn(prm[:, 8, :], d["d_gn_b"][l].rearrange("(h p) -> p h", p=64))
    fw.ts("dve", prm[:, 5, :], prm[:, 4, :], -1.0, 1.0, ALU.mult, ALU.add)
    wup = ph.sb([64, 2, 256], BF16, "wup")
    fw.ld(wup[:, 0, :], d["d_w_up"][l, 0], q="pool")
    fw.ld(wup[:, 1, :], d["d_w_up"][l, 1], q="pool")
    aup = ph.sb([64, 256], BF16, "aup")
    fw.ld(aup[:], d["d_a_up"][l], q="pool")
    gup = ph.sb([128, 256], BF16, "gup")
    fw.ld(gup[:], d["d_g_up"][l], q="pool")
    dmask = ph.sb([128, 2, 5, 128], BF16, "dmask")
    fw.ld(dmask[:], d["c_dmask"], q="pool")
    seg = ph.sb([64, 512], F32, "seg")
    fw.ld(seg[:], d["c_seg"])

    def pbc(i):
        return prm.bc((slice(None), i, slice(None)), 2, [64, 4, 128])

    ytok = [T(None, "ytok%d" % c) for c in range(n)]
    H = [ph.sb([64, 4, 64], F32, "Hf"), ph.sb([64, 4, 64], F32, "Hb")]
    Hb = [ph.sb([64, 4, 64], BF16, "Hfb"), ph.sb([64, 4, 64], BF16, "Hbb")]
    for dr in range(2):
        fw.memset("dve", H[dr][:], 0.0)
        fw.memset("pool", Hb[dr][:], 0.0)
    PST = ph.pspool(1, [128, 4, 4, 64], BF16, "pst")
    PP = []
    for dr in range(2):
        Q = Ctx()
        Q.PS = ph.pspool(4 if dr == 0 else 3, [128, 512], F32, "ps")
        Q.zts = ph.sbpool(1, [64, 15, 130], F32, "zt")
        Q.zgs = ph.sbpool(2, [128, 130], F32, "zg")
        Q.us = ph.sbpool(1, [64, 15, 128], F32, "u")
        Q.u2s = ph.sbpool(1, [64, 15, 128], F32, "u2")
        Q.ugs = ph.sbpool(1, [128, 128], F32, "ug")
        Q.F4 = ph.sbpool(10, [64, 4, 128], F32, "f4")
        Q.FL = ph.sbpool(3, [64, 4, 128], F32, "fl")
        Q.B4 = ph.sbpool(12, [64, 4, 128], BF16, "b4")
        Q.M4 = ph.sbpool(9, [128, 4, 128], BF16, "m4")
        Q.MK = ph.sbpool(4, [128, 4, 128], BF16, "mk")
        Q.TM = ph.sbpool(2, [128, 4, 4, 64], BF16, "tm")
        Q.SM = ph.sbpool(4, [128, 128], BF16, "sm")
        Q.S4 = ph.sbpool(6, [64, 4], F32, "s4")
        Q.UV = ph.sbpool(2, [128, 4, 64], F32, "uv")
        Q.UB = ph.sbpool(3, [128, 4, 64], BF16, "ub")
        PP.append(Q)

    def v3(t, npart=64):
        return View(t, t.ap[0:npart, :].rearrange("p (h i) -> p h i", h=4))

    def v3s(t):
        return View(t, t.ap[:, 0:256].rearrange("p (h i) -> p h i", h=4))

    def flat(t):
        return View(t, t.ap[:].rearrange("p h i -> p (h i)"))

    def unit(c, dr, first):
        Q = PP[dr]
        PS, F4, FL, B4, M4, MK, SM, S4 = Q.PS, Q.F4, Q.FL, Q.B4, Q.M4, Q.MK, Q.SM, Q.S4
        need_post = not first
        c0 = c * 128
        zt = Q.zts.next()
        zg = Q.zgs.next()
        lo = 1 if c == 0 else 0
        hi = 129 if c == n - 1 else 130
        if c == 0:
            fw.memset("pool", zt[:, :, 0:1], 0.0)
            fw.memset("pool", zg[:, 0:1], 0.0)
        if c == n - 1:
            fw.memset("pool", zt[:, :, 129:130], 0.0)
            fw.memset("pool", zg[:, 129:130], 0.0)
        fw.ld(zt[:, :, lo:hi], C.ZD[:, :, c0 - 1 + lo:c0 - 1 + hi])
        fw.ld(zg[:, lo:hi], C.ZG[:, c0 - 1 + lo:c0 - 1 + hi])
        if need_post:
            yprev = FL.next()
            fw.ld(yprev[:], View(ytok[c], C.YD[c]))
        yield
        u = Q.us.next()
        u2 = Q.u2s.next()
        fw.tt("dve", u[:], zt[:, :, 1:129], mu.bc((slice(None), 0, slice(None)), 2, [64, 15, 128]), ALU.mult)
        fw.tt("pool", u2[:], zt[:, :, 0:128], mu.bc((slice(None), 1, slice(None)), 2, [64, 15, 128]), ALU.mult)
        yield
        fw.tt("dve", u[:], u[:], u2[:], ALU.add)
        fw.tt("pool", u2[:], zt[:, :, 2:130], mu.bc((slice(None), 2, slice(None)), 2, [64, 15, 128]), ALU.mult)
        yield
        fw.tt("dve", u[:], u[:], u2[:], ALU.add)
        ug = Q.ugs.next()
        fw.ts("dve", ug[:], zg[:, 1:129], mug[:, 0:1], None, ALU.mult)
        fw.stt(ug[:], zg[:, 0:128], mug[:, 1:2], ug[:], ALU.mult, ALU.add)
        fw.stt(ug[:], zg[:, 2:130], mug[:, 2:3], ug[:], ALU.mult, ALU.add)
        yield
        tw = SM.next()
        fw.act(tw[0:64, :], u[:, 12 + dr, :], AF.Tanh)
        adb = SM.next()
        fw.copy("act", adb[0:64, :], u[:, 14, :])
        yw = PS.next()
        ya = PS.next()
        for h in range(4):
            fw.mm(yw[0:64, h * 128:(h + 1) * 128], wup[:, dr, h * 64:(h + 1) * 64], tw[0:64, :], sig=(h == 3))
        for h in range(4):
            fw.mm(ya[0:64, h * 128:(h + 1) * 128], aup[:, h * 64:(h + 1) * 64], adb[0:64, :], sig=(h == 3))
        yield
        sg = F4.next()
        fw.tt("dve", sg[:], v3(yw), pbc(dr), ALU.add)
        fw.act(sg[:], sg[:], AF.Tanh, scale=0.5)
        fw.ts("pool", sg[:], sg[:], 0.5, 0.5, ALU.mult, ALU.add)
        a = F4.next()
        fw.tt("dve", a[:], v3(ya), pbc(2), ALU.add)
        fw.act(a[:], a[:], AF.Tanh, scale=0.5)
        fw.ts("pool", a[:], a[:], 0.5, 0.5, ALU.mult, ALU.add)
        yield
        if need_post:
            sgd = SM.next()
            fw.act(sgd[:], ug[:], AF.Tanh, scale=0.5)
            fw.ts("pool", sgd[:], sgd[:], 0.5, 0.5, ALU.mult, ALU.add)
            gps_ = PS.next()
            for h in range(4):
                fw.mm(gps_[0:64, h * 128:(h + 1) * 128], gup[:, h * 64:(h + 1) * 64], sgd[:], sig=(h == 3))
            g_ = FL.next()
            fw.copy("act", g_[:], v3(gps_))
            yield
        r = u[:, 0:4, :]
        k = u[:, 4:8, :]
        v = u[:, 8:12, :]
        kk = F4.next()
        fw.tt("dve", kk[:], k, pbc(3), ALU.mult)
        sq = B4.next()
        fw.tt("pool", sq[:], kk[:], kk[:], ALU.mult)
        ssp = PS.next()
        fw.mm(ssp[0:64, :], K.ones_bf[0:64, 0:64], flat(sq))
        yield
        nr = F4.next()
        fw.ts("dve", nr[:], v3(ssp), 1e-24, None, ALU.max)
        fw.act(nr[:], nr[:], AF.Ln)
        fw.act(nr[:], nr[:], AF.Exp, scale=-0.5)
        kap = F4.next()
        fw.tt("dve", kap[:], kk[:], nr[:], ALU.mult)
        yield
        tk = F4.next()
        fw.tt("pool", tk[:], a[:], pbc(4), ALU.mult)
        fw.tt("pool", tk[:], tk[:], pbc(5), ALU.add)
        kh = FL.next()
        fw.tt("dve", kh[:], k, tk[:], ALU.mult)
        bb = F4.next()
        fw.tt("pool", bb[:], kap[:], a[:], ALU.mult)
        yield
        Pp = F4.next()
        fw.scan(flat(Pp), seg[:], flat(sg), 0.0, ALU.mult, ALU.add)
        Qp = F4.next()
        fw.tt("pool", Qp[:], Pp[:], sg[:], ALU.subtract)
        nT = S4.next()
        pT = S4.next()
        fw.ts("dve", nT[:], Pp[:, :, 127], -CW, None, ALU.mult)
        fw.ts("dve", pT[:], Pp[:, :, 127], CW, None, ALU.mult)
        gC = S4.next()
        fw.act(gC[:], nT[:], AF.Exp)
        yield
        rt = B4.next()
        kp = B4.next()
        kt = B4.next()
        bt = B4.next()
        Khf = B4.next()
        Bhf = B4.next()
        E1 = F4.next()
        E2 = F4.next()
        if dr == 0:
            fw.act(E1[:], Pp[:], AF.Exp, scale=-CW)
            fw.tt("dve", rt[:], r, E1[:], ALU.mult)
            fw.act(E2[:], Qp[:], AF.Exp, scale=-CW)
            fw.tt("pool", kp[:], kap[:], E2[:], ALU.mult)
            yield
            E3 = F4.next()
            fw.act(E3[:], Pp[:], AF.Exp, scale=CW)
            fw.tt("dve", kt[:], kh[:], E3[:], ALU.mult)
            fw.tt("pool", bt[:], bb[:], E3[:], ALU.mult)
            E4 = F4.next()
            for h in range(4):
                fw.act(E4[:, h, :], Pp[:, h, :], AF.Exp, scale=CW, bias=nT[:, h:h + 1])
        else:
            for h in range(4):
                fw.act(E1[:, h, :], Qp[:, h, :], AF.Exp, scale=CW, bias=nT[:, h:h + 1])
            fw.tt("dve", rt[:], r, E1[:], ALU.mult)
            for h in range(4):
                fw.act(E2[:, h, :], Pp[:, h, :], AF.Exp, scale=CW, bias=nT[:, h:h + 1])
            fw.tt("pool", kp[:], kap[:], E2[:], ALU.mult)
            yield
            E3 = F4.next()
            for h in range(4):
                fw.act(E3[:, h, :], Qp[:, h, :], AF.Exp, scale=-CW, bias=pT[:, h:h + 1])
            fw.tt("dve", kt[:], kh[:], E3[:], ALU.mult)
            fw.tt("pool", bt[:], bb[:], E3[:], ALU.mult)
            E4 = F4.next()
            fw.act(E4[:], Qp[:], AF.Exp, scale=-CW)
        yield
        fw.tt("dve", Khf[:], kh[:], E4[:], ALU.mult)
        fw.stt(Bhf[:], bb[:], -1.0, E4[:], ALU.mult, ALU.mult)
        vb = B4.next()
        fw.copy("act", vb[:], v)
        pst = PST.next()
        for ki, src_ in enumerate((kp, Khf, Bhf, vb)):
            for h in range(4):
                fw.tr(pst[:, ki, h, :], src_[:, h, :], K.ident[0:64, 0:64], sig=(ki == 3 and h == 3))
        tm = Q.TM.next()
        fw.copy("dve", tm[:], pst[:])
        yield
        mats = []
        pairs = [(kp, bt), (bt, kp), (kt, kp), (kt, rt), (bt, rt)]
        for ki, (lt, rt_) in enumerate(pairs):
            p_ = PS.next()
            for h in range(4):
                fw.mm(p_[:, h * 128:(h + 1) * 128], lt[:, h, :], rt_[:, h, :], sig=(h == 3))
            m_ = M4.next() if ki < 2 else MK.next()
            fw.tt("dve", m_[:], v3(p_, 128), dmask.bc((slice(None), dr, ki, slice(None)), 1, [128, 4, 128]), ALU.mult)
            mats.append(m_)
            yield
        X, XT, AkkT, ArkT, ArbT = mats
        PT = M4.next()
        fw.tt("pool", PT[:], XT[:], K.ident.bc((slice(None), slice(None)), 1, [128, 4, 128]), ALU.add)
        for lvl in range(6):
            p1 = PS.next()
            for h in range(4):
                fw.mm(p1[:, h * 128:(h + 1) * 128], XT[:, h, :], X[:, h, :], sig=(h == 3))
            if lvl < 5:
                p2 = PS.next()
                for h in range(4):
                    fw.mm(p2[:, h * 128:(h + 1) * 128], X[:, h, :], XT[:, h, :], sig=(h == 3))
            yield
            X2 = M4.next()
            fw.copy("act", X2[:], v3(p1, 128))
            if lvl < 5:
                XT2 = M4.next()
                fw.copy("dve", XT2[:], v3(p2, 128))
            p3 = PS.next()
            for h in range(4):
                fw.mm(p3[:, h * 128:(h + 1) * 128], X2[:, h, :], PT[:, h, :], sig=(h == 3))
            yield
            PT2 = M4.next()
            fw.tt("dve", PT2[:], v3(p3, 128), PT[:], ALU.add)
            PT = PT2
            X = X2
            if lvl < 5:
                XT = XT2
        pw = PS.next()
        for h in range(4):
            fw.mm(pw[0:64, h * 128:(h + 1) * 128], tm[:, 0, h, :], PT[:, h, :], sig=(h == 3))
        pa = PS.next()
        for h in range(4):
            fw.mm(pa[:, h * 64:(h + 1) * 64], AkkT[:, h, :], tm[:, 3, h, :], sig=(h == 3))
        yield
        WkT = B4.next()
        fw.copy("act", WkT[:], v3(pw))
        AV = Q.UB.next()
        fw.copy("dve", AV[:], v3s(pa))
        pu = PS.next()
        for h in range(4):
            fw.mm(pu[:, h * 64:(h + 1) * 64], PT[:, h, :], AV[:, h, :], sig=(h == 3))
        yield
        Uv = Q.UV.next()
        fw.copy("act", Uv[:], v3s(pu))
        pU = PS.next()
        for h in range(4):
            fw.mm(pU[:, h * 64:(h + 1) * 64], WkT[:, h, :], Hb[dr][:, h, :], sig=(h == 3))
        yield
        U = Q.UB.next()
        fw.tt("dve", U[:], v3s(pU), Uv[:], ALU.add)
        pY = PS.next()
        for h in range(4):
            yh = pY[0:64, h * 128:(h + 1) * 128]
            fw.mm(yh, Hb[dr][:, h, :], rt[:, h, :], start=True, stop=False, sig=False)
            fw.mm(yh, tm[:, 3, h, :], ArkT[:, h, :], start=False, stop=False, sig=False)
            fw.mm(yh, U[:, h, :], ArbT[:, h, :], start=False, stop=True, sig=(h == 3))
        pH = PS.next()
        for h in range(4):
            hh = pH[0:64, h * 64:(h + 1) * 64]
            fw.mm(hh, tm[:, 1, h, :], tm[:, 3, h, :], start=True, stop=False, sig=False)
            fw.mm(hh, tm[:, 2, h, :], U[:, h, :], start=False, stop=True, sig=(h == 3))
        fw.tt("pool", H[dr][:], H[dr][:], gC.bc((slice(None), slice(None)), 2, [64, 4, 64]), ALU.mult)
        yield
        fw.tt("dve", H[dr][:], H[dr][:], View(pH, pH.ap[0:64, 0:256].rearrange("p (h i) -> p h i", h=4)), ALU.add)
        fw.copy("act", Hb[dr][:], H[dr][:])
        if first:
            ysb = F4.next()
            fw.copy("act", ysb[:], v3(pY))
            fw.st(View(ytok[c], C.YD[c]), ysb[:])
            return
        y = F4.next()
        fw.tt("dve", y[:], v3(pY), yprev[:], ALU.add)
        pm_ = PS.next()
        fw.mm(pm_[0:64, :], K.ones_f[0:64, 0:64], flat(y))
        rk = F4.next()
        fw.tt("pool", rk[:], r, kh[:], ALU.mult)
        fw.tt("pool", rk[:], rk[:], pbc(6), ALU.mult)
        yield
        yc = F4.next()
        fw.stt(yc[:], v3(pm_), -1.0 / 64, y[:], ALU.mult, ALU.add)
        sq2 = F4.next()
        fw.act(sq2[:], yc[:], AF.Square)
        pv = PS.next()
        fw.mm(pv[0:64, :], K.ones_f[0:64, 0:64], flat(sq2))
        pr_ = PS.next()
        fw.mm(pr_[0:64, :], K.ones_f[0:64, 0:64], flat(rk))
        yield
        rs = F4.next()
        fw.ts("dve", rs[:], v3(pv), 1.0 / 64, GN_EPS, ALU.mult, ALU.add)
        fw.act(rs[:], rs[:], AF.Ln)
        fw.act(rs[:], rs[:], AF.Exp, scale=-0.5)
        fw.tt("dve", yc[:], yc[:], rs[:], ALU.mult)
        fw.tt("pool", yc[:], yc[:], pbc(7), ALU.mult)
        yield
        fw.tt("pool", yc[:], yc[:], pbc(8), ALU.add)
        t_ = F4.next()
        fw.tt("dve", t_[:], v3(pr_), v, ALU.mult)
        fw.tt("pool", yc[:], yc[:], t_[:], ALU.add)
        ob = B4.next()
        fw.tt("dve", ob[:], yc[:], g_[:], ALU.mult)
        for h in range(4):
            fw.st(C.OT[12 + h, :, c * 128:(c + 1) * 128], ob[:, h, :])

    for i in range(n):
        first = i < n // 2
        gens = [unit(i, 0, first), unit(n - 1 - i, 1, first)]
        alive = [True, True]
        while any(alive):
            for gi in range(2):
                if alive[gi]:
                    try:
                        next(gens[gi])
                    except StopIteration:
                        alive[gi] = False
    ph.close()


def build(seqs=(4096, 2048), depth=2, phases=None, debug=False):
    nc = bass.Bass("TRN2", target_bir_lowering=False)
    C = Ctx()
    C.seqs = list(seqs)
    C.d = {}
    C.xin = [nc.dram_tensor("x%d" % i, [t, D_MODEL], F32, kind="ExternalInput").ap() for i, t in enumerate(seqs)]
    C.y = [nc.dram_tensor("y%d" % i, [t, D_MODEL], F32, kind="ExternalOutput").ap() for i, t in enumerate(seqs)]
    for nm, shp in WEIGHT_SPECS:
        C.d[nm] = nc.dram_tensor(nm, list(shp), F32, kind="ExternalInput").ap()
    for nm, arr in make_consts().items():
        C.d[nm] = nc.dram_tensor(nm, list(arr.shape), F32, kind="ExternalInput").ap()
    TM = max(seqs)
    sk = "ExternalOutput" if debug else "Internal"

    def scratch(nm, shp, dt):
        return nc.dram_tensor(nm, list(shp), dt, kind=sk).ap()
    C.QA = scratch("QA", [64, 4, TM], BF16)
    C.KA = scratch("KA", [64, 2, TM], BF16)
    C.VA = scratch("VA", [TM, 128], BF16)
    C.QB = scratch("QB", [64, 4, TM], BF16)
    C.KB = scratch("KB", [64, 4, TM], BF16)
    C.VB = scratch("VB", [TM, 256], BF16)
    C.QC = scratch("QC", [64, 4, TM], BF16)
    C.KC = scratch("KC", [64, 4, TM], BF16)
    C.GC = scratch("GC", [64, 4, TM], BF16)
    C.VC = scratch("VC", [TM, 256], BF16)
    C.ZD = scratch("ZD", [64, 15, TM], F32)
    C.ZG = scratch("ZG", [128, TM], F32)
    C.OT = scratch("OT", [16, 64, TM], BF16)
    C.YD = scratch("YD", [TM // 128, 64, 4, 128], F32)
    allph = ["inproj", "A", "B", "C", "D", "outproj", "ffn"]
    phases = allph if phases is None else phases
    with ExitStack() as es:
        fw = FW(nc, es)
        ph0 = fw.phase()
        K = load_consts(fw, C, ph0)
        fw.barrier()
        for l in range(depth):
            for si in range(len(seqs)):
                if "inproj" in phases:
                    phase_inproj(fw, C, K, l, si)
                if "A" in phases:
                    phase_attnA(fw, C, K, l, si)
                if "B" in phases:
                    phase_attnB(fw, C, K, l, si)
                if "C" in phases:
                    phase_ret(fw, C, K, l, si)
                if "D" in phases:
                    phase_rwkv(fw, C, K, l, si)
                if "outproj" in phases:
                    phase_outproj(fw, C, K, l, si)
                if "ffn" in phases:
                    phase_ffn(fw, C, K, l, si)
        fw.barrier()
        ph0.es.close()
        C.nins = fw.nins
    return nc, C


_CACHE = {}


def kernel(**inputs):
    xp = np.ascontiguousarray(inputs["x_prompt"], dtype=np.float32)
    xs = np.ascontiguousarray(inputs["x_sample"], dtype=np.float32)
    nc, C = build()
    consts = make_consts()
    in_maps = []
    for i in range(8):
        m = {"x0": xp[i], "x1": xs[i]}
        for nm, _ in WEIGHT_SPECS:
            m[nm] = np.ascontiguousarray(inputs[nm], dtype=np.float32)
        m.update(consts)
        in_maps.append(m)
    res = run_bass_kernel_spmd(nc, in_maps, core_ids=list(range(8)))
    yp = np.stack([r["y0"] for r in res.results], 0).astype(np.float32)
    ys = np.stack([r["y1"] for r in res.results], 0).astype(np.float32)
    return (yp, ys)
```

```python
import math
from contextlib import ExitStack
import numpy as np
import concourse.bass as bass
import concourse.mybir as mybir
from concourse.bass_utils import run_bass_kernel_spmd

F32 = mybir.dt.float32
BF16 = mybir.dt.bfloat16
AF = mybir.ActivationFunctionType
ALU = mybir.AluOpType
AX = mybir.AxisListType

D_MODEL = 1024
IN_COLS = 3392
D_FF = 2816
NFF = D_FF // 128
EPS = 1e-6
GN_EPS = 64e-5
TMAX = 4096


class View:
    __slots__ = ("t", "ap")

    def __init__(self, t, ap):
        self.t = t
        self.ap = ap


class T:
    __slots__ = ("ap", "w", "r", "name")

    def __init__(self, ap, name=""):
        self.ap = ap
        self.w = {}
        self.r = {}
        self.name = name

    def __getitem__(self, idx):
        return View(self, self.ap[idx])

    def bc(self, idx, axis, shape):
        return View(self, self.ap[idx].unsqueeze(axis).to_broadcast(list(shape)))


class Eng:
    def __init__(self, name, handle, sem):
        self.name = name
        self.h = handle
        self.sem = sem
        self.count = 0
        self.seen = {}


class Pool:
    def __init__(self, items):
        self.items = items
        self.i = 0

    def next(self):
        t = self.items[self.i]
        self.i = (self.i + 1) % len(self.items)
        return t


class Phase:
    def __init__(self, fw):
        self.fw = fw
        self.es = ExitStack()
        self.n = 0

    def sb(self, shape, dt, name="t"):
        self.n += 1
        t = self.es.enter_context(self.fw.nc.sbuf_tensor("%s_%d_%d" % (name, self.fw.uid(), self.n), list(shape), dt))
        return T(t, name)

    def ps(self, shape, dt=F32, name="p"):
        self.n += 1
        t = self.es.enter_context(self.fw.nc.psum_tensor("%s_%d_%d" % (name, self.fw.uid(), self.n), list(shape), dt))
        return T(t, name)

    def sbpool(self, n, shape, dt, name="t"):
        return Pool([self.sb(shape, dt, name) for _ in range(n)])

    def pspool(self, n, shape, dt=F32, name="p"):
        return Pool([self.ps(shape, dt, name) for _ in range(n)])

    def close(self):
        self.fw.barrier()
        self.es.close()


class FW:
    NDMA = 32

    def __init__(self, nc, es):
        self.nc = nc
        self.es = es
        self._uid = 0
        self.E = {}
        for nm, h in (("pe", nc.tensor), ("act", nc.scalar), ("dve", nc.vector), ("pool", nc.gpsimd), ("sp", nc.sync)):
            sem = es.enter_context(nc.semaphore("s_" + nm))
            self.E[nm] = Eng(nm, h, sem)
        self.dsem = [es.enter_context(nc.semaphore("d%d" % i)) for i in range(self.NDMA)]
        self.dval = [0] * self.NDMA
        self.dnext = {"sp": 0, "pool": 0, "act": 0}
        self.drange = {"sp": (0, 14), "pool": (14, 28), "act": (28, 32)}
        self.semobj = {}
        for e in self.E.values():
            self.semobj[("e", e.name)] = e.sem
        for i, s in enumerate(self.dsem):
            self.semobj[("d", i)] = s
        self.nins = 0

    def uid(self):
        self._uid += 1
        return self._uid

    def phase(self):
        return Phase(self)

    def _needs(self, reads, writes):
        needs = {}
        for t in reads:
            for k, v in t.w.items():
                if needs.get(k, 0) < v:
                    needs[k] = v
        for t in writes:
            for d in (t.w, t.r):
                for k, v in d.items():
                    if needs.get(k, 0) < v:
                        needs[k] = v
        return needs

    def _emit_waits(self, e, needs, skip_self=False):
        for k, v in needs.items():
            if skip_self and k == ("e", e.name):
                continue
            if e.seen.get(k, 0) >= v:
                continue
            if k[0] == "e":
                assert self.E[k[1]].count >= v, "wait on future event %s %d" % (k, v)
            e.seen[k] = v
            e.h.wait_ge(self.semobj[k], v)
            self.nins += 1

    def op(self, eng, fn, reads=(), writes=(), signal=True):
        e = self.E[eng]
        reads = [v.t for v in reads]
        writes = [v.t for v in writes]
        needs = self._needs(reads, writes)
        self._emit_waits(e, needs, skip_self=(eng == "pe"))
        ins = fn(e.h)
        self.nins += 1
        if signal:
            e.count += 1
            ins.then_inc(e.sem, 1)
            ev = e.count
        else:
            ev = e.count + 1
        key = ("e", eng)
        for t in writes:
            t.w = {key: ev}
            t.r = {}
        for t in reads:
            if t.r.get(key, 0) < ev:
                t.r[key] = ev

    def dma(self, q, out, in_, **kw):
        e = self.E[q]
        reads = [in_.t] if isinstance(in_, View) else []
        writes = [out.t] if isinstance(out, View) else []
        needs = self._needs(reads, writes)
        lo_, hi_ = self.drange[q]
        i = lo_ + self.dnext[q]
        self.dnext[q] = (self.dnext[q] + 1) % (hi_ - lo_)
        key = ("d", i)
        if self.dval[i] > 0 and needs.get(key, 0) < self.dval[i]:
            needs[key] = self.dval[i]
        self._emit_waits(e, needs)
        self.dval[i] += 16
        v = self.dval[i]
        oap = out.ap if isinstance(out, View) else out
        iap = in_.ap if isinstance(in_, View) else in_
        e.h.dma_start(out=oap, in_=iap, **kw).then_inc(self.dsem[i], 16)
        self.nins += 1
        for t in writes:
            t.w = {key: v}
            t.r = {}
        for t in reads:
            t.r[key] = v

    def barrier(self):
        allev = {}
        for e in self.E.values():
            if e.count > 0:
                allev[("e", e.name)] = e.count
        for i in range(self.NDMA):
            if self.dval[i] > 0:
                allev[("d", i)] = self.dval[i]
        for e in self.E.values():
            self._emit_waits(e, dict(allev))

    def ld(self, dst, src, q="sp", **kw):
        self.dma(q, dst, src, **kw)

    def ldn(self, dst, src, q="sp"):
        self.dma(q, dst, src, allow_slow_non_contiguous=True)

    def st(self, dst, src, q="pool"):
        self.dma(q, dst, src)

    def mm(self, out, lhsT, rhs, start=True, stop=True, sig=None):
        if sig is None:
            sig = stop
        self.op("pe", lambda h: h.matmul(out.ap, lhsT=lhsT.ap, rhs=rhs.ap, start=start, stop=stop),
                reads=[lhsT, rhs], writes=[out], signal=sig)

    def tr(self, out, in_, ident, sig=True):
        self.op("pe", lambda h: h.transpose(out=out.ap, in_=in_.ap, identity=ident.ap),
                reads=[in_, ident], writes=[out], signal=sig)

    def act(self, out, in_, func, bias=None, scale=None, accum=None):
        reads = [in_]
        kw = {}
        if bias is not None:
            if isinstance(bias, View):
                reads.append(bias)
                kw["bias"] = bias.ap
            else:
                kw["bias"] = float(bias)
        if scale is not None:
            if isinstance(scale, View):
                reads.append(scale)
                kw["scale"] = scale.ap
            else:
                kw["scale"] = float(scale)
        writes = [out]
        if accum is not None:
            kw["accum_out"] = accum.ap
            writes.append(accum)
        self.op("act", lambda h: h.activation(out=out.ap, in_=in_.ap, func=func, **kw), reads=reads, writes=writes)

    def tt(self, eng, out, in0, in1, op):
        self.op(eng, lambda h: h.tensor_tensor(out=out.ap, in0=in0.ap, in1=in1.ap, op=op), reads=[in0, in1], writes=[out])

    def ts(self, eng, out, in0, s1, s2, op0, op1=None):
        reads = [in0]
        a1 = s1.ap if isinstance(s1, View) else s1
        a2 = s2.ap if isinstance(s2, View) else s2
        if isinstance(s1, View):
            reads.append(s1)
        if isinstance(s2, View):
            reads.append(s2)
        if op1 is None:
            self.op(eng, lambda h: h.tensor_scalar(out=out.ap, in0=in0.ap, scalar1=a1, scalar2=None, op0=op0), reads=reads, writes=[out])
        else:
            self.op(eng, lambda h: h.tensor_scalar(out=out.ap, in0=in0.ap, scalar1=a1, scalar2=a2, op0=op0, op1=op1), reads=reads, writes=[out])

    def stt(self, out, in0, scalar, in1, op0, op1):
        reads = [in0, in1]
        sa = scalar.ap if isinstance(scalar, View) else scalar
        if isinstance(scalar, View):
            reads.append(scalar)
        self.op("dve", lambda h: h.scalar_tensor_tensor(out=out.ap, in0=in0.ap, scalar=sa, in1=in1.ap, op0=op0, op1=op1), reads=reads, writes=[out])

    def copy(self, eng, out, in_):
        if eng == "act":
            self.op("act", lambda h: h.activation(out=out.ap, in_=in_.ap, func=AF.Identity), reads=[in_], writes=[out])
        else:
            self.op(eng, lambda h: h.tensor_copy(out=out.ap, in_=in_.ap), reads=[in_], writes=[out])

    def recip(self, out, in_):
        self.op("dve", lambda h: h.reciprocal(out=out.ap, in_=in_.ap), reads=[in_], writes=[out])

    def memset(self, eng, out, val):
        self.op(eng, lambda h: h.memset(out.ap, val), writes=[out])

    def reduce(self, out, in_, op, axis=AX.X):
        self.op("dve", lambda h: h.tensor_reduce(out=out.ap, in_=in_.ap, axis=axis, op=op), reads=[in_], writes=[out])

    def scan(self, out, d0, d1, init, op0, op1):
        self.op("dve", lambda h: h.tensor_tensor_scan(out=out.ap, data0=d0.ap, data1=d1.ap, initial=init, op0=op0, op1=op1), reads=[d0, d1], writes=[out])


def _angles(pos, rot_dim, theta):
    inv = (theta ** (-np.arange(0, rot_dim, 2, dtype=np.float32) / rot_dim)).astype(np.float32)
    return pos.astype(np.float32)[:, None] * inv[None, :]


def make_consts():
    c = {}
    c["c_ident"] = np.eye(128, dtype=np.float32)
    T = TMAX
    pos = np.arange(T)
    rope = np.zeros((64, 6, T), np.float32)
    ar = _angles(pos // 64, 32, 10000.0)
    ac = _angles(pos % 64, 32, 10000.0)
    for d in range(64):
        a = ar if d < 32 else ac
        idx = (d % 32) % 16
        rope[d, 0] = np.cos(a[:, idx])
        rope[d, 1] = np.sin(a[:, idx])
    ab = _angles(pos, 8, 500000.0)
    for d in range(64):
        dd = d % 32
        if dd < 8:
            rope[d, 2] = np.cos(ab[:, dd % 4])
            rope[d, 3] = np.sin(ab[:, dd % 4])
        else:
            rope[d, 2] = 1.0
            rope[d, 3] = 0.0
    acc = _angles(pos, 64, 10000.0)
    for d in range(64):
        rope[d, 4] = np.cos(acc[:, d % 32])
        rope[d, 5] = np.sin(acc[:, d % 32])
    c["c_rope"] = rope
    pm = np.zeros((64, 3, 64), np.float32)
    for m in range(64):
        mm_ = m % 32
        base = m - mm_
        if mm_ < 16:
            pm[base + mm_ + 16, 0, m] = -1.0
        else:
            pm[base + mm_ - 16, 0, m] = 1.0
        if mm_ < 4:
            pm[base + mm_ + 4, 1, m] = -1.0
        elif mm_ < 8:
            pm[base + mm_ - 4, 1, m] = 1.0
        if m < 32:
            pm[m + 32, 2, m] = -1.0
        else:
            pm[m - 32, 2, m] = 1.0
    c["c_pm"] = pm
    c["c_ones"] = np.ones((128, 128), np.float32)
    h = np.arange(4, dtype=np.float32)
    lgf = np.log1p(-np.exp2(-5.0 - h)).astype(np.float32)
    lgb = lgf[::-1].copy()
    i = np.arange(128, dtype=np.float32)
    diff = i[None, :] - i[:, None]
    rm = np.zeros((128, 4, 128), np.float32)
    for hh in range(4):
        rm[:, hh, :] = np.where(diff >= 0, np.exp(lgf[hh] * np.maximum(diff, 0)), 0.0) + \
            np.where(diff <= 0, np.exp(lgb[hh] * np.maximum(-diff, 0)), 0.0)
    c["c_retmask"] = rm
    qd = np.zeros((64, 2, 4, 128), np.float32)
    kd = np.zeros((128, 2, 4), np.float32)
    for hh in range(4):
        qd[:, 0, hh, :] = np.exp(lgf[hh] * (i + 1.0))[None, :]
        qd[:, 1, hh, :] = np.exp(lgb[hh] * (128.0 - i))[None, :]
        kd[:, 0, hh] = np.exp(lgf[hh] * (127.0 - i))
        kd[:, 1, hh] = np.exp(lgb[hh] * i)
    c["c_qdec"] = qd
    c["c_kdec"] = kd
    dm = np.zeros((128, 2, 5, 128), np.float32)
    p = np.arange(128)[:, None]
    f = np.arange(128)[None, :]
    dm[:, 0, 0, :] = -1.0 * (p > f)
    dm[:, 0, 1, :] = -1.0 * (f > p)
    dm[:, 0, 2, :] = (f > p)
    dm[:, 0, 3, :] = (f >= p)
    dm[:, 0, 4, :] = -1.0 * (f >= p)
    dm[:, 1, 0, :] = -1.0 * (p < f)
    dm[:, 1, 1, :] = -1.0 * (f < p)
    dm[:, 1, 2, :] = (f < p)
    dm[:, 1, 3, :] = (f <= p)
    dm[:, 1, 4, :] = -1.0 * (f <= p)
    c["c_dmask"] = dm.astype(np.float32)
    seg = np.ones((64, 512), np.float32)
    seg[:, ::128] = 0.0
    c["c_seg"] = seg
    return c


RET_G = [1.0 - 2.0 ** (-5 - h) for h in range(4)]

WEIGHT_SPECS = [
    ("norm_mix_pre", (2, 1024)), ("norm_mix_post", (2, 1024)), ("norm_ffn_pre", (2, 1024)), ("norm_ffn_post", (2, 1024)),
    ("w_in", (2, 1024, 3392)), ("w_out", (2, 1024, 1024)), ("a_q_gain", (2, 64)), ("a_k_gain", (2, 64)),
    ("b_lambda", (2, 4, 32)), ("b_subln_gain", (2, 64)), ("c_gn_gain", (2, 256)), ("d_mu_prev", (2, 1088)),
    ("d_mu_next", (2, 1088)), ("d_w0", (2, 2, 256)), ("d_w_up", (2, 2, 64, 256)), ("d_a0", (2, 256)),
    ("d_a_up", (2, 64, 256)), ("d_g_up", (2, 128, 256)), ("d_k_k", (2, 256)), ("d_k_a", (2, 256)),
    ("d_r_k", (2, 4, 64)), ("d_gn_w", (2, 256)), ("d_gn_b", (2, 256)),
    ("ffn_w_gate", (2, 1024, 2816)), ("ffn_w_up", (2, 1024, 2816)), ("ffn_w_down", (2, 2816, 1024)),
]


class Ctx:
    pass


def load_consts(fw, C, ph):
    K = Ctx()
    K.ident = ph.sb([128, 128], BF16, "ident")
    fw.ld(K.ident[:], C.d["c_ident"], q="pool")
    K.identf = ph.sb([128, 128], F32, "identf")
    fw.ld(K.identf[:], C.d["c_ident"])
    K.ones_bf = ph.sb([128, 128], BF16, "ones_bf")
    fw.ld(K.ones_bf[:], C.d["c_ones"], q="pool")
    K.ones_f = ph.sb([128, 128], F32, "ones_f")
    fw.ld(K.ones_f[:], C.d["c_ones"])
    K.pm = []
    for i in range(3):
        t = ph.sb([64, 64], BF16, "pm%d" % i)
        fw.ld(t[:], C.d["c_pm"][:, i, :], q="pool")
        K.pm.append(t)
    return K


def rms_rows(fw, xt, gain, hb, tmp, ss, rstd):
    fw.act(tmp[:], xt[:], AF.Square, accum=ss[:])
    fw.act(rstd[:], ss[:], AF.Sqrt, bias=EPS, scale=1.0 / D_MODEL)
    fw.recip(rstd[:], rstd[:])
    fw.stt(hb[:], xt[:], rstd[:, 0:1], gain[:], ALU.mult, ALU.mult)


def make_hT(fw, K, src_rows, gain, hT, j, P):
    xt = P.xt.next()
    fw.ld(xt[:], src_rows)
    hb = P.hb.next()
    ss = P.ss.next()
    rstd = P.rstd.next()
    tmp = P.tmp.next()
    rms_rows(fw, xt, gain, hb, tmp, ss, rstd)
    tp = P.tp.next()
    for k in range(8):
        fw.tr(tp[:, k, :], hb[:, k * 128:(k + 1) * 128], K.ident[:], sig=(k == 7))
    fw.copy("dve", hT[:, :, j * 128:(j + 1) * 128], tp[:])
    return xt


def phase_inproj(fw, C, K, l, si):
    T_ = C.seqs[si]
    src = C.xin[si] if l == 0 else C.y[si]
    d = C.d
    ph = fw.phase()
    W = ph.sb([128, 8, IN_COLS], BF16, "W")
    for k in range(8):
        fw.ld(W[:, k, :], d["w_in"][l, k * 128:(k + 1) * 128, :], q="pool")
    gain = ph.sb([128, 1024], F32, "gain")
    fw.ld(gain[:], d["norm_mix_pre"][l].partition_broadcast(128))
    gqk = ph.sb([64, 2], F32, "gqk")
    fw.ld(gqk[:, 0:1], d["a_q_gain"][l].rearrange("(p o) -> p o", o=1))
    fw.ld(gqk[:, 1:2], d["a_k_gain"][l].rearrange("(p o) -> p o", o=1))
    P = Ctx()
    P.xt = ph.sbpool(2, [128, 1024], F32, "xt")
    P.hb = ph.sbpool(2, [128, 1024], BF16, "hb")
    P.tmp = ph.sbpool(1, [128, 1024], F32, "tmp")
    P.ss = ph.sbpool(2, [128, 1], F32, "ss")
    P.rstd = ph.sbpool(2, [128, 1], F32, "rstd")
    P.tp = ph.pspool(1, [128, 8, 128], BF16, "tp")
    hTs = ph.sbpool(2, [128, 8, 512], BF16, "hT")
    tabs = ph.sbpool(2, [64, 6, 512], F32, "tab")
    zps = ph.pspool(4, [128, 512], F32, "zp")
    aps = ph.pspool(2, [128, 512], F32, "ap")
    tps = ph.pspool(1, [128, 512], F32, "tmps")
    f32p = ph.sbpool(24, [64, 512], F32, "f32p")
    bfp = ph.sbpool(16, [64, 512], BF16, "bfp")
    f32w = ph.sbpool(3, [128, 512], F32, "f32w")
    bfw = ph.sbpool(3, [128, 512], BF16, "bfw")
    chunks = []
    for h in range(4):
        chunks.append(("A", 0 + 64 * h, C.QA, h, 0))
    for h in range(2):
        chunks.append(("A", 256 + 64 * h, C.KA, h, 1))
    for h in range(4):
        chunks.append(("B", 512 + 64 * h, C.QB, h, 0))
    for h in range(4):
        chunks.append(("B", 768 + 64 * h, C.KB, h, 0))
    for h in range(4):
        chunks.append(("C", 1280 + 64 * h, C.QC, h, 0))
    for h in range(4):
        chunks.append(("C", 1536 + 64 * h, C.KC, h, 0))
    for h in range(4):
        chunks.append(("G", 2048 + 64 * h, C.GC, h, 0))
    for j in range(15):
        chunks.append(("D", 2304 + 64 * j, C.ZD, j, 0))
    DEPTHP = 4

    def chunk_gen(kind, col, dst, slot, gi, hT, tab, t0):
        zp = zps.next()
        for k in range(8):
            fw.mm(zp[0:64, :], W[:, k, col:col + 64], hT[:, k, :], start=(k == 0), stop=(k == 7))
        z = zp[0:64, :]
        yield
        if kind in ("A", "B", "C"):
            ti = {"A": 0, "B": 2, "C": 4}[kind]
            pmi = {"A": 0, "B": 1, "C": 2}[kind]
            qn = bfp.next()
            qf = f32p.next()
            fw.copy("act", qf[:], z)
            if kind == "A":
                sq = bfp.next()
                fw.act(sq[:], z, AF.Square)
                yield
                ms = aps.next()
                fw.mm(ms[0:64, :], K.ones_bf[0:64, 0:64], sq[:])
                rs = f32p.next()
                fw.ts("dve", rs[:], ms[0:64, :], 1.0 / 64, EPS, ALU.mult, ALU.add)
                fw.act(rs[:], rs[:], AF.Ln)
                fw.act(rs[:], rs[:], AF.Exp, scale=-0.5)
                qf2 = f32p.next()
                fw.stt(qf2[:], qf[:], gqk[:, gi:gi + 1], rs[:], ALU.mult, ALU.mult)
                qf = qf2
            fw.copy("act", qn[:], qf[:])
            yield
            pq = aps.next()
            fw.mm(pq[0:64, :], K.pm[pmi][:], qn[:])
            t2 = f32p.next()
            fw.tt("dve", t2[:], pq[0:64, :], tab[:, ti + 1, :], ALU.mult)
            t1 = f32p.next()
            fw.tt("pool", t1[:], qf[:], tab[:, ti, :], ALU.mult)
            ob = bfp.next()
            fw.tt("pool", ob[:], t1[:], t2[:], ALU.add)
            fw.st(dst[:, slot, t0:t0 + 512], ob[:])
        elif kind == "G":
            ob = bfp.next()
            fw.act(ob[:], z, AF.Silu)
            fw.st(dst[:, slot, t0:t0 + 512], ob[:])
        else:
            of = f32p.next()
            fw.copy("act", of[:], z)
            fw.st(dst[:, slot, t0:t0 + 512], of[:])

    def tail_gen(hT, t0):
        zp = zps.next()
        for k in range(8):
            fw.mm(zp[:, :], W[:, k, 3264:3392], hT[:, k, :], start=(k == 0), stop=(k == 7))
        of = f32w.next()
        fw.copy("act", of[:], zp[:])
        fw.st(C.ZG[:, t0:t0 + 512], of[:])
        yield
        for j in range(4):
            tp_ = tps.next()
            for k in range(8):
                fw.mm(tp_[:, 0:128], hT[:, k, j * 128:(j + 1) * 128], W[:, k, 384:512], start=(k == 0), stop=(k == 7), sig=False)
            for k in range(8):
                fw.mm(tp_[:, 128:384], hT[:, k, j * 128:(j + 1) * 128], W[:, k, 1024:1280], start=(k == 0), stop=(k == 7))
            ob = bfw.next()
            fw.copy("act", ob[:, 0:384], tp_[:, 0:384])
            r0 = t0 + j * 128
            fw.st(C.VA[r0:r0 + 128, :], ob[:, 0:128])
            fw.st(C.VB[r0:r0 + 128, :], ob[:, 128:384])
            yield
            tp2 = tps.next()
            for k in range(8):
                fw.mm(tp2[:, 0:256], hT[:, k, j * 128:(j + 1) * 128], W[:, k, 1792:2048], start=(k == 0), stop=(k == 7))
            ob2 = bfw.next()
            fw.copy("dve", ob2[:, 0:256], tp2[:, 0:256])
            fw.st(C.VC[r0:r0 + 128, :], ob2[:, 0:256])
            yield

    def hT_gen(b, hT):
        t0 = b * 512
        for j in range(4):
            make_hT(fw, K, src[t0 + j * 128:t0 + (j + 1) * 128, :], gain, hT, j, P)
            yield

    def run_pipe(gens, depth, side=None):
        gens = list(gens)
        active = []
        tick = 0
        while gens or active:
            while gens and len(active) < depth:
                active.append(gens.pop(0))
            for g in list(active):
                try:
                    next(g)
                except StopIteration:
                    active.remove(g)
            tick += 1
            if side is not None and tick % 8 == 0:
                try:
                    next(side)
                except StopIteration:
                    side = None
        if side is not None:
            for _ in side:
                pass

    nb = T_ // 512
    hT_cur = hTs.next()
    for _ in hT_gen(0, hT_cur):
        pass
    for b in range(nb):
        t0 = b * 512
        tab = tabs.next()
        fw.ld(tab[:], d["c_rope"][:, :, t0:t0 + 512])
        hT = hT_cur
        side = None
        if b + 1 < nb:
            hT_cur = hTs.next()
            side = hT_gen(b + 1, hT_cur)
        gens = [chunk_gen(kind, col, dst, slot, gi, hT, tab, t0) for (kind, col, dst, slot, gi) in chunks]
        gens.append(tail_gen(hT, t0))
        run_pipe(gens, DEPTHP, side)
    ph.close()


def attn_core(fw, ph, K, T_, QT, KT, V, heads, negM, scale, emit):
    LOOK = 2
    sps = ph.pspool(4, [128, 512], F32, "sps")
    ops_ = ph.pspool(2, [128, 512], F32, "ops")
    pts = ph.sbpool(4, [128, 512], BF16, "pt")
    osbs = ph.sbpool(4, [65, 512], F32, "osb")
    nkt = T_ // 128
    iters = []
    for qb in range(T_ // 512):
        for hd in heads:
            for kt in range(nkt):
                iters.append((qb, hd, kt))
    pend = []
    state = {"o_ps": None}

    def do_pv(item):
        (qb, hd, kt, pt) = item
        (qs, lo, hi, ks, vs, tag) = hd
        if kt == 0:
            state["o_ps"] = ops_.next()
        o_ps = state["o_ps"]
        fw.mm(o_ps[0:65, :], V[:, kt, vs, :], pt[:], start=(kt == 0), stop=(kt == nkt - 1))
        if kt == nkt - 1:
            osb = osbs.next()
            fw.copy("dve", osb[:], o_ps[0:65, :])
            emit(qb, tag, osb)

    for (qb, hd, kt) in iters:
        (qs, lo, hi, ks, vs, tag) = hd
        q0 = qb * 512
        st = sps.next()
        fw.mm(st[:, :], KT[lo:hi, ks, kt * 128:(kt + 1) * 128], QT[lo:hi, qs, q0:q0 + 512])
        pt = pts.next()
        if negM is None:
            fw.act(pt[:], st[:], AF.Exp, scale=scale)
        else:
            fw.act(pt[:], st[:], AF.Exp, scale=scale, bias=negM[:, 0:1])
        pend.append((qb, hd, kt, pt))
        if len(pend) > LOOK:
            do_pv(pend.pop(0))
    while pend:
        do_pv(pend.pop(0))


def attn_core2(fw, ph, T_, groups, V, negM, scale, emit, bcs):
    LOOK = 2
    sps = ph.pspool(3, [128, 1024], F32, "sps")
    ops_ = bcs
    pts = ph.sbpool(4, [128, 1024], BF16, "pt")
    osbs = ph.sbpool(4, [65, 512], F32, "osb")
    nkt = T_ // 128
    iters = []
    for qb in range(T_ // 512):
        for g in groups:
            for kt in range(nkt):
                iters.append((qb, g, kt))
    pend = []
    state = {}
    import os
    NW = int(os.environ.get("DBG_WARM", "0"))
    if NW:
        wz = pts.next()
        fw.memset("pool", wz[:], 1.0)
        st0 = sps.next()
        for i in range(NW):
            fw.mm(st0[:, 0:512], wz[:, 0:128], wz[:, 512:1024], sig=(i == NW - 1))

    def do_pv(item):
        (qb, g, kt, pt) = item
        (kt_ap, q_ap, vs, tag) = g
        if kt == 0:
            state["o"] = [ops_.next(), ops_.next()]
        o = state["o"]
        for j in range(2):
            fw.mm(o[j][0:65, :], V[:, kt, vs[j], :], pt[:, j * 512:(j + 1) * 512], start=(kt == 0), stop=(kt == nkt - 1))
        if kt == nkt - 1:
            osb = [osbs.next(), osbs.next()]
            for j in range(2):
                fw.copy("dve", osb[j][:], o[j][0:65, :])
            emit(qb, tag, osb)

    for (qb, g, kt) in iters:
        (kt_ap, q_ap, vs, tag) = g
        st = sps.next()
        for j in range(2):
            fw.mm(st[:, j * 512:(j + 1) * 512], kt_ap(j, kt), q_ap(j, qb), sig=(j == 1))
        pt = pts.next()
        if negM is None:
            fw.act(pt[:], st[:], AF.Exp, scale=scale)
        else:
            fw.act(pt[:], st[:], AF.Exp, scale=scale, bias=negM[:, 0:1])
        pend.append((qb, g, kt, pt))
        if len(pend) > LOOK:
            do_pv(pend.pop(0))
    while pend:
        do_pv(pend.pop(0))


def bcast_recip(fw, K, osb, recs, hls, bcs, mulv):
    rec = recs.next()
    fw.act(rec[64:65, :], osb[64:65, :], AF.Ln)
    fw.act(rec[64:65, :], rec[64:65, :], AF.Exp, scale=-1.0)
    if mulv is not None:
        fw.ts("dve", rec[64:65, :], rec[64:65, :], mulv[64:65, 0:1], None, ALU.mult)
    hl = hls.next()
    fw.copy("dve", hl[64:65, 0, :], rec[64:65, :])
    fw.tt("dve", hl[64:65, 1, :], rec[64:65, :], hl[64:65, 0, :], ALU.subtract)
    bc = bcs.next()
    fw.mm(bc[0:64, :], K.ones_bf[64:65, 0:64], hl[64:65, 0, :], start=True, stop=False, sig=False)
    fw.mm(bc[0:64, :], K.ones_bf[64:65, 0:64], hl[64:65, 1, :], start=False, stop=True)
    return bc


def phase_attnA(fw, C, K, l, si):
    T_ = C.seqs[si]
    d = C.d
    ph = fw.phase()
    QT = ph.sb([128, 2, T_], BF16, "QT")
    KT = ph.sb([128, T_], BF16, "KT")
    V = ph.sb([128, T_ // 128, 2, 65], BF16, "V")
    for p in range(2):
        fw.ld(QT[0:64, p, :], C.QA[:, p, 0:T_])
        fw.ld(QT[64:128, p, :], C.QA[:, p + 2, 0:T_])
    for h in range(2):
        fw.ld(KT[64 * h:64 * h + 64, :], C.KA[:, h, 0:T_])
    fw.memset("pool", V[:, :, :, 64:65], 1.0)
    for h in range(2):
        fw.ld(V[:, :, h, 0:64], C.VA[0:T_, h * 64:(h + 1) * 64].rearrange("(n p) c -> p n c", p=128))
    g2 = ph.sb([128, 2, 64], F32, "g2")
    fw.ld(g2[:, 0, :], d["a_q_gain"][l].partition_broadcast(128))
    fw.ld(g2[:, 1, :], d["a_k_gain"][l].partition_broadcast(128))
    fw.tt("dve", g2[:], g2[:], g2[:], ALU.mult)
    mx = ph.sb([128, 2], F32, "mx")
    fw.reduce(mx[:], g2[:], ALU.max)
    negM = ph.sb([128, 1], F32, "negM")
    fw.tt("dve", negM[:], mx[:, 0:1], mx[:, 1:2], ALU.mult)
    fw.act(negM[:], negM[:], AF.Sqrt, scale=64.0)
    fw.ts("dve", negM[:], negM[:], -1.0, None, ALU.mult)
    recs = ph.sbpool(2, [65, 512], F32, "rec")
    hls = ph.sbpool(2, [65, 2, 512], BF16, "hl")
    bcs = ph.pspool(2, [128, 512], F32, "bc")
    outs = ph.sbpool(3, [64, 512], BF16, "ob")

    def emit(qb, p, osbs_):
        for j in range(2):
            h = p + 2 * j
            osb = osbs_[j]
            bc = bcast_recip(fw, K, osb, recs, hls, bcs, None)
            ob = outs.next()
            fw.tt("dve", ob[:], osb[0:64, :], bc[0:64, :], ALU.mult)
            fw.st(C.OT[h, :, qb * 512:(qb + 1) * 512], ob[:])

    groups = []
    for p in range(2):
        groups.append((lambda j, kt: KT[64 * j:64 * j + 64, kt * 128:(kt + 1) * 128],
                       (lambda p_: (lambda j, qb: QT[64 * j:64 * j + 64, p_, qb * 512:(qb + 1) * 512]))(p),
                       (0, 1), p))
    attn_core2(fw, ph, T_, groups, V, negM, 0.125, emit, bcs)
    ph.close()


def phase_attnB(fw, C, K, l, si):
    T_ = C.seqs[si]
    d = C.d
    lam_init = 0.8 - 0.6 * math.exp(-0.3 * l)
    ph = fw.phase()
    QT = ph.sb([128, 2, 2, T_], BF16, "QT")
    KT = ph.sb([128, 2, T_], BF16, "KT")
    V = ph.sb([128, T_ // 128, 4, 65], BF16, "V")
    for hh in range(2):
        fw.memset("pool", QT[64 * hh + 32:64 * hh + 64, :, 0, :], 0.0)
        fw.memset("pool", QT[64 * hh:64 * hh + 32, :, 1, :], 0.0)
    for p in range(2):
        for hh in range(2):
            h = p + 2 * hh
            fw.ld(QT[64 * hh:64 * hh + 32, p, 0, :], C.QB[0:32, h, 0:T_])
            fw.ld(QT[64 * hh + 32:64 * hh + 64, p, 1, :], C.QB[32:64, h, 0:T_])
            fw.ld(KT[64 * hh:64 * hh + 64, p, :], C.KB[:, h, 0:T_])
    fw.memset("pool", V[:, :, :, 64:65], 1.0)
    for h in range(4):
        fw.ld(V[:, :, h, 0:64], C.VB[0:T_, h * 64:(h + 1) * 64].rearrange("(n p) c -> p n c", p=128))
    lp = ph.sb([128, 4, 32], F32, "lp")
    fw.ld(lp[:], d["b_lambda"][l].partition_broadcast(128))
    pr = ph.sb([128, 2, 32], F32, "pr")
    fw.tt("dve", pr[:, 0, :], lp[:, 0, :], lp[:, 1, :], ALU.mult)
    fw.tt("dve", pr[:, 1, :], lp[:, 2, :], lp[:, 3, :], ALU.mult)
    sm = ph.sb([128, 2], F32, "sm")
    fw.reduce(sm[:], pr[:], ALU.add)
    fw.act(sm[:], sm[:], AF.Exp)
    nlam = ph.sb([128, 1], F32, "nlam")
    fw.tt("dve", nlam[:], sm[:, 1:2], sm[:, 0:1], ALU.subtract)
    fw.ts("dve", nlam[:], nlam[:], -lam_init, None, ALU.add)
    gsub = ph.sb([64, 1], F32, "gsub")
    fw.ld(gsub[:], d["b_subln_gain"][l].rearrange("(p o) -> p o", o=1))
    fw.ts("dve", gsub[:], gsub[:], 1.0 - lam_init, None, ALU.mult)
    recs = ph.sbpool(2, [65, 512], F32, "rec")
    hls = ph.sbpool(2, [65, 2, 512], BF16, "hl")
    bcs = ph.pspool(2, [128, 512], F32, "bc")
    o1s = ph.sbpool(6, [64, 512], F32, "o1")
    rss = ph.sbpool(2, [64, 512], F32, "rsb")
    sqs = ph.sbpool(2, [64, 512], BF16, "sq")
    outs = ph.sbpool(3, [64, 512], BF16, "ob")

    pend_o = {}

    def emit(qb, tag, osbs_):
        (p, s_) = tag
        for hh in range(2):
            h = p + 2 * hh
            osb = osbs_[hh]
            bc = bcast_recip(fw, K, osb, recs, hls, bcs, nlam if s_ == 1 else None)
            o_ = o1s.next()
            fw.tt("dve", o_[:], osb[0:64, :], bc[0:64, :], ALU.mult)
            if s_ == 0:
                pend_o[h] = o_
            else:
                finish(qb, h, pend_o.pop(h), o_)

    def finish(qb, h, o1, o2):
        fw.tt("pool", o1[:], o1[:], o2[:], ALU.add)
        sq = sqs.next()
        fw.act(sq[:], o1[:], AF.Square)
        ms = bcs.next()
        fw.mm(ms[0:64, :], K.ones_bf[0:64, 0:64], sq[:])
        rs = rss.next()
        fw.ts("dve", rs[:], ms[0:64, :], 1.0 / 64, EPS, ALU.mult, ALU.add)
        fw.act(rs[:], rs[:], AF.Ln)
        fw.act(rs[:], rs[:], AF.Exp, scale=-0.5)
        ob = outs.next()
        fw.stt(ob[:], o1[:], gsub[:, 0:1], rs[:], ALU.mult, ALU.mult)
        fw.st(C.OT[4 + h, :, qb * 512:(qb + 1) * 512], ob[:])

    groups = []
    for p in range(2):
        for s_ in range(2):
            groups.append(((lambda p_: (lambda j, kt: KT[64 * j:64 * j + 64, p_, kt * 128:(kt + 1) * 128]))(p),
                           (lambda p_, s__: (lambda j, qb: QT[64 * j:64 * j + 64, p_, s__, qb * 512:(qb + 1) * 512]))(p, s_),
                           (p, p + 2), (p, s_)))
    attn_core2(fw, ph, T_, groups, V, None, 32 ** -0.5, emit, bcs)
    ph.close()


def phase_ret(fw, C, K, l, si):
    T_ = C.seqs[si]
    d = C.d
    n = T_ // 128
    ph = fw.phase()
    QT = ph.sb([64, 4, T_], BF16, "QT")
    KT = ph.sb([64, 4, T_], BF16, "KT")
    G = ph.sb([64, 4, T_], BF16, "G")
    V = ph.sb([128, n, 256], BF16, "V")
    for h in range(4):
        fw.ld(QT[:, h, :], C.QC[:, h, 0:T_])
        fw.ld(KT[:, h, :], C.KC[:, h, 0:T_])
        fw.ld(G[:, h, :], C.GC[:, h, 0:T_])
    fw.ld(V[:], C.VC[0:T_, :].rearrange("(n p) c -> p n c", p=128))
    mask = ph.sb([128, 4, 128], F32, "mask")
    fw.ld(mask[:], d["c_retmask"])
    qdec = ph.sb([64, 2, 4, 128], F32, "qdec")
    fw.ld(qdec[:], d["c_qdec"])
    kdec = ph.sb([128, 2, 4], F32, "kdec")
    fw.ld(kdec[:], d["c_kdec"])
    gng = ph.sb([64, 4], F32, "gng")
    fw.ldn(gng[:], d["c_gn_gain"][l].rearrange("(h p) -> p h", p=64))
    Sf = ph.sb([64, n, 4, 64], BF16, "Sf")
    Sb = ph.sb([64, n, 4, 64], BF16, "Sb")
    cur = [ph.sb([64, 4, 64], F32, "curf"), ph.sb([64, 4, 64], F32, "curb")]
    ktp = ph.pspool(1, [128, 4, 64], BF16, "ktp")
    ktm = ph.sbpool(2, [128, 4, 64], BF16, "ktm")
    kds = ph.sbpool(2, [128, 4, 64], BF16, "kds")
    kvp = ph.pspool(2, [128, 512], F32, "kvp")
    gam = [[RET_G[h] ** 128 for h in range(4)], [RET_G[3 - h] ** 128 for h in range(4)]]
    for dr in range(2):
        fw.memset("dve", cur[dr][:], 0.0)
        order = range(n) if dr == 0 else range(n - 1, -1, -1)
        Sx = Sf if dr == 0 else Sb
        for c in order:
            fw.copy("act", Sx[:, c, :, :], cur[dr][:])
            if (dr == 0 and c == n - 1) or (dr == 1 and c == 0):
                break
            tp = ktp.next()
            for h in range(4):
                fw.tr(tp[:, h, :], KT[:, h, c * 128:(c + 1) * 128], K.ident[0:64, 0:64], sig=(h == 3))
            kd = kds.next()
            fw.tt("dve", kd[:], tp[:], kdec.bc((slice(None), dr, slice(None)), 2, [128, 4, 64]), ALU.mult)
            kv = kvp.next()
            for h in range(4):
                fw.mm(kv[0:64, h * 64:(h + 1) * 64], kd[:, h, :], V[:, c, h * 64:(h + 1) * 64], sig=(h == 3))
            for h in range(4):
                fw.stt(cur[dr][:, h, :], cur[dr][:, h, :], gam[dr][h], kv[0:64, h * 64:(h + 1) * 64], ALU.mult, ALU.add)
    inp = ph.pspool(2, [128, 512], F32, "inp")
    ins_ = ph.sbpool(2, [128, 4, 128], BF16, "ins")
    qds = ph.sbpool(2, [64, 2, 4, 128], BF16, "qds")
    otp = ph.pspool(2, [128, 512], F32, "otp")
    osb = ph.sbpool(2, [64, 4, 128], F32, "osb")
    sqs = ph.sbpool(2, [64, 4, 128], BF16, "sq")
    msp = ph.pspool(1, [128, 512], F32, "msp")
    rss = ph.sbpool(2, [64, 4, 128], F32, "rs")
    obs = ph.sbpool(2, [64, 4, 128], BF16, "ob")
    for c in range(n):
        cs = slice(c * 128, (c + 1) * 128)
        ip = inp.next()
        for h in range(4):
            fw.mm(ip[:, h * 128:(h + 1) * 128], KT[:, h, cs], QT[:, h, cs], sig=(h == 3))
        it = ins_.next()
        fw.tt("dve", it[:], View(ip, ip.ap[:, :].rearrange("p (h i) -> p h i", h=4)), mask[:], ALU.mult)
        qd = qds.next()
        for dr in range(2):
            fw.tt("pool", qd[:, dr, :, :], QT[:, :, cs], qdec[:, dr, :, :], ALU.mult)
        op_ = otp.next()
        for h in range(4):
            o_h = op_[0:64, h * 128:(h + 1) * 128]
            fw.mm(o_h, V[:, c, h * 64:(h + 1) * 64], it[:, h, :], start=True, stop=False, sig=False)
            fw.mm(o_h, Sf[:, c, h, :], qd[:, 0, h, :], start=False, stop=False, sig=False)
            fw.mm(o_h, Sb[:, c, h, :], qd[:, 1, h, :], start=False, stop=True, sig=(h == 3))
        o = osb.next()
        fw.ts("dve", o[:], View(op_, op_.ap[0:64, :].rearrange("p (h i) -> p h i", h=4)), 0.125, None, ALU.mult)
        sq = sqs.next()
        fw.act(sq[:], o[:], AF.Square)
        ms = msp.next()
        fw.mm(ms[0:64, :], K.ones_bf[0:64, 0:64], View(sq, sq.ap[:].rearrange("p h i -> p (h i)")))
        rs = rss.next()
        fw.act(View(rs, rs.ap[:].rearrange("p h i -> p (h i)")), ms[0:64, :], AF.Sqrt, bias=EPS, scale=1.0 / 64)
        fw.recip(rs[:], rs[:])
        fw.tt("dve", o[:], o[:], rs[:], ALU.mult)
        fw.tt("pool", o[:], o[:], gng.bc((slice(None), slice(None)), 2, [64, 4, 128]), ALU.mult)
        ob = obs.next()
        fw.tt("dve", ob[:], o[:], G[:, :, cs], ALU.mult)
        for h in range(4):
            fw.st(C.OT[8 + h, :, cs], ob[:, h, :])
    ph.close()


def run_pipe(gens, depth, side=None, side_every=8):
    gens = list(gens)
    active = []
    tick = 0
    while gens or active:
        while gens and len(active) < depth:
            active.append(gens.pop(0))
        for g in list(active):
            try:
                next(g)
            except StopIteration:
                active.remove(g)
        tick += 1
        if side is not None and tick % side_every == 0:
            try:
                next(side)
            except StopIteration:
                side = None
    if side is not None:
        for _ in side:
            pass


def phase_outproj(fw, C, K, l, si):
    T_ = C.seqs[si]
    d = C.d
    src = C.xin[si] if l == 0 else C.y[si]
    ph = fw.phase()
    W = ph.sb([128, 8, 1024], BF16, "Wo")
    for k in range(8):
        fw.ld(W[:, k, :], d["w_out"][l, k * 128:(k + 1) * 128, :], q="pool")
    gain = ph.sb([128, 1024], F32, "gain")
    fw.ld(gain[:], d["norm_mix_post"][l].partition_broadcast(128))
    oTs = ph.sbpool(2, [128, 8, 512], BF16, "oT")
    xts = ph.sbpool(4, [128, 1024], F32, "xt")
    mps = ph.pspool(8, [128, 512], F32, "mp")
    tmps = ph.sbpool(4, [128, 1024], F32, "tmp")
    sss = ph.sbpool(4, [128, 2], F32, "ss")
    rstds = ph.sbpool(4, [128, 1], F32, "rstd")

    def sub_gen(oT, j, r0):
        xt = xts.next()
        fw.ld(xt[:], src[r0:r0 + 128, :])
        m = [mps.next(), mps.next()]
        for half in range(2):
            for k in range(8):
                fw.mm(m[half][:, :], oT[:, k, j * 128:(j + 1) * 128], W[:, k, half * 512:(half + 1) * 512], start=(k == 0), stop=(k == 7))
        tmp = tmps.next()
        ss = sss.next()
        for half in range(2):
            fw.act(tmp[:, half * 512:(half + 1) * 512], m[half][:, :], AF.Square, accum=ss[:, half:half + 1])
        yield
        rstd = rstds.next()
        fw.tt("dve", rstd[:], ss[:, 0:1], ss[:, 1:2], ALU.add)
        fw.ts("dve", rstd[:], rstd[:], 1.0 / D_MODEL, EPS, ALU.mult, ALU.add)
        fw.act(rstd[:], rstd[:], AF.Ln)
        fw.act(rstd[:], rstd[:], AF.Exp, scale=-0.5)
        yield
        for half in range(2):
            hs = slice(half * 512, (half + 1) * 512)
            fw.stt(tmp[:, hs], m[half][:, :], rstd[:, 0:1], gain[:, hs], ALU.mult, ALU.mult)
        fw.tt("pool", tmp[:], tmp[:], xt[:], ALU.add)
        fw.st(C.y[si][r0:r0 + 128, :], tmp[:])

    gens = []
    for b in range(T_ // 512):
        t0 = b * 512
        oT = oTs.next()

        def ldgen(oT=oT, t0=t0):
            for k in range(8):
                fw.ld(oT[0:64, k, :], C.OT[2 * k, :, t0:t0 + 512])
                fw.ld(oT[64:128, k, :], C.OT[2 * k + 1, :, t0:t0 + 512])
            return
            yield
        gens.append(("ld", oT, t0))
        for j in range(4):
            gens.append(("sub", oT, j, t0 + j * 128))

    def all_gens():
        for g in gens:
            if g[0] == "ld":
                _, oT, t0 = g
                for k in range(8):
                    fw.ld(oT[0:64, k, :], C.OT[2 * k, :, t0:t0 + 512])
                    fw.ld(oT[64:128, k, :], C.OT[2 * k + 1, :, t0:t0 + 512])
            else:
                yield sub_gen(g[1], g[2], g[3])

    class LazyList(list):
        pass
    pending = all_gens()
    active = []
    done = False
    while not done or active:
        while not done and len(active) < 3:
            try:
                active.append(next(pending))
            except StopIteration:
                done = True
        for g in list(active):
            try:
                next(g)
            except StopIteration:
                active.remove(g)
    ph.close()


def phase_ffn(fw, C, K, l, si):
    T_ = C.seqs[si]
    d = C.d
    TB = 1024
    ph = fw.phase()
    gain = ph.sb([128, 1024], F32, "gain")
    fw.ld(gain[:], d["norm_ffn_pre"][l].partition_broadcast(128))
    gpost = ph.sb([128, 1024], F32, "gpost")
    fw.ld(gpost[:], d["norm_ffn_post"][l].partition_broadcast(128))
    P = Ctx()
    P.xt = ph.sbpool(2, [128, 1024], F32, "xt")
    P.hb = ph.sbpool(2, [128, 1024], BF16, "hb")
    P.tmp = ph.sbpool(2, [128, 1024], F32, "tmp")
    P.ss = ph.sbpool(2, [128, 1], F32, "ss")
    P.rstd = ph.sbpool(2, [128, 1], F32, "rstd")
    P.tp = ph.pspool(1, [128, 8, 128], BF16, "tp")
    hT = ph.sb([128, 8, TB], BF16, "hT")
    act = ph.sb([128, NFF, TB], BF16, "act")
    wgs = ph.sbpool(3, [128, 8, 128], BF16, "wg")
    wus = ph.sbpool(3, [128, 8, 128], BF16, "wu")
    Wd = ph.sb([128, NFF, 1024], BF16, "Wd")
    for f in range(NFF):
        fw.ld(Wd[:, f, :], d["ffn_w_down"][l, f * 128:(f + 1) * 128, :], q="pool")
    gps = ph.pspool(2, [128, 512], F32, "gp")
    ups = ph.pspool(2, [128, 512], F32, "up")
    dps = ph.pspool(2, [128, 512], F32, "dp")
    sgs = ph.sbpool(2, [128, 512], F32, "sg")
    sss = ph.sbpool(2, [128, 2], F32, "ss2")
    wg_d = d["ffn_w_gate"][l].rearrange("(k p) n -> p k n", p=128)
    wu_d = d["ffn_w_up"][l].rearrange("(k p) n -> p k n", p=128)
    for b in range(T_ // TB):
        t0 = b * TB
        for j in range(TB // 128):
            make_hT(fw, K, C.y[si][t0 + j * 128:t0 + (j + 1) * 128, :], gain, hT, j, P)
        for f in range(NFF):
            wg = wgs.next()
            fw.ld(wg[:], wg_d[:, :, f * 128:(f + 1) * 128], q="pool")
            wu = wus.next()
            fw.ld(wu[:], wu_d[:, :, f * 128:(f + 1) * 128], q="pool")
            for tb in range(TB // 512):
                ts_ = slice(tb * 512, (tb + 1) * 512)
                gp = gps.next()
                up = ups.next()
                for k in range(8):
                    fw.mm(gp[:, :], wg[:, k, :], hT[:, k, ts_], start=(k == 0), stop=(k == 7))
                for k in range(8):
                    fw.mm(up[:, :], wu[:, k, :], hT[:, k, ts_], start=(k == 0), stop=(k == 7))
                sg = sgs.next()
                fw.act(sg[:], gp[:, :], AF.Silu)
                fw.tt("dve", act[:, f, ts_], sg[:], up[:, :], ALU.mult)
        ph_all = Pool(dps.items + gps.items + ups.items)
        for g0 in range(0, TB // 128, 3):
            subs = list(range(g0, min(g0 + 3, TB // 128)))
            banks = {(j, half): ph_all.next() for j in subs for half in range(2)}
            for f in range(NFF):
                for j in subs:
                    for half in range(2):
                        fw.mm(banks[(j, half)][:, :], act[:, f, j * 128:(j + 1) * 128], Wd[:, f, half * 512:(half + 1) * 512],
                              start=(f == 0), stop=(f == NFF - 1))
            for j in subs:
                r0 = t0 + j * 128
                xt = P.xt.next()
                fw.ld(xt[:], C.y[si][r0:r0 + 128, :])
                tmp = P.tmp.next()
                ss = sss.next()
                for half in range(2):
                    fw.act(tmp[:, half * 512:(half + 1) * 512], banks[(j, half)][:, :], AF.Square, accum=ss[:, half:half + 1])
                rstd = P.rstd.next()
                fw.tt("dve", rstd[:], ss[:, 0:1], ss[:, 1:2], ALU.add)
                fw.act(rstd[:], rstd[:], AF.Sqrt, bias=EPS, scale=1.0 / D_MODEL)
                fw.recip(rstd[:], rstd[:])
                for half in range(2):
                    hs = slice(half * 512, (half + 1) * 512)
                    fw.stt(tmp[:, hs], banks[(j, half)][:, :], rstd[:, 0:1], gpost[:, hs], ALU.mult, ALU.mult)
                fw.tt("pool", tmp[:], tmp[:], xt[:], ALU.add)
                fw.st(C.y[si][r0:r0 + 128, :], tmp[:])
    ph.close()


def phase_rwkv(fw, C, K, l, si):
    T_ = C.seqs[si]
    d = C.d
    n = T_ // 128
    CW = math.exp(-0.5)
    ph = fw.phase()
    mu = ph.sb([64, 3, 15], F32, "mu")
    fw.ldn(mu[:, 1, :], d["d_mu_prev"][l, 0:960].rearrange("(j p) -> p j", p=64))
    fw.ldn(mu[:, 2, :], d["d_mu_next"][l, 0:960].rearrange("(j p) -> p j", p=64))
    fw.tt("dve", mu[:, 0, :], mu[:, 1, :], mu[:, 2, :], ALU.add)
    fw.ts("dve", mu[:, 0, :], mu[:, 0, :], -1.0, 1.0, ALU.mult, ALU.add)
    mug = ph.sb([128, 3], F32, "mug")
    fw.ld(mug[:, 1:2], d["d_mu_prev"][l, 960:1088].rearrange("(p o) -> p o", o=1))
    fw.ld(mug[:, 2:3], d["d_mu_next"][l, 960:1088].rearrange("(p o) -> p o", o=1))
    fw.tt("dve", mug[:, 0:1], mug[:, 1:2], mug[:, 2:3], ALU.add)
    fw.ts("dve", mug[:, 0:1], mug[:, 0:1], -1.0, 1.0, ALU.mult, ALU.add)
    prm = ph.sb([64, 9, 4], F32, "prm")
    fw.ldn(prm[:, 0, :], d["d_w0"][l, 0].rearrange("(h p) -> p h", p=64))
    fw.ldn(prm[:, 1, :], d["d_w0"][l, 1].rearrange("(h p) -> p h", p=64))
    fw.ldn(prm[:, 2, :], d["d_a0"][l].rearrange("(h p) -> p h", p=64))
    fw.ldn(prm[:, 3, :], d["d_k_k"][l].rearrange("(h p) -> p h", p=64))
    fw.ldn(prm[:, 4, :], d["d_k_a"][l].rearrange("(h p) -> p h", p=64))
    fw.ldn(prm[:, 6, :], d["d_r_k"][l].rearrange("h p -> p h"))
    fw.ldn(prm[:, 7, :], d["d_gn_w"][l].rearrange("(h p) -> p h", p=64))
    fw.ldn(prm[:, 8, :], d["d_gn_b"][l].rearrange("(h p) -> p h", p=64))
    fw.ts("dve", prm[:, 5, :], prm[:, 4, :], -1.0, 1.0, ALU.mult, ALU.add)
    wup = ph.sb([64, 2, 256], BF16, "wup")
    fw.ld(wup[:, 0, :], d["d_w_up"][l, 0], q="pool")
    fw.ld(wup[:, 1, :], d["d_w_up"][l, 1], q="pool")
    aup = ph.sb([64, 256], BF16, "aup")
    fw.ld(aup[:], d["d_a_up"][l], q="pool")
    gup = ph.sb([128, 256], BF16, "gup")
    fw.ld(gup[:], d["d_g_up"][l], q="pool")
    dmask = ph.sb([128, 2, 5, 128], BF16, "dmask")
    fw.ld(dmask[:], d["c_dmask"], q="pool")
    seg = ph.sb([64, 512], F32, "seg")
    fw.ld(seg[:], d["c_seg"])

    def pbc(i):
        return prm.bc((slice(None), i, slice(None)), 2, [64, 4, 128])

    ytok = [T(None, "ytok%d" % c) for c in range(n)]
    H = [ph.sb([64, 4, 64], F32, "Hf"), ph.sb([64, 4, 64], F32, "Hb")]
    Hb = [ph.sb([64, 4, 64], BF16, "Hfb"), ph.sb([64, 4, 64], BF16, "Hbb")]
    for dr in range(2):
        fw.memset("dve", H[dr][:], 0.0)
        fw.memset("pool", Hb[dr][:], 0.0)
    PST = ph.pspool(1, [128, 4, 4, 64], BF16, "pst")
    PP = []
    for dr in range(2):
        Q = Ctx()
        Q.PS = ph.pspool(4 if dr == 0 else 3, [128, 512], F32, "ps")
        Q.zts = ph.sbpool(1, [64, 15, 130], F32, "zt")
        Q.zgs = ph.sbpool(2, [128, 130], F32, "zg")
        Q.us = ph.sbpool(1, [64, 15, 128], F32, "u")
        Q.u2s = ph.sbpool(1, [64, 15, 128], F32, "u2")
        Q.ugs = ph.sbpool(1, [128, 128], F32, "ug")
        Q.F4 = ph.sbpool(10, [64, 4, 128], F32, "f4")
        Q.FL = ph.sbpool(3, [64, 4, 128], F32, "fl")
        Q.B4 = ph.sbpool(12, [64, 4, 128], BF16, "b4")
        Q.M4 = ph.sbpool(9, [128, 4, 128], BF16, "m4")
        Q.MK = ph.sbpool(4, [128, 4, 128], BF16, "mk")
        Q.TM = ph.sbpool(2, [128, 4, 4, 64], BF16, "tm")
        Q.SM = ph.sbpool(4, [128, 128], BF16, "sm")
        Q.S4 = ph.sbpool(6, [64, 4], F32, "s4")
        Q.UV = ph.sbpool(2, [128, 4, 64], F32, "uv")
        Q.UB = ph.sbpool(3, [128, 4, 64], BF16, "ub")
        PP.append(Q)

    def v3(t, npart=64):
        return View(t, t.ap[0:npart, :].rearrange("p (h i) -> p h i", h=4))

    def v3s(t):
        return View(t, t.ap[:, 0:256].rearrange("p (h i) -> p h i", h=4))

    def flat(t):
        return View(t, t.ap[:].rearrange("p h i -> p (h i)"))

    def unit(c, dr, first):
        Q = PP[dr]
        PS, F4, FL, B4, M4, MK, SM, S4 = Q.PS, Q.F4, Q.FL, Q.B4, Q.M4, Q.MK, Q.SM, Q.S4
        need_post = not first
        c0 = c * 128
        zt = Q.zts.next()
        zg = Q.zgs.next()
        lo = 1 if c == 0 else 0
        hi = 129 if c == n - 1 else 130
        if c == 0:
            fw.memset("pool", zt[:, :, 0:1], 0.0)
            fw.memset("pool", zg[:, 0:1], 0.0)
        if c == n - 1:
            fw.memset("pool", zt[:, :, 129:130], 0.0)
            fw.memset("pool", zg[:, 129:130], 0.0)
        fw.ld(zt[:, :, lo:hi], C.ZD[:, :, c0 - 1 + lo:c0 - 1 + hi])
        fw.ld(zg[:, lo:hi], C.ZG[:, c0 - 1 + lo:c0 - 1 + hi])
        if need_post:
            yprev = FL.next()
            fw.ld(yprev[:], View(ytok[c], C.YD[c]))
        yield
        u = Q.us.next()
        u2 = Q.u2s.next()
        fw.tt("dve", u[:], zt[:, :, 1:129], mu.bc((slice(None), 0, slice(None)), 2, [64, 15, 128]), ALU.mult)
        fw.tt("pool", u2[:], zt[:, :, 0:128], mu.bc((slice(None), 1, slice(None)), 2, [64, 15, 128]), ALU.mult)
        yield
        fw.tt("dve", u[:], u[:], u2[:], ALU.add)
        fw.tt("pool", u2[:], zt[:, :, 2:130], mu.bc((slice(None), 2, slice(None)), 2, [64, 15, 128]), ALU.mult)
        yield
        fw.tt("dve", u[:], u[:], u2[:], ALU.add)
        ug = Q.ugs.next()
        fw.ts("dve", ug[:], zg[:, 1:129], mug[:, 0:1], None, ALU.mult)
        fw.stt(ug[:], zg[:, 0:128], mug[:, 1:2], ug[:], ALU.mult, ALU.add)
        fw.stt(ug[:], zg[:, 2:130], mug[:, 2:3], ug[:], ALU.mult, ALU.add)
        yield
        tw = SM.next()
        fw.act(tw[0:64, :], u[:, 12 + dr, :], AF.Tanh)
        adb = SM.next()
        fw.copy("act", adb[0:64, :], u[:, 14, :])
        yw = PS.next()
        ya = PS.next()
        for h in range(4):
            fw.mm(yw[0:64, h * 128:(h + 1) * 128], wup[:, dr, h * 64:(h + 1) * 64], tw[0:64, :], sig=(h == 3))
        for h in range(4):
            fw.mm(ya[0:64, h * 128:(h + 1) * 128], aup[:, h * 64:(h + 1) * 64], adb[0:64, :], sig=(h == 3))
        yield
        sg = F4.next()
        fw.tt("dve", sg[:], v3(yw), pbc(dr), ALU.add)
        fw.act(sg[:], sg[:], AF.Tanh, scale=0.5)
        fw.ts("pool", sg[:], sg[:], 0.5, 0.5, ALU.mult, ALU.add)
        a = F4.next()
        fw.tt("dve", a[:], v3(ya), pbc(2), ALU.add)
        fw.act(a[:], a[:], AF.Tanh, scale=0.5)
        fw.ts("pool", a[:], a[:], 0.5, 0.5, ALU.mult, ALU.add)
        yield
        if need_post:
            sgd = SM.next()
            fw.act(sgd[:], ug[:], AF.Tanh, scale=0.5)
            fw.ts("pool", sgd[:], sgd[:], 0.5, 0.5, ALU.mult, ALU.add)
            gps_ = PS.next()
            for h in range(4):
                fw.mm(gps_[0:64, h * 128:(h + 1) * 128], gup[:, h * 64:(h + 1) * 64], sgd[:], sig=(h == 3))
            g_ = FL.next()
            fw.copy("act", g_[:], v3(gps_))
            yield
        r = u[:, 0:4, :]
        k = u[:, 4:8, :]
        v = u[:, 8:12, :]
        kk = F4.next()
        fw.tt("dve", kk[:], k, pbc(3), ALU.mult)
        sq = B4.next()
        fw.tt("pool", sq[:], kk[:], kk[:], ALU.mult)
        ssp = PS.next()
        fw.mm(ssp[0:64, :], K.ones_bf[0:64, 0:64], flat(sq))
        yield
        nr = F4.next()
        fw.ts("dve", nr[:], v3(ssp), 1e-24, None, ALU.max)
        fw.act(nr[:], nr[:], AF.Ln)
        fw.act(nr[:], nr[:], AF.Exp, scale=-0.5)
        kap = F4.next()
        fw.tt("dve", kap[:], kk[:], nr[:], ALU.mult)
        yield
        tk = F4.next()
        fw.tt("pool", tk[:], a[:], pbc(4), ALU.mult)
        fw.tt("pool", tk[:], tk[:], pbc(5), ALU.add)
        kh = FL.next()
        fw.tt("dve", kh[:], k, tk[:], ALU.mult)
        bb = F4.next()
        fw.tt("pool", bb[:], kap[:], a[:], ALU.mult)
        yield
        Pp = F4.next()
        fw.scan(flat(Pp), seg[:], flat(sg), 0.0, ALU.mult, ALU.add)
        Qp = F4.next()
        fw.tt("pool", Qp[:], Pp[:], sg[:], ALU.subtract)
        nT = S4.next()
        pT = S4.next()
        fw.ts("dve", nT[:], Pp[:, :, 127], -CW, None, ALU.mult)
        fw.ts("dve", pT[:], Pp[:, :, 127], CW, None, ALU.mult)
        gC = S4.next()
        fw.act(gC[:], nT[:], AF.Exp)
        yield
        rt = B4.next()
        kp = B4.next()
        kt = B4.next()
        bt = B4.next()
        Khf = B4.next()
        Bhf = B4.next()
        E1 = F4.next()
        E2 = F4.next()
        if dr == 0:
            fw.act(E1[:], Pp[:], AF.Exp, scale=-CW)
            fw.tt("dve", rt[:], r, E1[:], ALU.mult)
            fw.act(E2[:], Qp[:], AF.Exp, scale=-CW)
            fw.tt("pool", kp[:], kap[:], E2[:], ALU.mult)
            yield
            E3 = F4.next()
            fw.act(E3[:], Pp[:], AF.Exp, scale=CW)
            fw.tt("dve", kt[:], kh[:], E3[:], ALU.mult)
            fw.tt("pool", bt[:], bb[:], E3[:], ALU.mult)
            E4 = F4.next()
            for h in range(4):
                fw.act(E4[:, h, :], Pp[:, h, :], AF.Exp, scale=CW, bias=nT[:, h:h + 1])
        else:
            for h in range(4):
                fw.act(E1[:, h, :], Qp[:, h, :], AF.Exp, scale=CW, bias=nT[:, h:h + 1])
            fw.tt("dve", rt[:], r, E1[:], ALU.mult)
            for h in range(4):
                fw.act(E2[:, h, :], Pp[:, h, :], AF.Exp, scale=CW, bias=nT[:, h:h + 1])
            fw.tt("pool", kp[:], kap[:], E2[:], ALU.mult)
            yield
            E3 = F4.next()
            for h in range(4):
                fw.act(E3[:, h, :], Qp[:, h, :], AF.Exp, scale=-CW, bias=pT[:, h:h + 1])
            fw.tt("dve", kt[:], kh[:], E3[:], ALU.mult)
            fw.tt("pool", bt[:], bb[:], E3[:], ALU.mult)
            E4 = F4.next()
            fw.act(E4[:], Qp[:], AF.Exp, scale=-CW)
        yield
        fw.tt("dve", Khf[:], kh[:], E4[:], ALU.mult)
        fw.stt(Bhf[:], bb[:], -1.0, E4[:], ALU.mult, ALU.mult)
        vb = B4.next()
        fw.copy("act", vb[:], v)
        pst = PST.next()
        for ki, src_ in enumerate((kp, Khf, Bhf, vb)):
            for h in range(4):
                fw.tr(pst[:, ki, h, :], src_[:, h, :], K.ident[0:64, 0:64], sig=(ki == 3 and h == 3))
        tm = Q.TM.next()
        fw.copy("dve", tm[:], pst[:])
        yield
        mats = []
        pairs = [(kp, bt), (bt, kp), (kt, kp), (kt, rt), (bt, rt)]
        for ki, (lt, rt_) in enumerate(pairs):
            p_ = PS.next()
            for h in range(4):
                fw.mm(p_[:, h * 128:(h + 1) * 128], lt[:, h, :], rt_[:, h, :], sig=(h == 3))
            m_ = M4.next() if ki < 2 else MK.next()
            fw.tt("dve", m_[:], v3(p_, 128), dmask.bc((slice(None), dr, ki, slice(None)), 1, [128, 4, 128]), ALU.mult)
            mats.append(m_)
            yield
        X, XT, AkkT, ArkT, ArbT = mats
        PT = M4.next()
        fw.tt("pool", PT[:], XT[:], K.ident.bc((slice(None), slice(None)), 1, [128, 4, 128]), ALU.add)
        for lvl in range(6):
            p1 = PS.next()
            for h in range(4):
                fw.mm(p1[:, h * 128:(h + 1) * 128], XT[:, h, :], X[:, h, :], sig=(h == 3))
            if lvl < 5:
                p2 = PS.next()
                for h in range(4):
                    fw.mm(p2[:, h * 128:(h + 1) * 128], X[:, h, :], XT[:, h, :], sig=(h == 3))
            yield
            X2 = M4.next()
            fw.copy("act", X2[:], v3(p1, 128))
            if lvl < 5:
                XT2 = M4.next()
                fw.copy("dve", XT2[:], v3(p2, 128))
            p3 = PS.next()
            for h in range(4):
                fw.mm(p3[:, h * 128:(h + 1) * 128], X2[:, h, :], PT[:, h, :], sig=(h == 3))
            yield
            PT2 = M4.next()
            fw.tt("dve", PT2[:], v3(p3, 128), PT[:], ALU.add)
            PT = PT2
            X = X2
            if lvl < 5:
                XT = XT2
        pw = PS.next()
        for h in range(4):
            fw.mm(pw[0:64, h * 128:(h + 1) * 128], tm[:, 0, h, :], PT[:, h, :], sig=(h == 3))
        pa = PS.next()
        for h in range(4):
            fw.mm(pa[:, h * 64:(h + 1) * 64], AkkT[:, h, :], tm[:, 3, h, :], sig=(h == 3))
        yield
        WkT = B4.next()
        fw.copy("act", WkT[:], v3(pw))
        AV = Q.UB.next()
        fw.copy("dve", AV[:], v3s(pa))
        pu = PS.next()
        for h in range(4):
            fw.mm(pu[:, h * 64:(h + 1) * 64], PT[:, h, :], AV[:, h, :], sig=(h == 3))
        yield
        Uv = Q.UV.next()
        fw.copy("act", Uv[:], v3s(pu))
        pU = PS.next()
        for h in range(4):
            fw.mm(pU[:, h * 64:(h + 1) * 64], WkT[:, h, :], Hb[dr][:, h, :], sig=(h == 3))
        yield
        U = Q.UB.next()
        fw.tt("dve", U[:], v3s(pU), Uv[:], ALU.add)
        pY = PS.next()
        for h in range(4):
            yh = pY[0:64, h * 128:(h + 1) * 128]
            fw.mm(yh, Hb[dr][:, h, :], rt[:, h, :], start=True, stop=False, sig=False)
            fw.mm(yh, tm[:, 3, h, :], ArkT[:, h, :], start=False, stop=False, sig=False)
            fw.mm(yh, U[:, h, :], ArbT[:, h, :], start=False, stop=True, sig=(h == 3))
        pH = PS.next()
        for h in range(4):
            hh = pH[0:64, h * 64:(h + 1) * 64]
            fw.mm(hh, tm[:, 1, h, :], tm[:, 3, h, :], start=True, stop=False, sig=False)
            fw.mm(hh, tm[:, 2, h, :], U[:, h, :], start=False, stop=True, sig=(h == 3))
        fw.tt("pool", H[dr][:], H[dr][:], gC.bc((slice(None), slice(None)), 2, [64, 4, 64]), ALU.mult)
        yield
        fw.tt("dve", H[dr][:], H[dr][:], View(pH, pH.ap[0:64, 0:256].rearrange("p (h i) -> p h i", h=4)), ALU.add)
        fw.copy("act", Hb[dr][:], H[dr][:])
        if first:
            ysb = F4.next()
            fw.copy("act", ysb[:], v3(pY))
            fw.st(View(ytok[c], C.YD[c]), ysb[:])
            return
        y = F4.next()
        fw.tt("dve", y[:], v3(pY), yprev[:], ALU.add)
        pm_ = PS.next()
        fw.mm(pm_[0:64, :], K.ones_f[0:64, 0:64], flat(y))
        rk = F4.next()
        fw.tt("pool", rk[:], r, kh[:], ALU.mult)
        fw.tt("pool", rk[:], rk[:], pbc(6), ALU.mult)
        yield
        yc = F4.next()
        fw.stt(yc[:], v3(pm_), -1.0 / 64, y[:], ALU.mult, ALU.add)
        sq2 = F4.next()
        fw.act(sq2[:], yc[:], AF.Square)
        pv = PS.next()
        fw.mm(pv[0:64, :], K.ones_f[0:64, 0:64], flat(sq2))
        pr_ = PS.next()
        fw.mm(pr_[0:64, :], K.ones_f[0:64, 0:64], flat(rk))
        yield
        rs = F4.next()
        fw.ts("dve", rs[:], v3(pv), 1.0 / 64, GN_EPS, ALU.mult, ALU.add)
        fw.act(rs[:], rs[:], AF.Ln)
        fw.act(rs[:], rs[:], AF.Exp, scale=-0.5)
        fw.tt("dve", yc[:], yc[:], rs[:], ALU.mult)
        fw.tt("pool", yc[:], yc[:], pbc(7), ALU.mult)
        yield
        fw.tt("pool", yc[:], yc[:], pbc(8), ALU.add)
        t_ = F4.next()
        fw.tt("dve", t_[:], v3(pr_), v, ALU.mult)
        fw.tt("pool", yc[:], yc[:], t_[:], ALU.add)
        ob = B4.next()
        fw.tt("dve", ob[:], yc[:], g_[:], ALU.mult)
        for h in range(4):
            fw.st(C.OT[12 + h, :, c * 128:(c + 1) * 128], ob[:, h, :])

    for i in range(n):
        first = i < n // 2
        gens = [unit(i, 0, first), unit(n - 1 - i, 1, first)]
        alive = [True, True]
        while any(alive):
            for gi in range(2):
                if alive[gi]:
                    try:
                        next(gens[gi])
                    except StopIteration:
                        alive[gi] = False
    ph.close()


def build(seqs=(4096, 2048), depth=2, phases=None, debug=False):
    nc = bass.Bass("TRN2", target_bir_lowering=False)
    C = Ctx()
    C.seqs = list(seqs)
    C.d = {}
    C.xin = [nc.dram_tensor("x%d" % i, [t, D_MODEL], F32, kind="ExternalInput").ap() for i, t in enumerate(seqs)]
    C.y = [nc.dram_tensor("y%d" % i, [t, D_MODEL], F32, kind="ExternalOutput").ap() for i, t in enumerate(seqs)]
    for nm, shp in WEIGHT_SPECS:
        C.d[nm] = nc.dram_tensor(nm, list(shp), F32, kind="ExternalInput").ap()
    for nm, arr in make_consts().items():
        C.d[nm] = nc.dram_tensor(nm, list(arr.shape), F32, kind="ExternalInput").ap()
    TM = max(seqs)
    sk = "ExternalOutput" if debug else "Internal"

    def scratch(nm, shp, dt):
        return nc.dram_tensor(nm, list(shp), dt, kind=sk).ap()
    C.QA = scratch("QA", [64, 4, TM], BF16)
    C.KA = scratch("KA", [64, 2, TM], BF16)
    C.VA = scratch("VA", [TM, 128], BF16)
    C.QB = scratch("QB", [64, 4, TM], BF16)
    C.KB = scratch("KB", [64, 4, TM], BF16)
    C.VB = scratch("VB", [TM, 256], BF16)
    C.QC = scratch("QC", [64, 4, TM], BF16)
    C.KC = scratch("KC", [64, 4, TM], BF16)
    C.GC = scratch("GC", [64, 4, TM], BF16)
    C.VC = scratch("VC", [TM, 256], BF16)
    C.ZD = scratch("ZD", [64, 15, TM], F32)
    C.ZG = scratch("ZG", [128, TM], F32)
    C.OT = scratch("OT", [16, 64, TM], BF16)
    C.YD = scratch("YD", [TM // 128, 64, 4, 128], F32)
    allph = ["inproj", "A", "B", "C", "D", "outproj", "ffn"]
    phases = allph if phases is None else phases
    with ExitStack() as es:
        fw = FW(nc, es)
        ph0 = fw.phase()
        K = load_consts(fw, C, ph0)
        fw.barrier()
        for l in range(depth):
            for si in range(len(seqs)):
                if "inproj" in phases:
                    phase_inproj(fw, C, K, l, si)
                if "A" in phases:
                    phase_attnA(fw, C, K, l, si)
                if "B" in phases:
                    phase_attnB(fw, C, K, l, si)
                if "C" in phases:
                    phase_ret(fw, C, K, l, si)
                if "D" in phases:
                    phase_rwkv(fw, C, K, l, si)
                if "outproj" in phases:
                    phase_outproj(fw, C, K, l, si)
                if "ffn" in phases:
                    phase_ffn(fw, C, K, l, si)
        fw.barrier()
        ph0.es.close()
        C.nins = fw.nins
    return nc, C


_CACHE = {}


def kernel(**inputs):
    xp = np.ascontiguousarray(inputs["x_prompt"], dtype=np.float32)
    xs = np.ascontiguousarray(inputs["x_sample"], dtype=np.float32)
    nc, C = build()
    consts = make_consts()
    in_maps = []
    for i in range(8):
        m = {"x0": xp[i], "x1": xs[i]}
        for nm, _ in WEIGHT_SPECS:
            m[nm] = np.ascontiguousarray(inputs[nm], dtype=np.float32)
        m.update(consts)
        in_maps.append(m)
    res = run_bass_kernel_spmd(nc, in_maps, core_ids=list(range(8)))
    yp = np.stack([r["y0"] for r in res.results], 0).astype(np.float32)
    ys = np.stack([r["y1"] for r in res.results], 0).astype(np.float32)
    return (yp, ys)
```

```python
import math
from contextlib import ExitStack
import numpy as np
import concourse.bass as bass
import concourse.mybir as mybir
from concourse.bass_utils import run_bass_kernel_spmd

F32 = mybir.dt.float32
BF16 = mybir.dt.bfloat16
AF = mybir.ActivationFunctionType
ALU = mybir.AluOpType
AX = mybir.AxisListType

D_MODEL = 1024
IN_COLS = 3392
D_FF = 2816
NFF = D_FF // 128
EPS = 1e-6
GN_EPS = 64e-5
TMAX = 4096


class View:
    __slots__ = ("t", "ap")

    def __init__(self, t, ap):
        self.t = t
        self.ap = ap


class T:
    __slots__ = ("ap", "w", "r", "name")

    def __init__(self, ap, name=""):
        self.ap = ap
        self.w = {}
        self.r = {}
        self.name = name

    def __getitem__(self, idx):
        return View(self, self.ap[idx])

    def bc(self, idx, axis, shape):
        return View(self, self.ap[idx].unsqueeze(axis).to_broadcast(list(shape)))


class Eng:
    def __init__(self, name, handle, sem):
        self.name = name
        self.h = handle
        self.sem = sem
        self.count = 0
        self.seen = {}


class Pool:
    def __init__(self, items):
        self.items = items
        self.i = 0

    def next(self):
        t = self.items[self.i]
        self.i = (self.i + 1) % len(self.items)
        return t


class Phase:
    def __init__(self, fw):
        self.fw = fw
        self.es = ExitStack()
        self.n = 0

    def sb(self, shape, dt, name="t"):
        self.n += 1
        t = self.es.enter_context(self.fw.nc.sbuf_tensor("%s_%d_%d" % (name, self.fw.uid(), self.n), list(shape), dt))
        return T(t, name)

    def ps(self, shape, dt=F32, name="p"):
        self.n += 1
        t = self.es.enter_context(self.fw.nc.psum_tensor("%s_%d_%d" % (name, self.fw.uid(), self.n), list(shape), dt))
        return T(t, name)

    def sbpool(self, n, shape, dt, name="t"):
        return Pool([self.sb(shape, dt, name) for _ in range(n)])

    def pspool(self, n, shape, dt=F32, name="p"):
        return Pool([self.ps(shape, dt, name) for _ in range(n)])

    def close(self):
        self.fw.barrier()
        self.es.close()


class FW:
    NDMA = 32

    def __init__(self, nc, es):
        self.nc = nc
        self.es = es
        self._uid = 0
        self.E = {}
        for nm, h in (("pe", nc.tensor), ("act", nc.scalar), ("dve", nc.vector), ("pool", nc.gpsimd), ("sp", nc.sync)):
            sem = es.enter_context(nc.semaphore("s_" + nm))
            self.E[nm] = Eng(nm, h, sem)
        self.dsem = [es.enter_context(nc.semaphore("d%d" % i)) for i in range(self.NDMA)]
        self.dval = [0] * self.NDMA
        self.dnext = {"sp": 0, "pool": 0, "act": 0}
        self.drange = {"sp": (0, 14), "pool": (14, 28), "act": (28, 32)}
        self.semobj = {}
        for e in self.E.values():
            self.semobj[("e", e.name)] = e.sem
        for i, s in enumerate(self.dsem):
            self.semobj[("d", i)] = s
        self.nins = 0

    def uid(self):
        self._uid += 1
        return self._uid

    def phase(self):
        return Phase(self)

    def _needs(self, reads, writes):
        needs = {}
        for t in reads:
            for k, v in t.w.items():
                if needs.get(k, 0) < v:
                    needs[k] = v
        for t in writes:
            for d in (t.w, t.r):
                for k, v in d.items():
                    if needs.get(k, 0) < v:
                        needs[k] = v
        return needs

    def _emit_waits(self, e, needs, skip_self=False):
        for k, v in needs.items():
            if skip_self and k == ("e", e.name):
                continue
            if e.seen.get(k, 0) >= v:
                continue
            if k[0] == "e":
                assert self.E[k[1]].count >= v, "wait on future event %s %d" % (k, v)
            e.seen[k] = v
            e.h.wait_ge(self.semobj[k], v)
            self.nins += 1

    def op(self, eng, fn, reads=(), writes=(), signal=True):
        e = self.E[eng]
        reads = [v.t for v in reads]
        writes = [v.t for v in writes]
        needs = self._needs(reads, writes)
        self._emit_waits(e, needs, skip_self=(eng == "pe"))
        ins = fn(e.h)
        self.nins += 1
        if signal:
            e.count += 1
            ins.then_inc(e.sem, 1)
            ev = e.count
        else:
            ev = e.count + 1
        key = ("e", eng)
        for t in writes:
            t.w = {key: ev}
            t.r = {}
        for t in reads:
            if t.r.get(key, 0) < ev:
                t.r[key] = ev

    def dma(self, q, out, in_, **kw):
        e = self.E[q]
        reads = [in_.t] if isinstance(in_, View) else []
        writes = [out.t] if isinstance(out, View) else []
        needs = self._needs(reads, writes)
        lo_, hi_ = self.drange[q]
        i = lo_ + self.dnext[q]
        self.dnext[q] = (self.dnext[q] + 1) % (hi_ - lo_)
        key = ("d", i)
        if self.dval[i] > 0 and needs.get(key, 0) < self.dval[i]:
            needs[key] = self.dval[i]
        self._emit_waits(e, needs)
        self.dval[i] += 16
        v = self.dval[i]
        oap = out.ap if isinstance(out, View) else out
        iap = in_.ap if isinstance(in_, View) else in_
        e.h.dma_start(out=oap, in_=iap, **kw).then_inc(self.dsem[i], 16)
        self.nins += 1
        for t in writes:
            t.w = {key: v}
            t.r = {}
        for t in reads:
            t.r[key] = v

    def barrier(self):
        allev = {}
        for e in self.E.values():
            if e.count > 0:
                allev[("e", e.name)] = e.count
        for i in range(self.NDMA):
            if self.dval[i] > 0:
                allev[("d", i)] = self.dval[i]
        for e in self.E.values():
            self._emit_waits(e, dict(allev))

    def ld(self, dst, src, q="sp", **kw):
        self.dma(q, dst, src, **kw)

    def ldn(self, dst, src, q="sp"):
        self.dma(q, dst, src, allow_slow_non_contiguous=True)

    def st(self, dst, src, q="pool"):
        self.dma(q, dst, src)

    def mm(self, out, lhsT, rhs, start=True, stop=True, sig=None):
        if sig is None:
            sig = stop
        self.op("pe", lambda h: h.matmul(out.ap, lhsT=lhsT.ap, rhs=rhs.ap, start=start, stop=stop),
                reads=[lhsT, rhs], writes=[out], signal=sig)

    def tr(self, out, in_, ident, sig=True):
        self.op("pe", lambda h: h.transpose(out=out.ap, in_=in_.ap, identity=ident.ap),
                reads=[in_, ident], writes=[out], signal=sig)

    def act(self, out, in_, func, bias=None, scale=None, accum=None):
        reads = [in_]
        kw = {}
        if bias is not None:
            if isinstance(bias, View):
                reads.append(bias)
                kw["bias"] = bias.ap
            else:
                kw["bias"] = float(bias)
        if scale is not None:
            if isinstance(scale, View):
                reads.append(scale)
                kw["scale"] = scale.ap
            else:
                kw["scale"] = float(scale)
        writes = [out]
        if accum is not None:
            kw["accum_out"] = accum.ap
            writes.append(accum)
        self.op("act", lambda h: h.activation(out=out.ap, in_=in_.ap, func=func, **kw), reads=reads, writes=writes)

    def tt(self, eng, out, in0, in1, op):
        self.op(eng, lambda h: h.tensor_tensor(out=out.ap, in0=in0.ap, in1=in1.ap, op=op), reads=[in0, in1], writes=[out])

    def ts(self, eng, out, in0, s1, s2, op0, op1=None):
        reads = [in0]
        a1 = s1.ap if isinstance(s1, View) else s1
        a2 = s2.ap if isinstance(s2, View) else s2
        if isinstance(s1, View):
            reads.append(s1)
        if isinstance(s2, View):
            reads.append(s2)
        if op1 is None:
            self.op(eng, lambda h: h.tensor_scalar(out=out.ap, in0=in0.ap, scalar1=a1, scalar2=None, op0=op0), reads=reads, writes=[out])
        else:
            self.op(eng, lambda h: h.tensor_scalar(out=out.ap, in0=in0.ap, scalar1=a1, scalar2=a2, op0=op0, op1=op1), reads=reads, writes=[out])

    def stt(self, out, in0, scalar, in1, op0, op1):
        reads = [in0, in1]
        sa = scalar.ap if isinstance(scalar, View) else scalar
        if isinstance(scalar, View):
            reads.append(scalar)
        self.op("dve", lambda h: h.scalar_tensor_tensor(out=out.ap, in0=in0.ap, scalar=sa, in1=in1.ap, op0=op0, op1=op1), reads=reads, writes=[out])

    def copy(self, eng, out, in_):
        if eng == "act":
            self.op("act", lambda h: h.activation(out=out.ap, in_=in_.ap, func=AF.Identity), reads=[in_], writes=[out])
        else:
            self.op(eng, lambda h: h.tensor_copy(out=out.ap, in_=in_.ap), reads=[in_], writes=[out])

    def recip(self, out, in_):
        self.op("dve", lambda h: h.reciprocal(out=out.ap, in_=in_.ap), reads=[in_], writes=[out])

    def memset(self, eng, out, val):
        self.op(eng, lambda h: h.memset(out.ap, val), writes=[out])

    def reduce(self, out, in_, op, axis=AX.X):
        self.op("dve", lambda h: h.tensor_reduce(out=out.ap, in_=in_.ap, axis=axis, op=op), reads=[in_], writes=[out])

    def scan(self, out, d0, d1, init, op0, op1):
        self.op("dve", lambda h: h.tensor_tensor_scan(out=out.ap, data0=d0.ap, data1=d1.ap, initial=init, op0=op0, op1=op1), reads=[d0, d1], writes=[out])


def run_pipe(gens, depth, side=None, side_every=8):
    gens = list(gens)
    active = []
    tick = 0
    while gens or active:
        while gens and len(active) < depth:
            active.append(gens.pop(0))
        for g in list(active):
            try:
                next(g)
            except StopIteration:
                active.remove(g)
        tick += 1
        if side is not None and tick % side_every == 0:
            try:
                next(side)
            except StopIteration:
                side = None
    if side is not None:
        for _ in side:
            pass


def _angles(pos, rot_dim, theta):
    inv = (theta ** (-np.arange(0, rot_dim, 2, dtype=np.float32) / rot_dim)).astype(np.float32)
    return pos.astype(np.float32)[:, None] * inv[None, :]


def make_consts():
    c = {}
    c["c_ident"] = np.eye(128, dtype=np.float32)
    T = TMAX
    pos = np.arange(T)
    rope = np.zeros((64, 6, T), np.float32)
    ar = _angles(pos // 64, 32, 10000.0)
    ac = _angles(pos % 64, 32, 10000.0)
    for d in range(64):
        a = ar if d < 32 else ac
        idx = (d % 32) % 16
        rope[d, 0] = np.cos(a[:, idx])
        rope[d, 1] = np.sin(a[:, idx])
    ab = _angles(pos, 8, 500000.0)
    for d in range(64):
        dd = d % 32
        if dd < 8:
            rope[d, 2] = np.cos(ab[:, dd % 4])
            rope[d, 3] = np.sin(ab[:, dd % 4])
        else:
            rope[d, 2] = 1.0
            rope[d, 3] = 0.0
    acc = _angles(pos, 64, 10000.0)
    for d in range(64):
        rope[d, 4] = np.cos(acc[:, d % 32])
        rope[d, 5] = np.sin(acc[:, d % 32])
    c["c_rope"] = rope
    pm = np.zeros((64, 3, 64), np.float32)
    for m in range(64):
        mm_ = m % 32
        base = m - mm_
        if mm_ < 16:
            pm[base + mm_ + 16, 0, m] = -1.0
        else:
            pm[base + mm_ - 16, 0, m] = 1.0
        if mm_ < 4:
            pm[base + mm_ + 4, 1, m] = -1.0
        elif mm_ < 8:
            pm[base + mm_ - 4, 1, m] = 1.0
        if m < 32:
            pm[m + 32, 2, m] = -1.0
        else:
            pm[m - 32, 2, m] = 1.0
    c["c_pm"] = pm
    c["c_ones"] = np.ones((128, 128), np.float32)
    h = np.arange(4, dtype=np.float32)
    lgf = np.log1p(-np.exp2(-5.0 - h)).astype(np.float32)
    lgb = lgf[::-1].copy()
    i = np.arange(128, dtype=np.float32)
    diff = i[None, :] - i[:, None]
    rm = np.zeros((128, 4, 128), np.float32)
    for hh in range(4):
        rm[:, hh, :] = np.where(diff >= 0, np.exp(lgf[hh] * np.maximum(diff, 0)), 0.0) + \
            np.where(diff <= 0, np.exp(lgb[hh] * np.maximum(-diff, 0)), 0.0)
    c["c_retmask"] = rm
    qd = np.zeros((64, 2, 4, 128), np.float32)
    kd = np.zeros((128, 2, 4), np.float32)
    for hh in range(4):
        qd[:, 0, hh, :] = np.exp(lgf[hh] * (i + 1.0))[None, :]
        qd[:, 1, hh, :] = np.exp(lgb[hh] * (128.0 - i))[None, :]
        kd[:, 0, hh] = np.exp(lgf[hh] * (127.0 - i))
        kd[:, 1, hh] = np.exp(lgb[hh] * i)
    c["c_qdec"] = qd
    c["c_kdec"] = kd
    dm = np.zeros((128, 2, 5, 128), np.float32)
    p = np.arange(128)[:, None]
    f = np.arange(128)[None, :]
    dm[:, 0, 0, :] = -1.0 * (p > f)
    dm[:, 0, 1, :] = -1.0 * (f > p)
    dm[:, 0, 2, :] = (f > p)
    dm[:, 0, 3, :] = (f >= p)
    dm[:, 0, 4, :] = -1.0 * (f >= p)
    dm[:, 1, 0, :] = -1.0 * (p < f)
    dm[:, 1, 1, :] = -1.0 * (f < p)
    dm[:, 1, 2, :] = (f < p)
    dm[:, 1, 3, :] = (f <= p)
    dm[:, 1, 4, :] = -1.0 * (f <= p)
    c["c_dmask"] = dm.astype(np.float32)
    seg = np.ones((64, 512), np.float32)
    seg[:, ::128] = 0.0
    c["c_seg"] = seg
    return c


RET_G = [1.0 - 2.0 ** (-5 - h) for h in range(4)]

WEIGHT_SPECS = [
    ("norm_mix_pre", (2, 1024)), ("norm_mix_post", (2, 1024)), ("norm_ffn_pre", (2, 1024)), ("norm_ffn_post", (2, 1024)),
    ("w_in", (2, 1024, 3392)), ("w_out", (2, 1024, 1024)), ("a_q_gain", (2, 64)), ("a_k_gain", (2, 64)),
    ("b_lambda", (2, 4, 32)), ("b_subln_gain", (2, 64)), ("c_gn_gain", (2, 256)), ("d_mu_prev", (2, 1088)),
    ("d_mu_next", (2, 1088)), ("d_w0", (2, 2, 256)), ("d_w_up", (2, 2, 64, 256)), ("d_a0", (2, 256)),
    ("d_a_up", (2, 64, 256)), ("d_g_up", (2, 128, 256)), ("d_k_k", (2, 256)), ("d_k_a", (2, 256)),
    ("d_r_k", (2, 4, 64)), ("d_gn_w", (2, 256)), ("d_gn_b", (2, 256)),
    ("ffn_w_gate", (2, 1024, 2816)), ("ffn_w_up", (2, 1024, 2816)), ("ffn_w_down", (2, 2816, 1024)),
]


class Ctx:
    pass


def load_consts(fw, C, ph):
    K = Ctx()
    K.ident = ph.sb([128, 128], BF16, "ident")
    fw.ld(K.ident[:], C.d["c_ident"], q="pool")
    K.identf = ph.sb([128, 128], F32, "identf")
    fw.ld(K.identf[:], C.d["c_ident"])
    K.ones_bf = ph.sb([128, 128], BF16, "ones_bf")
    fw.ld(K.ones_bf[:], C.d["c_ones"], q="pool")
    K.ones_f = ph.sb([128, 128], F32, "ones_f")
    fw.ld(K.ones_f[:], C.d["c_ones"])
    K.pm = []
    for i in range(3):
        t = ph.sb([64, 64], BF16, "pm%d" % i)
        fw.ld(t[:], C.d["c_pm"][:, i, :], q="pool")
        K.pm.append(t)
    return K


def rms_rows(fw, xt, gain, hb, tmp, ss, rstd):
    fw.act(tmp[:], xt[:], AF.Square, accum=ss[:])
    fw.act(rstd[:], ss[:], AF.Sqrt, bias=EPS, scale=1.0 / D_MODEL)
    fw.recip(rstd[:], rstd[:])
    fw.stt(hb[:], xt[:], rstd[:, 0:1], gain[:], ALU.mult, ALU.mult)


def make_hT(fw, K, src_rows, gain, hT, j, P):
    xt = P.xt.next()
    fw.ld(xt[:], src_rows)
    hb = P.hb.next()
    ss = P.ss.next()
    rstd = P.rstd.next()
    tmp = P.tmp.next()
    rms_rows(fw, xt, gain, hb, tmp, ss, rstd)
    tp = P.tp.next()
    for k in range(8):
        fw.tr(tp[:, k, :], hb[:, k * 128:(k + 1) * 128], K.ident[:], sig=(k == 7))
    fw.copy("dve", hT[:, :, j * 128:(j + 1) * 128], tp[:])
    return xt


def phase_inproj(fw, C, K, l, si):
    T_ = C.seqs[si]
    src = C.xin[si] if l == 0 else C.y[si]
    d = C.d
    ph = fw.phase()
    W = ph.sb([128, 8, IN_COLS], BF16, "W")
    for k in range(8):
        fw.ld(W[:, k, :], d["w_in"][l, k * 128:(k + 1) * 128, :], q="pool")
    gain = ph.sb([128, 1024], F32, "gain")
    fw.ld(gain[:], d["norm_mix_pre"][l].partition_broadcast(128))
    gqk = ph.sb([64, 2], F32, "gqk")
    fw.ld(gqk[:, 0:1], d["a_q_gain"][l].rearrange("(p o) -> p o", o=1))
    fw.ld(gqk[:, 1:2], d["a_k_gain"][l].rearrange("(p o) -> p o", o=1))
    P = Ctx()
    P.xt = ph.sbpool(2, [128, 1024], F32, "xt")
    P.hb = ph.sbpool(2, [128, 1024], BF16, "hb")
    P.tmp = ph.sbpool(1, [128, 1024], F32, "tmp")
    P.ss = ph.sbpool(2, [128, 1], F32, "ss")
    P.rstd = ph.sbpool(2, [128, 1], F32, "rstd")
    P.tp = ph.pspool(1, [128, 8, 128], BF16, "tp")
    hTs = ph.sbpool(2, [128, 8, 512], BF16, "hT")
    tabs = ph.sbpool(2, [64, 6, 512], F32, "tab")
    zps = ph.pspool(4, [128, 512], F32, "zp")
    aps = ph.pspool(2, [128, 512], F32, "ap")
    tps = ph.pspool(1, [128, 512], F32, "tmps")
    f32p = ph.sbpool(24, [64, 512], F32, "f32p")
    bfp = ph.sbpool(16, [64, 512], BF16, "bfp")
    f32w = ph.sbpool(3, [128, 512], F32, "f32w")
    bfw = ph.sbpool(3, [128, 512], BF16, "bfw")
    chunks = []
    for h in range(4):
        chunks.append(("A", 0 + 64 * h, C.QA, h, 0))
    for h in range(2):
        chunks.append(("A", 256 + 64 * h, C.KA, h, 1))
    for h in range(4):
        chunks.append(("B", 512 + 64 * h, C.QB, h, 0))
    for h in range(4):
        chunks.append(("B", 768 + 64 * h, C.KB, h, 0))
    for h in range(4):
        chunks.append(("C", 1280 + 64 * h, C.QC, h, 0))
    for h in range(4):
        chunks.append(("C", 1536 + 64 * h, C.KC, h, 0))
    for h in range(4):
        chunks.append(("G", 2048 + 64 * h, C.GC, h, 0))
    for j in range(15):
        chunks.append(("D", 2304 + 64 * j, C.ZD, j, 0))
    DEPTHP = 4

    def chunk_gen(kind, col, dst, slot, gi, hT, tab, t0):
        zp = zps.next()
        for k in range(8):
            fw.mm(zp[0:64, :], W[:, k, col:col + 64], hT[:, k, :], start=(k == 0), stop=(k == 7))
        z = zp[0:64, :]
        yield
        if kind in ("A", "B", "C"):
            ti = {"A": 0, "B": 2, "C": 4}[kind]
            pmi = {"A": 0, "B": 1, "C": 2}[kind]
            qn = bfp.next()
            qf = f32p.next()
            fw.copy("act", qf[:], z)
            if kind == "A":
                sq = bfp.next()
                fw.act(sq[:], z, AF.Square)
                yield
                ms = aps.next()
                fw.mm(ms[0:64, :], K.ones_bf[0:64, 0:64], sq[:])
                rs = f32p.next()
                fw.ts("dve", rs[:], ms[0:64, :], 1.0 / 64, EPS, ALU.mult, ALU.add)
                fw.act(rs[:], rs[:], AF.Ln)
                fw.act(rs[:], rs[:], AF.Exp, scale=-0.5)
                qf2 = f32p.next()
                fw.stt(qf2[:], qf[:], gqk[:, gi:gi + 1], rs[:], ALU.mult, ALU.mult)
                qf = qf2
            fw.copy("act", qn[:], qf[:])
            yield
            pq = aps.next()
            fw.mm(pq[0:64, :], K.pm[pmi][:], qn[:])
            t2 = f32p.next()
            fw.tt("dve", t2[:], pq[0:64, :], tab[:, ti + 1, :], ALU.mult)
            t1 = f32p.next()
            fw.tt("pool", t1[:], qf[:], tab[:, ti, :], ALU.mult)
            ob = bfp.next()
            fw.tt("pool", ob[:], t1[:], t2[:], ALU.add)
            fw.st(dst[:, slot, t0:t0 + 512], ob[:])
        elif kind == "G":
            ob = bfp.next()
            fw.act(ob[:], z, AF.Silu)
            fw.st(dst[:, slot, t0:t0 + 512], ob[:])
        else:
            of = f32p.next()
            fw.copy("act", of[:], z)
            fw.st(dst[:, slot, t0:t0 + 512], of[:])

    def tail_gen(hT, t0):
        zp = zps.next()
        for k in range(8):
            fw.mm(zp[:, :], W[:, k, 3264:3392], hT[:, k, :], start=(k == 0), stop=(k == 7))
        of = f32w.next()
        fw.copy("act", of[:], zp[:])
        fw.st(C.ZG[:, t0:t0 + 512], of[:])
        yield
        for j in range(4):
            tp_ = tps.next()
            for k in range(8):
                fw.mm(tp_[:, 0:128], hT[:, k, j * 128:(j + 1) * 128], W[:, k, 384:512], start=(k == 0), stop=(k == 7), sig=False)
            for k in range(8):
                fw.mm(tp_[:, 128:384], hT[:, k, j * 128:(j + 1) * 128], W[:, k, 1024:1280], start=(k == 0), stop=(k == 7))
            ob = bfw.next()
            fw.copy("act", ob[:, 0:384], tp_[:, 0:384])
            r0 = t0 + j * 128
            fw.st(C.VA[r0:r0 + 128, :], ob[:, 0:128])
            fw.st(C.VB[r0:r0 + 128, :], ob[:, 128:384])
            yield
            tp2 = tps.next()
            for k in range(8):
                fw.mm(tp2[:, 0:256], hT[:, k, j * 128:(j + 1) * 128], W[:, k, 1792:2048], start=(k == 0), stop=(k == 7))
            ob2 = bfw.next()
            fw.copy("dve", ob2[:, 0:256], tp2[:, 0:256])
            fw.st(C.VC[r0:r0 + 128, :], ob2[:, 0:256])
            yield

    def hT_gen(b, hT):
        t0 = b * 512
        for j in range(4):
            make_hT(fw, K, src[t0 + j * 128:t0 + (j + 1) * 128, :], gain, hT, j, P)
            yield

    def run_pipe(gens, depth, side=None):
        gens = list(gens)
        active = []
        tick = 0
        while gens or active:
            while gens and len(active) < depth:
                active.append(gens.pop(0))
            for g in list(active):
                try:
                    next(g)
                except StopIteration:
                    active.remove(g)
            tick += 1
            if side is not None and tick % 8 == 0:
                try:
                    next(side)
                except StopIteration:
                    side = None
        if side is not None:
            for _ in side:
                pass

    nb = T_ // 512
    hT_cur = hTs.next()
    for _ in hT_gen(0, hT_cur):
        pass
    for b in range(nb):
        t0 = b * 512
        tab = tabs.next()
        fw.ld(tab[:], d["c_rope"][:, :, t0:t0 + 512])
        hT = hT_cur
        side = None
        if b + 1 < nb:
            hT_cur = hTs.next()
            side = hT_gen(b + 1, hT_cur)
        gens = [chunk_gen(kind, col, dst, slot, gi, hT, tab, t0) for (kind, col, dst, slot, gi) in chunks]
        gens.append(tail_gen(hT, t0))
        run_pipe(gens, DEPTHP, side)
    ph.close()


def attn_core(fw, ph, K, T_, QT, KT, V, heads, negM, scale, emit):
    LOOK = 2
    sps = ph.pspool(4, [128, 512], F32, "sps")
    ops_ = ph.pspool(2, [128, 512], F32, "ops")
    pts = ph.sbpool(4, [128, 512], BF16, "pt")
    osbs = ph.sbpool(4, [65, 512], F32, "osb")
    nkt = T_ // 128
    iters = []
    for qb in range(T_ // 512):
        for hd in heads:
            for kt in range(nkt):
                iters.append((qb, hd, kt))
    pend = []
    state = {"o_ps": None}

    def do_pv(item):
        (qb, hd, kt, pt) = item
        (qs, lo, hi, ks, vs, tag) = hd
        if kt == 0:
            state["o_ps"] = ops_.next()
        o_ps = state["o_ps"]
        fw.mm(o_ps[0:65, :], V[:, kt, vs, :], pt[:], start=(kt == 0), stop=(kt == nkt - 1))
        if kt == nkt - 1:
            osb = osbs.next()
            fw.copy("dve", osb[:], o_ps[0:65, :])
            emit(qb, tag, osb)

    for (qb, hd, kt) in iters:
        (qs, lo, hi, ks, vs, tag) = hd
        q0 = qb * 512
        st = sps.next()
        fw.mm(st[:, :], KT[lo:hi, ks, kt * 128:(kt + 1) * 128], QT[lo:hi, qs, q0:q0 + 512])
        pt = pts.next()
        if negM is None:
            fw.act(pt[:], st[:], AF.Exp, scale=scale)
        else:
            fw.act(pt[:], st[:], AF.Exp, scale=scale, bias=negM[:, 0:1])
        pend.append((qb, hd, kt, pt))
        if len(pend) > LOOK:
            do_pv(pend.pop(0))
    while pend:
        do_pv(pend.pop(0))


def attn_core2(fw, ph, T_, groups, V, negM, scale, emit, bcs):
    LOOK = 2
    sps = ph.pspool(3, [128, 1024], F32, "sps")
    ops_ = bcs
    pts = ph.sbpool(4, [128, 1024], BF16, "pt")
    osbs = ph.sbpool(4, [65, 512], F32, "osb")
    nkt = T_ // 128
    iters = []
    for qb in range(T_ // 512):
        for g in groups:
            for kt in range(nkt):
                iters.append((qb, g, kt))
    pend = []
    state = {}
    import os
    NW = int(os.environ.get("DBG_WARM", "0"))
    if NW:
        wz = pts.next()
        fw.memset("pool", wz[:], 1.0)
        st0 = sps.next()
        for i in range(NW):
            fw.mm(st0[:, 0:512], wz[:, 0:128], wz[:, 512:1024], sig=(i == NW - 1))

    def do_pv(item):
        (qb, g, kt, pt) = item
        (kt_ap, q_ap, vs, tag) = g
        if kt == 0:
            state["o"] = [ops_.next(), ops_.next()]
        o = state["o"]
        for j in range(2):
            fw.mm(o[j][0:65, :], V[:, kt, vs[j], :], pt[:, j * 512:(j + 1) * 512], start=(kt == 0), stop=(kt == nkt - 1))
        if kt == nkt - 1:
            osb = [osbs.next(), osbs.next()]
            for j in range(2):
                fw.copy("dve", osb[j][:], o[j][0:65, :])
            emit(qb, tag, osb)

    for (qb, g, kt) in iters:
        (kt_ap, q_ap, vs, tag) = g
        st = sps.next()
        for j in range(2):
            fw.mm(st[:, j * 512:(j + 1) * 512], kt_ap(j, kt), q_ap(j, qb), sig=(j == 1))
        pt = pts.next()
        if negM is None:
            fw.act(pt[:], st[:], AF.Exp, scale=scale)
        else:
            fw.act(pt[:], st[:], AF.Exp, scale=scale, bias=negM[:, 0:1])
        pend.append((qb, g, kt, pt))
        if len(pend) > LOOK:
            do_pv(pend.pop(0))
    while pend:
        do_pv(pend.pop(0))


def bcast_recip(fw, K, osb, recs, hls, bcs, mulv):
    rec = recs.next()
    fw.act(rec[64:65, :], osb[64:65, :], AF.Ln)
    fw.act(rec[64:65, :], rec[64:65, :], AF.Exp, scale=-1.0)
    if mulv is not None:
        fw.ts("dve", rec[64:65, :], rec[64:65, :], mulv[64:65, 0:1], None, ALU.mult)
    hl = hls.next()
    fw.copy("dve", hl[64:65, 0, :], rec[64:65, :])
    fw.tt("dve", hl[64:65, 1, :], rec[64:65, :], hl[64:65, 0, :], ALU.subtract)
    bc = bcs.next()
    fw.mm(bc[0:64, :], K.ones_bf[64:65, 0:64], hl[64:65, 0, :], start=True, stop=False, sig=False)
    fw.mm(bc[0:64, :], K.ones_bf[64:65, 0:64], hl[64:65, 1, :], start=False, stop=True)
    return bc


def phase_attnA(fw, C, K, l, si):
    T_ = C.seqs[si]
    d = C.d
    ph = fw.phase()
    QT = ph.sb([128, 2, T_], BF16, "QT")
    KT = ph.sb([128, T_], BF16, "KT")
    V = ph.sb([128, T_ // 128, 2, 65], BF16, "V")
    for p in range(2):
        fw.ld(QT[0:64, p, :], C.QA[:, p, 0:T_])
        fw.ld(QT[64:128, p, :], C.QA[:, p + 2, 0:T_])
    for h in range(2):
        fw.ld(KT[64 * h:64 * h + 64, :], C.KA[:, h, 0:T_])
    fw.memset("pool", V[:, :, :, 64:65], 1.0)
    for h in range(2):
        fw.ld(V[:, :, h, 0:64], C.VA[0:T_, h * 64:(h + 1) * 64].rearrange("(n p) c -> p n c", p=128))
    g2 = ph.sb([128, 2, 64], F32, "g2")
    fw.ld(g2[:, 0, :], d["a_q_gain"][l].partition_broadcast(128))
    fw.ld(g2[:, 1, :], d["a_k_gain"][l].partition_broadcast(128))
    fw.tt("dve", g2[:], g2[:], g2[:], ALU.mult)
    mx = ph.sb([128, 2], F32, "mx")
    fw.reduce(mx[:], g2[:], ALU.max)
    negM = ph.sb([128, 1], F32, "negM")
    fw.tt("dve", negM[:], mx[:, 0:1], mx[:, 1:2], ALU.mult)
    fw.act(negM[:], negM[:], AF.Sqrt, scale=64.0)
    fw.ts("dve", negM[:], negM[:], -1.0, None, ALU.mult)
    recs = ph.sbpool(2, [65, 512], F32, "rec")
    hls = ph.sbpool(2, [65, 2, 512], BF16, "hl")
    bcs = ph.pspool(2, [128, 512], F32, "bc")
    outs = ph.sbpool(3, [64, 512], BF16, "ob")

    def emit(qb, p, osbs_):
        for j in range(2):
            h = p + 2 * j
            osb = osbs_[j]
            bc = bcast_recip(fw, K, osb, recs, hls, bcs, None)
            ob = outs.next()
            fw.tt("dve", ob[:], osb[0:64, :], bc[0:64, :], ALU.mult)
            fw.st(C.OT[h, :, qb * 512:(qb + 1) * 512], ob[:])

    groups = []
    for p in range(2):
        groups.append((lambda j, kt: KT[64 * j:64 * j + 64, kt * 128:(kt + 1) * 128],
                       (lambda p_: (lambda j, qb: QT[64 * j:64 * j + 64, p_, qb * 512:(qb + 1) * 512]))(p),
                       (0, 1), p))
    attn_core2(fw, ph, T_, groups, V, negM, 0.125, emit, bcs)
    ph.close()


def phase_attnB(fw, C, K, l, si):
    T_ = C.seqs[si]
    d = C.d
    lam_init = 0.8 - 0.6 * math.exp(-0.3 * l)
    ph = fw.phase()
    QT = ph.sb([128, 2, 2, T_], BF16, "QT")
    KT = ph.sb([128, 2, T_], BF16, "KT")
    V = ph.sb([128, T_ // 128, 4, 65], BF16, "V")
    for hh in range(2):
        fw.memset("pool", QT[64 * hh + 32:64 * hh + 64, :, 0, :], 0.0)
        fw.memset("pool", QT[64 * hh:64 * hh + 32, :, 1, :], 0.0)
    for p in range(2):
        for hh in range(2):
            h = p + 2 * hh
            fw.ld(QT[64 * hh:64 * hh + 32, p, 0, :], C.QB[0:32, h, 0:T_])
            fw.ld(QT[64 * hh + 32:64 * hh + 64, p, 1, :], C.QB[32:64, h, 0:T_])
            fw.ld(KT[64 * hh:64 * hh + 64, p, :], C.KB[:, h, 0:T_])
    fw.memset("pool", V[:, :, :, 64:65], 1.0)
    for h in range(4):
        fw.ld(V[:, :, h, 0:64], C.VB[0:T_, h * 64:(h + 1) * 64].rearrange("(n p) c -> p n c", p=128))
    lp = ph.sb([128, 4, 32], F32, "lp")
    fw.ld(lp[:], d["b_lambda"][l].partition_broadcast(128))
    pr = ph.sb([128, 2, 32], F32, "pr")
    fw.tt("dve", pr[:, 0, :], lp[:, 0, :], lp[:, 1, :], ALU.mult)
    fw.tt("dve", pr[:, 1, :], lp[:, 2, :], lp[:, 3, :], ALU.mult)
    sm = ph.sb([128, 2], F32, "sm")
    fw.reduce(sm[:], pr[:], ALU.add)
    fw.act(sm[:], sm[:], AF.Exp)
    nlam = ph.sb([128, 1], F32, "nlam")
    fw.tt("dve", nlam[:], sm[:, 1:2], sm[:, 0:1], ALU.subtract)
    fw.ts("dve", nlam[:], nlam[:], -lam_init, None, ALU.add)
    gsub = ph.sb([64, 1], F32, "gsub")
    fw.ld(gsub[:], d["b_subln_gain"][l].rearrange("(p o) -> p o", o=1))
    fw.ts("dve", gsub[:], gsub[:], 1.0 - lam_init, None, ALU.mult)
    recs = ph.sbpool(2, [65, 512], F32, "rec")
    hls = ph.sbpool(2, [65, 2, 512], BF16, "hl")
    bcs = ph.pspool(2, [128, 512], F32, "bc")
    o1s = ph.sbpool(6, [64, 512], F32, "o1")
    rss = ph.sbpool(2, [64, 512], F32, "rsb")
    sqs = ph.sbpool(2, [64, 512], BF16, "sq")
    outs = ph.sbpool(3, [64, 512], BF16, "ob")

    pend_o = {}

    def emit(qb, tag, osbs_):
        (p, s_) = tag
        for hh in range(2):
            h = p + 2 * hh
            osb = osbs_[hh]
            bc = bcast_recip(fw, K, osb, recs, hls, bcs, nlam if s_ == 1 else None)
            o_ = o1s.next()
            fw.tt("dve", o_[:], osb[0:64, :], bc[0:64, :], ALU.mult)
            if s_ == 0:
                pend_o[h] = o_
            else:
                finish(qb, h, pend_o.pop(h), o_)

    def finish(qb, h, o1, o2):
        fw.tt("pool", o1[:], o1[:], o2[:], ALU.add)
        sq = sqs.next()
        fw.act(sq[:], o1[:], AF.Square)
        ms = bcs.next()
        fw.mm(ms[0:64, :], K.ones_bf[0:64, 0:64], sq[:])
        rs = rss.next()
        fw.ts("dve", rs[:], ms[0:64, :], 1.0 / 64, EPS, ALU.mult, ALU.add)
        fw.act(rs[:], rs[:], AF.Ln)
        fw.act(rs[:], rs[:], AF.Exp, scale=-0.5)
        ob = outs.next()
        fw.stt(ob[:], o1[:], gsub[:, 0:1], rs[:], ALU.mult, ALU.mult)
        fw.st(C.OT[4 + h, :, qb * 512:(qb + 1) * 512], ob[:])

    groups = []
    for p in range(2):
        for s_ in range(2):
            groups.append(((lambda p_: (lambda j, kt: KT[64 * j:64 * j + 64, p_, kt * 128:(kt + 1) * 128]))(p),
                           (lambda p_, s__: (lambda j, qb: QT[64 * j:64 * j + 64, p_, s__, qb * 512:(qb + 1) * 512]))(p, s_),
                           (p, p + 2), (p, s_)))
    attn_core2(fw, ph, T_, groups, V, None, 32 ** -0.5, emit, bcs)
    ph.close()


def phase_ret(fw, C, K, l, si):
    T_ = C.seqs[si]
    d = C.d
    n = T_ // 128
    ph = fw.phase()
    QT = ph.sb([64, 4, T_], BF16, "QT")
    KT = ph.sb([64, 4, T_], BF16, "KT")
    G = ph.sb([64, 4, T_], BF16, "G")
    V = ph.sb([128, n, 256], BF16, "V")
    for h in range(4):
        fw.ld(QT[:, h, :], C.QC[:, h, 0:T_])
        fw.ld(KT[:, h, :], C.KC[:, h, 0:T_])
        fw.ld(G[:, h, :], C.GC[:, h, 0:T_])
    fw.ld(V[:], C.VC[0:T_, :].rearrange("(n p) c -> p n c", p=128))
    mask = ph.sb([128, 4, 128], F32, "mask")
    fw.ld(mask[:], d["c_retmask"])
    qdec = ph.sb([64, 2, 4, 128], F32, "qdec")
    fw.ld(qdec[:], d["c_qdec"])
    kdec = ph.sb([128, 2, 4], F32, "kdec")
    fw.ld(kdec[:], d["c_kdec"])
    gng = ph.sb([64, 4], F32, "gng")
    fw.ldn(gng[:], d["c_gn_gain"][l].rearrange("(h p) -> p h", p=64))
    Sf = ph.sb([64, n, 4, 64], BF16, "Sf")
    Sb = ph.sb([64, n, 4, 64], BF16, "Sb")
    cur = [ph.sb([64, 4, 64], F32, "curf"), ph.sb([64, 4, 64], F32, "curb")]
    ktp = ph.pspool(2, [128, 4, 64], BF16, "ktp")
    kds = ph.sbpool(4, [128, 4, 64], BF16, "kds")
    kvp = ph.pspool(2, [128, 512], F32, "kvp")
    gamt = ph.sb([64, 2, 4], F32, "gamt")
    for dr in range(2):
        for h in range(4):
            g_ = (RET_G[h] if dr == 0 else RET_G[3 - h]) ** 128
            fw.memset("pool", gamt[:, dr, h:h + 1], g_)
    tmpc = [ph.sbpool(2, [64, 4, 64], F32, "tmpc"), ph.sbpool(2, [64, 4, 64], F32, "tmpc")]

    def state_gen(dr):
        fw.memset("dve", cur[dr][:], 0.0)
        order = range(n) if dr == 0 else range(n - 1, -1, -1)
        Sx = Sf if dr == 0 else Sb
        for c in order:
            fw.copy("act", Sx[:, c, :, :], cur[dr][:])
            if (dr == 0 and c == n - 1) or (dr == 1 and c == 0):
                break
            tp = ktp.next()
            for h in range(4):
                fw.tr(tp[:, h, :], KT[:, h, c * 128:(c + 1) * 128], K.ident[0:64, 0:64], sig=(h == 3))
            kd = kds.next()
            fw.tt("dve", kd[:], tp[:], kdec.bc((slice(None), dr, slice(None)), 2, [128, 4, 64]), ALU.mult)
            tmp = tmpc[dr].next()
            fw.tt("pool", tmp[:], cur[dr][:], gamt.bc((slice(None), dr, slice(None)), 2, [64, 4, 64]), ALU.mult)
            yield
            kv = kvp.next()
            for h in range(4):
                fw.mm(kv[0:64, h * 64:(h + 1) * 64], kd[:, h, :], V[:, c, h * 64:(h + 1) * 64], sig=(h == 3))
            fw.tt("dve", cur[dr][:], tmp[:], View(kv, kv.ap[0:64, 0:256].rearrange("p (h e) -> p h e", h=4)), ALU.add)
            yield

    run_pipe([state_gen(0), state_gen(1)], 2)
    inp = ph.pspool(2, [128, 512], F32, "inp")
    ins_ = ph.sbpool(5, [128, 4, 128], BF16, "ins")
    qds = ph.sbpool(5, [64, 2, 4, 128], BF16, "qds")
    otp = ph.pspool(1, [128, 512], F32, "otp")
    osb = ph.sbpool(5, [64, 4, 128], F32, "osb")
    sqs = ph.sbpool(5, [64, 4, 128], BF16, "sq")
    msp = ph.pspool(1, [128, 512], F32, "msp")
    rss = ph.sbpool(2, [64, 4, 128], F32, "rs")
    obs = ph.sbpool(3, [64, 4, 128], BF16, "ob")

    def out_gen(c):
        cs = slice(c * 128, (c + 1) * 128)
        ip = inp.next()
        for h in range(4):
            fw.mm(ip[:, h * 128:(h + 1) * 128], KT[:, h, cs], QT[:, h, cs], sig=(h == 3))
        it = ins_.next()
        fw.tt("dve", it[:], View(ip, ip.ap[:, :].rearrange("p (h i) -> p h i", h=4)), mask[:], ALU.mult)
        qd = qds.next()
        for dr in range(2):
            fw.tt("pool", qd[:, dr, :, :], QT[:, :, cs], qdec[:, dr, :, :], ALU.mult)
        yield
        op_ = otp.next()
        for h in range(4):
            o_h = op_[0:64, h * 128:(h + 1) * 128]
            fw.mm(o_h, V[:, c, h * 64:(h + 1) * 64], it[:, h, :], start=True, stop=False, sig=False)
            fw.mm(o_h, Sf[:, c, h, :], qd[:, 0, h, :], start=False, stop=False, sig=False)
            fw.mm(o_h, Sb[:, c, h, :], qd[:, 1, h, :], start=False, stop=True, sig=(h == 3))
        o = osb.next()
        fw.ts("dve", o[:], View(op_, op_.ap[0:64, :].rearrange("p (h i) -> p h i", h=4)), 0.125, None, ALU.mult)
        sq = sqs.next()
        fw.act(sq[:], o[:], AF.Square)
        yield
        ms = msp.next()
        fw.mm(ms[0:64, :], K.ones_bf[0:64, 0:64], View(sq, sq.ap[:].rearrange("p h i -> p (h i)")))
        rs = rss.next()
        fw.ts("dve", View(rs, rs.ap[:].rearrange("p h i -> p (h i)")), ms[0:64, :], 1.0 / 64, EPS, ALU.mult, ALU.add)
        fw.act(rs[:], rs[:], AF.Ln)
        fw.act(rs[:], rs[:], AF.Exp, scale=-0.5)
        fw.tt("dve", o[:], o[:], rs[:], ALU.mult)
        fw.tt("pool", o[:], o[:], gng.bc((slice(None), slice(None)), 2, [64, 4, 128]), ALU.mult)
        ob = obs.next()
        fw.tt("dve", ob[:], o[:], G[:, :, cs], ALU.mult)
        for h in range(4):
            fw.st(C.OT[8 + h, :, cs], ob[:, h, :])

    run_pipe([out_gen(c) for c in range(n)], 3)
    ph.close()


def phase_outproj(fw, C, K, l, si):
    T_ = C.seqs[si]
    d = C.d
    src = C.xin[si] if l == 0 else C.y[si]
    ph = fw.phase()
    W = ph.sb([128, 8, 1024], BF16, "Wo")
    for k in range(8):
        fw.ld(W[:, k, :], d["w_out"][l, k * 128:(k + 1) * 128, :], q="pool")
    gain = ph.sb([128, 1024], F32, "gain")
    fw.ld(gain[:], d["norm_mix_post"][l].partition_broadcast(128))
    oTs = ph.sbpool(2, [128, 8, 512], BF16, "oT")
    xts = ph.sbpool(4, [128, 1024], F32, "xt")
    mps = ph.pspool(8, [128, 512], F32, "mp")
    tmps = ph.sbpool(4, [128, 1024], F32, "tmp")
    sss = ph.sbpool(4, [128, 2], F32, "ss")
    rstds = ph.sbpool(4, [128, 1], F32, "rstd")

    def sub_gen(oT, j, r0):
        xt = xts.next()
        fw.ld(xt[:], src[r0:r0 + 128, :])
        m = [mps.next(), mps.next()]
        for half in range(2):
            for k in range(8):
                fw.mm(m[half][:, :], oT[:, k, j * 128:(j + 1) * 128], W[:, k, half * 512:(half + 1) * 512], start=(k == 0), stop=(k == 7))
        tmp = tmps.next()
        ss = sss.next()
        for half in range(2):
            fw.act(tmp[:, half * 512:(half + 1) * 512], m[half][:, :], AF.Square, accum=ss[:, half:half + 1])
        yield
        rstd = rstds.next()
        fw.tt("dve", rstd[:], ss[:, 0:1], ss[:, 1:2], ALU.add)
        fw.ts("dve", rstd[:], rstd[:], 1.0 / D_MODEL, EPS, ALU.mult, ALU.add)
        fw.act(rstd[:], rstd[:], AF.Ln)
        fw.act(rstd[:], rstd[:], AF.Exp, scale=-0.5)
        yield
        for half in range(2):
            hs = slice(half * 512, (half + 1) * 512)
            fw.stt(tmp[:, hs], m[half][:, :], rstd[:, 0:1], gain[:, hs], ALU.mult, ALU.mult)
        fw.tt("pool", tmp[:], tmp[:], xt[:], ALU.add)
        fw.st(C.y[si][r0:r0 + 128, :], tmp[:])

    gens = []
    for b in range(T_ // 512):
        t0 = b * 512
        oT = oTs.next()

        def ldgen(oT=oT, t0=t0):
            for k in range(8):
                fw.ld(oT[0:64, k, :], C.OT[2 * k, :, t0:t0 + 512])
                fw.ld(oT[64:128, k, :], C.OT[2 * k + 1, :, t0:t0 + 512])
            return
            yield
        gens.append(("ld", oT, t0))
        for j in range(4):
            gens.append(("sub", oT, j, t0 + j * 128))

    def all_gens():
        for g in gens:
            if g[0] == "ld":
                _, oT, t0 = g
                for k in range(8):
                    fw.ld(oT[0:64, k, :], C.OT[2 * k, :, t0:t0 + 512])
                    fw.ld(oT[64:128, k, :], C.OT[2 * k + 1, :, t0:t0 + 512])
            else:
                yield sub_gen(g[1], g[2], g[3])

    class LazyList(list):
        pass
    pending = all_gens()
    active = []
    done = False
    while not done or active:
        while not done and len(active) < 3:
            try:
                active.append(next(pending))
            except StopIteration:
                done = True
        for g in list(active):
            try:
                next(g)
            except StopIteration:
                active.remove(g)
    ph.close()


def phase_ffn(fw, C, K, l, si):
    T_ = C.seqs[si]
    d = C.d
    TB = 1024
    ph = fw.phase()
    gain = ph.sb([128, 1024], F32, "gain")
    fw.ld(gain[:], d["norm_ffn_pre"][l].partition_broadcast(128))
    gpost = ph.sb([128, 1024], F32, "gpost")
    fw.ld(gpost[:], d["norm_ffn_post"][l].partition_broadcast(128))
    P = Ctx()
    P.xt = ph.sbpool(2, [128, 1024], F32, "xt")
    P.hb = ph.sbpool(2, [128, 1024], BF16, "hb")
    P.tmp = ph.sbpool(2, [128, 1024], F32, "tmp")
    P.ss = ph.sbpool(2, [128, 1], F32, "ss")
    P.rstd = ph.sbpool(2, [128, 1], F32, "rstd")
    P.tp = ph.pspool(1, [128, 8, 128], BF16, "tp")
    hTs = ph.sbpool(2, [128, 8, TB], BF16, "hT")
    act = ph.sb([128, NFF, TB], BF16, "act")
    wgs = ph.sbpool(3, [128, 8, 128], BF16, "wg")
    wus = ph.sbpool(3, [128, 8, 128], BF16, "wu")
    Wd = ph.sb([128, NFF, 1024], BF16, "Wd")
    for f in range(NFF):
        fw.ld(Wd[:, f, :], d["ffn_w_down"][l, f * 128:(f + 1) * 128, :], q="pool")
    gps = ph.pspool(2, [128, 512], F32, "gp")
    ups = ph.pspool(2, [128, 512], F32, "up")
    dps = ph.pspool(2, [128, 512], F32, "dp")
    sgs = ph.sbpool(2, [128, 512], F32, "sg")
    sss = ph.sbpool(2, [128, 2], F32, "ss2")
    wg_d = d["ffn_w_gate"][l].rearrange("(k p) n -> p k n", p=128)
    wu_d = d["ffn_w_up"][l].rearrange("(k p) n -> p k n", p=128)
    def hT_gen(b, hT_):
        t0_ = b * TB
        for j in range(TB // 128):
            make_hT(fw, K, C.y[si][t0_ + j * 128:t0_ + (j + 1) * 128, :], gain, hT_, j, P)
            yield

    nblk = T_ // TB
    hT_next = hTs.next()
    for _ in hT_gen(0, hT_next):
        pass
    for b in range(nblk):
        t0 = b * TB
        hT = hT_next
        side = None
        if b + 1 < nblk:
            hT_next = hTs.next()
            side = hT_gen(b + 1, hT_next)
        for f in range(NFF):
            if side is not None and f % 2 == 1:
                try:
                    next(side)
                except StopIteration:
                    side = None
            wg = wgs.next()
            fw.ld(wg[:], wg_d[:, :, f * 128:(f + 1) * 128], q="pool")
            wu = wus.next()
            fw.ld(wu[:], wu_d[:, :, f * 128:(f + 1) * 128], q="pool")
            for tb in range(TB // 512):
                ts_ = slice(tb * 512, (tb + 1) * 512)
                gp = gps.next()
                up = ups.next()
                for k in range(8):
                    fw.mm(gp[:, :], wg[:, k, :], hT[:, k, ts_], start=(k == 0), stop=(k == 7))
                for k in range(8):
                    fw.mm(up[:, :], wu[:, k, :], hT[:, k, ts_], start=(k == 0), stop=(k == 7))
                sg = sgs.next()
                fw.act(sg[:], gp[:, :], AF.Silu)
                fw.tt("dve", act[:, f, ts_], sg[:], up[:, :], ALU.mult)
        ph_all = Pool(dps.items + gps.items + ups.items)
        if side is not None:
            for _ in side:
                pass
        for g0 in range(0, TB // 128, 2):
            subs = list(range(g0, min(g0 + 2, TB // 128)))
            banks = {(j, half): ph_all.next() for j in subs for half in range(2)}
            for f in range(NFF):
                for j in subs:
                    for half in range(2):
                        fw.mm(banks[(j, half)][:, :], act[:, f, j * 128:(j + 1) * 128], Wd[:, f, half * 512:(half + 1) * 512],
                              start=(f == 0), stop=(f == NFF - 1))
            for j in subs:
                r0 = t0 + j * 128
                xt = P.xt.next()
                fw.ld(xt[:], C.y[si][r0:r0 + 128, :])
                tmp = P.tmp.next()
                ss = sss.next()
                for half in range(2):
                    fw.act(tmp[:, half * 512:(half + 1) * 512], banks[(j, half)][:, :], AF.Square, accum=ss[:, half:half + 1])
                rstd = P.rstd.next()
                fw.tt("dve", rstd[:], ss[:, 0:1], ss[:, 1:2], ALU.add)
                fw.act(rstd[:], rstd[:], AF.Sqrt, bias=EPS, scale=1.0 / D_MODEL)
                fw.recip(rstd[:], rstd[:])
                for half in range(2):
                    hs = slice(half * 512, (half + 1) * 512)
                    fw.stt(tmp[:, hs], banks[(j, half)][:, :], rstd[:, 0:1], gpost[:, hs], ALU.mult, ALU.mult)
                fw.tt("pool", tmp[:], tmp[:], xt[:], ALU.add)
                fw.st(C.y[si][r0:r0 + 128, :], tmp[:])
    ph.close()


def phase_rwkv(fw, C, K, l, si):
    T_ = C.seqs[si]
    d = C.d
    n = T_ // 128
    CW = math.exp(-0.5)
    ph = fw.phase()
    mu = ph.sb([64, 3, 15], F32, "mu")
    fw.ldn(mu[:, 1, :], d["d_mu_prev"][l, 0:960].rearrange("(j p) -> p j", p=64))
    fw.ldn(mu[:, 2, :], d["d_mu_next"][l, 0:960].rearrange("(j p) -> p j", p=64))
    fw.tt("dve", mu[:, 0, :], mu[:, 1, :], mu[:, 2, :], ALU.add)
    fw.ts("dve", mu[:, 0, :], mu[:, 0, :], -1.0, 1.0, ALU.mult, ALU.add)
    mug = ph.sb([128, 3], F32, "mug")
    fw.ld(mug[:, 1:2], d["d_mu_prev"][l, 960:1088].rearrange("(p o) -> p o", o=1))
    fw.ld(mug[:, 2:3], d["d_mu_next"][l, 960:1088].rearrange("(p o) -> p o", o=1))
    fw.tt("dve", mug[:, 0:1], mug[:, 1:2], mug[:, 2:3], ALU.add)
    fw.ts("dve", mug[:, 0:1], mug[:, 0:1], -1.0, 1.0, ALU.mult, ALU.add)
    prm = ph.sb([64, 9, 4], F32, "prm")
    fw.ldn(prm[:, 0, :], d["d_w0"][l, 0].rearrange("(h p) -> p h", p=64))
    fw.ldn(prm[:, 1, :], d["d_w0"][l, 1].rearrange("(h p) -> p h", p=64))
    fw.ldn(prm[:, 2, :], d["d_a0"][l].rearrange("(h p) -> p h", p=64))
    fw.ldn(prm[:, 3, :], d["d_k_k"][l].rearrange("(h p) -> p h", p=64))
    fw.ldn(prm[:, 4, :], d["d_k_a"][l].rearrange("(h p) -> p h", p=64))
    fw.ldn(prm[:, 6, :], d["d_r_k"][l].rearrange("h p -> p h"))
    fw.ldn(prm[:, 7, :], d["d_gn_w"][l].rearrange("(h p) -> p h", p=64))
    fw.ldn(prm[:, 8, :], d["d_gn_b"][l].rearrange("(h p) -> p h", p=64))
    fw.ts("dve", prm[:, 5, :], prm[:, 4, :], -1.0, 1.0, ALU.mult, ALU.add)
    wup = ph.sb([64, 2, 256], BF16, "wup")
    fw.ld(wup[:, 0, :], d["d_w_up"][l, 0], q="pool")
    fw.ld(wup[:, 1, :], d["d_w_up"][l, 1], q="pool")
    aup = ph.sb([64, 256], BF16, "aup")
    fw.ld(aup[:], d["d_a_up"][l], q="pool")
    gup = ph.sb([128, 256], BF16, "gup")
    fw.ld(gup[:], d["d_g_up"][l], q="pool")
    dmask = ph.sb([128, 2, 5, 128], BF16, "dmask")
    fw.ld(dmask[:], d["c_dmask"], q="pool")
    seg = ph.sb([64, 512], F32, "seg")
    fw.ld(seg[:], d["c_seg"])

    def pbc(i):
        return prm.bc((slice(None), i, slice(None)), 2, [64, 4, 128])

    ytok = [T(None, "ytok%d" % c) for c in range(n)]
    H = [ph.sb([64, 4, 64], F32, "Hf"), ph.sb([64, 4, 64], F32, "Hb")]
    Hb = [ph.sb([64, 4, 64], BF16, "Hfb"), ph.sb([64, 4, 64], BF16, "Hbb")]
    for dr in range(2):
        fw.memset("dve", H[dr][:], 0.0)
        fw.memset("pool", Hb[dr][:], 0.0)
    PST = ph.pspool(1, [128, 4, 4, 64], BF16, "pst")
    PP = []
    for dr in range(2):
        Q = Ctx()
        Q.PS = ph.pspool(4 if dr == 0 else 3, [128, 512], F32, "ps")
        Q.zts = ph.sbpool(1, [64, 15, 130], F32, "zt")
        Q.zgs = ph.sbpool(2, [128, 130], F32, "zg")
        Q.us = ph.sbpool(1, [64, 15, 128], F32, "u")
        Q.u2s = ph.sbpool(1, [64, 15, 128], F32, "u2")
        Q.ugs = ph.sbpool(1, [128, 128], F32, "ug")
        Q.F4 = ph.sbpool(10, [64, 4, 128], F32, "f4")
        Q.FL = ph.sbpool(3, [64, 4, 128], F32, "fl")
        Q.B4 = ph.sbpool(12, [64, 4, 128], BF16, "b4")
        Q.M4 = ph.sbpool(9, [128, 4, 128], BF16, "m4")
        Q.MK = ph.sbpool(4, [128, 4, 128], BF16, "mk")
        Q.TM = ph.sbpool(2, [128, 4, 4, 64], BF16, "tm")
        Q.SM = ph.sbpool(4, [128, 128], BF16, "sm")
        Q.S4 = ph.sbpool(6, [64, 4], F32, "s4")
        Q.UV = ph.sbpool(2, [128, 4, 64], F32, "uv")
        Q.UB = ph.sbpool(3, [128, 4, 64], BF16, "ub")
        PP.append(Q)

    def v3(t, npart=64):
        return View(t, t.ap[0:npart, :].rearrange("p (h i) -> p h i", h=4))

    def v3s(t):
        return View(t, t.ap[:, 0:256].rearrange("p (h i) -> p h i", h=4))

    def flat(t):
        return View(t, t.ap[:].rearrange("p h i -> p (h i)"))

    def unit(c, dr, first):
        Q = PP[dr]
        PS, F4, FL, B4, M4, MK, SM, S4 = Q.PS, Q.F4, Q.FL, Q.B4, Q.M4, Q.MK, Q.SM, Q.S4
        need_post = not first
        c0 = c * 128
        zt = Q.zts.next()
        zg = Q.zgs.next()
        lo = 1 if c == 0 else 0
        hi = 129 if c == n - 1 else 130
        if c == 0:
            fw.memset("pool", zt[:, :, 0:1], 0.0)
            fw.memset("pool", zg[:, 0:1], 0.0)
        if c == n - 1:
            fw.memset("pool", zt[:, :, 129:130], 0.0)
            fw.memset("pool", zg[:, 129:130], 0.0)
        fw.ld(zt[:, :, lo:hi], C.ZD[:, :, c0 - 1 + lo:c0 - 1 + hi])
        fw.ld(zg[:, lo:hi], C.ZG[:, c0 - 1 + lo:c0 - 1 + hi])
        if need_post:
            yprev = FL.next()
            fw.ld(yprev[:], View(ytok[c], C.YD[c]))
        yield
        u = Q.us.next()
        u2 = Q.u2s.next()
        fw.tt("dve", u[:], zt[:, :, 1:129], mu.bc((slice(None), 0, slice(None)), 2, [64, 15, 128]), ALU.mult)
        fw.tt("pool", u2[:], zt[:, :, 0:128], mu.bc((slice(None), 1, slice(None)), 2, [64, 15, 128]), ALU.mult)
        yield
        fw.tt("dve", u[:], u[:], u2[:], ALU.add)
        fw.tt("pool", u2[:], zt[:, :, 2:130], mu.bc((slice(None), 2, slice(None)), 2, [64, 15, 128]), ALU.mult)
        yield
        fw.tt("dve", u[:], u[:], u2[:], ALU.add)
        ug = Q.ugs.next()
        fw.ts("dve", ug[:], zg[:, 1:129], mug[:, 0:1], None, ALU.mult)
        fw.stt(ug[:], zg[:, 0:128], mug[:, 1:2], ug[:], ALU.mult, ALU.add)
        fw.stt(ug[:], zg[:, 2:130], mug[:, 2:3], ug[:], ALU.mult, ALU.add)
        yield
        tw = SM.next()
        fw.act(tw[0:64, :], u[:, 12 + dr, :], AF.Tanh)
        adb = SM.next()
        fw.copy("act", adb[0:64, :], u[:, 14, :])
        yw = PS.next()
        ya = PS.next()
        for h in range(4):
            fw.mm(yw[0:64, h * 128:(h + 1) * 128], wup[:, dr, h * 64:(h + 1) * 64], tw[0:64, :], sig=(h == 3))
        for h in range(4):
            fw.mm(ya[0:64, h * 128:(h + 1) * 128], aup[:, h * 64:(h + 1) * 64], adb[0:64, :], sig=(h == 3))
        yield
        sg = F4.next()
        fw.tt("dve", sg[:], v3(yw), pbc(dr), ALU.add)
        fw.act(sg[:], sg[:], AF.Tanh, scale=0.5)
        fw.ts("pool", sg[:], sg[:], 0.5, 0.5, ALU.mult, ALU.add)
        a = F4.next()
        fw.tt("dve", a[:], v3(ya), pbc(2), ALU.add)
        fw.act(a[:], a[:], AF.Tanh, scale=0.5)
        fw.ts("pool", a[:], a[:], 0.5, 0.5, ALU.mult, ALU.add)
        yield
        if need_post:
            sgd = SM.next()
            fw.act(sgd[:], ug[:], AF.Tanh, scale=0.5)
            fw.ts("pool", sgd[:], sgd[:], 0.5, 0.5, ALU.mult, ALU.add)
            gps_ = PS.next()
            for h in range(4):
                fw.mm(gps_[0:64, h * 128:(h + 1) * 128], gup[:, h * 64:(h + 1) * 64], sgd[:], sig=(h == 3))
            g_ = FL.next()
            fw.copy("act", g_[:], v3(gps_))
            yield
        r = u[:, 0:4, :]
        k = u[:, 4:8, :]
        v = u[:, 8:12, :]
        kk = F4.next()
        fw.tt("dve", kk[:], k, pbc(3), ALU.mult)
        sq = B4.next()
        fw.tt("pool", sq[:], kk[:], kk[:], ALU.mult)
        ssp = PS.next()
        fw.mm(ssp[0:64, :], K.ones_bf[0:64, 0:64], flat(sq))
        yield
        nr = F4.next()
        fw.ts("dve", nr[:], v3(ssp), 1e-24, None, ALU.max)
        fw.act(nr[:], nr[:], AF.Ln)
        fw.act(nr[:], nr[:], AF.Exp, scale=-0.5)
        kap = F4.next()
        fw.tt("dve", kap[:], kk[:], nr[:], ALU.mult)
        yield
        tk = F4.next()
        fw.tt("pool", tk[:], a[:], pbc(4), ALU.mult)
        fw.tt("pool", tk[:], tk[:], pbc(5), ALU.add)
        kh = FL.next()
        fw.tt("dve", kh[:], k, tk[:], ALU.mult)
        bb = F4.next()
        fw.tt("pool", bb[:], kap[:], a[:], ALU.mult)
        yield
        Pp = F4.next()
        fw.scan(flat(Pp), seg[:], flat(sg), 0.0, ALU.mult, ALU.add)
        Qp = F4.next()
        fw.tt("pool", Qp[:], Pp[:], sg[:], ALU.subtract)
        nT = S4.next()
        pT = S4.next()
        fw.ts("dve", nT[:], Pp[:, :, 127], -CW, None, ALU.mult)
        fw.ts("dve", pT[:], Pp[:, :, 127], CW, None, ALU.mult)
        gC = S4.next()
        fw.act(gC[:], nT[:], AF.Exp)
        yield
        rt = B4.next()
        kp = B4.next()
        kt = B4.next()
        bt = B4.next()
        Khf = B4.next()
        Bhf = B4.next()
        E1 = F4.next()
        E2 = F4.next()
        if dr == 0:
            fw.act(E1[:], Pp[:], AF.Exp, scale=-CW)
            fw.tt("dve", rt[:], r, E1[:], ALU.mult)
            fw.act(E2[:], Qp[:], AF.Exp, scale=-CW)
            fw.tt("pool", kp[:], kap[:], E2[:], ALU.mult)
            yield
            E3 = F4.next()
            fw.act(E3[:], Pp[:], AF.Exp, scale=CW)
            fw.tt("dve", kt[:], kh[:], E3[:], ALU.mult)
            fw.tt("pool", bt[:], bb[:], E3[:], ALU.mult)
            E4 = F4.next()
            for h in range(4):
                fw.act(E4[:, h, :], Pp[:, h, :], AF.Exp, scale=CW, bias=nT[:, h:h + 1])
        else:
            for h in range(4):
                fw.act(E1[:, h, :], Qp[:, h, :], AF.Exp, scale=CW, bias=nT[:, h:h + 1])
            fw.tt("dve", rt[:], r, E1[:], ALU.mult)
            for h in range(4):
                fw.act(E2[:, h, :], Pp[:, h, :], AF.Exp, scale=CW, bias=nT[:, h:h + 1])
            fw.tt("pool", kp[:], kap[:], E2[:], ALU.mult)
            yield
            E3 = F4.next()
            for h in range(4):
                fw.act(E3[:, h, :], Qp[:, h, :], AF.Exp, scale=-CW, bias=pT[:, h:h + 1])
            fw.tt("dve", kt[:], kh[:], E3[:], ALU.mult)
            fw.tt("pool", bt[:], bb[:], E3[:], ALU.mult)
            E4 = F4.next()
            fw.act(E4[:], Qp[:], AF.Exp, scale=-CW)
        yield
        fw.tt("dve", Khf[:], kh[:], E4[:], ALU.mult)
        fw.stt(Bhf[:], bb[:], -1.0, E4[:], ALU.mult, ALU.mult)
        vb = B4.next()
        fw.copy("act", vb[:], v)
        pst = PST.next()
        for ki, src_ in enumerate((kp, Khf, Bhf, vb)):
            for h in range(4):
                fw.tr(pst[:, ki, h, :], src_[:, h, :], K.ident[0:64, 0:64], sig=(ki == 3 and h == 3))
        tm = Q.TM.next()
        fw.copy("dve", tm[:], pst[:])
        yield
        mats = []
        pairs = [(kp, bt), (bt, kp), (kt, kp), (kt, rt), (bt, rt)]
        for ki, (lt, rt_) in enumerate(pairs):
            p_ = PS.next()
            for h in range(4):
                fw.mm(p_[:, h * 128:(h + 1) * 128], lt[:, h, :], rt_[:, h, :], sig=(h == 3))
            m_ = M4.next() if ki < 2 else MK.next()
            fw.tt("dve", m_[:], v3(p_, 128), dmask.bc((slice(None), dr, ki, slice(None)), 1, [128, 4, 128]), ALU.mult)
            mats.append(m_)
            yield
        X, XT, AkkT, ArkT, ArbT = mats
        PT = M4.next()
        fw.tt("pool", PT[:], XT[:], K.ident.bc((slice(None), slice(None)), 1, [128, 4, 128]), ALU.add)
        for lvl in range(6):
            p1 = PS.next()
            for h in range(4):
                fw.mm(p1[:, h * 128:(h + 1) * 128], XT[:, h, :], X[:, h, :], sig=(h == 3))
            if lvl < 5:
                p2 = PS.next()
                for h in range(4):
                    fw.mm(p2[:, h * 128:(h + 1) * 128], X[:, h, :], XT[:, h, :], sig=(h == 3))
            yield
            X2 = M4.next()
            fw.copy("act", X2[:], v3(p1, 128))
            if lvl < 5:
                XT2 = M4.next()
                fw.copy("dve", XT2[:], v3(p2, 128))
            p3 = PS.next()
            for h in range(4):
                fw.mm(p3[:, h * 128:(h + 1) * 128], X2[:, h, :], PT[:, h, :], sig=(h == 3))
            yield
            PT2 = M4.next()
            fw.tt("dve", PT2[:], v3(p3, 128), PT[:], ALU.add)
            PT = PT2
            X = X2
            if lvl < 5:
                XT = XT2
        pw = PS.next()
        for h in range(4):
            fw.mm(pw[0:64, h * 128:(h + 1) * 128], tm[:, 0, h, :], PT[:, h, :], sig=(h == 3))
        pa = PS.next()
        for h in range(4):
            fw.mm(pa[:, h * 64:(h + 1) * 64], AkkT[:, h, :], tm[:, 3, h, :], sig=(h == 3))
        yield
        WkT = B4.next()
        fw.copy("act", WkT[:], v3(pw))
        AV = Q.UB.next()
        fw.copy("dve", AV[:], v3s(pa))
        pu = PS.next()
        for h in range(4):
            fw.mm(pu[:, h * 64:(h + 1) * 64], PT[:, h, :], AV[:, h, :], sig=(h == 3))
        yield
        Uv = Q.UV.next()
        fw.copy("act", Uv[:], v3s(pu))
        pU = PS.next()
        for h in range(4):
            fw.mm(pU[:, h * 64:(h + 1) * 64], WkT[:, h, :], Hb[dr][:, h, :], sig=(h == 3))
        yield
        U = Q.UB.next()
        fw.tt("dve", U[:], v3s(pU), Uv[:], ALU.add)
        pY = PS.next()
        for h in range(4):
            yh = pY[0:64, h * 128:(h + 1) * 128]
            fw.mm(yh, Hb[dr][:, h, :], rt[:, h, :], start=True, stop=False, sig=False)
            fw.mm(yh, tm[:, 3, h, :], ArkT[:, h, :], start=False, stop=False, sig=False)
            fw.mm(yh, U[:, h, :], ArbT[:, h, :], start=False, stop=True, sig=(h == 3))
        pH = PS.next()
        for h in range(4):
            hh = pH[0:64, h * 64:(h + 1) * 64]
            fw.mm(hh, tm[:, 1, h, :], tm[:, 3, h, :], start=True, stop=False, sig=False)
            fw.mm(hh, tm[:, 2, h, :], U[:, h, :], start=False, stop=True, sig=(h == 3))
        fw.tt("pool", H[dr][:], H[dr][:], gC.bc((slice(None), slice(None)), 2, [64, 4, 64]), ALU.mult)
        yield
        fw.tt("dve", H[dr][:], H[dr][:], View(pH, pH.ap[0:64, 0:256].rearrange("p (h i) -> p h i", h=4)), ALU.add)
        fw.copy("act", Hb[dr][:], H[dr][:])
        if first:
            ysb = F4.next()
            fw.copy("act", ysb[:], v3(pY))
            fw.st(View(ytok[c], C.YD[c]), ysb[:])
            return
        y = F4.next()
        fw.tt("dve", y[:], v3(pY), yprev[:], ALU.add)
        pm_ = PS.next()
        fw.mm(pm_[0:64, :], K.ones_f[0:64, 0:64], flat(y))
        rk = F4.next()
        fw.tt("pool", rk[:], r, kh[:], ALU.mult)
        fw.tt("pool", rk[:], rk[:], pbc(6), ALU.mult)
        yield
        yc = F4.next()
        fw.stt(yc[:], v3(pm_), -1.0 / 64, y[:], ALU.mult, ALU.add)
        sq2 = F4.next()
        fw.act(sq2[:], yc[:], AF.Square)
        pv = PS.next()
        fw.mm(pv[0:64, :], K.ones_f[0:64, 0:64], flat(sq2))
        pr_ = PS.next()
        fw.mm(pr_[0:64, :], K.ones_f[0:64, 0:64], flat(rk))
        yield
        rs = F4.next()
        fw.ts("dve", rs[:], v3(pv), 1.0 / 64, GN_EPS, ALU.mult, ALU.add)
        fw.act(rs[:], rs[:], AF.Ln)
        fw.act(rs[:], rs[:], AF.Exp, scale=-0.5)
        fw.tt("dve", yc[:], yc[:], rs[:], ALU.mult)
        fw.tt("pool", yc[:], yc[:], pbc(7), ALU.mult)
        yield
        fw.tt("pool", yc[:], yc[:], pbc(8), ALU.add)
        t_ = F4.next()
        fw.tt("dve", t_[:], v3(pr_), v, ALU.mult)
        fw.tt("pool", yc[:], yc[:], t_[:], ALU.add)
        ob = B4.next()
        fw.tt("dve", ob[:], yc[:], g_[:], ALU.mult)
        for h in range(4):
            fw.st(C.OT[12 + h, :, c * 128:(c + 1) * 128], ob[:, h, :])

    for i in range(n):
        first = i < n // 2
        gens = [unit(i, 0, first), unit(n - 1 - i, 1, first)]
        alive = [True, True]
        while any(alive):
            for gi in range(2):
                if alive[gi]:
                    try:
                        next(gens[gi])
                    except StopIteration:
                        alive[gi] = False
    ph.close()


def build(seqs=(4096, 2048), depth=2, phases=None, debug=False):
    nc = bass.Bass("TRN2", target_bir_lowering=False)
    C = Ctx()
    C.seqs = list(seqs)
    C.d = {}
    C.xin = [nc.dram_tensor("x%d" % i, [t, D_MODEL], F32, kind="ExternalInput").ap() for i, t in enumerate(seqs)]
    C.y = [nc.dram_tensor("y%d" % i, [t, D_MODEL], F32, kind="ExternalOutput").ap() for i, t in enumerate(seqs)]
    for nm, shp in WEIGHT_SPECS:
        C.d[nm] = nc.dram_tensor(nm, list(shp), F32, kind="ExternalInput").ap()
    for nm, arr in make_consts().items():
        C.d[nm] = nc.dram_tensor(nm, list(arr.shape), F32, kind="ExternalInput").ap()
    TM = max(seqs)
    sk = "ExternalOutput" if debug else "Internal"

    def scratch(nm, shp, dt):
        return nc.dram_tensor(nm, list(shp), dt, kind=sk).ap()
    C.QA = scratch("QA", [64, 4, TM], BF16)
    C.KA = scratch("KA", [64, 2, TM], BF16)
    C.VA = scratch("VA", [TM, 128], BF16)
    C.QB = scratch("QB", [64, 4, TM], BF16)
    C.KB = scratch("KB", [64, 4, TM], BF16)
    C.VB = scratch("VB", [TM, 256], BF16)
    C.QC = scratch("QC", [64, 4, TM], BF16)
    C.KC = scratch("KC", [64, 4, TM], BF16)
    C.GC = scratch("GC", [64, 4, TM], BF16)
    C.VC = scratch("VC", [TM, 256], BF16)
    C.ZD = scratch("ZD", [64, 15, TM], F32)
    C.ZG = scratch("ZG", [128, TM], F32)
    C.OT = scratch("OT", [16, 64, TM], BF16)
    C.YD = scratch("YD", [TM // 128, 64, 4, 128], F32)
    allph = ["inproj", "A", "B", "C", "D", "outproj", "ffn"]
    phases = allph if phases is None else phases
    with ExitStack() as es:
        fw = FW(nc, es)
        ph0 = fw.phase()
        K = load_consts(fw, C, ph0)
        fw.barrier()
        for l in range(depth):
            for si in range(len(seqs)):
                if "inproj" in phases:
                    phase_inproj(fw, C, K, l, si)
                if "A" in phases:
                    phase_attnA(fw, C, K, l, si)
                if "B" in phases:
                    phase_attnB(fw, C, K, l, si)
                if "C" in phases:
                    phase_ret(fw, C, K, l, si)
                if "D" in phases:
                    phase_rwkv(fw, C, K, l, si)
                if "outproj" in phases:
                    phase_outproj(fw, C, K, l, si)
                if "ffn" in phases:
                    phase_ffn(fw, C, K, l, si)
        fw.barrier()
        ph0.es.close()
        C.nins = fw.nins
    return nc, C


_CACHE = {}


def kernel(**inputs):
    xp = np.ascontiguousarray(inputs["x_prompt"], dtype=np.float32)
    xs = np.ascontiguousarray(inputs["x_sample"], dtype=np.float32)
    nc, C = build()
    consts = make_consts()
    in_maps = []
    for i in range(8):
        m = {"x0": xp[i], "x1": xs[i]}
        for nm, _ in WEIGHT_SPECS:
            m[nm] = np.ascontiguousarray(inputs[nm], dtype=np.float32)
        m.update(consts)
        in_maps.append(m)
    res = run_bass_kernel_spmd(nc, in_maps, core_ids=list(range(8)))
    yp = np.stack([r["y0"] for r in res.results], 0).astype(np.float32)
    ys = np.stack([r["y1"] for r in res.results], 0).astype(np.float32)
    return (yp, ys)
```

```python
import math
from contextlib import ExitStack
import numpy as np
import concourse.bass as bass
import concourse.mybir as mybir
from concourse.bass_utils import run_bass_kernel_spmd

F32 = mybir.dt.float32
BF16 = mybir.dt.bfloat16
AF = mybir.ActivationFunctionType
ALU = mybir.AluOpType
AX = mybir.AxisListType

D_MODEL = 1024
IN_COLS = 3392
D_FF = 2816
NFF = D_FF // 128
EPS = 1e-6
GN_EPS = 64e-5
TMAX = 4096


class View:
    __slots__ = ("t", "ap")

    def __init__(self, t, ap):
        self.t = t
        self.ap = ap


class T:
    __slots__ = ("ap", "w", "r", "name")

    def __init__(self, ap, name=""):
        self.ap = ap
        self.w = {}
        self.r = {}
        self.name = name

    def __getitem__(self, idx):
        return View(self, self.ap[idx])

    def bc(self, idx, axis, shape):
        return View(self, self.ap[idx].unsqueeze(axis).to_broadcast(list(shape)))


class Eng:
    def __init__(self, name, handle, sem):
        self.name = name
        self.h = handle
        self.sem = sem
        self.count = 0
        self.seen = {}


class Pool:
    def __init__(self, items):
        self.items = items
        self.i = 0

    def next(self):
        t = self.items[self.i]
        self.i = (self.i + 1) % len(self.items)
        return t


class Phase:
    def __init__(self, fw):
        self.fw = fw
        self.es = ExitStack()
        self.n = 0

    def sb(self, shape, dt, name="t"):
        self.n += 1
        t = self.es.enter_context(self.fw.nc.sbuf_tensor("%s_%d_%d" % (name, self.fw.uid(), self.n), list(shape), dt))
        return T(t, name)

    def ps(self, shape, dt=F32, name="p"):
        self.n += 1
        t = self.es.enter_context(self.fw.nc.psum_tensor("%s_%d_%d" % (name, self.fw.uid(), self.n), list(shape), dt))
        return T(t, name)

    def sbpool(self, n, shape, dt, name="t"):
        return Pool([self.sb(shape, dt, name) for _ in range(n)])

    def pspool(self, n, shape, dt=F32, name="p"):
        return Pool([self.ps(shape, dt, name) for _ in range(n)])

    def close(self):
        self.fw.barrier()
        self.es.close()


class FW:
    NDMA = 32

    def __init__(self, nc, es):
        self.nc = nc
        self.es = es
        self._uid = 0
        self.E = {}
        for nm, h in (("pe", nc.tensor), ("act", nc.scalar), ("dve", nc.vector), ("pool", nc.gpsimd), ("sp", nc.sync)):
            sem = es.enter_context(nc.semaphore("s_" + nm))
            self.E[nm] = Eng(nm, h, sem)
        self.dsem = [es.enter_context(nc.semaphore("d%d" % i)) for i in range(self.NDMA)]
        self.dval = [0] * self.NDMA
        self.dnext = {"sp": 0, "pool": 0, "act": 0}
        self.drange = {"sp": (0, 14), "pool": (14, 28), "act": (28, 32)}
        self.semobj = {}
        for e in self.E.values():
            self.semobj[("e", e.name)] = e.sem
        for i, s in enumerate(self.dsem):
            self.semobj[("d", i)] = s
        self.nins = 0

    def uid(self):
        self._uid += 1
        return self._uid

    def phase(self):
        return Phase(self)

    def _needs(self, reads, writes):
        needs = {}
        for t in reads:
            for k, v in t.w.items():
                if needs.get(k, 0) < v:
                    needs[k] = v
        for t in writes:
            for d in (t.w, t.r):
                for k, v in d.items():
                    if needs.get(k, 0) < v:
                        needs[k] = v
        return needs

    def _emit_waits(self, e, needs, skip_self=False):
        for k, v in needs.items():
            if skip_self and k == ("e", e.name):
                continue
            if e.seen.get(k, 0) >= v:
                continue
            if k[0] == "e":
                assert self.E[k[1]].count >= v, "wait on future event %s %d" % (k, v)
            e.seen[k] = v
            e.h.wait_ge(self.semobj[k], v)
            self.nins += 1

    def op(self, eng, fn, reads=(), writes=(), signal=True):
        e = self.E[eng]
        reads = [v.t for v in reads]
        writes = [v.t for v in writes]
        needs = self._needs(reads, writes)
        self._emit_waits(e, needs, skip_self=(eng == "pe"))
        ins = fn(e.h)
        self.nins += 1
        if signal:
            e.count += 1
            ins.then_inc(e.sem, 1)
            ev = e.count
        else:
            ev = e.count + 1
        key = ("e", eng)
        for t in writes:
            t.w = {key: ev}
            t.r = {}
        for t in reads:
            if t.r.get(key, 0) < ev:
                t.r[key] = ev

    def dma(self, q, out, in_, **kw):
        e = self.E[q]
        reads = [in_.t] if isinstance(in_, View) else []
        writes = [out.t] if isinstance(out, View) else []
        needs = self._needs(reads, writes)
        lo_, hi_ = self.drange[q]
        i = lo_ + self.dnext[q]
        self.dnext[q] = (self.dnext[q] + 1) % (hi_ - lo_)
        key = ("d", i)
        if self.dval[i] > 0 and needs.get(key, 0) < self.dval[i]:
            needs[key] = self.dval[i]
        self._emit_waits(e, needs)
        self.dval[i] += 16
        v = self.dval[i]
        oap = out.ap if isinstance(out, View) else out
        iap = in_.ap if isinstance(in_, View) else in_
        e.h.dma_start(out=oap, in_=iap, **kw).then_inc(self.dsem[i], 16)
        self.nins += 1
        for t in writes:
            t.w = {key: v}
            t.r = {}
        for t in reads:
            t.r[key] = v

    def barrier(self):
        allev = {}
        for e in self.E.values():
            if e.count > 0:
                allev[("e", e.name)] = e.count
        for i in range(self.NDMA):
            if self.dval[i] > 0:
                allev[("d", i)] = self.dval[i]
        for e in self.E.values():
            self._emit_waits(e, dict(allev))

    def ld(self, dst, src, q="sp", **kw):
        self.dma(q, dst, src, **kw)

    def ldn(self, dst, src, q="sp"):
        self.dma(q, dst, src, allow_slow_non_contiguous=True)

    def st(self, dst, src, q="pool"):
        self.dma(q, dst, src)

    def mm(self, out, lhsT, rhs, start=True, stop=True, sig=None):
        if sig is None:
            sig = stop
        self.op("pe", lambda h: h.matmul(out.ap, lhsT=lhsT.ap, rhs=rhs.ap, start=start, stop=stop),
                reads=[lhsT, rhs], writes=[out], signal=sig)

    def tr(self, out, in_, ident, sig=True):
        self.op("pe", lambda h: h.transpose(out=out.ap, in_=in_.ap, identity=ident.ap),
                reads=[in_, ident], writes=[out], signal=sig)

    def act(self, out, in_, func, bias=None, scale=None, accum=None):
        reads = [in_]
        kw = {}
        if bias is not None:
            if isinstance(bias, View):
                reads.append(bias)
                kw["bias"] = bias.ap
            else:
                kw["bias"] = float(bias)
        if scale is not None:
            if isinstance(scale, View):
                reads.append(scale)
                kw["scale"] = scale.ap
            else:
                kw["scale"] = float(scale)
        writes = [out]
        if accum is not None:
            kw["accum_out"] = accum.ap
            writes.append(accum)
        self.op("act", lambda h: h.activation(out=out.ap, in_=in_.ap, func=func, **kw), reads=reads, writes=writes)

    def tt(self, eng, out, in0, in1, op):
        self.op(eng, lambda h: h.tensor_tensor(out=out.ap, in0=in0.ap, in1=in1.ap, op=op), reads=[in0, in1], writes=[out])

    def ts(self, eng, out, in0, s1, s2, op0, op1=None):
        reads = [in0]
        a1 = s1.ap if isinstance(s1, View) else s1
        a2 = s2.ap if isinstance(s2, View) else s2
        if isinstance(s1, View):
            reads.append(s1)
        if isinstance(s2, View):
            reads.append(s2)
        if op1 is None:
            self.op(eng, lambda h: h.tensor_scalar(out=out.ap, in0=in0.ap, scalar1=a1, scalar2=None, op0=op0), reads=reads, writes=[out])
        else:
            self.op(eng, lambda h: h.tensor_scalar(out=out.ap, in0=in0.ap, scalar1=a1, scalar2=a2, op0=op0, op1=op1), reads=reads, writes=[out])

    def stt(self, out, in0, scalar, in1, op0, op1):
        reads = [in0, in1]
        sa = scalar.ap if isinstance(scalar, View) else scalar
        if isinstance(scalar, View):
            reads.append(scalar)
        self.op("dve", lambda h: h.scalar_tensor_tensor(out=out.ap, in0=in0.ap, scalar=sa, in1=in1.ap, op0=op0, op1=op1), reads=reads, writes=[out])

    def copy(self, eng, out, in_):
        if eng == "act":
            self.op("act", lambda h: h.activation(out=out.ap, in_=in_.ap, func=AF.Identity), reads=[in_], writes=[out])
        else:
            self.op(eng, lambda h: h.tensor_copy(out=out.ap, in_=in_.ap), reads=[in_], writes=[out])

    def recip(self, out, in_):
        self.op("dve", lambda h: h.reciprocal(out=out.ap, in_=in_.ap), reads=[in_], writes=[out])

    def memset(self, eng, out, val):
        self.op(eng, lambda h: h.memset(out.ap, val), writes=[out])

    def reduce(self, out, in_, op, axis=AX.X):
        self.op("dve", lambda h: h.tensor_reduce(out=out.ap, in_=in_.ap, axis=axis, op=op), reads=[in_], writes=[out])

    def scan(self, out, d0, d1, init, op0, op1):
        self.op("dve", lambda h: h.tensor_tensor_scan(out=out.ap, data0=d0.ap, data1=d1.ap, initial=init, op0=op0, op1=op1), reads=[d0, d1], writes=[out])


def run_pipe(gens, depth, side=None, side_every=8):
    gens = list(gens)
    active = []
    tick = 0
    while gens or active:
        while gens and len(active) < depth:
            active.append(gens.pop(0))
        for g in list(active):
            try:
                next(g)
            except StopIteration:
                active.remove(g)
        tick += 1
        if side is not None and tick % side_every == 0:
            try:
                next(side)
            except StopIteration:
                side = None
    if side is not None:
        for _ in side:
            pass


def _angles(pos, rot_dim, theta):
    inv = (theta ** (-np.arange(0, rot_dim, 2, dtype=np.float32) / rot_dim)).astype(np.float32)
    return pos.astype(np.float32)[:, None] * inv[None, :]


def make_consts():
    c = {}
    c["c_ident"] = np.eye(128, dtype=np.float32)
    T = TMAX
    pos = np.arange(T)
    rope = np.zeros((64, 6, T), np.float32)
    ar = _angles(pos // 64, 32, 10000.0)
    ac = _angles(pos % 64, 32, 10000.0)
    for d in range(64):
        a = ar if d < 32 else ac
        idx = (d % 32) % 16
        rope[d, 0] = np.cos(a[:, idx])
        rope[d, 1] = np.sin(a[:, idx])
    ab = _angles(pos, 8, 500000.0)
    for d in range(64):
        dd = d % 32
        if dd < 8:
            rope[d, 2] = np.cos(ab[:, dd % 4])
            rope[d, 3] = np.sin(ab[:, dd % 4])
        else:
            rope[d, 2] = 1.0
            rope[d, 3] = 0.0
    acc = _angles(pos, 64, 10000.0)
    for d in range(64):
        rope[d, 4] = np.cos(acc[:, d % 32])
        rope[d, 5] = np.sin(acc[:, d % 32])
    c["c_rope"] = rope
    pm = np.zeros((64, 3, 64), np.float32)
    for m in range(64):
        mm_ = m % 32
        base = m - mm_
        if mm_ < 16:
            pm[base + mm_ + 16, 0, m] = -1.0
        else:
            pm[base + mm_ - 16, 0, m] = 1.0
        if mm_ < 4:
            pm[base + mm_ + 4, 1, m] = -1.0
        elif mm_ < 8:
            pm[base + mm_ - 4, 1, m] = 1.0
        if m < 32:
            pm[m + 32, 2, m] = -1.0
        else:
            pm[m - 32, 2, m] = 1.0
    c["c_pm"] = pm
    c["c_ones"] = np.ones((128, 128), np.float32)
    h = np.arange(4, dtype=np.float32)
    lgf = np.log1p(-np.exp2(-5.0 - h)).astype(np.float32)
    lgb = lgf[::-1].copy()
    i = np.arange(128, dtype=np.float32)
    diff = i[None, :] - i[:, None]
    rm = np.zeros((128, 4, 128), np.float32)
    for hh in range(4):
        rm[:, hh, :] = np.where(diff >= 0, np.exp(lgf[hh] * np.maximum(diff, 0)), 0.0) + \
            np.where(diff <= 0, np.exp(lgb[hh] * np.maximum(-diff, 0)), 0.0)
    c["c_retmask"] = rm
    qd = np.zeros((64, 2, 4, 128), np.float32)
    kd = np.zeros((128, 2, 4), np.float32)
    for hh in range(4):
        qd[:, 0, hh, :] = np.exp(lgf[hh] * (i + 1.0))[None, :]
        qd[:, 1, hh, :] = np.exp(lgb[hh] * (128.0 - i))[None, :]
        kd[:, 0, hh] = np.exp(lgf[hh] * (127.0 - i))
        kd[:, 1, hh] = np.exp(lgb[hh] * i)
    c["c_qdec"] = qd
    c["c_kdec"] = kd
    dm = np.zeros((128, 2, 5, 128), np.float32)
    p = np.arange(128)[:, None]
    f = np.arange(128)[None, :]
    dm[:, 0, 0, :] = -1.0 * (p > f)
    dm[:, 0, 1, :] = -1.0 * (f > p)
    dm[:, 0, 2, :] = (f > p)
    dm[:, 0, 3, :] = (f >= p)
    dm[:, 0, 4, :] = -1.0 * (f >= p)
    dm[:, 1, 0, :] = -1.0 * (p < f)
    dm[:, 1, 1, :] = -1.0 * (f < p)
    dm[:, 1, 2, :] = (f < p)
    dm[:, 1, 3, :] = (f <= p)
    dm[:, 1, 4, :] = -1.0 * (f <= p)
    c["c_dmask"] = dm.astype(np.float32)
    seg = np.ones((64, 512), np.float32)
    seg[:, ::128] = 0.0
    c["c_seg"] = seg
    return c


RET_G = [1.0 - 2.0 ** (-5 - h) for h in range(4)]

WEIGHT_SPECS = [
    ("norm_mix_pre", (2, 1024)), ("norm_mix_post", (2, 1024)), ("norm_ffn_pre", (2, 1024)), ("norm_ffn_post", (2, 1024)),
    ("w_in", (2, 1024, 3392)), ("w_out", (2, 1024, 1024)), ("a_q_gain", (2, 64)), ("a_k_gain", (2, 64)),
    ("b_lambda", (2, 4, 32)), ("b_subln_gain", (2, 64)), ("c_gn_gain", (2, 256)), ("d_mu_prev", (2, 1088)),
    ("d_mu_next", (2, 1088)), ("d_w0", (2, 2, 256)), ("d_w_up", (2, 2, 64, 256)), ("d_a0", (2, 256)),
    ("d_a_up", (2, 64, 256)), ("d_g_up", (2, 128, 256)), ("d_k_k", (2, 256)), ("d_k_a", (2, 256)),
    ("d_r_k", (2, 4, 64)), ("d_gn_w", (2, 256)), ("d_gn_b", (2, 256)),
    ("ffn_w_gate", (2, 1024, 2816)), ("ffn_w_up", (2, 1024, 2816)), ("ffn_w_down", (2, 2816, 1024)),
]


class Ctx:
    pass


def load_consts(fw, C, ph):
    K = Ctx()
    K.ident = ph.sb([128, 128], BF16, "ident")
    fw.ld(K.ident[:], C.d["c_ident"], q="pool")
    K.identf = ph.sb([128, 128], F32, "identf")
    fw.ld(K.identf[:], C.d["c_ident"])
    K.ones_bf = ph.sb([128, 128], BF16, "ones_bf")
    fw.ld(K.ones_bf[:], C.d["c_ones"], q="pool")
    K.ones_f = ph.sb([128, 128], F32, "ones_f")
    fw.ld(K.ones_f[:], C.d["c_ones"])
    K.pm = []
    for i in range(3):
        t = ph.sb([64, 64], BF16, "pm%d" % i)
        fw.ld(t[:], C.d["c_pm"][:, i, :], q="pool")
        K.pm.append(t)
    return K


def rms_rows(fw, xt, gain, hb, tmp, ss, rstd):
    fw.act(tmp[:], xt[:], AF.Square, accum=ss[:])
    fw.act(rstd[:], ss[:], AF.Sqrt, bias=EPS, scale=1.0 / D_MODEL)
    fw.recip(rstd[:], rstd[:])
    fw.stt(hb[:], xt[:], rstd[:, 0:1], gain[:], ALU.mult, ALU.mult)


def make_hT(fw, K, src_rows, gain, hT, j, P):
    xt = P.xt.next()
    fw.ld(xt[:], src_rows)
    hb = P.hb.next()
    ss = P.ss.next()
    rstd = P.rstd.next()
    tmp = P.tmp.next()
    rms_rows(fw, xt, gain, hb, tmp, ss, rstd)
    tp = P.tp.next()
    for k in range(8):
        fw.tr(tp[:, k, :], hb[:, k * 128:(k + 1) * 128], K.ident[:], sig=(k == 7))
    fw.copy("dve", hT[:, :, j * 128:(j + 1) * 128], tp[:])
    return xt


def phase_inproj(fw, C, K, l, si):
    T_ = C.seqs[si]
    src = C.xin[si] if l == 0 else C.y[si]
    d = C.d
    ph = fw.phase()
    W = ph.sb([128, 8, IN_COLS], BF16, "W")
    for k in range(8):
        fw.ld(W[:, k, :], d["w_in"][l, k * 128:(k + 1) * 128, :], q="pool")
    gain = ph.sb([128, 1024], F32, "gain")
    fw.ld(gain[:], d["norm_mix_pre"][l].partition_broadcast(128))
    gqk = ph.sb([64, 2], F32, "gqk")
    fw.ld(gqk[:, 0:1], d["a_q_gain"][l].rearrange("(p o) -> p o", o=1))
    fw.ld(gqk[:, 1:2], d["a_k_gain"][l].rearrange("(p o) -> p o", o=1))
    P = Ctx()
    P.xt = ph.sbpool(2, [128, 1024], F32, "xt")
    P.hb = ph.sbpool(2, [128, 1024], BF16, "hb")
    P.tmp = ph.sbpool(1, [128, 1024], F32, "tmp")
    P.ss = ph.sbpool(2, [128, 1], F32, "ss")
    P.rstd = ph.sbpool(2, [128, 1], F32, "rstd")
    P.tp = ph.pspool(1, [128, 8, 128], BF16, "tp")
    hTs = ph.sbpool(2, [128, 8, 512], BF16, "hT")
    tabs = ph.sbpool(2, [64, 6, 512], F32, "tab")
    zps = ph.pspool(4, [128, 512], F32, "zp")
    aps = ph.pspool(2, [128, 512], F32, "ap")
    tps = ph.pspool(1, [128, 512], F32, "tmps")
    f32p = ph.sbpool(24, [64, 512], F32, "f32p")
    bfp = ph.sbpool(16, [64, 512], BF16, "bfp")
    f32w = ph.sbpool(3, [128, 512], F32, "f32w")
    bfw = ph.sbpool(3, [128, 512], BF16, "bfw")
    chunks = []
    for h in range(4):
        chunks.append(("A", 0 + 64 * h, C.QA, h, 0))
    for h in range(2):
        chunks.append(("A", 256 + 64 * h, C.KA, h, 1))
    for h in range(4):
        chunks.append(("B", 512 + 64 * h, C.QB, h, 0))
    for h in range(4):
        chunks.append(("B", 768 + 64 * h, C.KB, h, 0))
    for h in range(4):
        chunks.append(("C", 1280 + 64 * h, C.QC, h, 0))
    for h in range(4):
        chunks.append(("C", 1536 + 64 * h, C.KC, h, 0))
    for h in range(4):
        chunks.append(("G", 2048 + 64 * h, C.GC, h, 0))
    for j in range(15):
        chunks.append(("D", 2304 + 64 * j, C.ZD, j, 0))
    DEPTHP = 4

    def chunk_gen(kind, col, dst, slot, gi, hT, tab, t0):
        zp = zps.next()
        for k in range(8):
            fw.mm(zp[0:64, :], W[:, k, col:col + 64], hT[:, k, :], start=(k == 0), stop=(k == 7))
        z = zp[0:64, :]
        yield
        if kind in ("A", "B", "C"):
            ti = {"A": 0, "B": 2, "C": 4}[kind]
            pmi = {"A": 0, "B": 1, "C": 2}[kind]
            qn = bfp.next()
            qf = f32p.next()
            fw.copy("act", qf[:], z)
            if kind == "A":
                sq = bfp.next()
                fw.act(sq[:], z, AF.Square)
                yield
                ms = aps.next()
                fw.mm(ms[0:64, :], K.ones_bf[0:64, 0:64], sq[:])
                rs = f32p.next()
                fw.ts("dve", rs[:], ms[0:64, :], 1.0 / 64, EPS, ALU.mult, ALU.add)
                fw.act(rs[:], rs[:], AF.Ln)
                fw.act(rs[:], rs[:], AF.Exp, scale=-0.5)
                qf2 = f32p.next()
                fw.stt(qf2[:], qf[:], gqk[:, gi:gi + 1], rs[:], ALU.mult, ALU.mult)
                qf = qf2
            fw.copy("act", qn[:], qf[:])
            yield
            pq = aps.next()
            fw.mm(pq[0:64, :], K.pm[pmi][:], qn[:])
            t2 = f32p.next()
            fw.tt("dve", t2[:], pq[0:64, :], tab[:, ti + 1, :], ALU.mult)
            t1 = f32p.next()
            fw.tt("pool", t1[:], qf[:], tab[:, ti, :], ALU.mult)
            ob = bfp.next()
            fw.tt("pool", ob[:], t1[:], t2[:], ALU.add)
            fw.st(dst[:, slot, t0:t0 + 512], ob[:])
        elif kind == "G":
            ob = bfp.next()
            fw.act(ob[:], z, AF.Silu)
            fw.st(dst[:, slot, t0:t0 + 512], ob[:])
        else:
            of = f32p.next()
            fw.copy("act", of[:], z)
            fw.st(dst[:, slot, t0:t0 + 512], of[:])

    def tail_gen(hT, t0):
        zp = zps.next()
        for k in range(8):
            fw.mm(zp[:, :], W[:, k, 3264:3392], hT[:, k, :], start=(k == 0), stop=(k == 7))
        of = f32w.next()
        fw.copy("act", of[:], zp[:])
        fw.st(C.ZG[:, t0:t0 + 512], of[:])
        yield
        for j in range(4):
            tp_ = tps.next()
            for k in range(8):
                fw.mm(tp_[:, 0:128], hT[:, k, j * 128:(j + 1) * 128], W[:, k, 384:512], start=(k == 0), stop=(k == 7), sig=False)
            for k in range(8):
                fw.mm(tp_[:, 128:384], hT[:, k, j * 128:(j + 1) * 128], W[:, k, 1024:1280], start=(k == 0), stop=(k == 7))
            ob = bfw.next()
            fw.copy("act", ob[:, 0:384], tp_[:, 0:384])
            r0 = t0 + j * 128
            fw.st(C.VA[r0:r0 + 128, :], ob[:, 0:128])
            fw.st(C.VB[r0:r0 + 128, :], ob[:, 128:384])
            yield
            tp2 = tps.next()
            for k in range(8):
                fw.mm(tp2[:, 0:256], hT[:, k, j * 128:(j + 1) * 128], W[:, k, 1792:2048], start=(k == 0), stop=(k == 7))
            ob2 = bfw.next()
            fw.copy("dve", ob2[:, 0:256], tp2[:, 0:256])
            fw.st(C.VC[r0:r0 + 128, :], ob2[:, 0:256])
            yield

    def hT_gen(b, hT):
        t0 = b * 512
        for j in range(4):
            make_hT(fw, K, src[t0 + j * 128:t0 + (j + 1) * 128, :], gain, hT, j, P)
            yield

    def run_pipe(gens, depth, side=None):
        gens = list(gens)
        active = []
        tick = 0
        while gens or active:
            while gens and len(active) < depth:
                active.append(gens.pop(0))
            for g in list(active):
                try:
                    next(g)
                except StopIteration:
                    active.remove(g)
            tick += 1
            if side is not None and tick % 8 == 0:
                try:
                    next(side)
                except StopIteration:
                    side = None
        if side is not None:
            for _ in side:
                pass

    nb = T_ // 512
    hT_cur = hTs.next()
    for _ in hT_gen(0, hT_cur):
        pass
    for b in range(nb):
        t0 = b * 512
        tab = tabs.next()
        fw.ld(tab[:], d["c_rope"][:, :, t0:t0 + 512])
        hT = hT_cur
        side = None
        if b + 1 < nb:
            hT_cur = hTs.next()
            side = hT_gen(b + 1, hT_cur)
        gens = [chunk_gen(kind, col, dst, slot, gi, hT, tab, t0) for (kind, col, dst, slot, gi) in chunks]
        gens.append(tail_gen(hT, t0))
        run_pipe(gens, DEPTHP, side)
    ph.close()


def attn_core(fw, ph, K, T_, QT, KT, V, heads, negM, scale, emit):
    LOOK = 2
    sps = ph.pspool(4, [128, 512], F32, "sps")
    ops_ = ph.pspool(2, [128, 512], F32, "ops")
    pts = ph.sbpool(4, [128, 512], BF16, "pt")
    osbs = ph.sbpool(4, [65, 512], F32, "osb")
    nkt = T_ // 128
    iters = []
    for qb in range(T_ // 512):
        for hd in heads:
            for kt in range(nkt):
                iters.append((qb, hd, kt))
    pend = []
    state = {"o_ps": None}

    def do_pv(item):
        (qb, hd, kt, pt) = item
        (qs, lo, hi, ks, vs, tag) = hd
        if kt == 0:
            state["o_ps"] = ops_.next()
        o_ps = state["o_ps"]
        fw.mm(o_ps[0:65, :], V[:, kt, vs, :], pt[:], start=(kt == 0), stop=(kt == nkt - 1))
        if kt == nkt - 1:
            osb = osbs.next()
            fw.copy("dve", osb[:], o_ps[0:65, :])
            emit(qb, tag, osb)

    for (qb, hd, kt) in iters:
        (qs, lo, hi, ks, vs, tag) = hd
        q0 = qb * 512
        st = sps.next()
        fw.mm(st[:, :], KT[lo:hi, ks, kt * 128:(kt + 1) * 128], QT[lo:hi, qs, q0:q0 + 512])
        pt = pts.next()
        if negM is None:
            fw.act(pt[:], st[:], AF.Exp, scale=scale)
        else:
            fw.act(pt[:], st[:], AF.Exp, scale=scale, bias=negM[:, 0:1])
        pend.append((qb, hd, kt, pt))
        if len(pend) > LOOK:
            do_pv(pend.pop(0))
    while pend:
        do_pv(pend.pop(0))


def attn_core2(fw, ph, T_, groups, V, negM, scale, emit, bcs):
    LOOK = 2
    sps = ph.pspool(3, [128, 1024], F32, "sps")
    ops_ = bcs
    pts = ph.sbpool(4, [128, 1024], BF16, "pt")
    osbs = ph.sbpool(4, [65, 512], F32, "osb")
    nkt = T_ // 128
    iters = []
    for qb in range(T_ // 512):
        for g in groups:
            for kt in range(nkt):
                iters.append((qb, g, kt))
    pend = []
    state = {}
    import os
    NW = int(os.environ.get("DBG_WARM", "0"))
    if NW:
        wz = pts.next()
        fw.memset("pool", wz[:], 1.0)
        st0 = sps.next()
        for i in range(NW):
            fw.mm(st0[:, 0:512], wz[:, 0:128], wz[:, 512:1024], sig=(i == NW - 1))

    def do_pv(item):
        (qb, g, kt, pt) = item
        (kt_ap, q_ap, vs, tag) = g
        if kt == 0:
            state["o"] = [ops_.next(), ops_.next()]
        o = state["o"]
        for j in range(2):
            fw.mm(o[j][0:65, :], V[:, kt, vs[j], :], pt[:, j * 512:(j + 1) * 512], start=(kt == 0), stop=(kt == nkt - 1))
        if kt == nkt - 1:
            osb = [osbs.next(), osbs.next()]
            for j in range(2):
                fw.copy("dve", osb[j][:], o[j][0:65, :])
            emit(qb, tag, osb)

    for (qb, g, kt) in iters:
        (kt_ap, q_ap, vs, tag) = g
        st = sps.next()
        for j in range(2):
            fw.mm(st[:, j * 512:(j + 1) * 512], kt_ap(j, kt), q_ap(j, qb), sig=(j == 1))
        pt = pts.next()
        if negM is None:
            fw.act(pt[:], st[:], AF.Exp, scale=scale)
        else:
            fw.act(pt[:], st[:], AF.Exp, scale=scale, bias=negM[:, 0:1])
        pend.append((qb, g, kt, pt))
        if len(pend) > LOOK:
            do_pv(pend.pop(0))
    while pend:
        do_pv(pend.pop(0))


def bcast_recip(fw, K, osb, recs, hls, bcs, mulv):
    rec = recs.next()
    fw.act(rec[64:65, :], osb[64:65, :], AF.Ln)
    fw.act(rec[64:65, :], rec[64:65, :], AF.Exp, scale=-1.0)
    if mulv is not None:
        fw.ts("dve", rec[64:65, :], rec[64:65, :], mulv[64:65, 0:1], None, ALU.mult)
    hl = hls.next()
    fw.copy("dve", hl[64:65, 0, :], rec[64:65, :])
    fw.tt("dve", hl[64:65, 1, :], rec[64:65, :], hl[64:65, 0, :], ALU.subtract)
    bc = bcs.next()
    fw.mm(bc[0:64, :], K.ones_bf[64:65, 0:64], hl[64:65, 0, :], start=True, stop=False, sig=False)
    fw.mm(bc[0:64, :], K.ones_bf[64:65, 0:64], hl[64:65, 1, :], start=False, stop=True)
    return bc


def phase_attnA(fw, C, K, l, si):
    T_ = C.seqs[si]
    d = C.d
    ph = fw.phase()
    QT = ph.sb([128, 2, T_], BF16, "QT")
    KT = ph.sb([128, T_], BF16, "KT")
    V = ph.sb([128, T_ // 128, 2, 65], BF16, "V")
    for p in range(2):
        fw.ld(QT[0:64, p, :], C.QA[:, p, 0:T_])
        fw.ld(QT[64:128, p, :], C.QA[:, p + 2, 0:T_])
    for h in range(2):
        fw.ld(KT[64 * h:64 * h + 64, :], C.KA[:, h, 0:T_])
    fw.memset("pool", V[:, :, :, 64:65], 1.0)
    for h in range(2):
        fw.ld(V[:, :, h, 0:64], C.VA[0:T_, h * 64:(h + 1) * 64].rearrange("(n p) c -> p n c", p=128))
    g2 = ph.sb([128, 2, 64], F32, "g2")
    fw.ld(g2[:, 0, :], d["a_q_gain"][l].partition_broadcast(128))
    fw.ld(g2[:, 1, :], d["a_k_gain"][l].partition_broadcast(128))
    fw.tt("dve", g2[:], g2[:], g2[:], ALU.mult)
    mx = ph.sb([128, 2], F32, "mx")
    fw.reduce(mx[:], g2[:], ALU.max)
    negM = ph.sb([128, 1], F32, "negM")
    fw.tt("dve", negM[:], mx[:, 0:1], mx[:, 1:2], ALU.mult)
    fw.act(negM[:], negM[:], AF.Sqrt, scale=64.0)
    fw.ts("dve", negM[:], negM[:], -1.0, None, ALU.mult)
    recs = ph.sbpool(2, [65, 512], F32, "rec")
    hls = ph.sbpool(2, [65, 2, 512], BF16, "hl")
    bcs = ph.pspool(2, [128, 512], F32, "bc")
    outs = ph.sbpool(3, [64, 512], BF16, "ob")

    def emit(qb, p, osbs_):
        for j in range(2):
            h = p + 2 * j
            osb = osbs_[j]
            bc = bcast_recip(fw, K, osb, recs, hls, bcs, None)
            ob = outs.next()
            fw.tt("dve", ob[:], osb[0:64, :], bc[0:64, :], ALU.mult)
            fw.st(C.OT[h, :, qb * 512:(qb + 1) * 512], ob[:])

    groups = []
    for p in range(2):
        groups.append((lambda j, kt: KT[64 * j:64 * j + 64, kt * 128:(kt + 1) * 128],
                       (lambda p_: (lambda j, qb: QT[64 * j:64 * j + 64, p_, qb * 512:(qb + 1) * 512]))(p),
                       (0, 1), p))
    attn_core2(fw, ph, T_, groups, V, negM, 0.125, emit, bcs)
    ph.close()


def phase_attnB(fw, C, K, l, si):
    T_ = C.seqs[si]
    d = C.d
    lam_init = 0.8 - 0.6 * math.exp(-0.3 * l)
    ph = fw.phase()
    QT = ph.sb([128, 2, 2, T_], BF16, "QT")
    KT = ph.sb([128, 2, T_], BF16, "KT")
    V = ph.sb([128, T_ // 128, 4, 65], BF16, "V")
    for hh in range(2):
        fw.memset("pool", QT[64 * hh + 32:64 * hh + 64, :, 0, :], 0.0)
        fw.memset("pool", QT[64 * hh:64 * hh + 32, :, 1, :], 0.0)
    for p in range(2):
        for hh in range(2):
            h = p + 2 * hh
            fw.ld(QT[64 * hh:64 * hh + 32, p, 0, :], C.QB[0:32, h, 0:T_])
            fw.ld(QT[64 * hh + 32:64 * hh + 64, p, 1, :], C.QB[32:64, h, 0:T_])
            fw.ld(KT[64 * hh:64 * hh + 64, p, :], C.KB[:, h, 0:T_])
    fw.memset("pool", V[:, :, :, 64:65], 1.0)
    for h in range(4):
        fw.ld(V[:, :, h, 0:64], C.VB[0:T_, h * 64:(h + 1) * 64].rearrange("(n p) c -> p n c", p=128))
    lp = ph.sb([128, 4, 32], F32, "lp")
    fw.ld(lp[:], d["b_lambda"][l].partition_broadcast(128))
    pr = ph.sb([128, 2, 32], F32, "pr")
    fw.tt("dve", pr[:, 0, :], lp[:, 0, :], lp[:, 1, :], ALU.mult)
    fw.tt("dve", pr[:, 1, :], lp[:, 2, :], lp[:, 3, :], ALU.mult)
    sm = ph.sb([128, 2], F32, "sm")
    fw.reduce(sm[:], pr[:], ALU.add)
    fw.act(sm[:], sm[:], AF.Exp)
    nlam = ph.sb([128, 1], F32, "nlam")
    fw.tt("dve", nlam[:], sm[:, 1:2], sm[:, 0:1], ALU.subtract)
    fw.ts("dve", nlam[:], nlam[:], -lam_init, None, ALU.add)
    gsub = ph.sb([64, 1], F32, "gsub")
    fw.ld(gsub[:], d["b_subln_gain"][l].rearrange("(p o) -> p o", o=1))
    fw.ts("dve", gsub[:], gsub[:], 1.0 - lam_init, None, ALU.mult)
    recs = ph.sbpool(2, [65, 512], F32, "rec")
    hls = ph.sbpool(2, [65, 2, 512], BF16, "hl")
    bcs = ph.pspool(2, [128, 512], F32, "bc")
    o1s = ph.sbpool(6, [64, 512], F32, "o1")
    rss = ph.sbpool(2, [64, 512], F32, "rsb")
    sqs = ph.sbpool(2, [64, 512], BF16, "sq")
    outs = ph.sbpool(3, [64, 512], BF16, "ob")

    pend_o = {}

    def emit(qb, tag, osbs_):
        (p, s_) = tag
        for hh in range(2):
            h = p + 2 * hh
            osb = osbs_[hh]
            bc = bcast_recip(fw, K, osb, recs, hls, bcs, nlam if s_ == 1 else None)
            o_ = o1s.next()
            fw.tt("dve", o_[:], osb[0:64, :], bc[0:64, :], ALU.mult)
            if s_ == 0:
                pend_o[h] = o_
            else:
                finish(qb, h, pend_o.pop(h), o_)

    def finish(qb, h, o1, o2):
        fw.tt("pool", o1[:], o1[:], o2[:], ALU.add)
        sq = sqs.next()
        fw.act(sq[:], o1[:], AF.Square)
        ms = bcs.next()
        fw.mm(ms[0:64, :], K.ones_bf[0:64, 0:64], sq[:])
        rs = rss.next()
        fw.ts("dve", rs[:], ms[0:64, :], 1.0 / 64, EPS, ALU.mult, ALU.add)
        fw.act(rs[:], rs[:], AF.Ln)
        fw.act(rs[:], rs[:], AF.Exp, scale=-0.5)
        ob = outs.next()
        fw.stt(ob[:], o1[:], gsub[:, 0:1], rs[:], ALU.mult, ALU.mult)
        fw.st(C.OT[4 + h, :, qb * 512:(qb + 1) * 512], ob[:])

    groups = []
    for p in range(2):
        for s_ in range(2):
            groups.append(((lambda p_: (lambda j, kt: KT[64 * j:64 * j + 64, p_, kt * 128:(kt + 1) * 128]))(p),
                           (lambda p_, s__: (lambda j, qb: QT[64 * j:64 * j + 64, p_, s__, qb * 512:(qb + 1) * 512]))(p, s_),
                           (p, p + 2), (p, s_)))
    attn_core2(fw, ph, T_, groups, V, None, 32 ** -0.5, emit, bcs)
    ph.close()


def phase_ret(fw, C, K, l, si):
    T_ = C.seqs[si]
    d = C.d
    n = T_ // 128
    ph = fw.phase()
    QT = ph.sb([64, 4, T_], BF16, "QT")
    KT = ph.sb([64, 4, T_], BF16, "KT")
    G = ph.sb([64, 4, T_], BF16, "G")
    V = ph.sb([128, n, 256], BF16, "V")
    for h in range(4):
        fw.ld(QT[:, h, :], C.QC[:, h, 0:T_])
        fw.ld(KT[:, h, :], C.KC[:, h, 0:T_])
        fw.ld(G[:, h, :], C.GC[:, h, 0:T_])
    fw.ld(V[:], C.VC[0:T_, :].rearrange("(n p) c -> p n c", p=128))
    mask = ph.sb([128, 4, 128], F32, "mask")
    fw.ld(mask[:], d["c_retmask"])
    qdec = ph.sb([64, 2, 4, 128], F32, "qdec")
    fw.ld(qdec[:], d["c_qdec"])
    kdec = ph.sb([128, 2, 4], F32, "kdec")
    fw.ld(kdec[:], d["c_kdec"])
    gng = ph.sb([64, 4], F32, "gng")
    fw.ldn(gng[:], d["c_gn_gain"][l].rearrange("(h p) -> p h", p=64))
    Sf = ph.sb([64, n, 4, 64], BF16, "Sf")
    Sb = ph.sb([64, n, 4, 64], BF16, "Sb")
    cur = [ph.sb([64, 4, 64], F32, "curf"), ph.sb([64, 4, 64], F32, "curb")]
    ktp = ph.pspool(2, [128, 4, 64], BF16, "ktp")
    kds = ph.sbpool(4, [128, 4, 64], BF16, "kds")
    kvp = ph.pspool(2, [128, 512], F32, "kvp")
    gamt = ph.sb([64, 2, 4], F32, "gamt")
    for dr in range(2):
        for h in range(4):
            g_ = (RET_G[h] if dr == 0 else RET_G[3 - h]) ** 128
            fw.memset("pool", gamt[:, dr, h:h + 1], g_)
    tmpc = [ph.sbpool(2, [64, 4, 64], F32, "tmpc"), ph.sbpool(2, [64, 4, 64], F32, "tmpc")]

    def state_gen(dr):
        fw.memset("dve", cur[dr][:], 0.0)
        order = range(n) if dr == 0 else range(n - 1, -1, -1)
        Sx = Sf if dr == 0 else Sb
        for c in order:
            fw.copy("act", Sx[:, c, :, :], cur[dr][:])
            if (dr == 0 and c == n - 1) or (dr == 1 and c == 0):
                break
            tp = ktp.next()
            for h in range(4):
                fw.tr(tp[:, h, :], KT[:, h, c * 128:(c + 1) * 128], K.ident[0:64, 0:64], sig=(h == 3))
            kd = kds.next()
            fw.tt("dve", kd[:], tp[:], kdec.bc((slice(None), dr, slice(None)), 2, [128, 4, 64]), ALU.mult)
            tmp = tmpc[dr].next()
            fw.tt("pool", tmp[:], cur[dr][:], gamt.bc((slice(None), dr, slice(None)), 2, [64, 4, 64]), ALU.mult)
            yield
            kv = kvp.next()
            for h in range(4):
                fw.mm(kv[0:64, h * 64:(h + 1) * 64], kd[:, h, :], V[:, c, h * 64:(h + 1) * 64], sig=(h == 3))
            fw.tt("dve", cur[dr][:], tmp[:], View(kv, kv.ap[0:64, 0:256].rearrange("p (h e) -> p h e", h=4)), ALU.add)
            yield

    run_pipe([state_gen(0), state_gen(1)], 2)
    inp = ph.pspool(2, [128, 512], F32, "inp")
    ins_ = ph.sbpool(5, [128, 4, 128], BF16, "ins")
    qds = ph.sbpool(5, [64, 2, 4, 128], BF16, "qds")
    otp = ph.pspool(1, [128, 512], F32, "otp")
    osb = ph.sbpool(5, [64, 4, 128], F32, "osb")
    sqs = ph.sbpool(5, [64, 4, 128], BF16, "sq")
    msp = ph.pspool(1, [128, 512], F32, "msp")
    rss = ph.sbpool(2, [64, 4, 128], F32, "rs")
    obs = ph.sbpool(3, [64, 4, 128], BF16, "ob")

    def out_gen(c):
        cs = slice(c * 128, (c + 1) * 128)
        ip = inp.next()
        for h in range(4):
            fw.mm(ip[:, h * 128:(h + 1) * 128], KT[:, h, cs], QT[:, h, cs], sig=(h == 3))
        it = ins_.next()
        fw.tt("dve", it[:], View(ip, ip.ap[:, :].rearrange("p (h i) -> p h i", h=4)), mask[:], ALU.mult)
        qd = qds.next()
        for dr in range(2):
            fw.tt("pool", qd[:, dr, :, :], QT[:, :, cs], qdec[:, dr, :, :], ALU.mult)
        yield
        op_ = otp.next()
        for h in range(4):
            o_h = op_[0:64, h * 128:(h + 1) * 128]
            fw.mm(o_h, V[:, c, h * 64:(h + 1) * 64], it[:, h, :], start=True, stop=False, sig=False)
            fw.mm(o_h, Sf[:, c, h, :], qd[:, 0, h, :], start=False, stop=False, sig=False)
            fw.mm(o_h, Sb[:, c, h, :], qd[:, 1, h, :], start=False, stop=True, sig=(h == 3))
        o = osb.next()
        fw.ts("dve", o[:], View(op_, op_.ap[0:64, :].rearrange("p (h i) -> p h i", h=4)), 0.125, None, ALU.mult)
        sq = sqs.next()
        fw.act(sq[:], o[:], AF.Square)
        yield
        ms = msp.next()
        fw.mm(ms[0:64, :], K.ones_bf[0:64, 0:64], View(sq, sq.ap[:].rearrange("p h i -> p (h i)")))
        rs = rss.next()
        fw.ts("dve", View(rs, rs.ap[:].rearrange("p h i -> p (h i)")), ms[0:64, :], 1.0 / 64, EPS, ALU.mult, ALU.add)
        fw.act(rs[:], rs[:], AF.Ln)
        fw.act(rs[:], rs[:], AF.Exp, scale=-0.5)
        fw.tt("dve", o[:], o[:], rs[:], ALU.mult)
        fw.tt("pool", o[:], o[:], gng.bc((slice(None), slice(None)), 2, [64, 4, 128]), ALU.mult)
        ob = obs.next()
        fw.tt("dve", ob[:], o[:], G[:, :, cs], ALU.mult)
        for h in range(4):
            fw.st(C.OT[8 + h, :, cs], ob[:, h, :])

    run_pipe([out_gen(c) for c in range(n)], 3)
    ph.close()


def phase_outproj(fw, C, K, l, si):
    T_ = C.seqs[si]
    d = C.d
    src = C.xin[si] if l == 0 else C.y[si]
    ph = fw.phase()
    W = ph.sb([128, 8, 1024], BF16, "Wo")
    for k in range(8):
        fw.ld(W[:, k, :], d["w_out"][l, k * 128:(k + 1) * 128, :], q="pool")
    gain = ph.sb([128, 1024], F32, "gain")
    fw.ld(gain[:], d["norm_mix_post"][l].partition_broadcast(128))
    oTs = ph.sbpool(2, [128, 8, 512], BF16, "oT")
    xts = ph.sbpool(4, [128, 1024], F32, "xt")
    mps = ph.pspool(8, [128, 512], F32, "mp")
    tmps = ph.sbpool(4, [128, 1024], F32, "tmp")
    sss = ph.sbpool(4, [128, 2], F32, "ss")
    rstds = ph.sbpool(4, [128, 1], F32, "rstd")

    def sub_gen(oT, j, r0):
        xt = xts.next()
        fw.ld(xt[:], src[r0:r0 + 128, :])
        m = [mps.next(), mps.next()]
        for half in range(2):
            for k in range(8):
                fw.mm(m[half][:, :], oT[:, k, j * 128:(j + 1) * 128], W[:, k, half * 512:(half + 1) * 512], start=(k == 0), stop=(k == 7))
        tmp = tmps.next()
        ss = sss.next()
        for half in range(2):
            fw.act(tmp[:, half * 512:(half + 1) * 512], m[half][:, :], AF.Square, accum=ss[:, half:half + 1])
        yield
        rstd = rstds.next()
        fw.tt("dve", rstd[:], ss[:, 0:1], ss[:, 1:2], ALU.add)
        fw.ts("dve", rstd[:], rstd[:], 1.0 / D_MODEL, EPS, ALU.mult, ALU.add)
        fw.act(rstd[:], rstd[:], AF.Ln)
        fw.act(rstd[:], rstd[:], AF.Exp, scale=-0.5)
        yield
        for half in range(2):
            hs = slice(half * 512, (half + 1) * 512)
            fw.stt(tmp[:, hs], m[half][:, :], rstd[:, 0:1], gain[:, hs], ALU.mult, ALU.mult)
        fw.tt("pool", tmp[:], tmp[:], xt[:], ALU.add)
        fw.st(C.y[si][r0:r0 + 128, :], tmp[:])

    gens = []
    for b in range(T_ // 512):
        t0 = b * 512
        oT = oTs.next()

        def ldgen(oT=oT, t0=t0):
            for k in range(8):
                fw.ld(oT[0:64, k, :], C.OT[2 * k, :, t0:t0 + 512])
                fw.ld(oT[64:128, k, :], C.OT[2 * k + 1, :, t0:t0 + 512])
            return
            yield
        gens.append(("ld", oT, t0))
        for j in range(4):
            gens.append(("sub", oT, j, t0 + j * 128))

    def all_gens():
        for g in gens:
            if g[0] == "ld":
                _, oT, t0 = g
                for k in range(8):
                    fw.ld(oT[0:64, k, :], C.OT[2 * k, :, t0:t0 + 512])
                    fw.ld(oT[64:128, k, :], C.OT[2 * k + 1, :, t0:t0 + 512])
            else:
                yield sub_gen(g[1], g[2], g[3])

    class LazyList(list):
        pass
    pending = all_gens()
    active = []
    done = False
    while not done or active:
        while not done and len(active) < 3:
            try:
                active.append(next(pending))
            except StopIteration:
                done = True
        for g in list(active):
            try:
                next(g)
            except StopIteration:
                active.remove(g)
    ph.close()


def phase_ffn(fw, C, K, l, si):
    T_ = C.seqs[si]
    d = C.d
    TB = 1024
    ph = fw.phase()
    gain = ph.sb([128, 1024], F32, "gain")
    fw.ld(gain[:], d["norm_ffn_pre"][l].partition_broadcast(128))
    gpost = ph.sb([128, 1024], F32, "gpost")
    fw.ld(gpost[:], d["norm_ffn_post"][l].partition_broadcast(128))
    P = Ctx()
    P.xt = ph.sbpool(2, [128, 1024], F32, "xt")
    P.hb = ph.sbpool(2, [128, 1024], BF16, "hb")
    P.tmp = ph.sbpool(2, [128, 1024], F32, "tmp")
    P.ss = ph.sbpool(2, [128, 1], F32, "ss")
    P.rstd = ph.sbpool(2, [128, 1], F32, "rstd")
    P.tp = ph.pspool(1, [128, 8, 128], BF16, "tp")
    hTs = ph.sbpool(1, [128, 8, TB], BF16, "hT")
    act = ph.sb([128, NFF, TB], BF16, "act")
    FG = 4
    wgs = ph.sbpool(2, [128, 8, 128 * FG], BF16, "wg")
    wus = ph.sbpool(2, [128, 8, 128 * FG], BF16, "wu")
    Wd = ph.sb([128, NFF, 1024], BF16, "Wd")
    for f in range(NFF):
        fw.ld(Wd[:, f, :], d["ffn_w_down"][l, f * 128:(f + 1) * 128, :], q="pool")
    gps = ph.pspool(2, [128, 512], F32, "gp")
    ups = ph.pspool(2, [128, 512], F32, "up")
    dps = ph.pspool(2, [128, 512], F32, "dp")
    sgs = ph.sbpool(2, [128, 512], F32, "sg")
    sss = ph.sbpool(4, [128, 2], F32, "ss2")
    dxt = ph.sbpool(3, [128, 1024], F32, "dxt")
    dtmp = ph.sbpool(3, [128, 1024], F32, "dtmp")
    drstd = ph.sbpool(4, [128, 1], F32, "drstd")
    wg_d = d["ffn_w_gate"][l].rearrange("(k p) n -> p k n", p=128)
    wu_d = d["ffn_w_up"][l].rearrange("(k p) n -> p k n", p=128)
    def hT_gen(b, hT_):
        t0_ = b * TB
        for j in range(TB // 128):
            make_hT(fw, K, C.y[si][t0_ + j * 128:t0_ + (j + 1) * 128, :], gain, hT_, j, P)
            yield

    nblk = T_ // TB
    for b in range(nblk):
        t0 = b * TB
        hT = hTs.next()
        for _ in hT_gen(b, hT):
            pass
        side = None
        for f in range(NFF):
            if f % FG == 0:
                nf = min(FG, NFF - f)
                wg_t = wgs.next()
                fw.ld(wg_t[:, :, 0:128 * nf], wg_d[:, :, f * 128:(f + nf) * 128], q="pool")
                wu_t = wus.next()
                fw.ld(wu_t[:, :, 0:128 * nf], wu_d[:, :, f * 128:(f + nf) * 128], q="pool")
            fo = (f % FG) * 128
            for tb in range(TB // 512):
                ts_ = slice(tb * 512, (tb + 1) * 512)
                gp = gps.next()
                up = ups.next()
                for k in range(8):
                    fw.mm(gp[:, :], wg_t[:, k, fo:fo + 128], hT[:, k, ts_], start=(k == 0), stop=(k == 7))
                for k in range(8):
                    fw.mm(up[:, :], wu_t[:, k, fo:fo + 128], hT[:, k, ts_], start=(k == 0), stop=(k == 7))
                sg = sgs.next()
                fw.act(sg[:], gp[:, :], AF.Silu)
                fw.tt("dve", act[:, f, ts_], sg[:], up[:, :], ALU.mult)
        ph_all = Pool(dps.items + gps.items + ups.items)
        if side is not None:
            for _ in side:
                pass
        def down_gen(j):
            r0 = t0 + j * 128
            bk = [ph_all.next(), ph_all.next()]
            xt = dxt.next()
            fw.ld(xt[:], C.y[si][r0:r0 + 128, :])
            for f in range(NFF):
                for half in range(2):
                    fw.mm(bk[half][:, :], act[:, f, j * 128:(j + 1) * 128], Wd[:, f, half * 512:(half + 1) * 512],
                          start=(f == 0), stop=(f == NFF - 1))
            tmp = dtmp.next()
            ss = sss.next()
            for half in range(2):
                fw.act(tmp[:, half * 512:(half + 1) * 512], bk[half][:, :], AF.Square, accum=ss[:, half:half + 1])
            yield
            rstd = drstd.next()
            fw.tt("dve", rstd[:], ss[:, 0:1], ss[:, 1:2], ALU.add)
            fw.ts("dve", rstd[:], rstd[:], 1.0 / D_MODEL, EPS, ALU.mult, ALU.add)
            fw.act(rstd[:], rstd[:], AF.Ln)
            fw.act(rstd[:], rstd[:], AF.Exp, scale=-0.5)
            yield
            for half in range(2):
                hs = slice(half * 512, (half + 1) * 512)
                fw.stt(tmp[:, hs], bk[half][:, :], rstd[:, 0:1], gpost[:, hs], ALU.mult, ALU.mult)
            fw.tt("pool", tmp[:], tmp[:], xt[:], ALU.add)
            fw.st(C.y[si][r0:r0 + 128, :], tmp[:])

        run_pipe([down_gen(j) for j in range(TB // 128)], 3)
    ph.close()


def phase_rwkv(fw, C, K, l, si):
    T_ = C.seqs[si]
    d = C.d
    n = T_ // 128
    CW = math.exp(-0.5)
    ph = fw.phase()
    mu = ph.sb([64, 3, 15], F32, "mu")
    fw.ldn(mu[:, 1, :], d["d_mu_prev"][l, 0:960].rearrange("(j p) -> p j", p=64))
    fw.ldn(mu[:, 2, :], d["d_mu_next"][l, 0:960].rearrange("(j p) -> p j", p=64))
    fw.tt("dve", mu[:, 0, :], mu[:, 1, :], mu[:, 2, :], ALU.add)
    fw.ts("dve", mu[:, 0, :], mu[:, 0, :], -1.0, 1.0, ALU.mult, ALU.add)
    mug = ph.sb([128, 3], F32, "mug")
    fw.ld(mug[:, 1:2], d["d_mu_prev"][l, 960:1088].rearrange("(p o) -> p o", o=1))
    fw.ld(mug[:, 2:3], d["d_mu_next"][l, 960:1088].rearrange("(p o) -> p o", o=1))
    fw.tt("dve", mug[:, 0:1], mug[:, 1:2], mug[:, 2:3], ALU.add)
    fw.ts("dve", mug[:, 0:1], mug[:, 0:1], -1.0, 1.0, ALU.mult, ALU.add)
    prm = ph.sb([64, 9, 4], F32, "prm")
    fw.ldn(prm[:, 0, :], d["d_w0"][l, 0].rearrange("(h p) -> p h", p=64))
    fw.ldn(prm[:, 1, :], d["d_w0"][l, 1].rearrange("(h p) -> p h", p=64))
    fw.ldn(prm[:, 2, :], d["d_a0"][l].rearrange("(h p) -> p h", p=64))
    fw.ldn(prm[:, 3, :], d["d_k_k"][l].rearrange("(h p) -> p h", p=64))
    fw.ldn(prm[:, 4, :], d["d_k_a"][l].rearrange("(h p) -> p h", p=64))
    fw.ldn(prm[:, 6, :], d["d_r_k"][l].rearrange("h p -> p h"))
    fw.ldn(prm[:, 7, :], d["d_gn_w"][l].rearrange("(h p) -> p h", p=64))
    fw.ldn(prm[:, 8, :], d["d_gn_b"][l].rearrange("(h p) -> p h", p=64))
    fw.ts("dve", prm[:, 5, :], prm[:, 4, :], -1.0, 1.0, ALU.mult, ALU.add)
    wup = ph.sb([64, 2, 256], BF16, "wup")
    fw.ld(wup[:, 0, :], d["d_w_up"][l, 0], q="pool")
    fw.ld(wup[:, 1, :], d["d_w_up"][l, 1], q="pool")
    aup = ph.sb([64, 256], BF16, "aup")
    fw.ld(aup[:], d["d_a_up"][l], q="pool")
    gup = ph.sb([128, 256], BF16, "gup")
    fw.ld(gup[:], d["d_g_up"][l], q="pool")
    dmask = ph.sb([128, 2, 5, 128], BF16, "dmask")
    fw.ld(dmask[:], d["c_dmask"], q="pool")
    seg = ph.sb([64, 512], F32, "seg")
    fw.ld(seg[:], d["c_seg"])

    def pbc(i):
        return prm.bc((slice(None), i, slice(None)), 2, [64, 4, 128])

    ytok = [T(None, "ytok%d" % c) for c in range(n)]
    H = [ph.sb([64, 4, 64], F32, "Hf"), ph.sb([64, 4, 64], F32, "Hb")]
    Hb = [ph.sb([64, 4, 64], BF16, "Hfb"), ph.sb([64, 4, 64], BF16, "Hbb")]
    for dr in range(2):
        fw.memset("dve", H[dr][:], 0.0)
        fw.memset("pool", Hb[dr][:], 0.0)
    PST = ph.pspool(1, [128, 4, 4, 64], BF16, "pst")
    PP = []
    for dr in range(2):
        Q = Ctx()
        Q.PS = ph.pspool(4 if dr == 0 else 3, [128, 512], F32, "ps")
        Q.zts = ph.sbpool(1, [64, 15, 130], F32, "zt")
        Q.zgs = ph.sbpool(2, [128, 130], F32, "zg")
        Q.us = ph.sbpool(1, [64, 15, 128], F32, "u")
        Q.u2s = ph.sbpool(1, [64, 15, 128], F32, "u2")
        Q.ugs = ph.sbpool(1, [128, 128], F32, "ug")
        Q.F4 = ph.sbpool(10, [64, 4, 128], F32, "f4")
        Q.FL = ph.sbpool(3, [64, 4, 128], F32, "fl")
        Q.B4 = ph.sbpool(12, [64, 4, 128], BF16, "b4")
        Q.M4 = ph.sbpool(9, [128, 4, 128], BF16, "m4")
        Q.MK = ph.sbpool(4, [128, 4, 128], BF16, "mk")
        Q.TM = ph.sbpool(2, [128, 4, 4, 64], BF16, "tm")
        Q.SM = ph.sbpool(4, [128, 128], BF16, "sm")
        Q.S4 = ph.sbpool(6, [64, 4], F32, "s4")
        Q.UV = ph.sbpool(2, [128, 4, 64], F32, "uv")
        Q.UB = ph.sbpool(3, [128, 4, 64], BF16, "ub")
        PP.append(Q)

    def v3(t, npart=64):
        return View(t, t.ap[0:npart, :].rearrange("p (h i) -> p h i", h=4))

    def v3s(t):
        return View(t, t.ap[:, 0:256].rearrange("p (h i) -> p h i", h=4))

    def flat(t):
        return View(t, t.ap[:].rearrange("p h i -> p (h i)"))

    def unit(c, dr, first):
        Q = PP[dr]
        PS, F4, FL, B4, M4, MK, SM, S4 = Q.PS, Q.F4, Q.FL, Q.B4, Q.M4, Q.MK, Q.SM, Q.S4
        need_post = not first
        c0 = c * 128
        zt = Q.zts.next()
        zg = Q.zgs.next()
        lo = 1 if c == 0 else 0
        hi = 129 if c == n - 1 else 130
        if c == 0:
            fw.memset("pool", zt[:, :, 0:1], 0.0)
            fw.memset("pool", zg[:, 0:1], 0.0)
        if c == n - 1:
            fw.memset("pool", zt[:, :, 129:130], 0.0)
            fw.memset("pool", zg[:, 129:130], 0.0)
        fw.ld(zt[:, :, lo:hi], C.ZD[:, :, c0 - 1 + lo:c0 - 1 + hi])
        fw.ld(zg[:, lo:hi], C.ZG[:, c0 - 1 + lo:c0 - 1 + hi])
        if need_post:
            yprev = FL.next()
            fw.ld(yprev[:], View(ytok[c], C.YD[c]))
        yield
        u = Q.us.next()
        u2 = Q.u2s.next()
        fw.tt("dve", u[:], zt[:, :, 1:129], mu.bc((slice(None), 0, slice(None)), 2, [64, 15, 128]), ALU.mult)
        fw.tt("pool", u2[:], zt[:, :, 0:128], mu.bc((slice(None), 1, slice(None)), 2, [64, 15, 128]), ALU.mult)
        yield
        fw.tt("dve", u[:], u[:], u2[:], ALU.add)
        fw.tt("pool", u2[:], zt[:, :, 2:130], mu.bc((slice(None), 2, slice(None)), 2, [64, 15, 128]), ALU.mult)
        yield
        fw.tt("dve", u[:], u[:], u2[:], ALU.add)
        ug = Q.ugs.next()
        fw.ts("dve", ug[:], zg[:, 1:129], mug[:, 0:1], None, ALU.mult)
        fw.stt(ug[:], zg[:, 0:128], mug[:, 1:2], ug[:], ALU.mult, ALU.add)
        fw.stt(ug[:], zg[:, 2:130], mug[:, 2:3], ug[:], ALU.mult, ALU.add)
        yield
        tw = SM.next()
        fw.act(tw[0:64, :], u[:, 12 + dr, :], AF.Tanh)
        adb = SM.next()
        fw.copy("act", adb[0:64, :], u[:, 14, :])
        yw = PS.next()
        ya = PS.next()
        for h in range(4):
            fw.mm(yw[0:64, h * 128:(h + 1) * 128], wup[:, dr, h * 64:(h + 1) * 64], tw[0:64, :], sig=(h == 3))
        for h in range(4):
            fw.mm(ya[0:64, h * 128:(h + 1) * 128], aup[:, h * 64:(h + 1) * 64], adb[0:64, :], sig=(h == 3))
        yield
        sg = F4.next()
        fw.tt("dve", sg[:], v3(yw), pbc(dr), ALU.add)
        fw.act(sg[:], sg[:], AF.Tanh, scale=0.5)
        fw.ts("pool", sg[:], sg[:], 0.5, 0.5, ALU.mult, ALU.add)
        a = F4.next()
        fw.tt("dve", a[:], v3(ya), pbc(2), ALU.add)
        fw.act(a[:], a[:], AF.Tanh, scale=0.5)
        fw.ts("pool", a[:], a[:], 0.5, 0.5, ALU.mult, ALU.add)
        yield
        if need_post:
            sgd = SM.next()
            fw.act(sgd[:], ug[:], AF.Tanh, scale=0.5)
            fw.ts("pool", sgd[:], sgd[:], 0.5, 0.5, ALU.mult, ALU.add)
            gps_ = PS.next()
            for h in range(4):
                fw.mm(gps_[0:64, h * 128:(h + 1) * 128], gup[:, h * 64:(h + 1) * 64], sgd[:], sig=(h == 3))
            g_ = FL.next()
            fw.copy("act", g_[:], v3(gps_))
            yield
        r = u[:, 0:4, :]
        k = u[:, 4:8, :]
        v = u[:, 8:12, :]
        kk = F4.next()
        fw.tt("dve", kk[:], k, pbc(3), ALU.mult)
        sq = B4.next()
        fw.tt("pool", sq[:], kk[:], kk[:], ALU.mult)
        ssp = PS.next()
        fw.mm(ssp[0:64, :], K.ones_bf[0:64, 0:64], flat(sq))
        yield
        nr = F4.next()
        fw.ts("dve", nr[:], v3(ssp), 1e-24, None, ALU.max)
        fw.act(nr[:], nr[:], AF.Ln)
        fw.act(nr[:], nr[:], AF.Exp, scale=-0.5)
        kap = F4.next()
        fw.tt("dve", kap[:], kk[:], nr[:], ALU.mult)
        yield
        tk = F4.next()
        fw.tt("pool", tk[:], a[:], pbc(4), ALU.mult)
        fw.tt("pool", tk[:], tk[:], pbc(5), ALU.add)
        kh = FL.next()
        fw.tt("dve", kh[:], k, tk[:], ALU.mult)
        bb = F4.next()
        fw.tt("pool", bb[:], kap[:], a[:], ALU.mult)
        yield
        Pp = F4.next()
        fw.scan(flat(Pp), seg[:], flat(sg), 0.0, ALU.mult, ALU.add)
        Qp = F4.next()
        fw.tt("pool", Qp[:], Pp[:], sg[:], ALU.subtract)
        nT = S4.next()
        pT = S4.next()
        fw.ts("dve", nT[:], Pp[:, :, 127], -CW, None, ALU.mult)
        fw.ts("dve", pT[:], Pp[:, :, 127], CW, None, ALU.mult)
        gC = S4.next()
        fw.act(gC[:], nT[:], AF.Exp)
        yield
        rt = B4.next()
        kp = B4.next()
        kt = B4.next()
        bt = B4.next()
        Khf = B4.next()
        Bhf = B4.next()
        E1 = F4.next()
        E2 = F4.next()
        if dr == 0:
            fw.act(E1[:], Pp[:], AF.Exp, scale=-CW)
            fw.tt("dve", rt[:], r, E1[:], ALU.mult)
            fw.act(E2[:], Qp[:], AF.Exp, scale=-CW)
            fw.tt("pool", kp[:], kap[:], E2[:], ALU.mult)
            yield
            E3 = F4.next()
            fw.act(E3[:], Pp[:], AF.Exp, scale=CW)
            fw.tt("dve", kt[:], kh[:], E3[:], ALU.mult)
            fw.tt("pool", bt[:], bb[:], E3[:], ALU.mult)
            E4 = F4.next()
            for h in range(4):
                fw.act(E4[:, h, :], Pp[:, h, :], AF.Exp, scale=CW, bias=nT[:, h:h + 1])
        else:
            for h in range(4):
                fw.act(E1[:, h, :], Qp[:, h, :], AF.Exp, scale=CW, bias=nT[:, h:h + 1])
            fw.tt("dve", rt[:], r, E1[:], ALU.mult)
            for h in range(4):
                fw.act(E2[:, h, :], Pp[:, h, :], AF.Exp, scale=CW, bias=nT[:, h:h + 1])
            fw.tt("pool", kp[:], kap[:], E2[:], ALU.mult)
            yield
            E3 = F4.next()
            for h in range(4):
                fw.act(E3[:, h, :], Qp[:, h, :], AF.Exp, scale=-CW, bias=pT[:, h:h + 1])
            fw.tt("dve", kt[:], kh[:], E3[:], ALU.mult)
            fw.tt("pool", bt[:], bb[:], E3[:], ALU.mult)
            E4 = F4.next()
            fw.act(E4[:], Qp[:], AF.Exp, scale=-CW)
        yield
        fw.tt("dve", Khf[:], kh[:], E4[:], ALU.mult)
        fw.stt(Bhf[:], bb[:], -1.0, E4[:], ALU.mult, ALU.mult)
        vb = B4.next()
        fw.copy("act", vb[:], v)
        pst = PST.next()
        for ki, src_ in enumerate((kp, Khf, Bhf, vb)):
            for h in range(4):
                fw.tr(pst[:, ki, h, :], src_[:, h, :], K.ident[0:64, 0:64], sig=(ki == 3 and h == 3))
        tm = Q.TM.next()
        fw.copy("dve", tm[:], pst[:])
        yield
        mats = []
        pairs = [(kp, bt), (bt, kp), (kt, kp), (kt, rt), (bt, rt)]
        for ki, (lt, rt_) in enumerate(pairs):
            p_ = PS.next()
            for h in range(4):
                fw.mm(p_[:, h * 128:(h + 1) * 128], lt[:, h, :], rt_[:, h, :], sig=(h == 3))
            m_ = M4.next() if ki < 2 else MK.next()
            fw.tt("dve", m_[:], v3(p_, 128), dmask.bc((slice(None), dr, ki, slice(None)), 1, [128, 4, 128]), ALU.mult)
            mats.append(m_)
            yield
        X, XT, AkkT, ArkT, ArbT = mats
        PT = M4.next()
        fw.tt("pool", PT[:], XT[:], K.ident.bc((slice(None), slice(None)), 1, [128, 4, 128]), ALU.add)
        for lvl in range(6):
            p1 = PS.next()
            for h in range(4):
                fw.mm(p1[:, h * 128:(h + 1) * 128], XT[:, h, :], X[:, h, :], sig=(h == 3))
            if lvl < 5:
                p2 = PS.next()
                for h in range(4):
                    fw.mm(p2[:, h * 128:(h + 1) * 128], X[:, h, :], XT[:, h, :], sig=(h == 3))
            yield
            X2 = M4.next()
            fw.copy("act", X2[:], v3(p1, 128))
            if lvl < 5:
                XT2 = M4.next()
                fw.copy("dve", XT2[:], v3(p2, 128))
            p3 = PS.next()
            for h in range(4):
                fw.mm(p3[:, h * 128:(h + 1) * 128], X2[:, h, :], PT[:, h, :], sig=(h == 3))
            yield
            PT2 = M4.next()
            fw.tt("dve", PT2[:], v3(p3, 128), PT[:], ALU.add)
            PT = PT2
            X = X2
            if lvl < 5:
                XT = XT2
        pw = PS.next()
        for h in range(4):
            fw.mm(pw[0:64, h * 128:(h + 1) * 128], tm[:, 0, h, :], PT[:, h, :], sig=(h == 3))
        pa = PS.next()
        for h in range(4):
            fw.mm(pa[:, h * 64:(h + 1) * 64], AkkT[:, h, :], tm[:, 3, h, :], sig=(h == 3))
        yield
        WkT = B4.next()
        fw.copy("act", WkT[:], v3(pw))
        AV = Q.UB.next()
        fw.copy("dve", AV[:], v3s(pa))
        pu = PS.next()
        for h in range(4):
            fw.mm(pu[:, h * 64:(h + 1) * 64], PT[:, h, :], AV[:, h, :], sig=(h == 3))
        yield
        Uv = Q.UV.next()
        fw.copy("act", Uv[:], v3s(pu))
        pU = PS.next()
        for h in range(4):
            fw.mm(pU[:, h * 64:(h + 1) * 64], WkT[:, h, :], Hb[dr][:, h, :], sig=(h == 3))
        yield
        U = Q.UB.next()
        fw.tt("dve", U[:], v3s(pU), Uv[:], ALU.add)
        pY = PS.next()
        for h in range(4):
            yh = pY[0:64, h * 128:(h + 1) * 128]
            fw.mm(yh, Hb[dr][:, h, :], rt[:, h, :], start=True, stop=False, sig=False)
            fw.mm(yh, tm[:, 3, h, :], ArkT[:, h, :], start=False, stop=False, sig=False)
            fw.mm(yh, U[:, h, :], ArbT[:, h, :], start=False, stop=True, sig=(h == 3))
        pH = PS.next()
        for h in range(4):
            hh = pH[0:64, h * 64:(h + 1) * 64]
            fw.mm(hh, tm[:, 1, h, :], tm[:, 3, h, :], start=True, stop=False, sig=False)
            fw.mm(hh, tm[:, 2, h, :], U[:, h, :], start=False, stop=True, sig=(h == 3))
        fw.tt("pool", H[dr][:], H[dr][:], gC.bc((slice(None), slice(None)), 2, [64, 4, 64]), ALU.mult)
        yield
        fw.tt("dve", H[dr][:], H[dr][:], View(pH, pH.ap[0:64, 0:256].rearrange("p (h i) -> p h i", h=4)), ALU.add)
        fw.copy("act", Hb[dr][:], H[dr][:])
        if first:
            ysb = F4.next()
            fw.copy("act", ysb[:], v3(pY))
            fw.st(View(ytok[c], C.YD[c]), ysb[:])
            return
        y = F4.next()
        fw.tt("dve", y[:], v3(pY), yprev[:], ALU.add)
        pm_ = PS.next()
        fw.mm(pm_[0:64, :], K.ones_f[0:64, 0:64], flat(y))
        rk = F4.next()
        fw.tt("pool", rk[:], r, kh[:], ALU.mult)
        fw.tt("pool", rk[:], rk[:], pbc(6), ALU.mult)
        yield
        yc = F4.next()
        fw.stt(yc[:], v3(pm_), -1.0 / 64, y[:], ALU.mult, ALU.add)
        sq2 = F4.next()
        fw.act(sq2[:], yc[:], AF.Square)
        pv = PS.next()
        fw.mm(pv[0:64, :], K.ones_f[0:64, 0:64], flat(sq2))
        pr_ = PS.next()
        fw.mm(pr_[0:64, :], K.ones_f[0:64, 0:64], flat(rk))
        yield
        rs = F4.next()
        fw.ts("dve", rs[:], v3(pv), 1.0 / 64, GN_EPS, ALU.mult, ALU.add)
        fw.act(rs[:], rs[:], AF.Ln)
        fw.act(rs[:], rs[:], AF.Exp, scale=-0.5)
        fw.tt("dve", yc[:], yc[:], rs[:], ALU.mult)
        fw.tt("pool", yc[:], yc[:], pbc(7), ALU.mult)
        yield
        fw.tt("pool", yc[:], yc[:], pbc(8), ALU.add)
        t_ = F4.next()
        fw.tt("dve", t_[:], v3(pr_), v, ALU.mult)
        fw.tt("pool", yc[:], yc[:], t_[:], ALU.add)
        ob = B4.next()
        fw.tt("dve", ob[:], yc[:], g_[:], ALU.mult)
        for h in range(4):
            fw.st(C.OT[12 + h, :, c * 128:(c + 1) * 128], ob[:, h, :])

    for i in range(n):
        first = i < n // 2
        gens = [unit(i, 0, first), unit(n - 1 - i, 1, first)]
        alive = [True, True]
        while any(alive):
            for gi in range(2):
                if alive[gi]:
                    try:
                        next(gens[gi])
                    except StopIteration:
                        alive[gi] = False
    ph.close()


def build(seqs=(4096, 2048), depth=2, phases=None, debug=False):
    nc = bass.Bass("TRN2", target_bir_lowering=False)
    C = Ctx()
    C.seqs = list(seqs)
    C.d = {}
    C.xin = [nc.dram_tensor("x%d" % i, [t, D_MODEL], F32, kind="ExternalInput").ap() for i, t in enumerate(seqs)]
    C.y = [nc.dram_tensor("y%d" % i, [t, D_MODEL], F32, kind="ExternalOutput").ap() for i, t in enumerate(seqs)]
    for nm, shp in WEIGHT_SPECS:
        C.d[nm] = nc.dram_tensor(nm, list(shp), F32, kind="ExternalInput").ap()
    for nm, arr in make_consts().items():
        C.d[nm] = nc.dram_tensor(nm, list(arr.shape), F32, kind="ExternalInput").ap()
    TM = max(seqs)
    sk = "ExternalOutput" if debug else "Internal"

    def scratch(nm, shp, dt):
        return nc.dram_tensor(nm, list(shp), dt, kind=sk).ap()
    C.QA = scratch("QA", [64, 4, TM], BF16)
    C.KA = scratch("KA", [64, 2, TM], BF16)
    C.VA = scratch("VA", [TM, 128], BF16)
    C.QB = scratch("QB", [64, 4, TM], BF16)
    C.KB = scratch("KB", [64, 4, TM], BF16)
    C.VB = scratch("VB", [TM, 256], BF16)
    C.QC = scratch("QC", [64, 4, TM], BF16)
    C.KC = scratch("KC", [64, 4, TM], BF16)
    C.GC = scratch("GC", [64, 4, TM], BF16)
    C.VC = scratch("VC", [TM, 256], BF16)
    C.ZD = scratch("ZD", [64, 15, TM], F32)
    C.ZG = scratch("ZG", [128, TM], F32)
    C.OT = scratch("OT", [16, 64, TM], BF16)
    C.YD = scratch("YD", [TM // 128, 64, 4, 128], F32)
    allph = ["inproj", "A", "B", "C", "D", "outproj", "ffn"]
    phases = allph if phases is None else phases
    with ExitStack() as es:
        fw = FW(nc, es)
        ph0 = fw.phase()
        K = load_consts(fw, C, ph0)
        fw.barrier()
        for l in range(depth):
            for si in range(len(seqs)):
                if "inproj" in phases:
                    phase_inproj(fw, C, K, l, si)
                if "A" in phases:
                    phase_attnA(fw, C, K, l, si)
                if "B" in phases:
                    phase_attnB(fw, C, K, l, si)
                if "C" in phases:
                    phase_ret(fw, C, K, l, si)
                if "D" in phases:
                    phase_rwkv(fw, C, K, l, si)
                if "outproj" in phases:
                    phase_outproj(fw, C, K, l, si)
                if "ffn" in phases:
                    phase_ffn(fw, C, K, l, si)
        fw.barrier()
        ph0.es.close()
        C.nins = fw.nins
    return nc, C


_CACHE = {}


def kernel(**inputs):
    xp = np.ascontiguousarray(inputs["x_prompt"], dtype=np.float32)
    xs = np.ascontiguousarray(inputs["x_sample"], dtype=np.float32)
    nc, C = build()
    consts = make_consts()
    in_maps = []
    for i in range(8):
        m = {"x0": xp[i], "x1": xs[i]}
        for nm, _ in WEIGHT_SPECS:
            m[nm] = np.ascontiguousarray(inputs[nm], dtype=np.float32)
        m.update(consts)
        in_maps.append(m)
    res = run_bass_kernel_spmd(nc, in_maps, core_ids=list(range(8)))
    yp = np.stack([r["y0"] for r in res.results], 0).astype(np.float32)
    ys = np.stack([r["y1"] for r in res.results], 0).astype(np.float32)
    return (yp, ys)
```

```python
import math
from contextlib import ExitStack
import numpy as np
import concourse.bass as bass
import concourse.mybir as mybir
from concourse.bass_utils import run_bass_kernel_spmd

F32 = mybir.dt.float32
BF16 = mybir.dt.bfloat16
AF = mybir.ActivationFunctionType
ALU = mybir.AluOpType
AX = mybir.AxisListType

D_MODEL = 1024
IN_COLS = 3392
D_FF = 2816
NFF = D_FF // 128
EPS = 1e-6
GN_EPS = 64e-5
TMAX = 4096


class View:
    __slots__ = ("t", "ap")

    def __init__(self, t, ap):
        self.t = t
        self.ap = ap


class T:
    __slots__ = ("ap", "w", "r", "name")

    def __init__(self, ap, name=""):
        self.ap = ap
        self.w = {}
        self.r = {}
        self.name = name

    def __getitem__(self, idx):
        return View(self, self.ap[idx])

    def bc(self, idx, axis, shape):
        return View(self, self.ap[idx].unsqueeze(axis).to_broadcast(list(shape)))


class Eng:
    def __init__(self, name, handle, sem):
        self.name = name
        self.h = handle
        self.sem = sem
        self.count = 0
        self.seen = {}


class Pool:
    def __init__(self, items):
        self.items = items
        self.i = 0

    def next(self):
        t = self.items[self.i]
        self.i = (self.i + 1) % len(self.items)
        return t


class Phase:
    def __init__(self, fw):
        self.fw = fw
        self.es = ExitStack()
        self.n = 0

    def sb(self, shape, dt, name="t"):
        self.n += 1
        t = self.es.enter_context(self.fw.nc.sbuf_tensor("%s_%d_%d" % (name, self.fw.uid(), self.n), list(shape), dt))
        return T(t, name)

    def ps(self, shape, dt=F32, name="p"):
        self.n += 1
        t = self.es.enter_context(self.fw.nc.psum_tensor("%s_%d_%d" % (name, self.fw.uid(), self.n), list(shape), dt))
        return T(t, name)

    def sbpool(self, n, shape, dt, name="t"):
        return Pool([self.sb(shape, dt, name) for _ in range(n)])

    def pspool(self, n, shape, dt=F32, name="p"):
        return Pool([self.ps(shape, dt, name) for _ in range(n)])

    def close(self):
        self.fw.barrier()
        self.es.close()


class FW:
    NDMA = 32

    def __init__(self, nc, es):
        self.nc = nc
        self.es = es
        self._uid = 0
        self.E = {}
        for nm, h in (("pe", nc.tensor), ("act", nc.scalar), ("dve", nc.vector), ("pool", nc.gpsimd), ("sp", nc.sync)):
            sem = es.enter_context(nc.semaphore("s_" + nm))
            self.E[nm] = Eng(nm, h, sem)
        self.dsem = [es.enter_context(nc.semaphore("d%d" % i)) for i in range(self.NDMA)]
        self.dval = [0] * self.NDMA
        self.dnext = {"sp": 0, "pool": 0, "act": 0}
        self.drange = {"sp": (0, 14), "pool": (14, 28), "act": (28, 32)}
        self.semobj = {}
        for e in self.E.values():
            self.semobj[("e", e.name)] = e.sem
        for i, s in enumerate(self.dsem):
            self.semobj[("d", i)] = s
        self.nins = 0

    def uid(self):
        self._uid += 1
        return self._uid

    def phase(self):
        return Phase(self)

    def _needs(self, reads, writes):
        needs = {}
        for t in reads:
            for k, v in t.w.items():
                if needs.get(k, 0) < v:
                    needs[k] = v
        for t in writes:
            for d in (t.w, t.r):
                for k, v in d.items():
                    if needs.get(k, 0) < v:
                        needs[k] = v
        return needs

    def _emit_waits(self, e, needs, skip_self=False):
        for k, v in needs.items():
            if skip_self and k == ("e", e.name):
                continue
            if e.seen.get(k, 0) >= v:
                continue
            if k[0] == "e":
                assert self.E[k[1]].count >= v, "wait on future event %s %d" % (k, v)
            e.seen[k] = v
            e.h.wait_ge(self.semobj[k], v)
            self.nins += 1

    def op(self, eng, fn, reads=(), writes=(), signal=True):
        e = self.E[eng]
        reads = [v.t for v in reads]
        writes = [v.t for v in writes]
        needs = self._needs(reads, writes)
        self._emit_waits(e, needs, skip_self=(eng == "pe"))
        ins = fn(e.h)
        self.nins += 1
        if signal:
            e.count += 1
            ins.then_inc(e.sem, 1)
            ev = e.count
        else:
            ev = e.count + 1
        key = ("e", eng)
        for t in writes:
            t.w = {key: ev}
            t.r = {}
        for t in reads:
            if t.r.get(key, 0) < ev:
                t.r[key] = ev

    def dma(self, q, out, in_, **kw):
        e = self.E[q]
        reads = [in_.t] if isinstance(in_, View) else []
        writes = [out.t] if isinstance(out, View) else []
        needs = self._needs(reads, writes)
        lo_, hi_ = self.drange[q]
        i = lo_ + self.dnext[q]
        self.dnext[q] = (self.dnext[q] + 1) % (hi_ - lo_)
        key = ("d", i)
        if self.dval[i] > 0 and needs.get(key, 0) < self.dval[i]:
            needs[key] = self.dval[i]
        self._emit_waits(e, needs)
        self.dval[i] += 16
        v = self.dval[i]
        oap = out.ap if isinstance(out, View) else out
        iap = in_.ap if isinstance(in_, View) else in_
        e.h.dma_start(out=oap, in_=iap, **kw).then_inc(self.dsem[i], 16)
        self.nins += 1
        for t in writes:
            t.w = {key: v}
            t.r = {}
        for t in reads:
            t.r[key] = v

    def barrier(self):
        allev = {}
        for e in self.E.values():
            if e.count > 0:
                allev[("e", e.name)] = e.count
        for i in range(self.NDMA):
            if self.dval[i] > 0:
                allev[("d", i)] = self.dval[i]
        for e in self.E.values():
            self._emit_waits(e, dict(allev))

    def ld(self, dst, src, q="sp", **kw):
        self.dma(q, dst, src, **kw)

    def ldn(self, dst, src, q="sp"):
        self.dma(q, dst, src, allow_slow_non_contiguous=True)

    def st(self, dst, src, q="pool"):
        self.dma(q, dst, src)

    def mm(self, out, lhsT, rhs, start=True, stop=True, sig=None):
        if sig is None:
            sig = stop
        self.op("pe", lambda h: h.matmul(out.ap, lhsT=lhsT.ap, rhs=rhs.ap, start=start, stop=stop),
                reads=[lhsT, rhs], writes=[out], signal=sig)

    def tr(self, out, in_, ident, sig=True):
        self.op("pe", lambda h: h.transpose(out=out.ap, in_=in_.ap, identity=ident.ap),
                reads=[in_, ident], writes=[out], signal=sig)

    def act(self, out, in_, func, bias=None, scale=None, accum=None):
        reads = [in_]
        kw = {}
        if bias is not None:
            if isinstance(bias, View):
                reads.append(bias)
                kw["bias"] = bias.ap
            else:
                kw["bias"] = float(bias)
        if scale is not None:
            if isinstance(scale, View):
                reads.append(scale)
                kw["scale"] = scale.ap
            else:
                kw["scale"] = float(scale)
        writes = [out]
        if accum is not None:
            kw["accum_out"] = accum.ap
            writes.append(accum)
        self.op("act", lambda h: h.activation(out=out.ap, in_=in_.ap, func=func, **kw), reads=reads, writes=writes)

    def tt(self, eng, out, in0, in1, op):
        self.op(eng, lambda h: h.tensor_tensor(out=out.ap, in0=in0.ap, in1=in1.ap, op=op), reads=[in0, in1], writes=[out])

    def ts(self, eng, out, in0, s1, s2, op0, op1=None):
        reads = [in0]
        a1 = s1.ap if isinstance(s1, View) else s1
        a2 = s2.ap if isinstance(s2, View) else s2
        if isinstance(s1, View):
            reads.append(s1)
        if isinstance(s2, View):
            reads.append(s2)
        if op1 is None:
            self.op(eng, lambda h: h.tensor_scalar(out=out.ap, in0=in0.ap, scalar1=a1, scalar2=None, op0=op0), reads=reads, writes=[out])
        else:
            self.op(eng, lambda h: h.tensor_scalar(out=out.ap, in0=in0.ap, scalar1=a1, scalar2=a2, op0=op0, op1=op1), reads=reads, writes=[out])

    def stt(self, out, in0, scalar, in1, op0, op1):
        reads = [in0, in1]
        sa = scalar.ap if isinstance(scalar, View) else scalar
        if isinstance(scalar, View):
            reads.append(scalar)
        self.op("dve", lambda h: h.scalar_tensor_tensor(out=out.ap, in0=in0.ap, scalar=sa, in1=in1.ap, op0=op0, op1=op1), reads=reads, writes=[out])

    def copy(self, eng, out, in_):
        if eng == "act":
            self.op("act", lambda h: h.activation(out=out.ap, in_=in_.ap, func=AF.Identity), reads=[in_], writes=[out])
        else:
            self.op(eng, lambda h: h.tensor_copy(out=out.ap, in_=in_.ap), reads=[in_], writes=[out])

    def recip(self, out, in_):
        self.op("dve", lambda h: h.reciprocal(out=out.ap, in_=in_.ap), reads=[in_], writes=[out])

    def memset(self, eng, out, val):
        self.op(eng, lambda h: h.memset(out.ap, val), writes=[out])

    def reduce(self, out, in_, op, axis=AX.X):
        self.op("dve", lambda h: h.tensor_reduce(out=out.ap, in_=in_.ap, axis=axis, op=op), reads=[in_], writes=[out])

    def scan(self, out, d0, d1, init, op0, op1):
        self.op("dve", lambda h: h.tensor_tensor_scan(out=out.ap, data0=d0.ap, data1=d1.ap, initial=init, op0=op0, op1=op1), reads=[d0, d1], writes=[out])


def run_pipe(gens, depth, side=None, side_every=8):
    gens = list(gens)
    active = []
    tick = 0
    while gens or active:
        while gens and len(active) < depth:
            active.append(gens.pop(0))
        for g in list(active):
            try:
                next(g)
            except StopIteration:
                active.remove(g)
        tick += 1
        if side is not None and tick % side_every == 0:
            try:
                next(side)
            except StopIteration:
                side = None
    if side is not None:
        for _ in side:
            pass


def _angles(pos, rot_dim, theta):
    inv = (theta ** (-np.arange(0, rot_dim, 2, dtype=np.float32) / rot_dim)).astype(np.float32)
    return pos.astype(np.float32)[:, None] * inv[None, :]


def make_consts():
    c = {}
    c["c_ident"] = np.eye(128, dtype=np.float32)
    T = TMAX
    pos = np.arange(T)
    rope = np.zeros((64, 6, T), np.float32)
    ar = _angles(pos // 64, 32, 10000.0)
    ac = _angles(pos % 64, 32, 10000.0)
    for d in range(64):
        a = ar if d < 32 else ac
        idx = (d % 32) % 16
        rope[d, 0] = np.cos(a[:, idx])
        rope[d, 1] = np.sin(a[:, idx])
    ab = _angles(pos, 8, 500000.0)
    for d in range(64):
        dd = d % 32
        if dd < 8:
            rope[d, 2] = np.cos(ab[:, dd % 4])
            rope[d, 3] = np.sin(ab[:, dd % 4])
        else:
            rope[d, 2] = 1.0
            rope[d, 3] = 0.0
    acc = _angles(pos, 64, 10000.0)
    for d in range(64):
        rope[d, 4] = np.cos(acc[:, d % 32])
        rope[d, 5] = np.sin(acc[:, d % 32])
    c["c_rope"] = rope
    pm = np.zeros((64, 3, 64), np.float32)
    for m in range(64):
        mm_ = m % 32
        base = m - mm_
        if mm_ < 16:
            pm[base + mm_ + 16, 0, m] = -1.0
        else:
            pm[base + mm_ - 16, 0, m] = 1.0
        if mm_ < 4:
            pm[base + mm_ + 4, 1, m] = -1.0
        elif mm_ < 8:
            pm[base + mm_ - 4, 1, m] = 1.0
        if m < 32:
            pm[m + 32, 2, m] = -1.0
        else:
            pm[m - 32, 2, m] = 1.0
    c["c_pm"] = pm
    c["c_ones"] = np.ones((128, 128), np.float32)
    h = np.arange(4, dtype=np.float32)
    lgf = np.log1p(-np.exp2(-5.0 - h)).astype(np.float32)
    lgb = lgf[::-1].copy()
    i = np.arange(128, dtype=np.float32)
    diff = i[None, :] - i[:, None]
    rm = np.zeros((128, 4, 128), np.float32)
    for hh in range(4):
        rm[:, hh, :] = np.where(diff >= 0, np.exp(lgf[hh] * np.maximum(diff, 0)), 0.0) + \
            np.where(diff <= 0, np.exp(lgb[hh] * np.maximum(-diff, 0)), 0.0)
    c["c_retmask"] = rm
    qd = np.zeros((64, 2, 4, 128), np.float32)
    kd = np.zeros((128, 2, 4), np.float32)
    for hh in range(4):
        qd[:, 0, hh, :] = np.exp(lgf[hh] * (i + 1.0))[None, :]
        qd[:, 1, hh, :] = np.exp(lgb[hh] * (128.0 - i))[None, :]
        kd[:, 0, hh] = np.exp(lgf[hh] * (127.0 - i))
        kd[:, 1, hh] = np.exp(lgb[hh] * i)
    c["c_qdec"] = qd
    c["c_kdec"] = kd
    dm = np.zeros((128, 2, 5, 128), np.float32)
    p = np.arange(128)[:, None]
    f = np.arange(128)[None, :]
    dm[:, 0, 0, :] = -1.0 * (p > f)
    dm[:, 0, 1, :] = -1.0 * (f > p)
    dm[:, 0, 2, :] = (f > p)
    dm[:, 0, 3, :] = (f >= p)
    dm[:, 0, 4, :] = -1.0 * (f >= p)
    dm[:, 1, 0, :] = -1.0 * (p < f)
    dm[:, 1, 1, :] = -1.0 * (f < p)
    dm[:, 1, 2, :] = (f < p)
    dm[:, 1, 3, :] = (f <= p)
    dm[:, 1, 4, :] = -1.0 * (f <= p)
    c["c_dmask"] = dm.astype(np.float32)
    seg = np.ones((64, 512), np.float32)
    seg[:, ::128] = 0.0
    c["c_seg"] = seg
    return c


RET_G = [1.0 - 2.0 ** (-5 - h) for h in range(4)]

WEIGHT_SPECS = [
    ("norm_mix_pre", (2, 1024)), ("norm_mix_post", (2, 1024)), ("norm_ffn_pre", (2, 1024)), ("norm_ffn_post", (2, 1024)),
    ("w_in", (2, 1024, 3392)), ("w_out", (2, 1024, 1024)), ("a_q_gain", (2, 64)), ("a_k_gain", (2, 64)),
    ("b_lambda", (2, 4, 32)), ("b_subln_gain", (2, 64)), ("c_gn_gain", (2, 256)), ("d_mu_prev", (2, 1088)),
    ("d_mu_next", (2, 1088)), ("d_w0", (2, 2, 256)), ("d_w_up", (2, 2, 64, 256)), ("d_a0", (2, 256)),
    ("d_a_up", (2, 64, 256)), ("d_g_up", (2, 128, 256)), ("d_k_k", (2, 256)), ("d_k_a", (2, 256)),
    ("d_r_k", (2, 4, 64)), ("d_gn_w", (2, 256)), ("d_gn_b", (2, 256)),
    ("ffn_w_gate", (2, 1024, 2816)), ("ffn_w_up", (2, 1024, 2816)), ("ffn_w_down", (2, 2816, 1024)),
]


class Ctx:
    pass


def load_consts(fw, C, ph):
    K = Ctx()
    K.ident = ph.sb([128, 128], BF16, "ident")
    fw.ld(K.ident[:], C.d["c_ident"], q="pool")
    K.identf = ph.sb([128, 128], F32, "identf")
    fw.ld(K.identf[:], C.d["c_ident"])
    K.ones_bf = ph.sb([128, 128], BF16, "ones_bf")
    fw.ld(K.ones_bf[:], C.d["c_ones"], q="pool")
    K.ones_f = ph.sb([128, 128], F32, "ones_f")
    fw.ld(K.ones_f[:], C.d["c_ones"])
    K.pm = []
    for i in range(3):
        t = ph.sb([64, 64], BF16, "pm%d" % i)
        fw.ld(t[:], C.d["c_pm"][:, i, :], q="pool")
        K.pm.append(t)
    return K


def rms_rows(fw, xt, gain, hb, tmp, ss, rstd):
    fw.act(tmp[:], xt[:], AF.Square, accum=ss[:])
    fw.act(rstd[:], ss[:], AF.Sqrt, bias=EPS, scale=1.0 / D_MODEL)
    fw.recip(rstd[:], rstd[:])
    fw.stt(hb[:], xt[:], rstd[:, 0:1], gain[:], ALU.mult, ALU.mult)


def make_hT(fw, K, src_rows, gain, hT, j, P):
    xt = P.xt.next()
    fw.ld(xt[:], src_rows)
    hb = P.hb.next()
    ss = P.ss.next()
    rstd = P.rstd.next()
    tmp = P.tmp.next()
    rms_rows(fw, xt, gain, hb, tmp, ss, rstd)
    tp = P.tp.next()
    for k in range(8):
        fw.tr(tp[:, k, :], hb[:, k * 128:(k + 1) * 128], K.ident[:], sig=(k == 7))
    fw.copy("dve", hT[:, :, j * 128:(j + 1) * 128], tp[:])
    return xt


def phase_inproj(fw, C, K, l, si):
    T_ = C.seqs[si]
    src = C.xin[si] if l == 0 else C.y[si]
    d = C.d
    ph = fw.phase()
    W = ph.sb([128, 8, IN_COLS], BF16, "W")
    for k in range(8):
        fw.ld(W[:, k, :], d["w_in"][l, k * 128:(k + 1) * 128, :], q="pool")
    gain = ph.sb([128, 1024], F32, "gain")
    fw.ld(gain[:], d["norm_mix_pre"][l].partition_broadcast(128))
    gqk = ph.sb([64, 2], F32, "gqk")
    fw.ld(gqk[:, 0:1], d["a_q_gain"][l].rearrange("(p o) -> p o", o=1))
    fw.ld(gqk[:, 1:2], d["a_k_gain"][l].rearrange("(p o) -> p o", o=1))
    P = Ctx()
    P.xt = ph.sbpool(2, [128, 1024], F32, "xt")
    P.hb = ph.sbpool(2, [128, 1024], BF16, "hb")
    P.tmp = ph.sbpool(1, [128, 1024], F32, "tmp")
    P.ss = ph.sbpool(2, [128, 1], F32, "ss")
    P.rstd = ph.sbpool(2, [128, 1], F32, "rstd")
    P.tp = ph.pspool(1, [128, 8, 128], BF16, "tp")
    hTs = ph.sbpool(2, [128, 8, 512], BF16, "hT")
    tabs = ph.sbpool(2, [64, 6, 512], F32, "tab")
    zps = ph.pspool(4, [128, 512], F32, "zp")
    aps = ph.pspool(2, [128, 512], F32, "ap")
    tps = ph.pspool(1, [128, 512], F32, "tmps")
    f32p = ph.sbpool(24, [64, 512], F32, "f32p")
    bfp = ph.sbpool(16, [64, 512], BF16, "bfp")
    f32w = ph.sbpool(3, [128, 512], F32, "f32w")
    bfw = ph.sbpool(3, [128, 512], BF16, "bfw")
    chunks = []
    for h in range(4):
        chunks.append(("A", 0 + 64 * h, C.QA, h, 0))
    for h in range(2):
        chunks.append(("A", 256 + 64 * h, C.KA, h, 1))
    for h in range(4):
        chunks.append(("B", 512 + 64 * h, C.QB, h, 0))
    for h in range(4):
        chunks.append(("B", 768 + 64 * h, C.KB, h, 0))
    for h in range(4):
        chunks.append(("C", 1280 + 64 * h, C.QC, h, 0))
    for h in range(4):
        chunks.append(("C", 1536 + 64 * h, C.KC, h, 0))
    for h in range(4):
        chunks.append(("G", 2048 + 64 * h, C.GC, h, 0))
    for j in range(15):
        chunks.append(("D", 2304 + 64 * j, C.ZD, j, 0))
    DEPTHP = 4

    def chunk_gen(kind, col, dst, slot, gi, hT, tab, t0):
        zp = zps.next()
        for k in range(8):
            fw.mm(zp[0:64, :], W[:, k, col:col + 64], hT[:, k, :], start=(k == 0), stop=(k == 7))
        z = zp[0:64, :]
        yield
        if kind in ("A", "B", "C"):
            ti = {"A": 0, "B": 2, "C": 4}[kind]
            pmi = {"A": 0, "B": 1, "C": 2}[kind]
            qn = bfp.next()
            qf = f32p.next()
            fw.copy("act", qf[:], z)
            if kind == "A":
                sq = bfp.next()
                fw.act(sq[:], z, AF.Square)
                yield
                ms = aps.next()
                fw.mm(ms[0:64, :], K.ones_bf[0:64, 0:64], sq[:])
                rs = f32p.next()
                fw.ts("dve", rs[:], ms[0:64, :], 1.0 / 64, EPS, ALU.mult, ALU.add)
                fw.act(rs[:], rs[:], AF.Ln)
                fw.act(rs[:], rs[:], AF.Exp, scale=-0.5)
                qf2 = f32p.next()
                fw.stt(qf2[:], qf[:], gqk[:, gi:gi + 1], rs[:], ALU.mult, ALU.mult)
                qf = qf2
            fw.copy("act", qn[:], qf[:])
            yield
            pq = aps.next()
            fw.mm(pq[0:64, :], K.pm[pmi][:], qn[:])
            t2 = f32p.next()
            fw.tt("dve", t2[:], pq[0:64, :], tab[:, ti + 1, :], ALU.mult)
            t1 = f32p.next()
            fw.tt("pool", t1[:], qf[:], tab[:, ti, :], ALU.mult)
            ob = bfp.next()
            fw.tt("pool", ob[:], t1[:], t2[:], ALU.add)
            fw.st(dst[:, slot, t0:t0 + 512], ob[:])
        elif kind == "G":
            ob = bfp.next()
            fw.act(ob[:], z, AF.Silu)
            fw.st(dst[:, slot, t0:t0 + 512], ob[:])
        else:
            of = f32p.next()
            fw.copy("act", of[:], z)
            fw.st(dst[:, slot, t0:t0 + 512], of[:])

    def tail_gen(hT, t0):
        zp = zps.next()
        for k in range(8):
            fw.mm(zp[:, :], W[:, k, 3264:3392], hT[:, k, :], start=(k == 0), stop=(k == 7))
        of = f32w.next()
        fw.copy("act", of[:], zp[:])
        fw.st(C.ZG[:, t0:t0 + 512], of[:])
        yield
        for j in range(4):
            tp_ = tps.next()
            for k in range(8):
                fw.mm(tp_[:, 0:128], hT[:, k, j * 128:(j + 1) * 128], W[:, k, 384:512], start=(k == 0), stop=(k == 7), sig=False)
            for k in range(8):
                fw.mm(tp_[:, 128:384], hT[:, k, j * 128:(j + 1) * 128], W[:, k, 1024:1280], start=(k == 0), stop=(k == 7))
            ob = bfw.next()
            fw.copy("act", ob[:, 0:384], tp_[:, 0:384])
            r0 = t0 + j * 128
            fw.st(C.VA[r0:r0 + 128, :], ob[:, 0:128])
            fw.st(C.VB[r0:r0 + 128, :], ob[:, 128:384])
            yield
            tp2 = tps.next()
            for k in range(8):
                fw.mm(tp2[:, 0:256], hT[:, k, j * 128:(j + 1) * 128], W[:, k, 1792:2048], start=(k == 0), stop=(k == 7))
            ob2 = bfw.next()
            fw.copy("dve", ob2[:, 0:256], tp2[:, 0:256])
            fw.st(C.VC[r0:r0 + 128, :], ob2[:, 0:256])
            yield

    def hT_gen(b, hT):
        t0 = b * 512
        for j in range(4):
            make_hT(fw, K, src[t0 + j * 128:t0 + (j + 1) * 128, :], gain, hT, j, P)
            yield

    def run_pipe(gens, depth, side=None):
        gens = list(gens)
        active = []
        tick = 0
        while gens or active:
            while gens and len(active) < depth:
                active.append(gens.pop(0))
            for g in list(active):
                try:
                    next(g)
                except StopIteration:
                    active.remove(g)
            tick += 1
            if side is not None and tick % 8 == 0:
                try:
                    next(side)
                except StopIteration:
                    side = None
        if side is not None:
            for _ in side:
                pass

    nb = T_ // 512
    hT_cur = hTs.next()
    for _ in hT_gen(0, hT_cur):
        pass
    for b in range(nb):
        t0 = b * 512
        tab = tabs.next()
        fw.ld(tab[:], d["c_rope"][:, :, t0:t0 + 512])
        hT = hT_cur
        side = None
        if b + 1 < nb:
            hT_cur = hTs.next()
            side = hT_gen(b + 1, hT_cur)
        gens = [chunk_gen(kind, col, dst, slot, gi, hT, tab, t0) for (kind, col, dst, slot, gi) in chunks]
        gens.append(tail_gen(hT, t0))
        run_pipe(gens, DEPTHP, side)
    ph.close()


def attn_core(fw, ph, K, T_, QT, KT, V, heads, negM, scale, emit):
    LOOK = 2
    sps = ph.pspool(4, [128, 512], F32, "sps")
    ops_ = ph.pspool(2, [128, 512], F32, "ops")
    pts = ph.sbpool(4, [128, 512], BF16, "pt")
    osbs = ph.sbpool(4, [65, 512], F32, "osb")
    nkt = T_ // 128
    iters = []
    for qb in range(T_ // 512):
        for hd in heads:
            for kt in range(nkt):
                iters.append((qb, hd, kt))
    pend = []
    state = {"o_ps": None}

    def do_pv(item):
        (qb, hd, kt, pt) = item
        (qs, lo, hi, ks, vs, tag) = hd
        if kt == 0:
            state["o_ps"] = ops_.next()
        o_ps = state["o_ps"]
        fw.mm(o_ps[0:65, :], V[:, kt, vs, :], pt[:], start=(kt == 0), stop=(kt == nkt - 1))
        if kt == nkt - 1:
            osb = osbs.next()
            fw.copy("dve", osb[:], o_ps[0:65, :])
            emit(qb, tag, osb)

    for (qb, hd, kt) in iters:
        (qs, lo, hi, ks, vs, tag) = hd
        q0 = qb * 512
        st = sps.next()
        fw.mm(st[:, :], KT[lo:hi, ks, kt * 128:(kt + 1) * 128], QT[lo:hi, qs, q0:q0 + 512])
        pt = pts.next()
        if negM is None:
            fw.act(pt[:], st[:], AF.Exp, scale=scale)
        else:
            fw.act(pt[:], st[:], AF.Exp, scale=scale, bias=negM[:, 0:1])
        pend.append((qb, hd, kt, pt))
        if len(pend) > LOOK:
            do_pv(pend.pop(0))
    while pend:
        do_pv(pend.pop(0))


def attn_core2(fw, ph, T_, groups, V, negM, scale, emit, bcs):
    LOOK = 2
    sps = ph.pspool(3, [128, 1024], F32, "sps")
    ops_ = bcs
    pts = ph.sbpool(4, [128, 1024], BF16, "pt")
    osbs = ph.sbpool(4, [65, 512], F32, "osb")
    nkt = T_ // 128
    iters = []
    for qb in range(T_ // 512):
        for g in groups:
            for kt in range(nkt):
                iters.append((qb, g, kt))
    pend = []
    state = {}
    import os
    NW = int(os.environ.get("DBG_WARM", "0"))
    if NW:
        wz = pts.next()
        fw.memset("pool", wz[:], 1.0)
        st0 = sps.next()
        for i in range(NW):
            fw.mm(st0[:, 0:512], wz[:, 0:128], wz[:, 512:1024], sig=(i == NW - 1))

    def do_pv(item):
        (qb, g, kt, pt) = item
        (kt_ap, q_ap, vs, tag) = g
        if kt == 0:
            state["o"] = [ops_.next(), ops_.next()]
        o = state["o"]
        for j in range(2):
            fw.mm(o[j][0:65, :], V[:, kt, vs[j], :], pt[:, j * 512:(j + 1) * 512], start=(kt == 0), stop=(kt == nkt - 1))
        if kt == nkt - 1:
            osb = [osbs.next(), osbs.next()]
            for j in range(2):
                fw.copy("dve", osb[j][:], o[j][0:65, :])
            emit(qb, tag, osb)

    for (qb, g, kt) in iters:
        (kt_ap, q_ap, vs, tag) = g
        st = sps.next()
        for j in range(2):
            fw.mm(st[:, j * 512:(j + 1) * 512], kt_ap(j, kt), q_ap(j, qb), sig=(j == 1))
        pt = pts.next()
        if negM is None:
            fw.act(pt[:], st[:], AF.Exp, scale=scale)
        else:
            fw.act(pt[:], st[:], AF.Exp, scale=scale, bias=negM[:, 0:1])
        pend.append((qb, g, kt, pt))
        if len(pend) > LOOK:
            do_pv(pend.pop(0))
    while pend:
        do_pv(pend.pop(0))


def bcast_recip(fw, K, osb, recs, hls, bcs, mulv):
    rec = recs.next()
    fw.act(rec[64:65, :], osb[64:65, :], AF.Ln)
    fw.act(rec[64:65, :], rec[64:65, :], AF.Exp, scale=-1.0)
    if mulv is not None:
        fw.ts("dve", rec[64:65, :], rec[64:65, :], mulv[64:65, 0:1], None, ALU.mult)
    hl = hls.next()
    fw.copy("dve", hl[64:65, 0, :], rec[64:65, :])
    fw.tt("dve", hl[64:65, 1, :], rec[64:65, :], hl[64:65, 0, :], ALU.subtract)
    bc = bcs.next()
    fw.mm(bc[0:64, :], K.ones_bf[64:65, 0:64], hl[64:65, 0, :], start=True, stop=False, sig=False)
    fw.mm(bc[0:64, :], K.ones_bf[64:65, 0:64], hl[64:65, 1, :], start=False, stop=True)
    return bc


def phase_attnA(fw, C, K, l, si):
    T_ = C.seqs[si]
    d = C.d
    ph = fw.phase()
    QT = ph.sb([128, 2, T_], BF16, "QT")
    KT = ph.sb([128, T_], BF16, "KT")
    V = ph.sb([128, T_ // 128, 2, 65], BF16, "V")
    for p in range(2):
        fw.ld(QT[0:64, p, :], C.QA[:, p, 0:T_])
        fw.ld(QT[64:128, p, :], C.QA[:, p + 2, 0:T_])
    for h in range(2):
        fw.ld(KT[64 * h:64 * h + 64, :], C.KA[:, h, 0:T_])
    fw.memset("pool", V[:, :, :, 64:65], 1.0)
    for h in range(2):
        fw.ld(V[:, :, h, 0:64], C.VA[0:T_, h * 64:(h + 1) * 64].rearrange("(n p) c -> p n c", p=128))
    g2 = ph.sb([128, 2, 64], F32, "g2")
    fw.ld(g2[:, 0, :], d["a_q_gain"][l].partition_broadcast(128))
    fw.ld(g2[:, 1, :], d["a_k_gain"][l].partition_broadcast(128))
    fw.tt("dve", g2[:], g2[:], g2[:], ALU.mult)
    mx = ph.sb([128, 2], F32, "mx")
    fw.reduce(mx[:], g2[:], ALU.max)
    negM = ph.sb([128, 1], F32, "negM")
    fw.tt("dve", negM[:], mx[:, 0:1], mx[:, 1:2], ALU.mult)
    fw.act(negM[:], negM[:], AF.Sqrt, scale=64.0)
    fw.ts("dve", negM[:], negM[:], -1.0, None, ALU.mult)
    recs = ph.sbpool(2, [65, 512], F32, "rec")
    hls = ph.sbpool(2, [65, 2, 512], BF16, "hl")
    bcs = ph.pspool(2, [128, 512], F32, "bc")
    outs = ph.sbpool(3, [64, 512], BF16, "ob")

    def emit(qb, p, osbs_):
        for j in range(2):
            h = p + 2 * j
            osb = osbs_[j]
            bc = bcast_recip(fw, K, osb, recs, hls, bcs, None)
            ob = outs.next()
            fw.tt("dve", ob[:], osb[0:64, :], bc[0:64, :], ALU.mult)
            fw.st(C.OT[h, :, qb * 512:(qb + 1) * 512], ob[:])

    groups = []
    for p in range(2):
        groups.append((lambda j, kt: KT[64 * j:64 * j + 64, kt * 128:(kt + 1) * 128],
                       (lambda p_: (lambda j, qb: QT[64 * j:64 * j + 64, p_, qb * 512:(qb + 1) * 512]))(p),
                       (0, 1), p))
    attn_core2(fw, ph, T_, groups, V, negM, 0.125, emit, bcs)
    ph.close()


def phase_attnB(fw, C, K, l, si):
    T_ = C.seqs[si]
    d = C.d
    lam_init = 0.8 - 0.6 * math.exp(-0.3 * l)
    ph = fw.phase()
    QT = ph.sb([128, 2, 2, T_], BF16, "QT")
    KT = ph.sb([128, 2, T_], BF16, "KT")
    V = ph.sb([128, T_ // 128, 4, 65], BF16, "V")
    for hh in range(2):
        fw.memset("pool", QT[64 * hh + 32:64 * hh + 64, :, 0, :], 0.0)
        fw.memset("pool", QT[64 * hh:64 * hh + 32, :, 1, :], 0.0)
    for p in range(2):
        for hh in range(2):
            h = p + 2 * hh
            fw.ld(QT[64 * hh:64 * hh + 32, p, 0, :], C.QB[0:32, h, 0:T_])
            fw.ld(QT[64 * hh + 32:64 * hh + 64, p, 1, :], C.QB[32:64, h, 0:T_])
            fw.ld(KT[64 * hh:64 * hh + 64, p, :], C.KB[:, h, 0:T_])
    fw.memset("pool", V[:, :, :, 64:65], 1.0)
    for h in range(4):
        fw.ld(V[:, :, h, 0:64], C.VB[0:T_, h * 64:(h + 1) * 64].rearrange("(n p) c -> p n c", p=128))
    lp = ph.sb([128, 4, 32], F32, "lp")
    fw.ld(lp[:], d["b_lambda"][l].partition_broadcast(128))
    pr = ph.sb([128, 2, 32], F32, "pr")
    fw.tt("dve", pr[:, 0, :], lp[:, 0, :], lp[:, 1, :], ALU.mult)
    fw.tt("dve", pr[:, 1, :], lp[:, 2, :], lp[:, 3, :], ALU.mult)
    sm = ph.sb([128, 2], F32, "sm")
    fw.reduce(sm[:], pr[:], ALU.add)
    fw.act(sm[:], sm[:], AF.Exp)
    nlam = ph.sb([128, 1], F32, "nlam")
    fw.tt("dve", nlam[:], sm[:, 1:2], sm[:, 0:1], ALU.subtract)
    fw.ts("dve", nlam[:], nlam[:], -lam_init, None, ALU.add)
    gsub = ph.sb([64, 1], F32, "gsub")
    fw.ld(gsub[:], d["b_subln_gain"][l].rearrange("(p o) -> p o", o=1))
    fw.ts("dve", gsub[:], gsub[:], 1.0 - lam_init, None, ALU.mult)
    recs = ph.sbpool(2, [65, 512], F32, "rec")
    hls = ph.sbpool(2, [65, 2, 512], BF16, "hl")
    bcs = ph.pspool(2, [128, 512], F32, "bc")
    o1s = ph.sbpool(6, [64, 512], F32, "o1")
    rss = ph.sbpool(2, [64, 512], F32, "rsb")
    sqs = ph.sbpool(2, [64, 512], BF16, "sq")
    outs = ph.sbpool(3, [64, 512], BF16, "ob")

    pend_o = {}

    def emit(qb, tag, osbs_):
        (p, s_) = tag
        for hh in range(2):
            h = p + 2 * hh
            osb = osbs_[hh]
            bc = bcast_recip(fw, K, osb, recs, hls, bcs, nlam if s_ == 1 else None)
            o_ = o1s.next()
            fw.tt("dve", o_[:], osb[0:64, :], bc[0:64, :], ALU.mult)
            if s_ == 0:
                pend_o[h] = o_
            else:
                finish(qb, h, pend_o.pop(h), o_)

    def finish(qb, h, o1, o2):
        fw.tt("pool", o1[:], o1[:], o2[:], ALU.add)
        sq = sqs.next()
        fw.act(sq[:], o1[:], AF.Square)
        ms = bcs.next()
        fw.mm(ms[0:64, :], K.ones_bf[0:64, 0:64], sq[:])
        rs = rss.next()
        fw.ts("dve", rs[:], ms[0:64, :], 1.0 / 64, EPS, ALU.mult, ALU.add)
        fw.act(rs[:], rs[:], AF.Ln)
        fw.act(rs[:], rs[:], AF.Exp, scale=-0.5)
        ob = outs.next()
        fw.stt(ob[:], o1[:], gsub[:, 0:1], rs[:], ALU.mult, ALU.mult)
        fw.st(C.OT[4 + h, :, qb * 512:(qb + 1) * 512], ob[:])

    groups = []
    for p in range(2):
        for s_ in range(2):
            groups.append(((lambda p_: (lambda j, kt: KT[64 * j:64 * j + 64, p_, kt * 128:(kt + 1) * 128]))(p),
                           (lambda p_, s__: (lambda j, qb: QT[64 * j:64 * j + 64, p_, s__, qb * 512:(qb + 1) * 512]))(p, s_),
                           (p, p + 2), (p, s_)))
    attn_core2(fw, ph, T_, groups, V, None, 32 ** -0.5, emit, bcs)
    ph.close()


def phase_ret(fw, C, K, l, si):
    T_ = C.seqs[si]
    d = C.d
    n = T_ // 128
    ph = fw.phase()
    QT = ph.sb([64, 4, T_], BF16, "QT")
    KT = ph.sb([64, 4, T_], BF16, "KT")
    G = ph.sb([64, 4, T_], BF16, "G")
    V = ph.sb([128, n, 256], BF16, "V")
    for h in range(4):
        fw.ld(QT[:, h, :], C.QC[:, h, 0:T_])
        fw.ld(KT[:, h, :], C.KC[:, h, 0:T_])
        fw.ld(G[:, h, :], C.GC[:, h, 0:T_])
    fw.ld(V[:], C.VC[0:T_, :].rearrange("(n p) c -> p n c", p=128))
    mask = ph.sb([128, 4, 128], F32, "mask")
    fw.ld(mask[:], d["c_retmask"])
    qdec = ph.sb([64, 2, 4, 128], F32, "qdec")
    fw.ld(qdec[:], d["c_qdec"])
    kdec = ph.sb([128, 2, 4], F32, "kdec")
    fw.ld(kdec[:], d["c_kdec"])
    gng = ph.sb([64, 4], F32, "gng")
    fw.ldn(gng[:], d["c_gn_gain"][l].rearrange("(h p) -> p h", p=64))
    Sf = ph.sb([64, n, 4, 64], BF16, "Sf")
    Sb = ph.sb([64, n, 4, 64], BF16, "Sb")
    cur = [ph.sb([64, 4, 64], F32, "curf"), ph.sb([64, 4, 64], F32, "curb")]
    ktp = ph.pspool(2, [128, 4, 64], BF16, "ktp")
    kds = ph.sbpool(4, [128, 4, 64], BF16, "kds")
    kvp = ph.pspool(2, [128, 512], F32, "kvp")
    gamt = ph.sb([64, 2, 4], F32, "gamt")
    for dr in range(2):
        for h in range(4):
            g_ = (RET_G[h] if dr == 0 else RET_G[3 - h]) ** 128
            fw.memset("pool", gamt[:, dr, h:h + 1], g_)
    tmpc = [ph.sbpool(2, [64, 4, 64], F32, "tmpc"), ph.sbpool(2, [64, 4, 64], F32, "tmpc")]

    def state_gen(dr):
        fw.memset("dve", cur[dr][:], 0.0)
        order = range(n) if dr == 0 else range(n - 1, -1, -1)
        Sx = Sf if dr == 0 else Sb
        for c in order:
            fw.copy("act", Sx[:, c, :, :], cur[dr][:])
            if (dr == 0 and c == n - 1) or (dr == 1 and c == 0):
                break
            tp = ktp.next()
            for h in range(4):
                fw.tr(tp[:, h, :], KT[:, h, c * 128:(c + 1) * 128], K.ident[0:64, 0:64], sig=(h == 3))
            kd = kds.next()
            fw.tt("dve", kd[:], tp[:], kdec.bc((slice(None), dr, slice(None)), 2, [128, 4, 64]), ALU.mult)
            tmp = tmpc[dr].next()
            fw.tt("pool", tmp[:], cur[dr][:], gamt.bc((slice(None), dr, slice(None)), 2, [64, 4, 64]), ALU.mult)
            yield
            kv = kvp.next()
            for h in range(4):
                fw.mm(kv[0:64, h * 64:(h + 1) * 64], kd[:, h, :], V[:, c, h * 64:(h + 1) * 64], sig=(h == 3))
            fw.tt("dve", cur[dr][:], tmp[:], View(kv, kv.ap[0:64, 0:256].rearrange("p (h e) -> p h e", h=4)), ALU.add)
            yield

    run_pipe([state_gen(0), state_gen(1)], 2)
    inp = ph.pspool(2, [128, 512], F32, "inp")
    ins_ = ph.sbpool(5, [128, 4, 128], BF16, "ins")
    qds = ph.sbpool(5, [64, 2, 4, 128], BF16, "qds")
    otp = ph.pspool(1, [128, 512], F32, "otp")
    osb = ph.sbpool(5, [64, 4, 128], F32, "osb")
    sqs = ph.sbpool(5, [64, 4, 128], BF16, "sq")
    msp = ph.pspool(1, [128, 512], F32, "msp")
    rss = ph.sbpool(2, [64, 4, 128], F32, "rs")
    obs = ph.sbpool(3, [64, 4, 128], BF16, "ob")

    def out_gen(c):
        cs = slice(c * 128, (c + 1) * 128)
        ip = inp.next()
        for h in range(4):
            fw.mm(ip[:, h * 128:(h + 1) * 128], KT[:, h, cs], QT[:, h, cs], sig=(h == 3))
        it = ins_.next()
        fw.tt("dve", it[:], View(ip, ip.ap[:, :].rearrange("p (h i) -> p h i", h=4)), mask[:], ALU.mult)
        qd = qds.next()
        for dr in range(2):
            fw.tt("pool", qd[:, dr, :, :], QT[:, :, cs], qdec[:, dr, :, :], ALU.mult)
        yield
        op_ = otp.next()
        for h in range(4):
            o_h = op_[0:64, h * 128:(h + 1) * 128]
            fw.mm(o_h, V[:, c, h * 64:(h + 1) * 64], it[:, h, :], start=True, stop=False, sig=False)
            fw.mm(o_h, Sf[:, c, h, :], qd[:, 0, h, :], start=False, stop=False, sig=False)
            fw.mm(o_h, Sb[:, c, h, :], qd[:, 1, h, :], start=False, stop=True, sig=(h == 3))
        o = osb.next()
        fw.ts("dve", o[:], View(op_, op_.ap[0:64, :].rearrange("p (h i) -> p h i", h=4)), 0.125, None, ALU.mult)
        sq = sqs.next()
        fw.act(sq[:], o[:], AF.Square)
        yield
        ms = msp.next()
        fw.mm(ms[0:64, :], K.ones_bf[0:64, 0:64], View(sq, sq.ap[:].rearrange("p h i -> p (h i)")))
        rs = rss.next()
        fw.ts("dve", View(rs, rs.ap[:].rearrange("p h i -> p (h i)")), ms[0:64, :], 1.0 / 64, EPS, ALU.mult, ALU.add)
        fw.act(rs[:], rs[:], AF.Ln)
        fw.act(rs[:], rs[:], AF.Exp, scale=-0.5)
        fw.tt("dve", o[:], o[:], rs[:], ALU.mult)
        fw.tt("pool", o[:], o[:], gng.bc((slice(None), slice(None)), 2, [64, 4, 128]), ALU.mult)
        ob = obs.next()
        fw.tt("dve", ob[:], o[:], G[:, :, cs], ALU.mult)
        for h in range(4):
            fw.st(C.OT[8 + h, :, cs], ob[:, h, :])

    run_pipe([out_gen(c) for c in range(n)], 3)
    ph.close()


def phase_outproj(fw, C, K, l, si):
    T_ = C.seqs[si]
    d = C.d
    src = C.xin[si] if l == 0 else C.y[si]
    ph = fw.phase()
    W = ph.sb([128, 8, 1024], BF16, "Wo")
    for k in range(8):
        fw.ld(W[:, k, :], d["w_out"][l, k * 128:(k + 1) * 128, :], q="pool")
    gain = ph.sb([128, 1024], F32, "gain")
    fw.ld(gain[:], d["norm_mix_post"][l].partition_broadcast(128))
    oTs = ph.sbpool(2, [128, 8, 512], BF16, "oT")
    xts = ph.sbpool(4, [128, 1024], F32, "xt")
    mps = ph.pspool(8, [128, 512], F32, "mp")
    tmps = ph.sbpool(4, [128, 1024], F32, "tmp")
    sss = ph.sbpool(4, [128, 2], F32, "ss")
    rstds = ph.sbpool(4, [128, 1], F32, "rstd")

    def sub_gen(oT, j, r0):
        xt = xts.next()
        fw.ld(xt[:], src[r0:r0 + 128, :])
        m = [mps.next(), mps.next()]
        for half in range(2):
            for k in range(8):
                fw.mm(m[half][:, :], oT[:, k, j * 128:(j + 1) * 128], W[:, k, half * 512:(half + 1) * 512], start=(k == 0), stop=(k == 7))
        tmp = tmps.next()
        ss = sss.next()
        for half in range(2):
            fw.act(tmp[:, half * 512:(half + 1) * 512], m[half][:, :], AF.Square, accum=ss[:, half:half + 1])
        yield
        rstd = rstds.next()
        fw.tt("dve", rstd[:], ss[:, 0:1], ss[:, 1:2], ALU.add)
        fw.ts("dve", rstd[:], rstd[:], 1.0 / D_MODEL, EPS, ALU.mult, ALU.add)
        fw.act(rstd[:], rstd[:], AF.Ln)
        fw.act(rstd[:], rstd[:], AF.Exp, scale=-0.5)
        yield
        for half in range(2):
            hs = slice(half * 512, (half + 1) * 512)
            fw.stt(tmp[:, hs], m[half][:, :], rstd[:, 0:1], gain[:, hs], ALU.mult, ALU.mult)
        fw.tt("pool", tmp[:], tmp[:], xt[:], ALU.add)
        fw.st(C.y[si][r0:r0 + 128, :], tmp[:])

    gens = []
    for b in range(T_ // 512):
        t0 = b * 512
        oT = oTs.next()

        def ldgen(oT=oT, t0=t0):
            for k in range(8):
                fw.ld(oT[0:64, k, :], C.OT[2 * k, :, t0:t0 + 512])
                fw.ld(oT[64:128, k, :], C.OT[2 * k + 1, :, t0:t0 + 512])
            return
            yield
        gens.append(("ld", oT, t0))
        for j in range(4):
            gens.append(("sub", oT, j, t0 + j * 128))

    def all_gens():
        for g in gens:
            if g[0] == "ld":
                _, oT, t0 = g
                for k in range(8):
                    fw.ld(oT[0:64, k, :], C.OT[2 * k, :, t0:t0 + 512])
                    fw.ld(oT[64:128, k, :], C.OT[2 * k + 1, :, t0:t0 + 512])
            else:
                yield sub_gen(g[1], g[2], g[3])

    class LazyList(list):
        pass
    pending = all_gens()
    active = []
    done = False
    while not done or active:
        while not done and len(active) < 3:
            try:
                active.append(next(pending))
            except StopIteration:
                done = True
        for g in list(active):
            try:
                next(g)
            except StopIteration:
                active.remove(g)
    ph.close()


def phase_ffn(fw, C, K, l, si):
    T_ = C.seqs[si]
    d = C.d
    TB = 1024
    ph = fw.phase()
    gain = ph.sb([128, 1024], F32, "gain")
    fw.ld(gain[:], d["norm_ffn_pre"][l].partition_broadcast(128))
    gpost = ph.sb([128, 1024], F32, "gpost")
    fw.ld(gpost[:], d["norm_ffn_post"][l].partition_broadcast(128))
    P = Ctx()
    P.xt = ph.sbpool(2, [128, 1024], F32, "xt")
    P.hb = ph.sbpool(2, [128, 1024], BF16, "hb")
    P.tmp = ph.sbpool(2, [128, 1024], F32, "tmp")
    P.ss = ph.sbpool(2, [128, 1], F32, "ss")
    P.rstd = ph.sbpool(2, [128, 1], F32, "rstd")
    P.tp = ph.pspool(1, [128, 8, 128], BF16, "tp")
    hTs = ph.sbpool(1, [128, 8, TB], BF16, "hT")
    act = ph.sb([128, NFF, TB], BF16, "act")
    FG = 4
    wgs = ph.sbpool(2, [128, 8, 128 * FG], BF16, "wg")
    wus = ph.sbpool(2, [128, 8, 128 * FG], BF16, "wu")
    Wd = [ph.sb([128, 1024], BF16, "Wd%d" % f) for f in range(NFF)]
    gps = ph.pspool(2, [128, 512], F32, "gp")
    ups = ph.pspool(2, [128, 512], F32, "up")
    dps = ph.pspool(2, [128, 512], F32, "dp")
    sgs = ph.sbpool(2, [128, 512], F32, "sg")
    sss = ph.sbpool(4, [128, 2], F32, "ss2")
    dxt = ph.sbpool(3, [128, 1024], F32, "dxt")
    dtmp = ph.sbpool(3, [128, 1024], F32, "dtmp")
    drstd = ph.sbpool(4, [128, 1], F32, "drstd")
    wg_d = d["ffn_w_gate"][l].rearrange("(k p) n -> p k n", p=128)
    wu_d = d["ffn_w_up"][l].rearrange("(k p) n -> p k n", p=128)
    def hT_gen(b, hT_):
        t0_ = b * TB
        for j in range(TB // 128):
            make_hT(fw, K, C.y[si][t0_ + j * 128:t0_ + (j + 1) * 128, :], gain, hT_, j, P)
            yield

    nblk = T_ // TB
    for b in range(nblk):
        t0 = b * TB
        hT = hTs.next()
        for _ in hT_gen(b, hT):
            pass
        side = None
        for f in range(NFF):
            if b == 0:
                fw.ld(Wd[f][:], d["ffn_w_down"][l, f * 128:(f + 1) * 128, :], q="pool")
            if f % FG == 0:
                nf = min(FG, NFF - f)
                wg_t = wgs.next()
                fw.ld(wg_t[:, :, 0:128 * nf], wg_d[:, :, f * 128:(f + nf) * 128], q="pool")
                wu_t = wus.next()
                fw.ld(wu_t[:, :, 0:128 * nf], wu_d[:, :, f * 128:(f + nf) * 128], q="pool")
            fo = (f % FG) * 128
            for tb in range(TB // 512):
                ts_ = slice(tb * 512, (tb + 1) * 512)
                gp = gps.next()
                up = ups.next()
                for k in range(8):
                    fw.mm(gp[:, :], wg_t[:, k, fo:fo + 128], hT[:, k, ts_], start=(k == 0), stop=(k == 7))
                for k in range(8):
                    fw.mm(up[:, :], wu_t[:, k, fo:fo + 128], hT[:, k, ts_], start=(k == 0), stop=(k == 7))
                sg = sgs.next()
                fw.act(sg[:], gp[:, :], AF.Silu)
                fw.tt("dve", act[:, f, ts_], sg[:], up[:, :], ALU.mult)
        ph_all = Pool(dps.items + gps.items + ups.items)
        if side is not None:
            for _ in side:
                pass
        def down_gen(j):
            r0 = t0 + j * 128
            bk = [ph_all.next(), ph_all.next()]
            xt = dxt.next()
            fw.ld(xt[:], C.y[si][r0:r0 + 128, :])
            for f in range(NFF):
                for half in range(2):
                    fw.mm(bk[half][:, :], act[:, f, j * 128:(j + 1) * 128], Wd[f][:, half * 512:(half + 1) * 512],
                          start=(f == 0), stop=(f == NFF - 1))
            tmp = dtmp.next()
            ss = sss.next()
            for half in range(2):
                fw.act(tmp[:, half * 512:(half + 1) * 512], bk[half][:, :], AF.Square, accum=ss[:, half:half + 1])
            yield
            rstd = drstd.next()
            fw.tt("dve", rstd[:], ss[:, 0:1], ss[:, 1:2], ALU.add)
            fw.ts("dve", rstd[:], rstd[:], 1.0 / D_MODEL, EPS, ALU.mult, ALU.add)
            fw.act(rstd[:], rstd[:], AF.Ln)
            fw.act(rstd[:], rstd[:], AF.Exp, scale=-0.5)
            yield
            for half in range(2):
                hs = slice(half * 512, (half + 1) * 512)
                fw.stt(tmp[:, hs], bk[half][:, :], rstd[:, 0:1], gpost[:, hs], ALU.mult, ALU.mult)
            fw.tt("pool", tmp[:], tmp[:], xt[:], ALU.add)
            fw.st(C.y[si][r0:r0 + 128, :], tmp[:])

        run_pipe([down_gen(j) for j in range(TB // 128)], 3)
    ph.close()


def phase_rwkv(fw, C, K, l, si):
    T_ = C.seqs[si]
    d = C.d
    n = T_ // 128
    CW = math.exp(-0.5)
    ph = fw.phase()
    mu = ph.sb([64, 3, 15], F32, "mu")
    fw.ldn(mu[:, 1, :], d["d_mu_prev"][l, 0:960].rearrange("(j p) -> p j", p=64))
    fw.ldn(mu[:, 2, :], d["d_mu_next"][l, 0:960].rearrange("(j p) -> p j", p=64))
    fw.tt("dve", mu[:, 0, :], mu[:, 1, :], mu[:, 2, :], ALU.add)
    fw.ts("dve", mu[:, 0, :], mu[:, 0, :], -1.0, 1.0, ALU.mult, ALU.add)
    mug = ph.sb([128, 3], F32, "mug")
    fw.ld(mug[:, 1:2], d["d_mu_prev"][l, 960:1088].rearrange("(p o) -> p o", o=1))
    fw.ld(mug[:, 2:3], d["d_mu_next"][l, 960:1088].rearrange("(p o) -> p o", o=1))
    fw.tt("dve", mug[:, 0:1], mug[:, 1:2], mug[:, 2:3], ALU.add)
    fw.ts("dve", mug[:, 0:1], mug[:, 0:1], -1.0, 1.0, ALU.mult, ALU.add)
    prm = ph.sb([64, 9, 4], F32, "prm")
    fw.ldn(prm[:, 0, :], d["d_w0"][l, 0].rearrange("(h p) -> p h", p=64))
    fw.ldn(prm[:, 1, :], d["d_w0"][l, 1].rearrange("(h p) -> p h", p=64))
    fw.ldn(prm[:, 2, :], d["d_a0"][l].rearrange("(h p) -> p h", p=64))
    fw.ldn(prm[:, 3, :], d["d_k_k"][l].rearrange("(h p) -> p h", p=64))
    fw.ldn(prm[:, 4, :], d["d_k_a"][l].rearrange("(h p) -> p h", p=64))
    fw.ldn(prm[:, 6, :], d["d_r_k"][l].rearrange("h p -> p h"))
    fw.ldn(prm[:, 7, :], d["d_gn_w"][l].rearrange("(h p) -> p h", p=64))
    fw.ldn(prm[:, 8, :], d["d_gn_b"][l].rearrange("(h p) -> p h", p=64))
    fw.ts("dve", prm[:, 5, :], prm[:, 4, :], -1.0, 1.0, ALU.mult, ALU.add)
    wup = ph.sb([64, 2, 256], BF16, "wup")
    fw.ld(wup[:, 0, :], d["d_w_up"][l, 0], q="pool")
    fw.ld(wup[:, 1, :], d["d_w_up"][l, 1], q="pool")
    aup = ph.sb([64, 256], BF16, "aup")
    fw.ld(aup[:], d["d_a_up"][l], q="pool")
    gup = ph.sb([128, 256], BF16, "gup")
    fw.ld(gup[:], d["d_g_up"][l], q="pool")
    dmask = ph.sb([128, 2, 5, 128], BF16, "dmask")
    fw.ld(dmask[:], d["c_dmask"], q="pool")
    seg = ph.sb([64, 512], F32, "seg")
    fw.ld(seg[:], d["c_seg"])

    def pbc(i):
        return prm.bc((slice(None), i, slice(None)), 2, [64, 4, 128])

    ytok = [T(None, "ytok%d" % c) for c in range(n)]
    H = [ph.sb([64, 4, 64], F32, "Hf"), ph.sb([64, 4, 64], F32, "Hb")]
    Hb = [ph.sb([64, 4, 64], BF16, "Hfb"), ph.sb([64, 4, 64], BF16, "Hbb")]
    for dr in range(2):
        fw.memset("dve", H[dr][:], 0.0)
        fw.memset("pool", Hb[dr][:], 0.0)
    PST = ph.pspool(1, [128, 4, 4, 64], BF16, "pst")
    PP = []
    for dr in range(2):
        Q = Ctx()
        Q.PS = ph.pspool(4 if dr == 0 else 3, [128, 512], F32, "ps")
        Q.zts = ph.sbpool(1, [64, 15, 130], F32, "zt")
        Q.zgs = ph.sbpool(2, [128, 130], F32, "zg")
        Q.us = ph.sbpool(1, [64, 15, 128], F32, "u")
        Q.u2s = ph.sbpool(1, [64, 15, 128], F32, "u2")
        Q.ugs = ph.sbpool(1, [128, 128], F32, "ug")
        Q.F4 = ph.sbpool(10, [64, 4, 128], F32, "f4")
        Q.FL = ph.sbpool(3, [64, 4, 128], F32, "fl")
        Q.B4 = ph.sbpool(12, [64, 4, 128], BF16, "b4")
        Q.M4 = ph.sbpool(9, [128, 4, 128], BF16, "m4")
        Q.MK = ph.sbpool(4, [128, 4, 128], BF16, "mk")
        Q.TM = ph.sbpool(2, [128, 4, 4, 64], BF16, "tm")
        Q.SM = ph.sbpool(4, [128, 128], BF16, "sm")
        Q.S4 = ph.sbpool(6, [64, 4], F32, "s4")
        Q.UV = ph.sbpool(2, [128, 4, 64], F32, "uv")
        Q.UB = ph.sbpool(3, [128, 4, 64], BF16, "ub")
        PP.append(Q)

    def v3(t, npart=64):
        return View(t, t.ap[0:npart, :].rearrange("p (h i) -> p h i", h=4))

    def v3s(t):
        return View(t, t.ap[:, 0:256].rearrange("p (h i) -> p h i", h=4))

    def flat(t):
        return View(t, t.ap[:].rearrange("p h i -> p (h i)"))

    def unit(c, dr, first):
        Q = PP[dr]
        PS, F4, FL, B4, M4, MK, SM, S4 = Q.PS, Q.F4, Q.FL, Q.B4, Q.M4, Q.MK, Q.SM, Q.S4
        need_post = not first
        c0 = c * 128
        zt = Q.zts.next()
        zg = Q.zgs.next()
        lo = 1 if c == 0 else 0
        hi = 129 if c == n - 1 else 130
        if c == 0:
            fw.memset("pool", zt[:, :, 0:1], 0.0)
            fw.memset("pool", zg[:, 0:1], 0.0)
        if c == n - 1:
            fw.memset("pool", zt[:, :, 129:130], 0.0)
            fw.memset("pool", zg[:, 129:130], 0.0)
        fw.ld(zt[:, :, lo:hi], C.ZD[:, :, c0 - 1 + lo:c0 - 1 + hi])
        fw.ld(zg[:, lo:hi], C.ZG[:, c0 - 1 + lo:c0 - 1 + hi])
        if need_post:
            yprev = FL.next()
            fw.ld(yprev[:], View(ytok[c], C.YD[c]))
        yield
        u = Q.us.next()
        u2 = Q.u2s.next()
        fw.tt("dve", u[:], zt[:, :, 1:129], mu.bc((slice(None), 0, slice(None)), 2, [64, 15, 128]), ALU.mult)
        fw.tt("pool", u2[:], zt[:, :, 0:128], mu.bc((slice(None), 1, slice(None)), 2, [64, 15, 128]), ALU.mult)
        yield
        fw.tt("dve", u[:], u[:], u2[:], ALU.add)
        fw.tt("pool", u2[:], zt[:, :, 2:130], mu.bc((slice(None), 2, slice(None)), 2, [64, 15, 128]), ALU.mult)
        yield
        fw.tt("dve", u[:], u[:], u2[:], ALU.add)
        ug = Q.ugs.next()
        fw.ts("dve", ug[:], zg[:, 1:129], mug[:, 0:1], None, ALU.mult)
        fw.stt(ug[:], zg[:, 0:128], mug[:, 1:2], ug[:], ALU.mult, ALU.add)
        fw.stt(ug[:], zg[:, 2:130], mug[:, 2:3], ug[:], ALU.mult, ALU.add)
        yield
        tw = SM.next()
        fw.act(tw[0:64, :], u[:, 12 + dr, :], AF.Tanh)
        adb = SM.next()
        fw.copy("act", adb[0:64, :], u[:, 14, :])
        yw = PS.next()
        ya = PS.next()
        for h in range(4):
            fw.mm(yw[0:64, h * 128:(h + 1) * 128], wup[:, dr, h * 64:(h + 1) * 64], tw[0:64, :], sig=(h == 3))
        for h in range(4):
            fw.mm(ya[0:64, h * 128:(h + 1) * 128], aup[:, h * 64:(h + 1) * 64], adb[0:64, :], sig=(h == 3))
        yield
        sg = F4.next()
        fw.tt("dve", sg[:], v3(yw), pbc(dr), ALU.add)
        fw.act(sg[:], sg[:], AF.Tanh, scale=0.5)
        fw.ts("pool", sg[:], sg[:], 0.5, 0.5, ALU.mult, ALU.add)
        a = F4.next()
        fw.tt("dve", a[:], v3(ya), pbc(2), ALU.add)
        fw.act(a[:], a[:], AF.Tanh, scale=0.5)
        fw.ts("pool", a[:], a[:], 0.5, 0.5, ALU.mult, ALU.add)
        yield
        if need_post:
            sgd = SM.next()
            fw.act(sgd[:], ug[:], AF.Tanh, scale=0.5)
            fw.ts("pool", sgd[:], sgd[:], 0.5, 0.5, ALU.mult, ALU.add)
            gps_ = PS.next()
            for h in range(4):
                fw.mm(gps_[0:64, h * 128:(h + 1) * 128], gup[:, h * 64:(h + 1) * 64], sgd[:], sig=(h == 3))
            g_ = FL.next()
            fw.copy("act", g_[:], v3(gps_))
            yield
        r = u[:, 0:4, :]
        k = u[:, 4:8, :]
        v = u[:, 8:12, :]
        kk = F4.next()
        fw.tt("dve", kk[:], k, pbc(3), ALU.mult)
        sq = B4.next()
        fw.tt("pool", sq[:], kk[:], kk[:], ALU.mult)
        ssp = PS.next()
        fw.mm(ssp[0:64, :], K.ones_bf[0:64, 0:64], flat(sq))
        yield
        nr = F4.next()
        fw.ts("dve", nr[:], v3(ssp), 1e-24, None, ALU.max)
        fw.act(nr[:], nr[:], AF.Ln)
        fw.act(nr[:], nr[:], AF.Exp, scale=-0.5)
        kap = F4.next()
        fw.tt("dve", kap[:], kk[:], nr[:], ALU.mult)
        yield
        tk = F4.next()
        fw.tt("pool", tk[:], a[:], pbc(4), ALU.mult)
        fw.tt("pool", tk[:], tk[:], pbc(5), ALU.add)
        kh = FL.next()
        fw.tt("dve", kh[:], k, tk[:], ALU.mult)
        bb = F4.next()
        fw.tt("pool", bb[:], kap[:], a[:], ALU.mult)
        yield
        Pp = F4.next()
        fw.scan(flat(Pp), seg[:], flat(sg), 0.0, ALU.mult, ALU.add)
        Qp = F4.next()
        fw.tt("pool", Qp[:], Pp[:], sg[:], ALU.subtract)
        nT = S4.next()
        pT = S4.next()
        fw.ts("dve", nT[:], Pp[:, :, 127], -CW, None, ALU.mult)
        fw.ts("dve", pT[:], Pp[:, :, 127], CW, None, ALU.mult)
        gC = S4.next()
        fw.act(gC[:], nT[:], AF.Exp)
        yield
        rt = B4.next()
        kp = B4.next()
        kt = B4.next()
        bt = B4.next()
        Khf = B4.next()
        Bhf = B4.next()
        E1 = F4.next()
        E2 = F4.next()
        if dr == 0:
            fw.act(E1[:], Pp[:], AF.Exp, scale=-CW)
            fw.tt("dve", rt[:], r, E1[:], ALU.mult)
            fw.act(E2[:], Qp[:], AF.Exp, scale=-CW)
            fw.tt("pool", kp[:], kap[:], E2[:], ALU.mult)
            yield
            E3 = F4.next()
            fw.act(E3[:], Pp[:], AF.Exp, scale=CW)
            fw.tt("dve", kt[:], kh[:], E3[:], ALU.mult)
            fw.tt("pool", bt[:], bb[:], E3[:], ALU.mult)
            E4 = F4.next()
            for h in range(4):
                fw.act(E4[:, h, :], Pp[:, h, :], AF.Exp, scale=CW, bias=nT[:, h:h + 1])
        else:
            for h in range(4):
                fw.act(E1[:, h, :], Qp[:, h, :], AF.Exp, scale=CW, bias=nT[:, h:h + 1])
            fw.tt("dve", rt[:], r, E1[:], ALU.mult)
            for h in range(4):
                fw.act(E2[:, h, :], Pp[:, h, :], AF.Exp, scale=CW, bias=nT[:, h:h + 1])
            fw.tt("pool", kp[:], kap[:], E2[:], ALU.mult)
            yield
            E3 = F4.next()
            for h in range(4):
                fw.act(E3[:, h, :], Qp[:, h, :], AF.Exp, scale=-CW, bias=pT[:, h:h + 1])
            fw.tt("dve", kt[:], kh[:], E3[:], ALU.mult)
            fw.tt("pool", bt[:], bb[:], E3[:], ALU.mult)
            E4 = F4.next()
            fw.act(E4[:], Qp[:], AF.Exp, scale=-CW)
        yield
        fw.tt("dve", Khf[:], kh[:], E4[:], ALU.mult)
        fw.stt(Bhf[:], bb[:], -1.0, E4[:], ALU.mult, ALU.mult)
        vb = B4.next()
        fw.copy("act", vb[:], v)
        pst = PST.next()
        for ki, src_ in enumerate((kp, Khf, Bhf, vb)):
            for h in range(4):
                fw.tr(pst[:, ki, h, :], src_[:, h, :], K.ident[0:64, 0:64], sig=(ki == 3 and h == 3))
        tm = Q.TM.next()
        fw.copy("dve", tm[:], pst[:])
        yield
        mats = []
        pairs = [(kp, bt), (bt, kp), (kt, kp), (kt, rt), (bt, rt)]
        for ki, (lt, rt_) in enumerate(pairs):
            p_ = PS.next()
            for h in range(4):
                fw.mm(p_[:, h * 128:(h + 1) * 128], lt[:, h, :], rt_[:, h, :], sig=(h == 3))
            m_ = M4.next() if ki < 2 else MK.next()
            fw.tt("dve", m_[:], v3(p_, 128), dmask.bc((slice(None), dr, ki, slice(None)), 1, [128, 4, 128]), ALU.mult)
            mats.append(m_)
            yield
        X, XT, AkkT, ArkT, ArbT = mats
        PT = M4.next()
        fw.tt("pool", PT[:], XT[:], K.ident.bc((slice(None), slice(None)), 1, [128, 4, 128]), ALU.add)
        for lvl in range(6):
            p1 = PS.next()
            for h in range(4):
                fw.mm(p1[:, h * 128:(h + 1) * 128], XT[:, h, :], X[:, h, :], sig=(h == 3))
            if lvl < 5:
                p2 = PS.next()
                for h in range(4):
                    fw.mm(p2[:, h * 128:(h + 1) * 128], X[:, h, :], XT[:, h, :], sig=(h == 3))
            yield
            X2 = M4.next()
            fw.copy("act", X2[:], v3(p1, 128))
            if lvl < 5:
                XT2 = M4.next()
                fw.copy("dve", XT2[:], v3(p2, 128))
            p3 = PS.next()
            for h in range(4):
                fw.mm(p3[:, h * 128:(h + 1) * 128], X2[:, h, :], PT[:, h, :], sig=(h == 3))
            yield
            PT2 = M4.next()
            fw.tt("dve", PT2[:], v3(p3, 128), PT[:], ALU.add)
            PT = PT2
            X = X2
            if lvl < 5:
                XT = XT2
        pw = PS.next()
        for h in range(4):
            fw.mm(pw[0:64, h * 128:(h + 1) * 128], tm[:, 0, h, :], PT[:, h, :], sig=(h == 3))
        pa = PS.next()
        for h in range(4):
            fw.mm(pa[:, h * 64:(h + 1) * 64], AkkT[:, h, :], tm[:, 3, h, :], sig=(h == 3))
        yield
        WkT = B4.next()
        fw.copy("act", WkT[:], v3(pw))
        AV = Q.UB.next()
        fw.copy("dve", AV[:], v3s(pa))
        pu = PS.next()
        for h in range(4):
            fw.mm(pu[:, h * 64:(h + 1) * 64], PT[:, h, :], AV[:, h, :], sig=(h == 3))
        yield
        Uv = Q.UV.next()
        fw.copy("act", Uv[:], v3s(pu))
        pU = PS.next()
        for h in range(4):
            fw.mm(pU[:, h * 64:(h + 1) * 64], WkT[:, h, :], Hb[dr][:, h, :], sig=(h == 3))
        yield
        U = Q.UB.next()
        fw.tt("dve", U[:], v3s(pU), Uv[:], ALU.add)
        pY = PS.next()
        for h in range(4):
            yh = pY[0:64, h * 128:(h + 1) * 128]
            fw.mm(yh, Hb[dr][:, h, :], rt[:, h, :], start=True, stop=False, sig=False)
            fw.mm(yh, tm[:, 3, h, :], ArkT[:, h, :], start=False, stop=False, sig=False)
            fw.mm(yh, U[:, h, :], ArbT[:, h, :], start=False, stop=True, sig=(h == 3))
        pH = PS.next()
        for h in range(4):
            hh = pH[0:64, h * 64:(h + 1) * 64]
            fw.mm(hh, tm[:, 1, h, :], tm[:, 3, h, :], start=True, stop=False, sig=False)
            fw.mm(hh, tm[:, 2, h, :], U[:, h, :], start=False, stop=True, sig=(h == 3))
        fw.tt("pool", H[dr][:], H[dr][:], gC.bc((slice(None), slice(None)), 2, [64, 4, 64]), ALU.mult)
        yield
        fw.tt("dve", H[dr][:], H[dr][:], View(pH, pH.ap[0:64, 0:256].rearrange("p (h i) -> p h i", h=4)), ALU.add)
        fw.copy("act", Hb[dr][:], H[dr][:])
        if first:
            ysb = F4.next()
            fw.copy("act", ysb[:], v3(pY))
            fw.st(View(ytok[c], C.YD[c]), ysb[:])
            return
        y = F4.next()
        fw.tt("dve", y[:], v3(pY), yprev[:], ALU.add)
        pm_ = PS.next()
        fw.mm(pm_[0:64, :], K.ones_f[0:64, 0:64], flat(y))
        rk = F4.next()
        fw.tt("pool", rk[:], r, kh[:], ALU.mult)
        fw.tt("pool", rk[:], rk[:], pbc(6), ALU.mult)
        yield
        yc = F4.next()
        fw.stt(yc[:], v3(pm_), -1.0 / 64, y[:], ALU.mult, ALU.add)
        sq2 = F4.next()
        fw.act(sq2[:], yc[:], AF.Square)
        pv = PS.next()
        fw.mm(pv[0:64, :], K.ones_f[0:64, 0:64], flat(sq2))
        pr_ = PS.next()
        fw.mm(pr_[0:64, :], K.ones_f[0:64, 0:64], flat(rk))
        yield
        rs = F4.next()
        fw.ts("dve", rs[:], v3(pv), 1.0 / 64, GN_EPS, ALU.mult, ALU.add)
        fw.act(rs[:], rs[:], AF.Ln)
        fw.act(rs[:], rs[:], AF.Exp, scale=-0.5)
        fw.tt("dve", yc[:], yc[:], rs[:], ALU.mult)
        fw.tt("pool", yc[:], yc[:], pbc(7), ALU.mult)
        yield
        fw.tt("pool", yc[:], yc[:], pbc(8), ALU.add)
        t_ = F4.next()
        fw.tt("dve", t_[:], v3(pr_), v, ALU.mult)
        fw.tt("pool", yc[:], yc[:], t_[:], ALU.add)
        ob = B4.next()
        fw.tt("dve", ob[:], yc[:], g_[:], ALU.mult)
        for h in range(4):
            fw.st(C.OT[12 + h, :, c * 128:(c + 1) * 128], ob[:, h, :])

    for i in range(n):
        first = i < n // 2
        gens = [unit(i, 0, first), unit(n - 1 - i, 1, first)]
        alive = [True, True]
        while any(alive):
            for gi in range(2):
                if alive[gi]:
                    try:
                        next(gens[gi])
                    except StopIteration:
                        alive[gi] = False
    ph.close()


def build(seqs=(4096, 2048), depth=2, phases=None, debug=False):
    nc = bass.Bass("TRN2", target_bir_lowering=False)
    C = Ctx()
    C.seqs = list(seqs)
    C.d = {}
    C.xin = [nc.dram_tensor("x%d" % i, [t, D_MODEL], F32, kind="ExternalInput").ap() for i, t in enumerate(seqs)]
    C.y = [nc.dram_tensor("y%d" % i, [t, D_MODEL], F32, kind="ExternalOutput").ap() for i, t in enumerate(seqs)]
    for nm, shp in WEIGHT_SPECS:
        C.d[nm] = nc.dram_tensor(nm, list(shp), F32, kind="ExternalInput").ap()
    for nm, arr in make_consts().items():
        C.d[nm] = nc.dram_tensor(nm, list(arr.shape), F32, kind="ExternalInput").ap()
    TM = max(seqs)
    sk = "ExternalOutput" if debug else "Internal"

    def scratch(nm, shp, dt):
        return nc.dram_tensor(nm, list(shp), dt, kind=sk).ap()
    C.QA = scratch("QA", [64, 4, TM], BF16)
    C.KA = scratch("KA", [64, 2, TM], BF16)
    C.VA = scratch("VA", [TM, 128], BF16)
    C.QB = scratch("QB", [64, 4, TM], BF16)
    C.KB = scratch("KB", [64, 4, TM], BF16)
    C.VB = scratch("VB", [TM, 256], BF16)
    C.QC = scratch("QC", [64, 4, TM], BF16)
    C.KC = scratch("KC", [64, 4, TM], BF16)
    C.GC = scratch("GC", [64, 4, TM], BF16)
    C.VC = scratch("VC", [TM, 256], BF16)
    C.ZD = scratch("ZD", [64, 15, TM], F32)
    C.ZG = scratch("ZG", [128, TM], F32)
    C.OT = scratch("OT", [16, 64, TM], BF16)
    C.YD = scratch("YD", [TM // 128, 64, 4, 128], F32)
    allph = ["inproj", "A", "B", "C", "D", "outproj", "ffn"]
    phases = allph if phases is None else phases
    with ExitStack() as es:
        fw = FW(nc, es)
        ph0 = fw.phase()
        K = load_consts(fw, C, ph0)
        fw.barrier()
        for l in range(depth):
            for si in range(len(seqs)):
                if "inproj" in phases:
                    phase_inproj(fw, C, K, l, si)
                if "A" in phases:
                    phase_attnA(fw, C, K, l, si)
                if "B" in phases:
                    phase_attnB(fw, C, K, l, si)
                if "C" in phases:
                    phase_ret(fw, C, K, l, si)
                if "D" in phases:
                    phase_rwkv(fw, C, K, l, si)
                if "outproj" in phases:
                    phase_outproj(fw, C, K, l, si)
                if "ffn" in phases:
                    phase_ffn(fw, C, K, l, si)
        fw.barrier()
        ph0.es.close()
        C.nins = fw.nins
    return nc, C


_CACHE = {}


def kernel(**inputs):
    xp = np.ascontiguousarray(inputs["x_prompt"], dtype=np.float32)
    xs = np.ascontiguousarray(inputs["x_sample"], dtype=np.float32)
    nc, C = build()
    consts = make_consts()
    in_maps = []
    for i in range(8):
        m = {"x0": xp[i], "x1": xs[i]}
        for nm, _ in WEIGHT_SPECS:
            m[nm] = np.ascontiguousarray(inputs[nm], dtype=np.float32)
        m.update(consts)
        in_maps.append(m)
    res = run_bass_kernel_spmd(nc, in_maps, core_ids=list(range(8)))
    yp = np.stack([r["y0"] for r in res.results], 0).astype(np.float32)
    ys = np.stack([r["y1"] for r in res.results], 0).astype(np.float32)
    return (yp, ys)
```

```python
import math
from contextlib import ExitStack
import numpy as np
import concourse.bass as bass
import concourse.mybir as mybir
from concourse.bass_utils import run_bass_kernel_spmd

F32 = mybir.dt.float32
BF16 = mybir.dt.bfloat16
AF = mybir.ActivationFunctionType
ALU = mybir.AluOpType
AX = mybir.AxisListType

D_MODEL = 1024
IN_COLS = 3392
D_FF = 2816
NFF = D_FF // 128
EPS = 1e-6
GN_EPS = 64e-5
TMAX = 4096


class View:
    __slots__ = ("t", "ap")

    def __init__(self, t, ap):
        self.t = t
        self.ap = ap


class T:
    __slots__ = ("ap", "w", "r", "name")

    def __init__(self, ap, name=""):
        self.ap = ap
        self.w = {}
        self.r = {}
        self.name = name

    def __getitem__(self, idx):
        return View(self, self.ap[idx])

    def bc(self, idx, axis, shape):
        return View(self, self.ap[idx].unsqueeze(axis).to_broadcast(list(shape)))


class Eng:
    def __init__(self, name, handle, sem):
        self.name = name
        self.h = handle
        self.sem = sem
        self.count = 0
        self.seen = {}


class Pool:
    def __init__(self, items):
        self.items = items
        self.i = 0

    def next(self):
        t = self.items[self.i]
        self.i = (self.i + 1) % len(self.items)
        return t


class Phase:
    def __init__(self, fw):
        self.fw = fw
        self.es = ExitStack()
        self.n = 0

    def sb(self, shape, dt, name="t"):
        self.n += 1
        t = self.es.enter_context(self.fw.nc.sbuf_tensor("%s_%d_%d" % (name, self.fw.uid(), self.n), list(shape), dt))
        return T(t, name)

    def ps(self, shape, dt=F32, name="p"):
        self.n += 1
        t = self.es.enter_context(self.fw.nc.psum_tensor("%s_%d_%d" % (name, self.fw.uid(), self.n), list(shape), dt))
        return T(t, name)

    def sbpool(self, n, shape, dt, name="t"):
        return Pool([self.sb(shape, dt, name) for _ in range(n)])

    def pspool(self, n, shape, dt=F32, name="p"):
        return Pool([self.ps(shape, dt, name) for _ in range(n)])

    def close(self):
        self.fw.barrier()
        self.es.close()


class FW:
    NDMA = 32

    def __init__(self, nc, es):
        self.nc = nc
        self.es = es
        self._uid = 0
        self.E = {}
        for nm, h in (("pe", nc.tensor), ("act", nc.scalar), ("dve", nc.vector), ("pool", nc.gpsimd), ("sp", nc.sync)):
            sem = es.enter_context(nc.semaphore("s_" + nm))
            self.E[nm] = Eng(nm, h, sem)
        self.dsem = [es.enter_context(nc.semaphore("d%d" % i)) for i in range(self.NDMA)]
        self.dval = [0] * self.NDMA
        self.dnext = {"sp": 0, "pool": 0, "act": 0}
        self.drange = {"sp": (0, 14), "pool": (14, 28), "act": (28, 32)}
        self.semobj = {}
        for e in self.E.values():
            self.semobj[("e", e.name)] = e.sem
        for i, s in enumerate(self.dsem):
            self.semobj[("d", i)] = s
        self.nins = 0

    def uid(self):
        self._uid += 1
        return self._uid

    def phase(self):
        return Phase(self)

    def _needs(self, reads, writes):
        needs = {}
        for t in reads:
            for k, v in t.w.items():
                if needs.get(k, 0) < v:
                    needs[k] = v
        for t in writes:
            for d in (t.w, t.r):
                for k, v in d.items():
                    if needs.get(k, 0) < v:
                        needs[k] = v
        return needs

    def _emit_waits(self, e, needs, skip_self=False):
        for k, v in needs.items():
            if skip_self and k == ("e", e.name):
                continue
            if e.seen.get(k, 0) >= v:
                continue
            if k[0] == "e":
                assert self.E[k[1]].count >= v, "wait on future event %s %d" % (k, v)
            e.seen[k] = v
            e.h.wait_ge(self.semobj[k], v)
            self.nins += 1

    def op(self, eng, fn, reads=(), writes=(), signal=True):
        e = self.E[eng]
        reads = [v.t for v in reads]
        writes = [v.t for v in writes]
        needs = self._needs(reads, writes)
        self._emit_waits(e, needs, skip_self=(eng == "pe"))
        ins = fn(e.h)
        self.nins += 1
        if signal:
            e.count += 1
            ins.then_inc(e.sem, 1)
            ev = e.count
        else:
            ev = e.count + 1
        key = ("e", eng)
        for t in writes:
            t.w = {key: ev}
            t.r = {}
        for t in reads:
            if t.r.get(key, 0) < ev:
                t.r[key] = ev

    def dma(self, q, out, in_, **kw):
        e = self.E[q]
        reads = [in_.t] if isinstance(in_, View) else []
        writes = [out.t] if isinstance(out, View) else []
        acc = isinstance(out, View) and not isinstance(in_, View) and out.t.ap is not None
        if acc:
            needs = {}
            t_ = out.t
            for k, v in t_.r.items():
                if needs.get(k, 0) < v:
                    needs[k] = v
            for k, v in t_.w.items():
                if k[0] != "d" and needs.get(k, 0) < v:
                    needs[k] = v
        else:
            needs = self._needs(reads, writes)
        lo_, hi_ = self.drange[q]
        i = lo_ + self.dnext[q]
        self.dnext[q] = (self.dnext[q] + 1) % (hi_ - lo_)
        key = ("d", i)
        if self.dval[i] > 0 and needs.get(key, 0) < self.dval[i]:
            needs[key] = self.dval[i]
        self._emit_waits(e, needs)
        self.dval[i] += 16
        v = self.dval[i]
        oap = out.ap if isinstance(out, View) else out
        iap = in_.ap if isinstance(in_, View) else in_
        e.h.dma_start(out=oap, in_=iap, **kw).then_inc(self.dsem[i], 16)
        self.nins += 1
        for t in writes:
            if acc:
                t.w[key] = v
            else:
                t.w = {key: v}
                t.r = {}
        for t in reads:
            t.r[key] = v

    def barrier(self):
        allev = {}
        for e in self.E.values():
            if e.count > 0:
                allev[("e", e.name)] = e.count
        for i in range(self.NDMA):
            if self.dval[i] > 0:
                allev[("d", i)] = self.dval[i]
        for e in self.E.values():
            self._emit_waits(e, dict(allev))

    def ld(self, dst, src, q="sp", **kw):
        self.dma(q, dst, src, **kw)

    def ldn(self, dst, src, q="sp"):
        self.dma(q, dst, src, allow_slow_non_contiguous=True)

    def st(self, dst, src, q="pool"):
        self.dma(q, dst, src)

    def mm(self, out, lhsT, rhs, start=True, stop=True, sig=None):
        if sig is None:
            sig = stop
        self.op("pe", lambda h: h.matmul(out.ap, lhsT=lhsT.ap, rhs=rhs.ap, start=start, stop=stop),
                reads=[lhsT, rhs], writes=[out], signal=sig)

    def tr(self, out, in_, ident, sig=True):
        self.op("pe", lambda h: h.transpose(out=out.ap, in_=in_.ap, identity=ident.ap),
                reads=[in_, ident], writes=[out], signal=sig)

    def act(self, out, in_, func, bias=None, scale=None, accum=None):
        reads = [in_]
        kw = {}
        if bias is not None:
            if isinstance(bias, View):
                reads.append(bias)
                kw["bias"] = bias.ap
            else:
                kw["bias"] = float(bias)
        if scale is not None:
            if isinstance(scale, View):
                reads.append(scale)
                kw["scale"] = scale.ap
            else:
                kw["scale"] = float(scale)
        writes = [out]
        if accum is not None:
            kw["accum_out"] = accum.ap
            writes.append(accum)
        self.op("act", lambda h: h.activation(out=out.ap, in_=in_.ap, func=func, **kw), reads=reads, writes=writes)

    def tt(self, eng, out, in0, in1, op):
        self.op(eng, lambda h: h.tensor_tensor(out=out.ap, in0=in0.ap, in1=in1.ap, op=op), reads=[in0, in1], writes=[out])

    def ts(self, eng, out, in0, s1, s2, op0, op1=None):
        reads = [in0]
        a1 = s1.ap if isinstance(s1, View) else s1
        a2 = s2.ap if isinstance(s2, View) else s2
        if isinstance(s1, View):
            reads.append(s1)
        if isinstance(s2, View):
            reads.append(s2)
        if op1 is None:
            self.op(eng, lambda h: h.tensor_scalar(out=out.ap, in0=in0.ap, scalar1=a1, scalar2=None, op0=op0), reads=reads, writes=[out])
        else:
            self.op(eng, lambda h: h.tensor_scalar(out=out.ap, in0=in0.ap, scalar1=a1, scalar2=a2, op0=op0, op1=op1), reads=reads, writes=[out])

    def stt(self, out, in0, scalar, in1, op0, op1):
        reads = [in0, in1]
        sa = scalar.ap if isinstance(scalar, View) else scalar
        if isinstance(scalar, View):
            reads.append(scalar)
        self.op("dve", lambda h: h.scalar_tensor_tensor(out=out.ap, in0=in0.ap, scalar=sa, in1=in1.ap, op0=op0, op1=op1), reads=reads, writes=[out])

    def copy(self, eng, out, in_):
        if eng == "act":
            self.op("act", lambda h: h.activation(out=out.ap, in_=in_.ap, func=AF.Identity), reads=[in_], writes=[out])
        else:
            self.op(eng, lambda h: h.tensor_copy(out=out.ap, in_=in_.ap), reads=[in_], writes=[out])

    def recip(self, out, in_):
        self.op("dve", lambda h: h.reciprocal(out=out.ap, in_=in_.ap), reads=[in_], writes=[out])

    def memset(self, eng, out, val):
        self.op(eng, lambda h: h.memset(out.ap, val), writes=[out])

    def reduce(self, out, in_, op, axis=AX.X):
        self.op("dve", lambda h: h.tensor_reduce(out=out.ap, in_=in_.ap, axis=axis, op=op), reads=[in_], writes=[out])

    def scan(self, out, d0, d1, init, op0, op1):
        self.op("dve", lambda h: h.tensor_tensor_scan(out=out.ap, data0=d0.ap, data1=d1.ap, initial=init, op0=op0, op1=op1), reads=[d0, d1], writes=[out])


def run_pipe(gens, depth, side=None, side_every=8):
    gens = list(gens)
    active = []
    tick = 0
    while gens or active:
        while gens and len(active) < depth:
            active.append(gens.pop(0))
        for g in list(active):
            try:
                next(g)
            except StopIteration:
                active.remove(g)
        tick += 1
        if side is not None and tick % side_every == 0:
            try:
                next(side)
            except StopIteration:
                side = None
    if side is not None:
        for _ in side:
            pass


def _angles(pos, rot_dim, theta):
    inv = (theta ** (-np.arange(0, rot_dim, 2, dtype=np.float32) / rot_dim)).astype(np.float32)
    return pos.astype(np.float32)[:, None] * inv[None, :]


def make_consts():
    c = {}
    c["c_ident"] = np.eye(128, dtype=np.float32)
    T = TMAX
    pos = np.arange(T)
    rope = np.zeros((64, 6, T), np.float32)
    ar = _angles(pos // 64, 32, 10000.0)
    ac = _angles(pos % 64, 32, 10000.0)
    for d in range(64):
        a = ar if d < 32 else ac
        idx = (d % 32) % 16
        rope[d, 0] = np.cos(a[:, idx])
        rope[d, 1] = np.sin(a[:, idx])
    ab = _angles(pos, 8, 500000.0)
    for d in range(64):
        dd = d % 32
        if dd < 8:
            rope[d, 2] = np.cos(ab[:, dd % 4])
            rope[d, 3] = np.sin(ab[:, dd % 4])
        else:
            rope[d, 2] = 1.0
            rope[d, 3] = 0.0
    acc = _angles(pos, 64, 10000.0)
    for d in range(64):
        rope[d, 4] = np.cos(acc[:, d % 32])
        rope[d, 5] = np.sin(acc[:, d % 32])
    c["c_rope"] = rope
    pm = np.zeros((64, 3, 64), np.float32)
    for m in range(64):
        mm_ = m % 32
        base = m - mm_
        if mm_ < 16:
            pm[base + mm_ + 16, 0, m] = -1.0
        else:
            pm[base + mm_ - 16, 0, m] = 1.0
        if mm_ < 4:
            pm[base + mm_ + 4, 1, m] = -1.0
        elif mm_ < 8:
            pm[base + mm_ - 4, 1, m] = 1.0
        if m < 32:
            pm[m + 32, 2, m] = -1.0
        else:
            pm[m - 32, 2, m] = 1.0
    c["c_pm"] = pm
    c["c_ones"] = np.ones((128, 128), np.float32)
    h = np.arange(4, dtype=np.float32)
    lgf = np.log1p(-np.exp2(-5.0 - h)).astype(np.float32)
    lgb = lgf[::-1].copy()
    i = np.arange(128, dtype=np.float32)
    diff = i[None, :] - i[:, None]
    rm = np.zeros((128, 4, 128), np.float32)
    for hh in range(4):
        rm[:, hh, :] = np.where(diff >= 0, np.exp(lgf[hh] * np.maximum(diff, 0)), 0.0) + \
            np.where(diff <= 0, np.exp(lgb[hh] * np.maximum(-diff, 0)), 0.0)
    c["c_retmask"] = rm
    qd = np.zeros((64, 2, 4, 128), np.float32)
    kd = np.zeros((128, 2, 4), np.float32)
    for hh in range(4):
        qd[:, 0, hh, :] = np.exp(lgf[hh] * (i + 1.0))[None, :]
        qd[:, 1, hh, :] = np.exp(lgb[hh] * (128.0 - i))[None, :]
        kd[:, 0, hh] = np.exp(lgf[hh] * (127.0 - i))
        kd[:, 1, hh] = np.exp(lgb[hh] * i)
    c["c_qdec"] = qd
    c["c_kdec"] = kd
    dm = np.zeros((128, 2, 5, 128), np.float32)
    p = np.arange(128)[:, None]
    f = np.arange(128)[None, :]
    dm[:, 0, 0, :] = -1.0 * (p > f)
    dm[:, 0, 1, :] = -1.0 * (f > p)
    dm[:, 0, 2, :] = (f > p)
    dm[:, 0, 3, :] = (f >= p)
    dm[:, 0, 4, :] = -1.0 * (f >= p)
    dm[:, 1, 0, :] = -1.0 * (p < f)
    dm[:, 1, 1, :] = -1.0 * (f < p)
    dm[:, 1, 2, :] = (f < p)
    dm[:, 1, 3, :] = (f <= p)
    dm[:, 1, 4, :] = -1.0 * (f <= p)
    c["c_dmask"] = dm.astype(np.float32)
    seg = np.ones((64, 512), np.float32)
    seg[:, ::128] = 0.0
    c["c_seg"] = seg
    return c


RET_G = [1.0 - 2.0 ** (-5 - h) for h in range(4)]

WEIGHT_SPECS = [
    ("norm_mix_pre", (2, 1024)), ("norm_mix_post", (2, 1024)), ("norm_ffn_pre", (2, 1024)), ("norm_ffn_post", (2, 1024)),
    ("w_in", (2, 1024, 3392)), ("w_out", (2, 1024, 1024)), ("a_q_gain", (2, 64)), ("a_k_gain", (2, 64)),
    ("b_lambda", (2, 4, 32)), ("b_subln_gain", (2, 64)), ("c_gn_gain", (2, 256)), ("d_mu_prev", (2, 1088)),
    ("d_mu_next", (2, 1088)), ("d_w0", (2, 2, 256)), ("d_w_up", (2, 2, 64, 256)), ("d_a0", (2, 256)),
    ("d_a_up", (2, 64, 256)), ("d_g_up", (2, 128, 256)), ("d_k_k", (2, 256)), ("d_k_a", (2, 256)),
    ("d_r_k", (2, 4, 64)), ("d_gn_w", (2, 256)), ("d_gn_b", (2, 256)),
    ("ffn_w_gate", (2, 1024, 2816)), ("ffn_w_up", (2, 1024, 2816)), ("ffn_w_down", (2, 2816, 1024)),
]


class Ctx:
    pass


def load_consts(fw, C, ph):
    K = Ctx()
    K.ident = ph.sb([128, 128], BF16, "ident")
    fw.ld(K.ident[:], C.d["c_ident"], q="pool")
    K.identf = ph.sb([128, 128], F32, "identf")
    fw.ld(K.identf[:], C.d["c_ident"])
    K.ones_bf = ph.sb([128, 128], BF16, "ones_bf")
    fw.ld(K.ones_bf[:], C.d["c_ones"], q="pool")
    K.ones_f = ph.sb([128, 128], F32, "ones_f")
    fw.ld(K.ones_f[:], C.d["c_ones"])
    K.pm = []
    for i in range(3):
        t = ph.sb([64, 64], BF16, "pm%d" % i)
        fw.ld(t[:], C.d["c_pm"][:, i, :], q="pool")
        K.pm.append(t)
    return K


def rms_rows(fw, xt, gain, hb, tmp, ss, rstd):
    fw.act(tmp[:], xt[:], AF.Square, accum=ss[:])
    fw.act(rstd[:], ss[:], AF.Sqrt, bias=EPS, scale=1.0 / D_MODEL)
    fw.recip(rstd[:], rstd[:])
    fw.stt(hb[:], xt[:], rstd[:, 0:1], gain[:], ALU.mult, ALU.mult)


def make_hT(fw, K, src_rows, gain, hT, j, P):
    xt = P.xt.next()
    fw.ld(xt[:], src_rows)
    hb = P.hb.next()
    ss = P.ss.next()
    rstd = P.rstd.next()
    tmp = P.tmp.next()
    rms_rows(fw, xt, gain, hb, tmp, ss, rstd)
    tp = P.tp.next()
    for k in range(8):
        fw.tr(tp[:, k, :], hb[:, k * 128:(k + 1) * 128], K.ident[:], sig=(k == 7))
    fw.copy("dve", hT[:, :, j * 128:(j + 1) * 128], tp[:])
    return xt


def phase_inproj(fw, C, K, l, si):
    T_ = C.seqs[si]
    src = C.xin[si] if l == 0 else C.y[si]
    d = C.d
    ph = fw.phase()
    W = ph.sb([128, 8, IN_COLS], BF16, "W")
    for k in range(8):
        fw.ld(W[:, k, :], d["w_in"][l, k * 128:(k + 1) * 128, :], q="pool")
    gain = ph.sb([128, 1024], F32, "gain")
    fw.ld(gain[:], d["norm_mix_pre"][l].partition_broadcast(128))
    gqk = ph.sb([64, 2], F32, "gqk")
    fw.ld(gqk[:, 0:1], d["a_q_gain"][l].rearrange("(p o) -> p o", o=1))
    fw.ld(gqk[:, 1:2], d["a_k_gain"][l].rearrange("(p o) -> p o", o=1))
    P = Ctx()
    P.xt = ph.sbpool(2, [128, 1024], F32, "xt")
    P.hb = ph.sbpool(2, [128, 1024], BF16, "hb")
    P.tmp = ph.sbpool(1, [128, 1024], F32, "tmp")
    P.ss = ph.sbpool(2, [128, 1], F32, "ss")
    P.rstd = ph.sbpool(2, [128, 1], F32, "rstd")
    P.tp = ph.pspool(1, [128, 8, 128], BF16, "tp")
    hTs = ph.sbpool(2, [128, 8, 512], BF16, "hT")
    tabs = ph.sbpool(2, [64, 6, 512], F32, "tab")
    zps = ph.pspool(4, [128, 512], F32, "zp")
    aps = ph.pspool(2, [128, 512], F32, "ap")
    tps = ph.pspool(1, [128, 512], F32, "tmps")
    f32p = ph.sbpool(24, [64, 512], F32, "f32p")
    bfp = ph.sbpool(16, [64, 512], BF16, "bfp")
    f32w = ph.sbpool(3, [128, 512], F32, "f32w")
    bfw = ph.sbpool(3, [128, 512], BF16, "bfw")
    chunks = []
    for h in range(4):
        chunks.append(("A", 0 + 64 * h, C.QA, h, 0))
    for h in range(2):
        chunks.append(("A", 256 + 64 * h, C.KA, h, 1))
    for h in range(4):
        chunks.append(("B", 512 + 64 * h, C.QB, h, 0))
    for h in range(4):
        chunks.append(("B", 768 + 64 * h, C.KB, h, 0))
    for h in range(4):
        chunks.append(("C", 1280 + 64 * h, C.QC, h, 0))
    for h in range(4):
        chunks.append(("C", 1536 + 64 * h, C.KC, h, 0))
    for h in range(4):
        chunks.append(("G", 2048 + 64 * h, C.GC, h, 0))
    for j in range(15):
        chunks.append(("D", 2304 + 64 * j, C.ZD, j, 0))
    DEPTHP = 4

    def chunk_gen(kind, col, dst, slot, gi, hT, tab, t0):
        zp = zps.next()
        for k in range(8):
            fw.mm(zp[0:64, :], W[:, k, col:col + 64], hT[:, k, :], start=(k == 0), stop=(k == 7))
        z = zp[0:64, :]
        yield
        if kind in ("A", "B", "C"):
            ti = {"A": 0, "B": 2, "C": 4}[kind]
            pmi = {"A": 0, "B": 1, "C": 2}[kind]
            qn = bfp.next()
            qf = f32p.next()
            fw.copy("act", qf[:], z)
            if kind == "A":
                sq = bfp.next()
                fw.act(sq[:], z, AF.Square)
                yield
                ms = aps.next()
                fw.mm(ms[0:64, :], K.ones_bf[0:64, 0:64], sq[:])
                rs = f32p.next()
                fw.ts("dve", rs[:], ms[0:64, :], 1.0 / 64, EPS, ALU.mult, ALU.add)
                fw.act(rs[:], rs[:], AF.Ln)
                fw.act(rs[:], rs[:], AF.Exp, scale=-0.5)
                qf2 = f32p.next()
                fw.stt(qf2[:], qf[:], gqk[:, gi:gi + 1], rs[:], ALU.mult, ALU.mult)
                qf = qf2
            fw.copy("act", qn[:], qf[:])
            yield
            pq = aps.next()
            fw.mm(pq[0:64, :], K.pm[pmi][:], qn[:])
            t2 = f32p.next()
            fw.tt("dve", t2[:], pq[0:64, :], tab[:, ti + 1, :], ALU.mult)
            t1 = f32p.next()
            fw.tt("pool", t1[:], qf[:], tab[:, ti, :], ALU.mult)
            ob = bfp.next()
            fw.tt("pool", ob[:], t1[:], t2[:], ALU.add)
            fw.st(dst[:, slot, t0:t0 + 512], ob[:])
        elif kind == "G":
            ob = bfp.next()
            fw.act(ob[:], z, AF.Silu)
            fw.st(dst[:, slot, t0:t0 + 512], ob[:])
        else:
            of = f32p.next()
            fw.copy("act", of[:], z)
            fw.st(dst[:, slot, t0:t0 + 512], of[:])

    def tail_gen(hT, t0):
        zp = zps.next()
        for k in range(8):
            fw.mm(zp[:, :], W[:, k, 3264:3392], hT[:, k, :], start=(k == 0), stop=(k == 7))
        of = f32w.next()
        fw.copy("act", of[:], zp[:])
        fw.st(C.ZG[:, t0:t0 + 512], of[:])
        yield
        for j in range(4):
            tp_ = tps.next()
            for k in range(8):
                fw.mm(tp_[:, 0:128], hT[:, k, j * 128:(j + 1) * 128], W[:, k, 384:512], start=(k == 0), stop=(k == 7), sig=False)
            for k in range(8):
                fw.mm(tp_[:, 128:384], hT[:, k, j * 128:(j + 1) * 128], W[:, k, 1024:1280], start=(k == 0), stop=(k == 7))
            ob = bfw.next()
            fw.copy("act", ob[:, 0:384], tp_[:, 0:384])
            r0 = t0 + j * 128
            fw.st(C.VA[r0:r0 + 128, :], ob[:, 0:128])
            fw.st(C.VB[r0:r0 + 128, :], ob[:, 128:384])
            yield
            tp2 = tps.next()
            for k in range(8):
                fw.mm(tp2[:, 0:256], hT[:, k, j * 128:(j + 1) * 128], W[:, k, 1792:2048], start=(k == 0), stop=(k == 7))
            ob2 = bfw.next()
            fw.copy("dve", ob2[:, 0:256], tp2[:, 0:256])
            fw.st(C.VC[r0:r0 + 128, :], ob2[:, 0:256])
            yield

    def hT_gen(b, hT):
        t0 = b * 512
        for j in range(4):
            make_hT(fw, K, src[t0 + j * 128:t0 + (j + 1) * 128, :], gain, hT, j, P)
            yield

    def run_pipe(gens, depth, side=None):
        gens = list(gens)
        active = []
        tick = 0
        while gens or active:
            while gens and len(active) < depth:
                active.append(gens.pop(0))
            for g in list(active):
                try:
                    next(g)
                except StopIteration:
                    active.remove(g)
            tick += 1
            if side is not None and tick % 8 == 0:
                try:
                    next(side)
                except StopIteration:
                    side = None
        if side is not None:
            for _ in side:
                pass

    nb = T_ // 512
    hT_cur = hTs.next()
    for _ in hT_gen(0, hT_cur):
        pass
    for b in range(nb):
        t0 = b * 512
        tab = tabs.next()
        fw.ld(tab[:], d["c_rope"][:, :, t0:t0 + 512])
        hT = hT_cur
        side = None
        if b + 1 < nb:
            hT_cur = hTs.next()
            side = hT_gen(b + 1, hT_cur)
        gens = [chunk_gen(kind, col, dst, slot, gi, hT, tab, t0) for (kind, col, dst, slot, gi) in chunks]
        gens.append(tail_gen(hT, t0))
        run_pipe(gens, DEPTHP, side)
    ph.close()


def attn_core(fw, ph, K, T_, QT, KT, V, heads, negM, scale, emit):
    LOOK = 2
    sps = ph.pspool(4, [128, 512], F32, "sps")
    ops_ = ph.pspool(2, [128, 512], F32, "ops")
    pts = ph.sbpool(4, [128, 512], BF16, "pt")
    osbs = ph.sbpool(4, [65, 512], F32, "osb")
    nkt = T_ // 128
    iters = []
    for qb in range(T_ // 512):
        for hd in heads:
            for kt in range(nkt):
                iters.append((qb, hd, kt))
    pend = []
    state = {"o_ps": None}

    def do_pv(item):
        (qb, hd, kt, pt) = item
        (qs, lo, hi, ks, vs, tag) = hd
        if kt == 0:
            state["o_ps"] = ops_.next()
        o_ps = state["o_ps"]
        fw.mm(o_ps[0:65, :], V[:, kt, vs, :], pt[:], start=(kt == 0), stop=(kt == nkt - 1))
        if kt == nkt - 1:
            osb = osbs.next()
            fw.copy("dve", osb[:], o_ps[0:65, :])
            emit(qb, tag, osb)

    for (qb, hd, kt) in iters:
        (qs, lo, hi, ks, vs, tag) = hd
        q0 = qb * 512
        st = sps.next()
        fw.mm(st[:, :], KT[lo:hi, ks, kt * 128:(kt + 1) * 128], QT[lo:hi, qs, q0:q0 + 512])
        pt = pts.next()
        if negM is None:
            fw.act(pt[:], st[:], AF.Exp, scale=scale)
        else:
            fw.act(pt[:], st[:], AF.Exp, scale=scale, bias=negM[:, 0:1])
        pend.append((qb, hd, kt, pt))
        if len(pend) > LOOK:
            do_pv(pend.pop(0))
    while pend:
        do_pv(pend.pop(0))


def attn_core2(fw, ph, T_, groups, V, negM, scale, emit, bcs):
    LOOK = 2
    sps = ph.pspool(3, [128, 1024], F32, "sps")
    ops_ = bcs
    pts = ph.sbpool(4, [128, 1024], BF16, "pt")
    osbs = ph.sbpool(4, [65, 512], F32, "osb")
    nkt = T_ // 128
    iters = []
    for qb in range(T_ // 512):
        for g in groups:
            for kt in range(nkt):
                iters.append((qb, g, kt))
    pend = []
    state = {}
    import os
    NW = int(os.environ.get("DBG_WARM", "0"))
    if NW:
        wz = pts.next()
        fw.memset("pool", wz[:], 1.0)
        st0 = sps.next()
        for i in range(NW):
            fw.mm(st0[:, 0:512], wz[:, 0:128], wz[:, 512:1024], sig=(i == NW - 1))

    def do_pv(item):
        (qb, g, kt, pt) = item
        (kt_ap, q_ap, vs, tag) = g
        if kt == 0:
            state["o"] = [ops_.next(), ops_.next()]
        o = state["o"]
        for j in range(2):
            fw.mm(o[j][0:65, :], V[:, kt, vs[j], :], pt[:, j * 512:(j + 1) * 512], start=(kt == 0), stop=(kt == nkt - 1))
        if kt == nkt - 1:
            osb = [osbs.next(), osbs.next()]
            for j in range(2):
                fw.copy("dve", osb[j][:], o[j][0:65, :])
            emit(qb, tag, osb)

    for (qb, g, kt) in iters:
        (kt_ap, q_ap, vs, tag) = g
        st = sps.next()
        for j in range(2):
            fw.mm(st[:, j * 512:(j + 1) * 512], kt_ap(j, kt), q_ap(j, qb), sig=(j == 1))
        pt = pts.next()
        if negM is None:
            fw.act(pt[:], st[:], AF.Exp, scale=scale)
        else:
            fw.act(pt[:], st[:], AF.Exp, scale=scale, bias=negM[:, 0:1])
        pend.append((qb, g, kt, pt))
        if len(pend) > LOOK:
            do_pv(pend.pop(0))
    while pend:
        do_pv(pend.pop(0))


def bcast_recip(fw, K, osb, recs, hls, bcs, mulv):
    rec = recs.next()
    fw.act(rec[64:65, :], osb[64:65, :], AF.Ln)
    fw.act(rec[64:65, :], rec[64:65, :], AF.Exp, scale=-1.0)
    if mulv is not None:
        fw.ts("dve", rec[64:65, :], rec[64:65, :], mulv[64:65, 0:1], None, ALU.mult)
    hl = hls.next()
    fw.copy("dve", hl[64:65, 0, :], rec[64:65, :])
    fw.tt("dve", hl[64:65, 1, :], rec[64:65, :], hl[64:65, 0, :], ALU.subtract)
    bc = bcs.next()
    fw.mm(bc[0:64, :], K.ones_bf[64:65, 0:64], hl[64:65, 0, :], start=True, stop=False, sig=False)
    fw.mm(bc[0:64, :], K.ones_bf[64:65, 0:64], hl[64:65, 1, :], start=False, stop=True)
    return bc


def phase_attnA(fw, C, K, l, si):
    T_ = C.seqs[si]
    d = C.d
    ph = fw.phase()
    QT = ph.sb([128, 2, T_], BF16, "QT")
    KT = ph.sb([128, T_], BF16, "KT")
    V = ph.sb([128, T_ // 128, 2, 65], BF16, "V")
    for p in range(2):
        fw.ld(QT[0:64, p, :], C.QA[:, p, 0:T_])
        fw.ld(QT[64:128, p, :], C.QA[:, p + 2, 0:T_])
    for h in range(2):
        fw.ld(KT[64 * h:64 * h + 64, :], C.KA[:, h, 0:T_])
    fw.memset("pool", V[:, :, :, 64:65], 1.0)
    for h in range(2):
        fw.ld(V[:, :, h, 0:64], C.VA[0:T_, h * 64:(h + 1) * 64].rearrange("(n p) c -> p n c", p=128))
    g2 = ph.sb([128, 2, 64], F32, "g2")
    fw.ld(g2[:, 0, :], d["a_q_gain"][l].partition_broadcast(128))
    fw.ld(g2[:, 1, :], d["a_k_gain"][l].partition_broadcast(128))
    fw.tt("dve", g2[:], g2[:], g2[:], ALU.mult)
    mx = ph.sb([128, 2], F32, "mx")
    fw.reduce(mx[:], g2[:], ALU.max)
    negM = ph.sb([128, 1], F32, "negM")
    fw.tt("dve", negM[:], mx[:, 0:1], mx[:, 1:2], ALU.mult)
    fw.act(negM[:], negM[:], AF.Sqrt, scale=64.0)
    fw.ts("dve", negM[:], negM[:], -1.0, None, ALU.mult)
    recs = ph.sbpool(2, [65, 512], F32, "rec")
    hls = ph.sbpool(2, [65, 2, 512], BF16, "hl")
    bcs = ph.pspool(2, [128, 512], F32, "bc")
    outs = ph.sbpool(3, [64, 512], BF16, "ob")

    def emit(qb, p, osbs_):
        for j in range(2):
            h = p + 2 * j
            osb = osbs_[j]
            bc = bcast_recip(fw, K, osb, recs, hls, bcs, None)
            ob = outs.next()
            fw.tt("dve", ob[:], osb[0:64, :], bc[0:64, :], ALU.mult)
            fw.st(C.OT[h, :, qb * 512:(qb + 1) * 512], ob[:])

    groups = []
    for p in range(2):
        groups.append((lambda j, kt: KT[64 * j:64 * j + 64, kt * 128:(kt + 1) * 128],
                       (lambda p_: (lambda j, qb: QT[64 * j:64 * j + 64, p_, qb * 512:(qb + 1) * 512]))(p),
                       (0, 1), p))
    attn_core2(fw, ph, T_, groups, V, negM, 0.125, emit, bcs)
    ph.close()


def phase_attnB(fw, C, K, l, si):
    T_ = C.seqs[si]
    d = C.d
    lam_init = 0.8 - 0.6 * math.exp(-0.3 * l)
    ph = fw.phase()
    QT = ph.sb([128, 2, 2, T_], BF16, "QT")
    KT = ph.sb([128, 2, T_], BF16, "KT")
    V = ph.sb([128, T_ // 128, 4, 65], BF16, "V")
    for hh in range(2):
        fw.memset("pool", QT[64 * hh + 32:64 * hh + 64, :, 0, :], 0.0)
        fw.memset("pool", QT[64 * hh:64 * hh + 32, :, 1, :], 0.0)
    for p in range(2):
        for hh in range(2):
            h = p + 2 * hh
            fw.ld(QT[64 * hh:64 * hh + 32, p, 0, :], C.QB[0:32, h, 0:T_])
            fw.ld(QT[64 * hh + 32:64 * hh + 64, p, 1, :], C.QB[32:64, h, 0:T_])
            fw.ld(KT[64 * hh:64 * hh + 64, p, :], C.KB[:, h, 0:T_])
    fw.memset("pool", V[:, :, :, 64:65], 1.0)
    for h in range(4):
        fw.ld(V[:, :, h, 0:64], C.VB[0:T_, h * 64:(h + 1) * 64].rearrange("(n p) c -> p n c", p=128))
    lp = ph.sb([128, 4, 32], F32, "lp")
    fw.ld(lp[:], d["b_lambda"][l].partition_broadcast(128))
    pr = ph.sb([128, 2, 32], F32, "pr")
    fw.tt("dve", pr[:, 0, :], lp[:, 0, :], lp[:, 1, :], ALU.mult)
    fw.tt("dve", pr[:, 1, :], lp[:, 2, :], lp[:, 3, :], ALU.mult)
    sm = ph.sb([128, 2], F32, "sm")
    fw.reduce(sm[:], pr[:], ALU.add)
    fw.act(sm[:], sm[:], AF.Exp)
    nlam = ph.sb([128, 1], F32, "nlam")
    fw.tt("dve", nlam[:], sm[:, 1:2], sm[:, 0:1], ALU.subtract)
    fw.ts("dve", nlam[:], nlam[:], -lam_init, None, ALU.add)
    gsub = ph.sb([64, 1], F32, "gsub")
    fw.ld(gsub[:], d["b_subln_gain"][l].rearrange("(p o) -> p o", o=1))
    fw.ts("dve", gsub[:], gsub[:], 1.0 - lam_init, None, ALU.mult)
    recs = ph.sbpool(2, [65, 512], F32, "rec")
    hls = ph.sbpool(2, [65, 2, 512], BF16, "hl")
    bcs = ph.pspool(2, [128, 512], F32, "bc")
    o1s = ph.sbpool(6, [64, 512], F32, "o1")
    rss = ph.sbpool(2, [64, 512], F32, "rsb")
    sqs = ph.sbpool(2, [64, 512], BF16, "sq")
    outs = ph.sbpool(3, [64, 512], BF16, "ob")

    pend_o = {}

    def emit(qb, tag, osbs_):
        (p, s_) = tag
        for hh in range(2):
            h = p + 2 * hh
            osb = osbs_[hh]
            bc = bcast_recip(fw, K, osb, recs, hls, bcs, nlam if s_ == 1 else None)
            o_ = o1s.next()
            fw.tt("dve", o_[:], osb[0:64, :], bc[0:64, :], ALU.mult)
            if s_ == 0:
                pend_o[h] = o_
            else:
                finish(qb, h, pend_o.pop(h), o_)

    def finish(qb, h, o1, o2):
        fw.tt("pool", o1[:], o1[:], o2[:], ALU.add)
        sq = sqs.next()
        fw.act(sq[:], o1[:], AF.Square)
        ms = bcs.next()
        fw.mm(ms[0:64, :], K.ones_bf[0:64, 0:64], sq[:])
        rs = rss.next()
        fw.ts("dve", rs[:], ms[0:64, :], 1.0 / 64, EPS, ALU.mult, ALU.add)
        fw.act(rs[:], rs[:], AF.Ln)
        fw.act(rs[:], rs[:], AF.Exp, scale=-0.5)
        ob = outs.next()
        fw.stt(ob[:], o1[:], gsub[:, 0:1], rs[:], ALU.mult, ALU.mult)
        fw.st(C.OT[4 + h, :, qb * 512:(qb + 1) * 512], ob[:])

    groups = []
    for p in range(2):
        for s_ in range(2):
            groups.append(((lambda p_: (lambda j, kt: KT[64 * j:64 * j + 64, p_, kt * 128:(kt + 1) * 128]))(p),
                           (lambda p_, s__: (lambda j, qb: QT[64 * j:64 * j + 64, p_, s__, qb * 512:(qb + 1) * 512]))(p, s_),
                           (p, p + 2), (p, s_)))
    attn_core2(fw, ph, T_, groups, V, None, 32 ** -0.5, emit, bcs)
    ph.close()


def phase_ret(fw, C, K, l, si):
    T_ = C.seqs[si]
    d = C.d
    n = T_ // 128
    ph = fw.phase()
    QT = ph.sb([64, 4, T_], BF16, "QT")
    KT = ph.sb([64, 4, T_], BF16, "KT")
    G = ph.sb([64, 4, T_], BF16, "G")
    V = ph.sb([128, n, 256], BF16, "V")
    for h in range(4):
        fw.ld(QT[:, h, :], C.QC[:, h, 0:T_])
        fw.ld(KT[:, h, :], C.KC[:, h, 0:T_])
        fw.ld(G[:, h, :], C.GC[:, h, 0:T_])
    fw.ld(V[:], C.VC[0:T_, :].rearrange("(n p) c -> p n c", p=128))
    mask = ph.sb([128, 4, 128], F32, "mask")
    fw.ld(mask[:], d["c_retmask"])
    qdec = ph.sb([64, 2, 4, 128], F32, "qdec")
    fw.ld(qdec[:], d["c_qdec"])
    kdec = ph.sb([128, 2, 4], F32, "kdec")
    fw.ld(kdec[:], d["c_kdec"])
    gng = ph.sb([64, 4], F32, "gng")
    fw.ldn(gng[:], d["c_gn_gain"][l].rearrange("(h p) -> p h", p=64))
    Sf = ph.sb([64, n, 4, 64], BF16, "Sf")
    Sb = ph.sb([64, n, 4, 64], BF16, "Sb")
    cur = [ph.sb([64, 4, 64], F32, "curf"), ph.sb([64, 4, 64], F32, "curb")]
    ktp = ph.pspool(2, [128, 4, 64], BF16, "ktp")
    kds = ph.sbpool(4, [128, 4, 64], BF16, "kds")
    kvp = ph.pspool(2, [128, 512], F32, "kvp")
    gamt = ph.sb([64, 2, 4], F32, "gamt")
    for dr in range(2):
        for h in range(4):
            g_ = (RET_G[h] if dr == 0 else RET_G[3 - h]) ** 128
            fw.memset("pool", gamt[:, dr, h:h + 1], g_)
    tmpc = [ph.sbpool(2, [64, 4, 64], F32, "tmpc"), ph.sbpool(2, [64, 4, 64], F32, "tmpc")]

    def state_gen(dr):
        fw.memset("dve", cur[dr][:], 0.0)
        order = range(n) if dr == 0 else range(n - 1, -1, -1)
        Sx = Sf if dr == 0 else Sb
        for c in order:
            fw.copy("act", Sx[:, c, :, :], cur[dr][:])
            if (dr == 0 and c == n - 1) or (dr == 1 and c == 0):
                break
            tp = ktp.next()
            for h in range(4):
                fw.tr(tp[:, h, :], KT[:, h, c * 128:(c + 1) * 128], K.ident[0:64, 0:64], sig=(h == 3))
            kd = kds.next()
            fw.tt("dve", kd[:], tp[:], kdec.bc((slice(None), dr, slice(None)), 2, [128, 4, 64]), ALU.mult)
            tmp = tmpc[dr].next()
            fw.tt("pool", tmp[:], cur[dr][:], gamt.bc((slice(None), dr, slice(None)), 2, [64, 4, 64]), ALU.mult)
            yield
            kv = kvp.next()
            for h in range(4):
                fw.mm(kv[0:64, h * 64:(h + 1) * 64], kd[:, h, :], V[:, c, h * 64:(h + 1) * 64], sig=(h == 3))
            fw.tt("dve", cur[dr][:], tmp[:], View(kv, kv.ap[0:64, 0:256].rearrange("p (h e) -> p h e", h=4)), ALU.add)
            yield

    run_pipe([state_gen(0), state_gen(1)], 2)
    inp = ph.pspool(2, [128, 512], F32, "inp")
    ins_ = ph.sbpool(5, [128, 4, 128], BF16, "ins")
    qds = ph.sbpool(5, [64, 2, 4, 128], BF16, "qds")
    otp = ph.pspool(1, [128, 512], F32, "otp")
    osb = ph.sbpool(5, [64, 4, 128], F32, "osb")
    sqs = ph.sbpool(5, [64, 4, 128], BF16, "sq")
    msp = ph.pspool(1, [128, 512], F32, "msp")
    rss = ph.sbpool(2, [64, 4, 128], F32, "rs")
    obs = ph.sbpool(3, [64, 4, 128], BF16, "ob")

    def out_gen(c):
        cs = slice(c * 128, (c + 1) * 128)
        ip = inp.next()
        for h in range(4):
            fw.mm(ip[:, h * 128:(h + 1) * 128], KT[:, h, cs], QT[:, h, cs], sig=(h == 3))
        it = ins_.next()
        fw.tt("dve", it[:], View(ip, ip.ap[:, :].rearrange("p (h i) -> p h i", h=4)), mask[:], ALU.mult)
        qd = qds.next()
        for dr in range(2):
            fw.tt("pool", qd[:, dr, :, :], QT[:, :, cs], qdec[:, dr, :, :], ALU.mult)
        yield
        op_ = otp.next()
        for h in range(4):
            o_h = op_[0:64, h * 128:(h + 1) * 128]
            fw.mm(o_h, V[:, c, h * 64:(h + 1) * 64], it[:, h, :], start=True, stop=False, sig=False)
            fw.mm(o_h, Sf[:, c, h, :], qd[:, 0, h, :], start=False, stop=False, sig=False)
            fw.mm(o_h, Sb[:, c, h, :], qd[:, 1, h, :], start=False, stop=True, sig=(h == 3))
        o = osb.next()
        fw.ts("dve", o[:], View(op_, op_.ap[0:64, :].rearrange("p (h i) -> p h i", h=4)), 0.125, None, ALU.mult)
        sq = sqs.next()
        fw.act(sq[:], o[:], AF.Square)
        yield
        ms = msp.next()
        fw.mm(ms[0:64, :], K.ones_bf[0:64, 0:64], View(sq, sq.ap[:].rearrange("p h i -> p (h i)")))
        rs = rss.next()
        fw.ts("dve", View(rs, rs.ap[:].rearrange("p h i -> p (h i)")), ms[0:64, :], 1.0 / 64, EPS, ALU.mult, ALU.add)
        fw.act(rs[:], rs[:], AF.Ln)
        fw.act(rs[:], rs[:], AF.Exp, scale=-0.5)
        fw.tt("dve", o[:], o[:], rs[:], ALU.mult)
        fw.tt("pool", o[:], o[:], gng.bc((slice(None), slice(None)), 2, [64, 4, 128]), ALU.mult)
        ob = obs.next()
        fw.tt("dve", ob[:], o[:], G[:, :, cs], ALU.mult)
        for h in range(4):
            fw.st(C.OT[8 + h, :, cs], ob[:, h, :])

    run_pipe([out_gen(c) for c in range(n)], 3)
    ph.close()


def phase_outproj(fw, C, K, l, si):
    T_ = C.seqs[si]
    d = C.d
    src = C.xin[si] if l == 0 else C.y[si]
    ph = fw.phase()
    W = ph.sb([128, 8, 1024], BF16, "Wo")
    for k in range(8):
        fw.ld(W[:, k, :], d["w_out"][l, k * 128:(k + 1) * 128, :], q="pool")
    gain = ph.sb([128, 1024], F32, "gain")
    fw.ld(gain[:], d["norm_mix_post"][l].partition_broadcast(128))
    oTs = ph.sbpool(2, [128, 8, 512], BF16, "oT")
    xts = ph.sbpool(4, [128, 1024], F32, "xt")
    mps = ph.pspool(8, [128, 512], F32, "mp")
    tmps = ph.sbpool(4, [128, 1024], F32, "tmp")
    sss = ph.sbpool(4, [128, 2], F32, "ss")
    rstds = ph.sbpool(4, [128, 1], F32, "rstd")

    def sub_gen(oT, j, r0):
        xt = xts.next()
        fw.ld(xt[:], src[r0:r0 + 128, :])
        m = [mps.next(), mps.next()]
        for half in range(2):
            for k in range(8):
                fw.mm(m[half][:, :], oT[:, k, j * 128:(j + 1) * 128], W[:, k, half * 512:(half + 1) * 512], start=(k == 0), stop=(k == 7))
        tmp = tmps.next()
        ss = sss.next()
        for half in range(2):
            fw.act(tmp[:, half * 512:(half + 1) * 512], m[half][:, :], AF.Square, accum=ss[:, half:half + 1])
        yield
        rstd = rstds.next()
        fw.tt("dve", rstd[:], ss[:, 0:1], ss[:, 1:2], ALU.add)
        fw.ts("dve", rstd[:], rstd[:], 1.0 / D_MODEL, EPS, ALU.mult, ALU.add)
        fw.act(rstd[:], rstd[:], AF.Ln)
        fw.act(rstd[:], rstd[:], AF.Exp, scale=-0.5)
        yield
        for half in range(2):
            hs = slice(half * 512, (half + 1) * 512)
            fw.stt(tmp[:, hs], m[half][:, :], rstd[:, 0:1], gain[:, hs], ALU.mult, ALU.mult)
        fw.tt("pool", tmp[:], tmp[:], xt[:], ALU.add)
        fw.st(C.y[si][r0:r0 + 128, :], tmp[:])

    gens = []
    for b in range(T_ // 512):
        t0 = b * 512
        oT = oTs.next()

        def ldgen(oT=oT, t0=t0):
            for k in range(8):
                fw.ld(oT[0:64, k, :], C.OT[2 * k, :, t0:t0 + 512])
                fw.ld(oT[64:128, k, :], C.OT[2 * k + 1, :, t0:t0 + 512])
            return
            yield
        gens.append(("ld", oT, t0))
        for j in range(4):
            gens.append(("sub", oT, j, t0 + j * 128))

    def all_gens():
        for g in gens:
            if g[0] == "ld":
                _, oT, t0 = g
                for k in range(8):
                    fw.ld(oT[0:64, k, :], C.OT[2 * k, :, t0:t0 + 512])
                    fw.ld(oT[64:128, k, :], C.OT[2 * k + 1, :, t0:t0 + 512])
            else:
                yield sub_gen(g[1], g[2], g[3])

    class LazyList(list):
        pass
    pending = all_gens()
    active = []
    done = False
    while not done or active:
        while not done and len(active) < 3:
            try:
                active.append(next(pending))
            except StopIteration:
                done = True
        for g in list(active):
            try:
                next(g)
            except StopIteration:
                active.remove(g)
    ph.close()


def phase_ffn(fw, C, K, l, si):
    T_ = C.seqs[si]
    d = C.d
    TB = 1024
    ph = fw.phase()
    gain = ph.sb([128, 1024], F32, "gain")
    fw.ld(gain[:], d["norm_ffn_pre"][l].partition_broadcast(128))
    gpost = ph.sb([128, 1024], F32, "gpost")
    fw.ld(gpost[:], d["norm_ffn_post"][l].partition_broadcast(128))
    P = Ctx()
    P.xt = ph.sbpool(2, [128, 1024], F32, "xt")
    P.hb = ph.sbpool(2, [128, 1024], BF16, "hb")
    P.tmp = ph.sbpool(2, [128, 1024], F32, "tmp")
    P.ss = ph.sbpool(2, [128, 1], F32, "ss")
    P.rstd = ph.sbpool(2, [128, 1], F32, "rstd")
    P.tp = ph.pspool(1, [128, 8, 128], BF16, "tp")
    hTs = ph.sbpool(1, [128, 8, TB], BF16, "hT")
    act = ph.sb([128, NFF, TB], BF16, "act")
    FG = 4
    wgs = ph.sbpool(2, [128, 8, 128 * FG], BF16, "wg")
    wus = ph.sbpool(2, [128, 8, 128 * FG], BF16, "wu")
    Wd = [ph.sb([128, 1024], BF16, "Wd%d" % f) for f in range(NFF)]
    gps = ph.pspool(2, [128, 512], F32, "gp")
    ups = ph.pspool(2, [128, 512], F32, "up")
    dps = ph.pspool(2, [128, 512], F32, "dp")
    sgs = ph.sbpool(2, [128, 512], F32, "sg")
    sss = ph.sbpool(4, [128, 2], F32, "ss2")
    dxt = ph.sbpool(3, [128, 1024], F32, "dxt")
    dtmp = ph.sbpool(3, [128, 1024], F32, "dtmp")
    drstd = ph.sbpool(4, [128, 1], F32, "drstd")
    wg_d = d["ffn_w_gate"][l].rearrange("(k p) n -> p k n", p=128)
    wu_d = d["ffn_w_up"][l].rearrange("(k p) n -> p k n", p=128)
    def hT_gen(b, hT_):
        t0_ = b * TB
        for j in range(TB // 128):
            make_hT(fw, K, C.y[si][t0_ + j * 128:t0_ + (j + 1) * 128, :], gain, hT_, j, P)
            yield

    nblk = T_ // TB
    for b in range(nblk):
        t0 = b * TB
        hT = hTs.next()
        for _ in hT_gen(b, hT):
            pass
        side = None
        for f in range(NFF):
            if b == 0:
                fw.ld(Wd[f][:], d["ffn_w_down"][l, f * 128:(f + 1) * 128, :], q="pool")
            if f % FG == 0:
                nf = min(FG, NFF - f)
                wg_t = wgs.next()
                fw.ld(wg_t[:, :, 0:128 * nf], wg_d[:, :, f * 128:(f + nf) * 128], q="pool")
                wu_t = wus.next()
                fw.ld(wu_t[:, :, 0:128 * nf], wu_d[:, :, f * 128:(f + nf) * 128], q="pool")
            fo = (f % FG) * 128
            for tb in range(TB // 512):
                ts_ = slice(tb * 512, (tb + 1) * 512)
                gp = gps.next()
                up = ups.next()
                for k in range(8):
                    fw.mm(gp[:, :], wg_t[:, k, fo:fo + 128], hT[:, k, ts_], start=(k == 0), stop=(k == 7))
                for k in range(8):
                    fw.mm(up[:, :], wu_t[:, k, fo:fo + 128], hT[:, k, ts_], start=(k == 0), stop=(k == 7))
                sg = sgs.next()
                fw.act(sg[:], gp[:, :], AF.Silu)
                fw.tt("dve", act[:, f, ts_], sg[:], up[:, :], ALU.mult)
        ph_all = Pool(dps.items + gps.items + ups.items)
        if side is not None:
            for _ in side:
                pass
        def down_gen(j):
            r0 = t0 + j * 128
            bk = [ph_all.next(), ph_all.next()]
            xt = dxt.next()
            fw.ld(xt[:], C.y[si][r0:r0 + 128, :])
            for f in range(NFF):
                for half in range(2):
                    fw.mm(bk[half][:, :], act[:, f, j * 128:(j + 1) * 128], Wd[f][:, half * 512:(half + 1) * 512],
                          start=(f == 0), stop=(f == NFF - 1))
            tmp = dtmp.next()
            ss = sss.next()
            for half in range(2):
                fw.act(tmp[:, half * 512:(half + 1) * 512], bk[half][:, :], AF.Square, accum=ss[:, half:half + 1])
            yield
            rstd = drstd.next()
            fw.tt("dve", rstd[:], ss[:, 0:1], ss[:, 1:2], ALU.add)
            fw.ts("dve", rstd[:], rstd[:], 1.0 / D_MODEL, EPS, ALU.mult, ALU.add)
            fw.act(rstd[:], rstd[:], AF.Ln)
            fw.act(rstd[:], rstd[:], AF.Exp, scale=-0.5)
            yield
            for half in range(2):
                hs = slice(half * 512, (half + 1) * 512)
                fw.stt(tmp[:, hs], bk[half][:, :], rstd[:, 0:1], gpost[:, hs], ALU.mult, ALU.mult)
            fw.tt("pool", tmp[:], tmp[:], xt[:], ALU.add)
            fw.st(C.y[si][r0:r0 + 128, :], tmp[:])

        run_pipe([down_gen(j) for j in range(TB // 128)], 3)
    ph.close()


def phase_rwkv(fw, C, K, l, si):
    T_ = C.seqs[si]
    d = C.d
    n = T_ // 128
    CW = math.exp(-0.5)
    ph = fw.phase()
    mu = ph.sb([64, 3, 15], F32, "mu")
    fw.ldn(mu[:, 1, :], d["d_mu_prev"][l, 0:960].rearrange("(j p) -> p j", p=64))
    fw.ldn(mu[:, 2, :], d["d_mu_next"][l, 0:960].rearrange("(j p) -> p j", p=64))
    fw.tt("dve", mu[:, 0, :], mu[:, 1, :], mu[:, 2, :], ALU.add)
    fw.ts("dve", mu[:, 0, :], mu[:, 0, :], -1.0, 1.0, ALU.mult, ALU.add)
    mug = ph.sb([128, 3], F32, "mug")
    fw.ld(mug[:, 1:2], d["d_mu_prev"][l, 960:1088].rearrange("(p o) -> p o", o=1))
    fw.ld(mug[:, 2:3], d["d_mu_next"][l, 960:1088].rearrange("(p o) -> p o", o=1))
    fw.tt("dve", mug[:, 0:1], mug[:, 1:2], mug[:, 2:3], ALU.add)
    fw.ts("dve", mug[:, 0:1], mug[:, 0:1], -1.0, 1.0, ALU.mult, ALU.add)
    prm = ph.sb([64, 9, 4], F32, "prm")
    fw.ldn(prm[:, 0, :], d["d_w0"][l, 0].rearrange("(h p) -> p h", p=64))
    fw.ldn(prm[:, 1, :], d["d_w0"][l, 1].rearrange("(h p) -> p h", p=64))
    fw.ldn(prm[:, 2, :], d["d_a0"][l].rearrange("(h p) -> p h", p=64))
    fw.ldn(prm[:, 3, :], d["d_k_k"][l].rearrange("(h p) -> p h", p=64))
    fw.ldn(prm[:, 4, :], d["d_k_a"][l].rearrange("(h p) -> p h", p=64))
    fw.ldn(prm[:, 6, :], d["d_r_k"][l].rearrange("h p -> p h"))
    fw.ldn(prm[:, 7, :], d["d_gn_w"][l].rearrange("(h p) -> p h", p=64))
    fw.ldn(prm[:, 8, :], d["d_gn_b"][l].rearrange("(h p) -> p h", p=64))
    fw.ts("dve", prm[:, 5, :], prm[:, 4, :], -1.0, 1.0, ALU.mult, ALU.add)
    wup = ph.sb([64, 2, 256], BF16, "wup")
    fw.ld(wup[:, 0, :], d["d_w_up"][l, 0], q="pool")
    fw.ld(wup[:, 1, :], d["d_w_up"][l, 1], q="pool")
    aup = ph.sb([64, 256], BF16, "aup")
    fw.ld(aup[:], d["d_a_up"][l], q="pool")
    gup = ph.sb([128, 256], BF16, "gup")
    fw.ld(gup[:], d["d_g_up"][l], q="pool")
    dmask = ph.sb([128, 2, 5, 128], BF16, "dmask")
    fw.ld(dmask[:], d["c_dmask"], q="pool")
    seg = ph.sb([64, 512], F32, "seg")
    fw.ld(seg[:], d["c_seg"])

    def pbc(i):
        return prm.bc((slice(None), i, slice(None)), 2, [64, 4, 128])

    ytok = [T(None, "ytok%d" % c) for c in range(n)]
    H = [ph.sb([64, 4, 64], F32, "Hf"), ph.sb([64, 4, 64], F32, "Hb")]
    Hb = [ph.sb([64, 4, 64], BF16, "Hfb"), ph.sb([64, 4, 64], BF16, "Hbb")]
    for dr in range(2):
        fw.memset("dve", H[dr][:], 0.0)
        fw.memset("pool", Hb[dr][:], 0.0)
    PST = ph.pspool(1, [128, 4, 4, 64], BF16, "pst")
    PP = []
    for dr in range(2):
        Q = Ctx()
        Q.PS = ph.pspool(4 if dr == 0 else 3, [128, 512], F32, "ps")
        Q.zts = ph.sbpool(1, [64, 15, 130], F32, "zt")
        Q.zgs = ph.sbpool(2, [128, 130], F32, "zg")
        Q.us = ph.sbpool(1, [64, 15, 128], F32, "u")
        Q.u2s = ph.sbpool(1, [64, 15, 128], F32, "u2")
        Q.ugs = ph.sbpool(1, [128, 128], F32, "ug")
        Q.F4 = ph.sbpool(10, [64, 4, 128], F32, "f4")
        Q.FL = ph.sbpool(3, [64, 4, 128], F32, "fl")
        Q.B4 = ph.sbpool(12, [64, 4, 128], BF16, "b4")
        Q.M4 = ph.sbpool(9, [128, 4, 128], BF16, "m4")
        Q.MK = ph.sbpool(4, [128, 4, 128], BF16, "mk")
        Q.TM = ph.sbpool(2, [128, 4, 4, 64], BF16, "tm")
        Q.SM = ph.sbpool(4, [128, 128], BF16, "sm")
        Q.S4 = ph.sbpool(6, [64, 4], F32, "s4")
        Q.UV = ph.sbpool(2, [128, 4, 64], F32, "uv")
        Q.UB = ph.sbpool(3, [128, 4, 64], BF16, "ub")
        PP.append(Q)

    def v3(t, npart=64):
        return View(t, t.ap[0:npart, :].rearrange("p (h i) -> p h i", h=4))

    def v3s(t):
        return View(t, t.ap[:, 0:256].rearrange("p (h i) -> p h i", h=4))

    def flat(t):
        return View(t, t.ap[:].rearrange("p h i -> p (h i)"))

    def unit(c, dr, first):
        Q = PP[dr]
        PS, F4, FL, B4, M4, MK, SM, S4 = Q.PS, Q.F4, Q.FL, Q.B4, Q.M4, Q.MK, Q.SM, Q.S4
        need_post = not first
        c0 = c * 128
        zt = Q.zts.next()
        zg = Q.zgs.next()
        lo = 1 if c == 0 else 0
        hi = 129 if c == n - 1 else 130
        if c == 0:
            fw.memset("pool", zt[:, :, 0:1], 0.0)
            fw.memset("pool", zg[:, 0:1], 0.0)
        if c == n - 1:
            fw.memset("pool", zt[:, :, 129:130], 0.0)
            fw.memset("pool", zg[:, 129:130], 0.0)
        fw.ld(zt[:, :, lo:hi], C.ZD[:, :, c0 - 1 + lo:c0 - 1 + hi])
        fw.ld(zg[:, lo:hi], C.ZG[:, c0 - 1 + lo:c0 - 1 + hi])
        if need_post:
            yprev = FL.next()
            fw.ld(yprev[:], View(ytok[c], C.YD[c]))
        yield
        u = Q.us.next()
        u2 = Q.u2s.next()
        fw.tt("dve", u[:], zt[:, :, 1:129], mu.bc((slice(None), 0, slice(None)), 2, [64, 15, 128]), ALU.mult)
        fw.tt("pool", u2[:], zt[:, :, 0:128], mu.bc((slice(None), 1, slice(None)), 2, [64, 15, 128]), ALU.mult)
        yield
        fw.tt("dve", u[:], u[:], u2[:], ALU.add)
        fw.tt("pool", u2[:], zt[:, :, 2:130], mu.bc((slice(None), 2, slice(None)), 2, [64, 15, 128]), ALU.mult)
        yield
        fw.tt("dve", u[:], u[:], u2[:], ALU.add)
        ug = Q.ugs.next()
        fw.ts("dve", ug[:], zg[:, 1:129], mug[:, 0:1], None, ALU.mult)
        fw.stt(ug[:], zg[:, 0:128], mug[:, 1:2], ug[:], ALU.mult, ALU.add)
        fw.stt(ug[:], zg[:, 2:130], mug[:, 2:3], ug[:], ALU.mult, ALU.add)
        yield
        tw = SM.next()
        fw.act(tw[0:64, :], u[:, 12 + dr, :], AF.Tanh)
        adb = SM.next()
        fw.copy("act", adb[0:64, :], u[:, 14, :])
        yw = PS.next()
        ya = PS.next()
        for h in range(4):
            fw.mm(yw[0:64, h * 128:(h + 1) * 128], wup[:, dr, h * 64:(h + 1) * 64], tw[0:64, :], sig=(h == 3))
        for h in range(4):
            fw.mm(ya[0:64, h * 128:(h + 1) * 128], aup[:, h * 64:(h + 1) * 64], adb[0:64, :], sig=(h == 3))
        yield
        sg = F4.next()
        fw.tt("dve", sg[:], v3(yw), pbc(dr), ALU.add)
        fw.act(sg[:], sg[:], AF.Tanh, scale=0.5)
        fw.ts("pool", sg[:], sg[:], 0.5, 0.5, ALU.mult, ALU.add)
        a = F4.next()
        fw.tt("dve", a[:], v3(ya), pbc(2), ALU.add)
        fw.act(a[:], a[:], AF.Tanh, scale=0.5)
        fw.ts("pool", a[:], a[:], 0.5, 0.5, ALU.mult, ALU.add)
        yield
        if need_post:
            sgd = SM.next()
            fw.act(sgd[:], ug[:], AF.Tanh, scale=0.5)
            fw.ts("pool", sgd[:], sgd[:], 0.5, 0.5, ALU.mult, ALU.add)
            gps_ = PS.next()
            for h in range(4):
                fw.mm(gps_[0:64, h * 128:(h + 1) * 128], gup[:, h * 64:(h + 1) * 64], sgd[:], sig=(h == 3))
            g_ = FL.next()
            fw.copy("act", g_[:], v3(gps_))
            yield
        r = u[:, 0:4, :]
        k = u[:, 4:8, :]
        v = u[:, 8:12, :]
        kk = F4.next()
        fw.tt("dve", kk[:], k, pbc(3), ALU.mult)
        sq = B4.next()
        fw.tt("pool", sq[:], kk[:], kk[:], ALU.mult)
        ssp = PS.next()
        fw.mm(ssp[0:64, :], K.ones_bf[0:64, 0:64], flat(sq))
        yield
        nr = F4.next()
        fw.ts("dve", nr[:], v3(ssp), 1e-24, None, ALU.max)
        fw.act(nr[:], nr[:], AF.Ln)
        fw.act(nr[:], nr[:], AF.Exp, scale=-0.5)
        kap = F4.next()
        fw.tt("dve", kap[:], kk[:], nr[:], ALU.mult)
        yield
        tk = F4.next()
        fw.tt("pool", tk[:], a[:], pbc(4), ALU.mult)
        fw.tt("pool", tk[:], tk[:], pbc(5), ALU.add)
        kh = FL.next()
        fw.tt("dve", kh[:], k, tk[:], ALU.mult)
        bb = F4.next()
        fw.tt("pool", bb[:], kap[:], a[:], ALU.mult)
        yield
        Pp = F4.next()
        fw.scan(flat(Pp), seg[:], flat(sg), 0.0, ALU.mult, ALU.add)
        Qp = F4.next()
        fw.tt("pool", Qp[:], Pp[:], sg[:], ALU.subtract)
        nT = S4.next()
        pT = S4.next()
        fw.ts("dve", nT[:], Pp[:, :, 127], -CW, None, ALU.mult)
        fw.ts("dve", pT[:], Pp[:, :, 127], CW, None, ALU.mult)
        gC = S4.next()
        fw.act(gC[:], nT[:], AF.Exp)
        yield
        rt = B4.next()
        kp = B4.next()
        kt = B4.next()
        bt = B4.next()
        Khf = B4.next()
        Bhf = B4.next()
        E1 = F4.next()
        E2 = F4.next()
        if dr == 0:
            fw.act(E1[:], Pp[:], AF.Exp, scale=-CW)
            fw.tt("dve", rt[:], r, E1[:], ALU.mult)
            fw.act(E2[:], Qp[:], AF.Exp, scale=-CW)
            fw.tt("pool", kp[:], kap[:], E2[:], ALU.mult)
            yield
            E3 = F4.next()
            fw.act(E3[:], Pp[:], AF.Exp, scale=CW)
            fw.tt("dve", kt[:], kh[:], E3[:], ALU.mult)
            fw.tt("pool", bt[:], bb[:], E3[:], ALU.mult)
            E4 = F4.next()
            for h in range(4):
                fw.act(E4[:, h, :], Pp[:, h, :], AF.Exp, scale=CW, bias=nT[:, h:h + 1])
        else:
            for h in range(4):
                fw.act(E1[:, h, :], Qp[:, h, :], AF.Exp, scale=CW, bias=nT[:, h:h + 1])
            fw.tt("dve", rt[:], r, E1[:], ALU.mult)
            for h in range(4):
                fw.act(E2[:, h, :], Pp[:, h, :], AF.Exp, scale=CW, bias=nT[:, h:h + 1])
            fw.tt("pool", kp[:], kap[:], E2[:], ALU.mult)
            yield
            E3 = F4.next()
            for h in range(4):
                fw.act(E3[:, h, :], Qp[:, h, :], AF.Exp, scale=-CW, bias=pT[:, h:h + 1])
            fw.tt("dve", kt[:], kh[:], E3[:], ALU.mult)
            fw.tt("pool", bt[:], bb[:], E3[:], ALU.mult)
            E4 = F4.next()
            fw.act(E4[:], Qp[:], AF.Exp, scale=-CW)
        yield
        fw.tt("dve", Khf[:], kh[:], E4[:], ALU.mult)
        fw.stt(Bhf[:], bb[:], -1.0, E4[:], ALU.mult, ALU.mult)
        vb = B4.next()
        fw.copy("act", vb[:], v)
        pst = PST.next()
        for ki, src_ in enumerate((kp, Khf, Bhf, vb)):
            for h in range(4):
                fw.tr(pst[:, ki, h, :], src_[:, h, :], K.ident[0:64, 0:64], sig=(ki == 3 and h == 3))
        tm = Q.TM.next()
        fw.copy("dve", tm[:], pst[:])
        yield
        mats = []
        pairs = [(kp, bt), (bt, kp), (kt, kp), (kt, rt), (bt, rt)]
        for ki, (lt, rt_) in enumerate(pairs):
            p_ = PS.next()
            for h in range(4):
                fw.mm(p_[:, h * 128:(h + 1) * 128], lt[:, h, :], rt_[:, h, :], sig=(h == 3))
            m_ = M4.next() if ki < 2 else MK.next()
            fw.tt("dve", m_[:], v3(p_, 128), dmask.bc((slice(None), dr, ki, slice(None)), 1, [128, 4, 128]), ALU.mult)
            mats.append(m_)
            yield
        X, XT, AkkT, ArkT, ArbT = mats
        PT = M4.next()
        fw.tt("pool", PT[:], XT[:], K.ident.bc((slice(None), slice(None)), 1, [128, 4, 128]), ALU.add)
        for lvl in range(6):
            p1 = PS.next()
            for h in range(4):
                fw.mm(p1[:, h * 128:(h + 1) * 128], XT[:, h, :], X[:, h, :], sig=(h == 3))
            if lvl < 5:
                p2 = PS.next()
                for h in range(4):
                    fw.mm(p2[:, h * 128:(h + 1) * 128], X[:, h, :], XT[:, h, :], sig=(h == 3))
            yield
            X2 = M4.next()
            fw.copy("act", X2[:], v3(p1, 128))
            if lvl < 5:
                XT2 = M4.next()
                fw.copy("dve", XT2[:], v3(p2, 128))
            p3 = PS.next()
            for h in range(4):
                fw.mm(p3[:, h * 128:(h + 1) * 128], X2[:, h, :], PT[:, h, :], sig=(h == 3))
            yield
            PT2 = M4.next()
            fw.tt("dve", PT2[:], v3(p3, 128), PT[:], ALU.add)
            PT = PT2
            X = X2
            if lvl < 5:
                XT = XT2
        pw = PS.next()
        for h in range(4):
            fw.mm(pw[0:64, h * 128:(h + 1) * 128], tm[:, 0, h, :], PT[:, h, :], sig=(h == 3))
        pa = PS.next()
        for h in range(4):
            fw.mm(pa[:, h * 64:(h + 1) * 64], AkkT[:, h, :], tm[:, 3, h, :], sig=(h == 3))
        yield
        WkT = B4.next()
        fw.copy("act", WkT[:], v3(pw))
        AV = Q.UB.next()
        fw.copy("dve", AV[:], v3s(pa))
        pu = PS.next()
        for h in range(4):
            fw.mm(pu[:, h * 64:(h + 1) * 64], PT[:, h, :], AV[:, h, :], sig=(h == 3))
        yield
        Uv = Q.UV.next()
        fw.copy("act", Uv[:], v3s(pu))
        pU = PS.next()
        for h in range(4):
            fw.mm(pU[:, h * 64:(h + 1) * 64], WkT[:, h, :], Hb[dr][:, h, :], sig=(h == 3))
        yield
        U = Q.UB.next()
        fw.tt("dve", U[:], v3s(pU), Uv[:], ALU.add)
        pY = PS.next()
        for h in range(4):
            yh = pY[0:64, h * 128:(h + 1) * 128]
            fw.mm(yh, Hb[dr][:, h, :], rt[:, h, :], start=True, stop=False, sig=False)
            fw.mm(yh, tm[:, 3, h, :], ArkT[:, h, :], start=False, stop=False, sig=False)
            fw.mm(yh, U[:, h, :], ArbT[:, h, :], start=False, stop=True, sig=(h == 3))
        pH = PS.next()
        for h in range(4):
            hh = pH[0:64, h * 64:(h + 1) * 64]
            fw.mm(hh, tm[:, 1, h, :], tm[:, 3, h, :], start=True, stop=False, sig=False)
            fw.mm(hh, tm[:, 2, h, :], U[:, h, :], start=False, stop=True, sig=(h == 3))
        fw.tt("pool", H[dr][:], H[dr][:], gC.bc((slice(None), slice(None)), 2, [64, 4, 64]), ALU.mult)
        yield
        fw.tt("dve", H[dr][:], H[dr][:], View(pH, pH.ap[0:64, 0:256].rearrange("p (h i) -> p h i", h=4)), ALU.add)
        fw.copy("act", Hb[dr][:], H[dr][:])
        if first:
            ysb = F4.next()
            fw.copy("act", ysb[:], v3(pY))
            fw.st(View(ytok[c], C.YD[c]), ysb[:])
            return
        y = F4.next()
        fw.tt("dve", y[:], v3(pY), yprev[:], ALU.add)
        pm_ = PS.next()
        fw.mm(pm_[0:64, :], K.ones_f[0:64, 0:64], flat(y))
        rk = F4.next()
        fw.tt("pool", rk[:], r, kh[:], ALU.mult)
        fw.tt("pool", rk[:], rk[:], pbc(6), ALU.mult)
        yield
        yc = F4.next()
        fw.stt(yc[:], v3(pm_), -1.0 / 64, y[:], ALU.mult, ALU.add)
        sq2 = F4.next()
        fw.act(sq2[:], yc[:], AF.Square)
        pv = PS.next()
        fw.mm(pv[0:64, :], K.ones_f[0:64, 0:64], flat(sq2))
        pr_ = PS.next()
        fw.mm(pr_[0:64, :], K.ones_f[0:64, 0:64], flat(rk))
        yield
        rs = F4.next()
        fw.ts("dve", rs[:], v3(pv), 1.0 / 64, GN_EPS, ALU.mult, ALU.add)
        fw.act(rs[:], rs[:], AF.Ln)
        fw.act(rs[:], rs[:], AF.Exp, scale=-0.5)
        fw.tt("dve", yc[:], yc[:], rs[:], ALU.mult)
        fw.tt("pool", yc[:], yc[:], pbc(7), ALU.mult)
        yield
        fw.tt("pool", yc[:], yc[:], pbc(8), ALU.add)
        t_ = F4.next()
        fw.tt("dve", t_[:], v3(pr_), v, ALU.mult)
        fw.tt("pool", yc[:], yc[:], t_[:], ALU.add)
        ob = B4.next()
        fw.tt("dve", ob[:], yc[:], g_[:], ALU.mult)
        for h in range(4):
            fw.st(C.OT[12 + h, :, c * 128:(c + 1) * 128], ob[:, h, :])

    for i in range(n):
        first = i < n // 2
        gens = [unit(i, 0, first), unit(n - 1 - i, 1, first)]
        alive = [True, True]
        while any(alive):
            for gi in range(2):
                if alive[gi]:
                    try:
                        next(gens[gi])
                    except StopIteration:
                        alive[gi] = False
    ph.close()


def build(seqs=(4096, 2048), depth=2, phases=None, debug=False):
    nc = bass.Bass("TRN2", target_bir_lowering=False)
    C = Ctx()
    C.seqs = list(seqs)
    C.d = {}
    C.xin = [nc.dram_tensor("x%d" % i, [t, D_MODEL], F32, kind="ExternalInput").ap() for i, t in enumerate(seqs)]
    C.y = [nc.dram_tensor("y%d" % i, [t, D_MODEL], F32, kind="ExternalOutput").ap() for i, t in enumerate(seqs)]
    for nm, shp in WEIGHT_SPECS:
        C.d[nm] = nc.dram_tensor(nm, list(shp), F32, kind="ExternalInput").ap()
    for nm, arr in make_consts().items():
        C.d[nm] = nc.dram_tensor(nm, list(arr.shape), F32, kind="ExternalInput").ap()
    TM = max(seqs)
    sk = "ExternalOutput" if debug else "Internal"

    def scratch(nm, shp, dt):
        return nc.dram_tensor(nm, list(shp), dt, kind=sk).ap()
    C.QA = scratch("QA", [64, 4, TM], BF16)
    C.KA = scratch("KA", [64, 2, TM], BF16)
    C.VA = scratch("VA", [TM, 128], BF16)
    C.QB = scratch("QB", [64, 4, TM], BF16)
    C.KB = scratch("KB", [64, 4, TM], BF16)
    C.VB = scratch("VB", [TM, 256], BF16)
    C.QC = scratch("QC", [64, 4, TM], BF16)
    C.KC = scratch("KC", [64, 4, TM], BF16)
    C.GC = scratch("GC", [64, 4, TM], BF16)
    C.VC = scratch("VC", [TM, 256], BF16)
    C.ZD = scratch("ZD", [64, 15, TM], F32)
    C.ZG = scratch("ZG", [128, TM], F32)
    C.OT = scratch("OT", [16, 64, TM], BF16)
    C.YD = scratch("YD", [TM // 128, 64, 4, 128], F32)
    allph = ["inproj", "A", "B", "C", "D", "outproj", "ffn"]
    phases = allph if phases is None else phases
    with ExitStack() as es:
        fw = FW(nc, es)
        ph0 = fw.phase()
        K = load_consts(fw, C, ph0)
        fw.barrier()
        for l in range(depth):
            for si in range(len(seqs)):
                if "inproj" in phases:
                    phase_inproj(fw, C, K, l, si)
                if "A" in phases:
                    phase_attnA(fw, C, K, l, si)
                if "B" in phases:
                    phase_attnB(fw, C, K, l, si)
                if "C" in phases:
                    phase_ret(fw, C, K, l, si)
                if "D" in phases:
                    phase_rwkv(fw, C, K, l, si)
                if "outproj" in phases:
                    phase_outproj(fw, C, K, l, si)
                if "ffn" in phases:
                    phase_ffn(fw, C, K, l, si)
        fw.barrier()
        ph0.es.close()
        C.nins = fw.nins
    return nc, C


_CACHE = {}


def kernel(**inputs):
    xp = np.ascontiguousarray(inputs["x_prompt"], dtype=np.float32)
    xs = np.ascontiguousarray(inputs["x_sample"], dtype=np.float32)
    nc, C = build()
    consts = make_consts()
    in_maps = []
    for i in range(8):
        m = {"x0": xp[i], "x1": xs[i]}
        for nm, _ in WEIGHT_SPECS:
            m[nm] = np.ascontiguousarray(inputs[nm], dtype=np.float32)
        m.update(consts)
        in_maps.append(m)
    res = run_bass_kernel_spmd(nc, in_maps, core_ids=list(range(8)))
    yp = np.stack([r["y0"] for r in res.results], 0).astype(np.float32)
    ys = np.stack([r["y1"] for r in res.results], 0).astype(np.float32)
    return (yp, ys)
```

```python
import math
from contextlib import ExitStack
import numpy as np
import concourse.bass as bass
import concourse.mybir as mybir
from concourse.bass_utils import run_bass_kernel_spmd

F32 = mybir.dt.float32
BF16 = mybir.dt.bfloat16
AF = mybir.ActivationFunctionType
ALU = mybir.AluOpType
AX = mybir.AxisListType

D_MODEL = 1024
IN_COLS = 3392
D_FF = 2816
NFF = D_FF // 128
EPS = 1e-6
GN_EPS = 64e-5
TMAX = 4096


class View:
    __slots__ = ("t", "ap")

    def __init__(self, t, ap):
        self.t = t
        self.ap = ap


class T:
    __slots__ = ("ap", "w", "r", "name")

    def __init__(self, ap, name=""):
        self.ap = ap
        self.w = {}
        self.r = {}
        self.name = name

    def __getitem__(self, idx):
        return View(self, self.ap[idx])

    def bc(self, idx, axis, shape):
        return View(self, self.ap[idx].unsqueeze(axis).to_broadcast(list(shape)))


class Eng:
    def __init__(self, name, handle, sem):
        self.name = name
        self.h = handle
        self.sem = sem
        self.count = 0
        self.seen = {}


class Pool:
    def __init__(self, items):
        self.items = items
        self.i = 0

    def next(self):
        t = self.items[self.i]
        self.i = (self.i + 1) % len(self.items)
        return t


class Phase:
    def __init__(self, fw):
        self.fw = fw
        self.es = ExitStack()
        self.n = 0

    def sb(self, shape, dt, name="t"):
        self.n += 1
        t = self.es.enter_context(self.fw.nc.sbuf_tensor("%s_%d_%d" % (name, self.fw.uid(), self.n), list(shape), dt))
        return T(t, name)

    def ps(self, shape, dt=F32, name="p"):
        self.n += 1
        t = self.es.enter_context(self.fw.nc.psum_tensor("%s_%d_%d" % (name, self.fw.uid(), self.n), list(shape), dt))
        return T(t, name)

    def sbpool(self, n, shape, dt, name="t"):
        return Pool([self.sb(shape, dt, name) for _ in range(n)])

    def pspool(self, n, shape, dt=F32, name="p"):
        return Pool([self.ps(shape, dt, name) for _ in range(n)])

    def close(self):
        self.fw.barrier()
        self.es.close()


class FW:
    NDMA = 32

    def __init__(self, nc, es):
        self.nc = nc
        self.es = es
        self._uid = 0
        self.E = {}
        for nm, h in (("pe", nc.tensor), ("act", nc.scalar), ("dve", nc.vector), ("pool", nc.gpsimd), ("sp", nc.sync)):
            sem = es.enter_context(nc.semaphore("s_" + nm))
            self.E[nm] = Eng(nm, h, sem)
        self.dsem = [es.enter_context(nc.semaphore("d%d" % i)) for i in range(self.NDMA)]
        self.dval = [0] * self.NDMA
        self.dnext = {"sp": 0, "pool": 0, "act": 0}
        self.drange = {"sp": (0, 14), "pool": (14, 28), "act": (28, 32)}
        self.semobj = {}
        for e in self.E.values():
            self.semobj[("e", e.name)] = e.sem
        for i, s in enumerate(self.dsem):
            self.semobj[("d", i)] = s
        self.nins = 0

    def uid(self):
        self._uid += 1
        return self._uid

    def phase(self):
        return Phase(self)

    def _needs(self, reads, writes):
        needs = {}
        for t in reads:
            for k, v in t.w.items():
                if needs.get(k, 0) < v:
                    needs[k] = v
        for t in writes:
            for d in (t.w, t.r):
                for k, v in d.items():
                    if needs.get(k, 0) < v:
                        needs[k] = v
        return needs

    def _emit_waits(self, e, needs, skip_self=False):
        for k, v in needs.items():
            if skip_self and k == ("e", e.name):
                continue
            if e.seen.get(k, 0) >= v:
                continue
            if k[0] == "e":
                assert self.E[k[1]].count >= v, "wait on future event %s %d" % (k, v)
            e.seen[k] = v
            e.h.wait_ge(self.semobj[k], v)
            self.nins += 1

    def op(self, eng, fn, reads=(), writes=(), signal=True):
        e = self.E[eng]
        reads = [v.t for v in reads]
        writes = [v.t for v in writes]
        needs = self._needs(reads, writes)
        self._emit_waits(e, needs, skip_self=(eng == "pe"))
        ins = fn(e.h)
        self.nins += 1
        if signal:
            e.count += 1
            ins.then_inc(e.sem, 1)
            ev = e.count
        else:
            ev = e.count + 1
        key = ("e", eng)
        for t in writes:
            t.w = {key: ev}
            t.r = {}
        for t in reads:
            if t.r.get(key, 0) < ev:
                t.r[key] = ev

    def dma(self, q, out, in_, **kw):
        e = self.E[q]
        reads = [in_.t] if isinstance(in_, View) else []
        writes = [out.t] if isinstance(out, View) else []
        acc = isinstance(out, View) and not isinstance(in_, View) and out.t.ap is not None
        if acc:
            needs = {}
            t_ = out.t
            for k, v in t_.r.items():
                if needs.get(k, 0) < v:
                    needs[k] = v
            for k, v in t_.w.items():
                if k[0] != "d" and needs.get(k, 0) < v:
                    needs[k] = v
        else:
            needs = self._needs(reads, writes)
        lo_, hi_ = self.drange[q]
        i = lo_ + self.dnext[q]
        self.dnext[q] = (self.dnext[q] + 1) % (hi_ - lo_)
        key = ("d", i)
        if self.dval[i] > 0 and needs.get(key, 0) < self.dval[i]:
            needs[key] = self.dval[i]
        self._emit_waits(e, needs)
        self.dval[i] += 16
        v = self.dval[i]
        oap = out.ap if isinstance(out, View) else out
        iap = in_.ap if isinstance(in_, View) else in_
        e.h.dma_start(out=oap, in_=iap, **kw).then_inc(self.dsem[i], 16)
        self.nins += 1
        for t in writes:
            if acc:
                t.w[key] = v
            else:
                t.w = {key: v}
                t.r = {}
        for t in reads:
            t.r[key] = v

    def barrier(self):
        allev = {}
        for e in self.E.values():
            if e.count > 0:
                allev[("e", e.name)] = e.count
        for i in range(self.NDMA):
            if self.dval[i] > 0:
                allev[("d", i)] = self.dval[i]
        for e in self.E.values():
            self._emit_waits(e, dict(allev))

    def ld(self, dst, src, q="sp", **kw):
        self.dma(q, dst, src, **kw)

    def ldn(self, dst, src, q="sp"):
        self.dma(q, dst, src, allow_slow_non_contiguous=True)

    def st(self, dst, src, q="pool"):
        self.dma(q, dst, src)

    def mm(self, out, lhsT, rhs, start=True, stop=True, sig=None):
        if sig is None:
            sig = stop
        self.op("pe", lambda h: h.matmul(out.ap, lhsT=lhsT.ap, rhs=rhs.ap, start=start, stop=stop),
                reads=[lhsT, rhs], writes=[out], signal=sig)

    def tr(self, out, in_, ident, sig=True):
        self.op("pe", lambda h: h.transpose(out=out.ap, in_=in_.ap, identity=ident.ap),
                reads=[in_, ident], writes=[out], signal=sig)

    def act(self, out, in_, func, bias=None, scale=None, accum=None):
        reads = [in_]
        kw = {}
        if bias is not None:
            if isinstance(bias, View):
                reads.append(bias)
                kw["bias"] = bias.ap
            else:
                kw["bias"] = float(bias)
        if scale is not None:
            if isinstance(scale, View):
                reads.append(scale)
                kw["scale"] = scale.ap
            else:
                kw["scale"] = float(scale)
        writes = [out]
        if accum is not None:
            kw["accum_out"] = accum.ap
            writes.append(accum)
        self.op("act", lambda h: h.activation(out=out.ap, in_=in_.ap, func=func, **kw), reads=reads, writes=writes)

    def tt(self, eng, out, in0, in1, op):
        self.op(eng, lambda h: h.tensor_tensor(out=out.ap, in0=in0.ap, in1=in1.ap, op=op), reads=[in0, in1], writes=[out])

    def ts(self, eng, out, in0, s1, s2, op0, op1=None):
        reads = [in0]
        a1 = s1.ap if isinstance(s1, View) else s1
        a2 = s2.ap if isinstance(s2, View) else s2
        if isinstance(s1, View):
            reads.append(s1)
        if isinstance(s2, View):
            reads.append(s2)
        if op1 is None:
            self.op(eng, lambda h: h.tensor_scalar(out=out.ap, in0=in0.ap, scalar1=a1, scalar2=None, op0=op0), reads=reads, writes=[out])
        else:
            self.op(eng, lambda h: h.tensor_scalar(out=out.ap, in0=in0.ap, scalar1=a1, scalar2=a2, op0=op0, op1=op1), reads=reads, writes=[out])

    def stt(self, out, in0, scalar, in1, op0, op1):
        reads = [in0, in1]
        sa = scalar.ap if isinstance(scalar, View) else scalar
        if isinstance(scalar, View):
            reads.append(scalar)
        self.op("dve", lambda h: h.scalar_tensor_tensor(out=out.ap, in0=in0.ap, scalar=sa, in1=in1.ap, op0=op0, op1=op1), reads=reads, writes=[out])

    def copy(self, eng, out, in_):
        if eng == "act":
            self.op("act", lambda h: h.activation(out=out.ap, in_=in_.ap, func=AF.Identity), reads=[in_], writes=[out])
        else:
            self.op(eng, lambda h: h.tensor_copy(out=out.ap, in_=in_.ap), reads=[in_], writes=[out])

    def recip(self, out, in_):
        self.op("dve", lambda h: h.reciprocal(out=out.ap, in_=in_.ap), reads=[in_], writes=[out])

    def memset(self, eng, out, val):
        self.op(eng, lambda h: h.memset(out.ap, val), writes=[out])

    def reduce(self, out, in_, op, axis=AX.X):
        self.op("dve", lambda h: h.tensor_reduce(out=out.ap, in_=in_.ap, axis=axis, op=op), reads=[in_], writes=[out])

    def scan(self, out, d0, d1, init, op0, op1):
        self.op("dve", lambda h: h.tensor_tensor_scan(out=out.ap, data0=d0.ap, data1=d1.ap, initial=init, op0=op0, op1=op1), reads=[d0, d1], writes=[out])


def run_pipe(gens, depth, side=None, side_every=8):
    gens = list(gens)
    active = []
    tick = 0
    while gens or active:
        while gens and len(active) < depth:
            active.append(gens.pop(0))
        for g in list(active):
            try:
                next(g)
            except StopIteration:
                active.remove(g)
        tick += 1
        if side is not None and tick % side_every == 0:
            try:
                next(side)
            except StopIteration:
                side = None
    if side is not None:
        for _ in side:
            pass


def _angles(pos, rot_dim, theta):
    inv = (theta ** (-np.arange(0, rot_dim, 2, dtype=np.float32) / rot_dim)).astype(np.float32)
    return pos.astype(np.float32)[:, None] * inv[None, :]


def make_consts():
    c = {}
    c["c_ident"] = np.eye(128, dtype=np.float32)
    T = TMAX
    pos = np.arange(T)
    rope = np.zeros((64, 6, T), np.float32)
    ar = _angles(pos // 64, 32, 10000.0)
    ac = _angles(pos % 64, 32, 10000.0)
    for d in range(64):
        a = ar if d < 32 else ac
        idx = (d % 32) % 16
        rope[d, 0] = np.cos(a[:, idx])
        rope[d, 1] = np.sin(a[:, idx])
    ab = _angles(pos, 8, 500000.0)
    for d in range(64):
        dd = d % 32
        if dd < 8:
            rope[d, 2] = np.cos(ab[:, dd % 4])
            rope[d, 3] = np.sin(ab[:, dd % 4])
        else:
            rope[d, 2] = 1.0
            rope[d, 3] = 0.0
    acc = _angles(pos, 64, 10000.0)
    for d in range(64):
        rope[d, 4] = np.cos(acc[:, d % 32])
        rope[d, 5] = np.sin(acc[:, d % 32])
    c["c_rope"] = rope
    pm = np.zeros((64, 3, 64), np.float32)
    for m in range(64):
        mm_ = m % 32
        base = m - mm_
        if mm_ < 16:
            pm[base + mm_ + 16, 0, m] = -1.0
        else:
            pm[base + mm_ - 16, 0, m] = 1.0
        if mm_ < 4:
            pm[base + mm_ + 4, 1, m] = -1.0
        elif mm_ < 8:
            pm[base + mm_ - 4, 1, m] = 1.0
        if m < 32:
            pm[m + 32, 2, m] = -1.0
        else:
            pm[m - 32, 2, m] = 1.0
    c["c_pm"] = pm
    c["c_ones"] = np.ones((128, 128), np.float32)
    h = np.arange(4, dtype=np.float32)
    lgf = np.log1p(-np.exp2(-5.0 - h)).astype(np.float32)
    lgb = lgf[::-1].copy()
    i = np.arange(128, dtype=np.float32)
    diff = i[None, :] - i[:, None]
    rm = np.zeros((128, 4, 128), np.float32)
    for hh in range(4):
        rm[:, hh, :] = np.where(diff >= 0, np.exp(lgf[hh] * np.maximum(diff, 0)), 0.0) + \
            np.where(diff <= 0, np.exp(lgb[hh] * np.maximum(-diff, 0)), 0.0)
    c["c_retmask"] = rm
    qd = np.zeros((64, 2, 4, 128), np.float32)
    kd = np.zeros((128, 2, 4), np.float32)
    for hh in range(4):
        qd[:, 0, hh, :] = np.exp(lgf[hh] * (i + 1.0))[None, :]
        qd[:, 1, hh, :] = np.exp(lgb[hh] * (128.0 - i))[None, :]
        kd[:, 0, hh] = np.exp(lgf[hh] * (127.0 - i))
        kd[:, 1, hh] = np.exp(lgb[hh] * i)
    c["c_qdec"] = qd
    c["c_kdec"] = kd
    dm = np.zeros((128, 2, 5, 128), np.float32)
    p = np.arange(128)[:, None]
    f = np.arange(128)[None, :]
    dm[:, 0, 0, :] = -1.0 * (p > f)
    dm[:, 0, 1, :] = -1.0 * (f > p)
    dm[:, 0, 2, :] = (f > p)
    dm[:, 0, 3, :] = (f >= p)
    dm[:, 0, 4, :] = -1.0 * (f >= p)
    dm[:, 1, 0, :] = -1.0 * (p < f)
    dm[:, 1, 1, :] = -1.0 * (f < p)
    dm[:, 1, 2, :] = (f < p)
    dm[:, 1, 3, :] = (f <= p)
    dm[:, 1, 4, :] = -1.0 * (f <= p)
    c["c_dmask"] = dm.astype(np.float32)
    seg = np.ones((64, 512), np.float32)
    seg[:, ::128] = 0.0
    c["c_seg"] = seg
    return c


RET_G = [1.0 - 2.0 ** (-5 - h) for h in range(4)]

WEIGHT_SPECS = [
    ("norm_mix_pre", (2, 1024)), ("norm_mix_post", (2, 1024)), ("norm_ffn_pre", (2, 1024)), ("norm_ffn_post", (2, 1024)),
    ("w_in", (2, 1024, 3392)), ("w_out", (2, 1024, 1024)), ("a_q_gain", (2, 64)), ("a_k_gain", (2, 64)),
    ("b_lambda", (2, 4, 32)), ("b_subln_gain", (2, 64)), ("c_gn_gain", (2, 256)), ("d_mu_prev", (2, 1088)),
    ("d_mu_next", (2, 1088)), ("d_w0", (2, 2, 256)), ("d_w_up", (2, 2, 64, 256)), ("d_a0", (2, 256)),
    ("d_a_up", (2, 64, 256)), ("d_g_up", (2, 128, 256)), ("d_k_k", (2, 256)), ("d_k_a", (2, 256)),
    ("d_r_k", (2, 4, 64)), ("d_gn_w", (2, 256)), ("d_gn_b", (2, 256)),
    ("ffn_w_gate", (2, 1024, 2816)), ("ffn_w_up", (2, 1024, 2816)), ("ffn_w_down", (2, 2816, 1024)),
]


class Ctx:
    pass


def load_consts(fw, C, ph):
    K = Ctx()
    K.ident = ph.sb([128, 128], BF16, "ident")
    fw.ld(K.ident[:], C.d["c_ident"], q="pool")
    K.identf = ph.sb([128, 128], F32, "identf")
    fw.ld(K.identf[:], C.d["c_ident"])
    K.ones_bf = ph.sb([128, 128], BF16, "ones_bf")
    fw.ld(K.ones_bf[:], C.d["c_ones"], q="pool")
    K.ones_f = ph.sb([128, 128], F32, "ones_f")
    fw.ld(K.ones_f[:], C.d["c_ones"])
    K.pm = []
    for i in range(3):
        t = ph.sb([64, 64], BF16, "pm%d" % i)
        fw.ld(t[:], C.d["c_pm"][:, i, :], q="pool")
        K.pm.append(t)
    return K


def rms_rows(fw, xt, gain, hb, tmp, ss, rstd):
    fw.act(tmp[:], xt[:], AF.Square, accum=ss[:])
    fw.act(rstd[:], ss[:], AF.Sqrt, bias=EPS, scale=1.0 / D_MODEL)
    fw.recip(rstd[:], rstd[:])
    fw.stt(hb[:], xt[:], rstd[:, 0:1], gain[:], ALU.mult, ALU.mult)


def make_hT(fw, K, src_rows, gain, hT, j, P):
    xt = P.xt.next()
    fw.ld(xt[:], src_rows)
    hb = P.hb.next()
    ss = P.ss.next()
    rstd = P.rstd.next()
    tmp = P.tmp.next()
    rms_rows(fw, xt, gain, hb, tmp, ss, rstd)
    tp = P.tp.next()
    for k in range(8):
        fw.tr(tp[:, k, :], hb[:, k * 128:(k + 1) * 128], K.ident[:], sig=(k == 7))
    fw.copy("dve", hT[:, :, j * 128:(j + 1) * 128], tp[:])
    return xt


def phase_inproj(fw, C, K, l, si):
    T_ = C.seqs[si]
    src = C.xin[si] if l == 0 else C.y[si]
    d = C.d
    ph = fw.phase()
    W = ph.sb([128, 8, IN_COLS], BF16, "W")
    for k in range(8):
        fw.ld(W[:, k, :], d["w_in"][l, k * 128:(k + 1) * 128, :], q="pool")
    gain = ph.sb([128, 1024], F32, "gain")
    fw.ld(gain[:], d["norm_mix_pre"][l].partition_broadcast(128))
    gqk = ph.sb([64, 2], F32, "gqk")
    fw.ld(gqk[:, 0:1], d["a_q_gain"][l].rearrange("(p o) -> p o", o=1))
    fw.ld(gqk[:, 1:2], d["a_k_gain"][l].rearrange("(p o) -> p o", o=1))
    P = Ctx()
    P.xt = ph.sbpool(2, [128, 1024], F32, "xt")
    P.hb = ph.sbpool(2, [128, 1024], BF16, "hb")
    P.tmp = ph.sbpool(1, [128, 1024], F32, "tmp")
    P.ss = ph.sbpool(2, [128, 1], F32, "ss")
    P.rstd = ph.sbpool(2, [128, 1], F32, "rstd")
    P.tp = ph.pspool(1, [128, 8, 128], BF16, "tp")
    hTs = ph.sbpool(2, [128, 8, 512], BF16, "hT")
    tabs = ph.sbpool(2, [64, 6, 512], F32, "tab")
    zps = ph.pspool(4, [128, 512], F32, "zp")
    aps = ph.pspool(2, [128, 512], F32, "ap")
    tps = ph.pspool(1, [128, 512], F32, "tmps")
    f32p = ph.sbpool(24, [64, 512], F32, "f32p")
    bfp = ph.sbpool(16, [64, 512], BF16, "bfp")
    f32w = ph.sbpool(3, [128, 512], F32, "f32w")
    bfw = ph.sbpool(3, [128, 512], BF16, "bfw")
    chunks = []
    for h in range(4):
        chunks.append(("A", 0 + 64 * h, C.QA, h, 0))
    for h in range(2):
        chunks.append(("A", 256 + 64 * h, C.KA, h, 1))
    for h in range(4):
        chunks.append(("B", 512 + 64 * h, C.QB, h, 0))
    for h in range(4):
        chunks.append(("B", 768 + 64 * h, C.KB, h, 0))
    for h in range(4):
        chunks.append(("C", 1280 + 64 * h, C.QC, h, 0))
    for h in range(4):
        chunks.append(("C", 1536 + 64 * h, C.KC, h, 0))
    for h in range(4):
        chunks.append(("G", 2048 + 64 * h, C.GC, h, 0))
    for j in range(15):
        chunks.append(("D", 2304 + 64 * j, C.ZD, j, 0))
    DEPTHP = 4

    def chunk_gen(kind, col, dst, slot, gi, hT, tab, t0):
        zp = zps.next()
        for k in range(8):
            fw.mm(zp[0:64, :], W[:, k, col:col + 64], hT[:, k, :], start=(k == 0), stop=(k == 7))
        z = zp[0:64, :]
        yield
        if kind in ("A", "B", "C"):
            ti = {"A": 0, "B": 2, "C": 4}[kind]
            pmi = {"A": 0, "B": 1, "C": 2}[kind]
            qn = bfp.next()
            qf = f32p.next()
            fw.copy("act", qf[:], z)
            if kind == "A":
                sq = bfp.next()
                fw.act(sq[:], z, AF.Square)
                yield
                ms = aps.next()
                fw.mm(ms[0:64, :], K.ones_bf[0:64, 0:64], sq[:])
                rs = f32p.next()
                fw.ts("dve", rs[:], ms[0:64, :], 1.0 / 64, EPS, ALU.mult, ALU.add)
                fw.act(rs[:], rs[:], AF.Ln)
                fw.act(rs[:], rs[:], AF.Exp, scale=-0.5)
                qf2 = f32p.next()
                fw.stt(qf2[:], qf[:], gqk[:, gi:gi + 1], rs[:], ALU.mult, ALU.mult)
                qf = qf2
            fw.copy("act", qn[:], qf[:])
            yield
            pq = aps.next()
            fw.mm(pq[0:64, :], K.pm[pmi][:], qn[:])
            t2 = f32p.next()
            fw.tt("dve", t2[:], pq[0:64, :], tab[:, ti + 1, :], ALU.mult)
            t1 = f32p.next()
            fw.tt("pool", t1[:], qf[:], tab[:, ti, :], ALU.mult)
            ob = bfp.next()
            fw.tt("pool", ob[:], t1[:], t2[:], ALU.add)
            fw.st(dst[:, slot, t0:t0 + 512], ob[:])
        elif kind == "G":
            ob = bfp.next()
            fw.act(ob[:], z, AF.Silu)
            fw.st(dst[:, slot, t0:t0 + 512], ob[:])
        else:
            of = f32p.next()
            fw.copy("act", of[:], z)
            fw.st(dst[:, slot, t0:t0 + 512], of[:])

    def tail_gen(hT, t0):
        zp = zps.next()
        for k in range(8):
            fw.mm(zp[:, :], W[:, k, 3264:3392], hT[:, k, :], start=(k == 0), stop=(k == 7))
        of = f32w.next()
        fw.copy("act", of[:], zp[:])
        fw.st(C.ZG[:, t0:t0 + 512], of[:])
        yield
        for j in range(4):
            tp_ = tps.next()
            for k in range(8):
                fw.mm(tp_[:, 0:128], hT[:, k, j * 128:(j + 1) * 128], W[:, k, 384:512], start=(k == 0), stop=(k == 7), sig=False)
            for k in range(8):
                fw.mm(tp_[:, 128:384], hT[:, k, j * 128:(j + 1) * 128], W[:, k, 1024:1280], start=(k == 0), stop=(k == 7))
            ob = bfw.next()
            fw.copy("act", ob[:, 0:384], tp_[:, 0:384])
            r0 = t0 + j * 128
            fw.st(C.VA[r0:r0 + 128, :], ob[:, 0:128])
            fw.st(C.VB[r0:r0 + 128, :], ob[:, 128:384])
            yield
            tp2 = tps.next()
            for k in range(8):
                fw.mm(tp2[:, 0:256], hT[:, k, j * 128:(j + 1) * 128], W[:, k, 1792:2048], start=(k == 0), stop=(k == 7))
            ob2 = bfw.next()
            fw.copy("dve", ob2[:, 0:256], tp2[:, 0:256])
            fw.st(C.VC[r0:r0 + 128, :], ob2[:, 0:256])
            yield

    def hT_gen(b, hT):
        t0 = b * 512
        for j in range(4):
            make_hT(fw, K, src[t0 + j * 128:t0 + (j + 1) * 128, :], gain, hT, j, P)
            yield

    def run_pipe(gens, depth, side=None):
        gens = list(gens)
        active = []
        tick = 0
        while gens or active:
            while gens and len(active) < depth:
                active.append(gens.pop(0))
            for g in list(active):
                try:
                    next(g)
                except StopIteration:
                    active.remove(g)
            tick += 1
            if side is not None and tick % 8 == 0:
                try:
                    next(side)
                except StopIteration:
                    side = None
        if side is not None:
            for _ in side:
                pass

    nb = T_ // 512
    hT_cur = hTs.next()
    for _ in hT_gen(0, hT_cur):
        pass
    for b in range(nb):
        t0 = b * 512
        tab = tabs.next()
        fw.ld(tab[:], d["c_rope"][:, :, t0:t0 + 512])
        hT = hT_cur
        side = None
        if b + 1 < nb:
            hT_cur = hTs.next()
            side = hT_gen(b + 1, hT_cur)
        gens = [chunk_gen(kind, col, dst, slot, gi, hT, tab, t0) for (kind, col, dst, slot, gi) in chunks]
        gens.append(tail_gen(hT, t0))
        run_pipe(gens, DEPTHP, side)
    ph.close()


def attn_core(fw, ph, K, T_, QT, KT, V, heads, negM, scale, emit):
    LOOK = 2
    sps = ph.pspool(4, [128, 512], F32, "sps")
    ops_ = ph.pspool(2, [128, 512], F32, "ops")
    pts = ph.sbpool(4, [128, 512], BF16, "pt")
    osbs = ph.sbpool(4, [65, 512], F32, "osb")
    nkt = T_ // 128
    iters = []
    for qb in range(T_ // 512):
        for hd in heads:
            for kt in range(nkt):
                iters.append((qb, hd, kt))
    pend = []
    state = {"o_ps": None}

    def do_pv(item):
        (qb, hd, kt, pt) = item
        (qs, lo, hi, ks, vs, tag) = hd
        if kt == 0:
            state["o_ps"] = ops_.next()
        o_ps = state["o_ps"]
        fw.mm(o_ps[0:65, :], V[:, kt, vs, :], pt[:], start=(kt == 0), stop=(kt == nkt - 1))
        if kt == nkt - 1:
            osb = osbs.next()
            fw.copy("dve", osb[:], o_ps[0:65, :])
            emit(qb, tag, osb)

    for (qb, hd, kt) in iters:
        (qs, lo, hi, ks, vs, tag) = hd
        q0 = qb * 512
        st = sps.next()
        fw.mm(st[:, :], KT[lo:hi, ks, kt * 128:(kt + 1) * 128], QT[lo:hi, qs, q0:q0 + 512])
        pt = pts.next()
        if negM is None:
            fw.act(pt[:], st[:], AF.Exp, scale=scale)
        else:
            fw.act(pt[:], st[:], AF.Exp, scale=scale, bias=negM[:, 0:1])
        pend.append((qb, hd, kt, pt))
        if len(pend) > LOOK:
            do_pv(pend.pop(0))
    while pend:
        do_pv(pend.pop(0))


def attn_core2(fw, ph, T_, groups, V, negM, scale, emit, bcs):
    LOOK = 2
    sps = ph.pspool(3, [128, 1024], F32, "sps")
    ops_ = bcs
    pts = ph.sbpool(4, [128, 1024], BF16, "pt")
    osbs = ph.sbpool(4, [65, 512], F32, "osb")
    nkt = T_ // 128
    iters = []
    for qb in range(T_ // 512):
        for g in groups:
            for kt in range(nkt):
                iters.append((qb, g, kt))
    pend = []
    state = {}
    import os
    NW = int(os.environ.get("DBG_WARM", "0"))
    if NW:
        wz = pts.next()
        fw.memset("pool", wz[:], 1.0)
        st0 = sps.next()
        for i in range(NW):
            fw.mm(st0[:, 0:512], wz[:, 0:128], wz[:, 512:1024], sig=(i == NW - 1))

    def do_pv(item):
        (qb, g, kt, pt) = item
        (kt_ap, q_ap, vs, tag) = g
        if kt == 0:
            state["o"] = [ops_.next(), ops_.next()]
        o = state["o"]
        for j in range(2):
            fw.mm(o[j][0:65, :], V[:, kt, vs[j], :], pt[:, j * 512:(j + 1) * 512], start=(kt == 0), stop=(kt == nkt - 1))
        if kt == nkt - 1:
            osb = [osbs.next(), osbs.next()]
            for j in range(2):
                fw.copy("dve", osb[j][:], o[j][0:65, :])
            emit(qb, tag, osb)

    for (qb, g, kt) in iters:
        (kt_ap, q_ap, vs, tag) = g
        st = sps.next()
        for j in range(2):
            fw.mm(st[:, j * 512:(j + 1) * 512], kt_ap(j, kt), q_ap(j, qb), sig=(j == 1))
        pt = pts.next()
        if negM is None:
            fw.act(pt[:], st[:], AF.Exp, scale=scale)
        else:
            fw.act(pt[:], st[:], AF.Exp, scale=scale, bias=negM[:, 0:1])
        pend.append((qb, g, kt, pt))
        if len(pend) > LOOK:
            do_pv(pend.pop(0))
    while pend:
        do_pv(pend.pop(0))


def bcast_recip(fw, K, osb, recs, hls, bcs, mulv):
    rec = recs.next()
    fw.act(rec[64:65, :], osb[64:65, :], AF.Ln)
    fw.act(rec[64:65, :], rec[64:65, :], AF.Exp, scale=-1.0)
    if mulv is not None:
        fw.ts("dve", rec[64:65, :], rec[64:65, :], mulv[64:65, 0:1], None, ALU.mult)
    hl = hls.next()
    fw.copy("dve", hl[64:65, 0, :], rec[64:65, :])
    fw.tt("dve", hl[64:65, 1, :], rec[64:65, :], hl[64:65, 0, :], ALU.subtract)
    bc = bcs.next()
    fw.mm(bc[0:64, :], K.ones_bf[64:65, 0:64], hl[64:65, 0, :], start=True, stop=False, sig=False)
    fw.mm(bc[0:64, :], K.ones_bf[64:65, 0:64], hl[64:65, 1, :], start=False, stop=True)
    return bc


def phase_attnA(fw, C, K, l, si):
    T_ = C.seqs[si]
    d = C.d
    ph = fw.phase()
    QT = ph.sb([128, 2, T_], BF16, "QT")
    KT = ph.sb([128, T_], BF16, "KT")
    V = ph.sb([128, T_ // 128, 2, 65], BF16, "V")
    for p in range(2):
        fw.ld(QT[0:64, p, :], C.QA[:, p, 0:T_])
        fw.ld(QT[64:128, p, :], C.QA[:, p + 2, 0:T_])
    for h in range(2):
        fw.ld(KT[64 * h:64 * h + 64, :], C.KA[:, h, 0:T_])
    fw.memset("pool", V[:, :, :, 64:65], 1.0)
    for h in range(2):
        fw.ld(V[:, :, h, 0:64], C.VA[0:T_, h * 64:(h + 1) * 64].rearrange("(n p) c -> p n c", p=128))
    g2 = ph.sb([128, 2, 64], F32, "g2")
    fw.ld(g2[:, 0, :], d["a_q_gain"][l].partition_broadcast(128))
    fw.ld(g2[:, 1, :], d["a_k_gain"][l].partition_broadcast(128))
    fw.tt("dve", g2[:], g2[:], g2[:], ALU.mult)
    mx = ph.sb([128, 2], F32, "mx")
    fw.reduce(mx[:], g2[:], ALU.max)
    negM = ph.sb([128, 1], F32, "negM")
    fw.tt("dve", negM[:], mx[:, 0:1], mx[:, 1:2], ALU.mult)
    fw.act(negM[:], negM[:], AF.Sqrt, scale=64.0)
    fw.ts("dve", negM[:], negM[:], -1.0, None, ALU.mult)
    recs = ph.sbpool(2, [65, 512], F32, "rec")
    hls = ph.sbpool(2, [65, 2, 512], BF16, "hl")
    bcs = ph.pspool(2, [128, 512], F32, "bc")
    outs = ph.sbpool(3, [64, 512], BF16, "ob")

    def emit(qb, p, osbs_):
        for j in range(2):
            h = p + 2 * j
            osb = osbs_[j]
            bc = bcast_recip(fw, K, osb, recs, hls, bcs, None)
            ob = outs.next()
            fw.tt("dve", ob[:], osb[0:64, :], bc[0:64, :], ALU.mult)
            fw.st(C.OT[h, :, qb * 512:(qb + 1) * 512], ob[:])

    groups = []
    for p in range(2):
        groups.append((lambda j, kt: KT[64 * j:64 * j + 64, kt * 128:(kt + 1) * 128],
                       (lambda p_: (lambda j, qb: QT[64 * j:64 * j + 64, p_, qb * 512:(qb + 1) * 512]))(p),
                       (0, 1), p))
    attn_core2(fw, ph, T_, groups, V, negM, 0.125, emit, bcs)
    ph.close()


def phase_attnB(fw, C, K, l, si):
    T_ = C.seqs[si]
    d = C.d
    lam_init = 0.8 - 0.6 * math.exp(-0.3 * l)
    ph = fw.phase()
    QT = ph.sb([128, 2, 2, T_], BF16, "QT")
    KT = ph.sb([128, 2, T_], BF16, "KT")
    V = ph.sb([128, T_ // 128, 4, 65], BF16, "V")
    for hh in range(2):
        fw.memset("pool", QT[64 * hh + 32:64 * hh + 64, :, 0, :], 0.0)
        fw.memset("pool", QT[64 * hh:64 * hh + 32, :, 1, :], 0.0)
    for p in range(2):
        for hh in range(2):
            h = p + 2 * hh
            fw.ld(QT[64 * hh:64 * hh + 32, p, 0, :], C.QB[0:32, h, 0:T_])
            fw.ld(QT[64 * hh + 32:64 * hh + 64, p, 1, :], C.QB[32:64, h, 0:T_])
            fw.ld(KT[64 * hh:64 * hh + 64, p, :], C.KB[:, h, 0:T_])
    fw.memset("pool", V[:, :, :, 64:65], 1.0)
    for h in range(4):
        fw.ld(V[:, :, h, 0:64], C.VB[0:T_, h * 64:(h + 1) * 64].rearrange("(n p) c -> p n c", p=128))
    lp = ph.sb([128, 4, 32], F32, "lp")
    fw.ld(lp[:], d["b_lambda"][l].partition_broadcast(128))
    pr = ph.sb([128, 2, 32], F32, "pr")
    fw.tt("dve", pr[:, 0, :], lp[:, 0, :], lp[:, 1, :], ALU.mult)
    fw.tt("dve", pr[:, 1, :], lp[:, 2, :], lp[:, 3, :], ALU.mult)
    sm = ph.sb([128, 2], F32, "sm")
    fw.reduce(sm[:], pr[:], ALU.add)
    fw.act(sm[:], sm[:], AF.Exp)
    nlam = ph.sb([128, 1], F32, "nlam")
    fw.tt("dve", nlam[:], sm[:, 1:2], sm[:, 0:1], ALU.subtract)
    fw.ts("dve", nlam[:], nlam[:], -lam_init, None, ALU.add)
    gsub = ph.sb([64, 1], F32, "gsub")
    fw.ld(gsub[:], d["b_subln_gain"][l].rearrange("(p o) -> p o", o=1))
    fw.ts("dve", gsub[:], gsub[:], 1.0 - lam_init, None, ALU.mult)
    recs = ph.sbpool(2, [65, 512], F32, "rec")
    hls = ph.sbpool(2, [65, 2, 512], BF16, "hl")
    bcs = ph.pspool(2, [128, 512], F32, "bc")
    o1s = ph.sbpool(6, [64, 512], F32, "o1")
    rss = ph.sbpool(2, [64, 512], F32, "rsb")
    sqs = ph.sbpool(2, [64, 512], BF16, "sq")
    outs = ph.sbpool(3, [64, 512], BF16, "ob")

    pend_o = {}

    def emit(qb, tag, osbs_):
        (p, s_) = tag
        for hh in range(2):
            h = p + 2 * hh
            osb = osbs_[hh]
            bc = bcast_recip(fw, K, osb, recs, hls, bcs, nlam if s_ == 1 else None)
            o_ = o1s.next()
            fw.tt("dve", o_[:], osb[0:64, :], bc[0:64, :], ALU.mult)
            if s_ == 0:
                pend_o[h] = o_
            else:
                finish(qb, h, pend_o.pop(h), o_)

    def finish(qb, h, o1, o2):
        fw.tt("pool", o1[:], o1[:], o2[:], ALU.add)
        sq = sqs.next()
        fw.act(sq[:], o1[:], AF.Square)
        ms = bcs.next()
        fw.mm(ms[0:64, :], K.ones_bf[0:64, 0:64], sq[:])
        rs = rss.next()
        fw.ts("dve", rs[:], ms[0:64, :], 1.0 / 64, EPS, ALU.mult, ALU.add)
        fw.act(rs[:], rs[:], AF.Ln)
        fw.act(rs[:], rs[:], AF.Exp, scale=-0.5)
        ob = outs.next()
        fw.stt(ob[:], o1[:], gsub[:, 0:1], rs[:], ALU.mult, ALU.mult)
        fw.st(C.OT[4 + h, :, qb * 512:(qb + 1) * 512], ob[:])

    groups = []
    for p in range(2):
        for s_ in range(2):
            groups.append(((lambda p_: (lambda j, kt: KT[64 * j:64 * j + 64, p_, kt * 128:(kt + 1) * 128]))(p),
                           (lambda p_, s__: (lambda j, qb: QT[64 * j:64 * j + 64, p_, s__, qb * 512:(qb + 1) * 512]))(p, s_),
                           (p, p + 2), (p, s_)))
    attn_core2(fw, ph, T_, groups, V, None, 32 ** -0.5, emit, bcs)
    ph.close()


def phase_ret(fw, C, K, l, si):
    T_ = C.seqs[si]
    d = C.d
    n = T_ // 128
    ph = fw.phase()
    QT = ph.sb([64, 4, T_], BF16, "QT")
    KT = ph.sb([64, 4, T_], BF16, "KT")
    G = ph.sb([64, 4, T_], BF16, "G")
    V = ph.sb([128, n, 256], BF16, "V")
    for h in range(4):
        fw.ld(QT[:, h, :], C.QC[:, h, 0:T_])
        fw.ld(KT[:, h, :], C.KC[:, h, 0:T_])
        fw.ld(G[:, h, :], C.GC[:, h, 0:T_])
    fw.ld(V[:], C.VC[0:T_, :].rearrange("(n p) c -> p n c", p=128))
    mask = ph.sb([128, 4, 128], F32, "mask")
    fw.ld(mask[:], d["c_retmask"])
    qdec = ph.sb([64, 2, 4, 128], F32, "qdec")
    fw.ld(qdec[:], d["c_qdec"])
    kdec = ph.sb([128, 2, 4], F32, "kdec")
    fw.ld(kdec[:], d["c_kdec"])
    gng = ph.sb([64, 4], F32, "gng")
    fw.ldn(gng[:], d["c_gn_gain"][l].rearrange("(h p) -> p h", p=64))
    Sf = ph.sb([64, n, 4, 64], BF16, "Sf")
    Sb = ph.sb([64, n, 4, 64], BF16, "Sb")
    cur = [ph.sb([64, 4, 64], F32, "curf"), ph.sb([64, 4, 64], F32, "curb")]
    ktp = ph.pspool(2, [128, 4, 64], BF16, "ktp")
    kds = ph.sbpool(4, [128, 4, 64], BF16, "kds")
    kvp = ph.pspool(2, [128, 512], F32, "kvp")
    gamt = ph.sb([64, 2, 4], F32, "gamt")
    for dr in range(2):
        for h in range(4):
            g_ = (RET_G[h] if dr == 0 else RET_G[3 - h]) ** 128
            fw.memset("pool", gamt[:, dr, h:h + 1], g_)
    tmpc = [ph.sbpool(2, [64, 4, 64], F32, "tmpc"), ph.sbpool(2, [64, 4, 64], F32, "tmpc")]

    def state_gen(dr):
        fw.memset("dve", cur[dr][:], 0.0)
        order = range(n) if dr == 0 else range(n - 1, -1, -1)
        Sx = Sf if dr == 0 else Sb
        for c in order:
            fw.copy("act", Sx[:, c, :, :], cur[dr][:])
            if (dr == 0 and c == n - 1) or (dr == 1 and c == 0):
                break
            tp = ktp.next()
            for h in range(4):
                fw.tr(tp[:, h, :], KT[:, h, c * 128:(c + 1) * 128], K.ident[0:64, 0:64], sig=(h == 3))
            kd = kds.next()
            fw.tt("dve", kd[:], tp[:], kdec.bc((slice(None), dr, slice(None)), 2, [128, 4, 64]), ALU.mult)
            tmp = tmpc[dr].next()
            fw.tt("pool", tmp[:], cur[dr][:], gamt.bc((slice(None), dr, slice(None)), 2, [64, 4, 64]), ALU.mult)
            yield
            kv = kvp.next()
            for h in range(4):
                fw.mm(kv[0:64, h * 64:(h + 1) * 64], kd[:, h, :], V[:, c, h * 64:(h + 1) * 64], sig=(h == 3))
            fw.tt("dve", cur[dr][:], tmp[:], View(kv, kv.ap[0:64, 0:256].rearrange("p (h e) -> p h e", h=4)), ALU.add)
            yield

    run_pipe([state_gen(0), state_gen(1)], 2)
    inp = ph.pspool(2, [128, 512], F32, "inp")
    ins_ = ph.sbpool(5, [128, 4, 128], BF16, "ins")
    qds = ph.sbpool(5, [64, 2, 4, 128], BF16, "qds")
    otp = ph.pspool(1, [128, 512], F32, "otp")
    osb = ph.sbpool(5, [64, 4, 128], F32, "osb")
    sqs = ph.sbpool(5, [64, 4, 128], BF16, "sq")
    msp = ph.pspool(1, [128, 512], F32, "msp")
    rss = ph.sbpool(2, [64, 4, 128], F32, "rs")
    obs = ph.sbpool(3, [64, 4, 128], BF16, "ob")

    def out_gen(c):
        cs = slice(c * 128, (c + 1) * 128)
        ip = inp.next()
        for h in range(4):
            fw.mm(ip[:, h * 128:(h + 1) * 128], KT[:, h, cs], QT[:, h, cs], sig=(h == 3))
        it = ins_.next()
        fw.tt("dve", it[:], View(ip, ip.ap[:, :].rearrange("p (h i) -> p h i", h=4)), mask[:], ALU.mult)
        qd = qds.next()
        for dr in range(2):
            fw.tt("pool", qd[:, dr, :, :], QT[:, :, cs], qdec[:, dr, :, :], ALU.mult)
        yield
        op_ = otp.next()
        for h in range(4):
            o_h = op_[0:64, h * 128:(h + 1) * 128]
            fw.mm(o_h, V[:, c, h * 64:(h + 1) * 64], it[:, h, :], start=True, stop=False, sig=False)
            fw.mm(o_h, Sf[:, c, h, :], qd[:, 0, h, :], start=False, stop=False, sig=False)
            fw.mm(o_h, Sb[:, c, h, :], qd[:, 1, h, :], start=False, stop=True, sig=(h == 3))
        o = osb.next()
        fw.ts("dve", o[:], View(op_, op_.ap[0:64, :].rearrange("p (h i) -> p h i", h=4)), 0.125, None, ALU.mult)
        sq = sqs.next()
        fw.act(sq[:], o[:], AF.Square)
        yield
        ms = msp.next()
        fw.mm(ms[0:64, :], K.ones_bf[0:64, 0:64], View(sq, sq.ap[:].rearrange("p h i -> p (h i)")))
        rs = rss.next()
        fw.ts("dve", View(rs, rs.ap[:].rearrange("p h i -> p (h i)")), ms[0:64, :], 1.0 / 64, EPS, ALU.mult, ALU.add)
        fw.act(rs[:], rs[:], AF.Ln)
        fw.act(rs[:], rs[:], AF.Exp, scale=-0.5)
        fw.tt("dve", o[:], o[:], rs[:], ALU.mult)
        fw.tt("pool", o[:], o[:], gng.bc((slice(None), slice(None)), 2, [64, 4, 128]), ALU.mult)
        ob = obs.next()
        fw.tt("dve", ob[:], o[:], G[:, :, cs], ALU.mult)
        for h in range(4):
            fw.st(C.OT[8 + h, :, cs], ob[:, h, :])

    run_pipe([out_gen(c) for c in range(n)], 4)
    ph.close()


def phase_outproj(fw, C, K, l, si):
    T_ = C.seqs[si]
    d = C.d
    src = C.xin[si] if l == 0 else C.y[si]
    ph = fw.phase()
    W = ph.sb([128, 8, 1024], BF16, "Wo")
    for k in range(8):
        fw.ld(W[:, k, :], d["w_out"][l, k * 128:(k + 1) * 128, :], q="pool")
    gain = ph.sb([128, 1024], F32, "gain")
    fw.ld(gain[:], d["norm_mix_post"][l].partition_broadcast(128))
    oTs = ph.sbpool(2, [128, 8, 512], BF16, "oT")
    xts = ph.sbpool(4, [128, 1024], F32, "xt")
    mps = ph.pspool(8, [128, 512], F32, "mp")
    tmps = ph.sbpool(4, [128, 1024], F32, "tmp")
    sss = ph.sbpool(4, [128, 2], F32, "ss")
    rstds = ph.sbpool(4, [128, 1], F32, "rstd")

    def sub_gen(oT, j, r0):
        xt = xts.next()
        fw.ld(xt[:], src[r0:r0 + 128, :])
        m = [mps.next(), mps.next()]
        for half in range(2):
            for k in range(8):
                fw.mm(m[half][:, :], oT[:, k, j * 128:(j + 1) * 128], W[:, k, half * 512:(half + 1) * 512], start=(k == 0), stop=(k == 7))
        tmp = tmps.next()
        ss = sss.next()
        for half in range(2):
            fw.act(tmp[:, half * 512:(half + 1) * 512], m[half][:, :], AF.Square, accum=ss[:, half:half + 1])
        yield
        rstd = rstds.next()
        fw.tt("dve", rstd[:], ss[:, 0:1], ss[:, 1:2], ALU.add)
        fw.ts("dve", rstd[:], rstd[:], 1.0 / D_MODEL, EPS, ALU.mult, ALU.add)
        fw.act(rstd[:], rstd[:], AF.Ln)
        fw.act(rstd[:], rstd[:], AF.Exp, scale=-0.5)
        yield
        for half in range(2):
            hs = slice(half * 512, (half + 1) * 512)
            fw.stt(tmp[:, hs], m[half][:, :], rstd[:, 0:1], gain[:, hs], ALU.mult, ALU.mult)
        fw.tt("pool", tmp[:], tmp[:], xt[:], ALU.add)
        fw.st(C.y[si][r0:r0 + 128, :], tmp[:])

    gens = []
    for b in range(T_ // 512):
        t0 = b * 512
        oT = oTs.next()

        def ldgen(oT=oT, t0=t0):
            for k in range(8):
                fw.ld(oT[0:64, k, :], C.OT[2 * k, :, t0:t0 + 512])
                fw.ld(oT[64:128, k, :], C.OT[2 * k + 1, :, t0:t0 + 512])
            return
            yield
        gens.append(("ld", oT, t0))
        for j in range(4):
            gens.append(("sub", oT, j, t0 + j * 128))

    def all_gens():
        for g in gens:
            if g[0] == "ld":
                _, oT, t0 = g
                for k in range(8):
                    fw.ld(oT[0:64, k, :], C.OT[2 * k, :, t0:t0 + 512])
                    fw.ld(oT[64:128, k, :], C.OT[2 * k + 1, :, t0:t0 + 512])
            else:
                yield sub_gen(g[1], g[2], g[3])

    class LazyList(list):
        pass
    pending = all_gens()
    active = []
    done = False
    while not done or active:
        while not done and len(active) < 4:
            try:
                active.append(next(pending))
            except StopIteration:
                done = True
        for g in list(active):
            try:
                next(g)
            except StopIteration:
                active.remove(g)
    ph.close()


def phase_ffn(fw, C, K, l, si):
    T_ = C.seqs[si]
    d = C.d
    TB = 1024
    ph = fw.phase()
    gain = ph.sb([128, 1024], F32, "gain")
    fw.ld(gain[:], d["norm_ffn_pre"][l].partition_broadcast(128))
    gpost = ph.sb([128, 1024], F32, "gpost")
    fw.ld(gpost[:], d["norm_ffn_post"][l].partition_broadcast(128))
    P = Ctx()
    P.xt = ph.sbpool(2, [128, 1024], F32, "xt")
    P.hb = ph.sbpool(2, [128, 1024], BF16, "hb")
    P.tmp = ph.sbpool(2, [128, 1024], F32, "tmp")
    P.ss = ph.sbpool(2, [128, 1], F32, "ss")
    P.rstd = ph.sbpool(2, [128, 1], F32, "rstd")
    P.tp = ph.pspool(1, [128, 8, 128], BF16, "tp")
    hTs = ph.sbpool(1, [128, 8, TB], BF16, "hT")
    act = ph.sb([128, NFF, TB], BF16, "act")
    FG = 4
    wgs = ph.sbpool(2, [128, 8, 128 * FG], BF16, "wg")
    wus = ph.sbpool(2, [128, 8, 128 * FG], BF16, "wu")
    Wd = [ph.sb([128, 1024], BF16, "Wd%d" % f) for f in range(NFF)]
    gps = ph.pspool(2, [128, 512], F32, "gp")
    ups = ph.pspool(2, [128, 512], F32, "up")
    dps = ph.pspool(2, [128, 512], F32, "dp")
    sgs = ph.sbpool(2, [128, 512], F32, "sg")
    sss = ph.sbpool(4, [128, 2], F32, "ss2")
    dxt = ph.sbpool(3, [128, 1024], F32, "dxt")
    dtmp = ph.sbpool(3, [128, 1024], F32, "dtmp")
    drstd = ph.sbpool(4, [128, 1], F32, "drstd")
    wg_d = d["ffn_w_gate"][l].rearrange("(k p) n -> p k n", p=128)
    wu_d = d["ffn_w_up"][l].rearrange("(k p) n -> p k n", p=128)
    def hT_gen(b, hT_):
        t0_ = b * TB
        for j in range(TB // 128):
            make_hT(fw, K, C.y[si][t0_ + j * 128:t0_ + (j + 1) * 128, :], gain, hT_, j, P)
            yield

    nblk = T_ // TB
    for b in range(nblk):
        t0 = b * TB
        hT = hTs.next()
        for _ in hT_gen(b, hT):
            pass
        side = None
        for f in range(NFF):
            if b == 0:
                fw.ld(Wd[f][:], d["ffn_w_down"][l, f * 128:(f + 1) * 128, :], q="pool")
            if f % FG == 0:
                nf = min(FG, NFF - f)
                wg_t = wgs.next()
                fw.ld(wg_t[:, :, 0:128 * nf], wg_d[:, :, f * 128:(f + nf) * 128], q="pool")
                wu_t = wus.next()
                fw.ld(wu_t[:, :, 0:128 * nf], wu_d[:, :, f * 128:(f + nf) * 128], q="pool")
            fo = (f % FG) * 128
            for tb in range(TB // 512):
                ts_ = slice(tb * 512, (tb + 1) * 512)
                gp = gps.next()
                up = ups.next()
                for k in range(8):
                    fw.mm(gp[:, :], wg_t[:, k, fo:fo + 128], hT[:, k, ts_], start=(k == 0), stop=(k == 7))
                for k in range(8):
                    fw.mm(up[:, :], wu_t[:, k, fo:fo + 128], hT[:, k, ts_], start=(k == 0), stop=(k == 7))
                sg = sgs.next()
                fw.act(sg[:], gp[:, :], AF.Silu)
                fw.tt("dve", act[:, f, ts_], sg[:], up[:, :], ALU.mult)
        ph_all = Pool(dps.items + gps.items + ups.items)
        if side is not None:
            for _ in side:
                pass
        def down_gen(j):
            r0 = t0 + j * 128
            bk = [ph_all.next(), ph_all.next()]
            xt = dxt.next()
            fw.ld(xt[:], C.y[si][r0:r0 + 128, :])
            for f in range(NFF):
                for half in range(2):
                    fw.mm(bk[half][:, :], act[:, f, j * 128:(j + 1) * 128], Wd[f][:, half * 512:(half + 1) * 512],
                          start=(f == 0), stop=(f == NFF - 1))
            tmp = dtmp.next()
            ss = sss.next()
            for half in range(2):
                fw.act(tmp[:, half * 512:(half + 1) * 512], bk[half][:, :], AF.Square, accum=ss[:, half:half + 1])
            yield
            rstd = drstd.next()
            fw.tt("dve", rstd[:], ss[:, 0:1], ss[:, 1:2], ALU.add)
            fw.ts("dve", rstd[:], rstd[:], 1.0 / D_MODEL, EPS, ALU.mult, ALU.add)
            fw.act(rstd[:], rstd[:], AF.Ln)
            fw.act(rstd[:], rstd[:], AF.Exp, scale=-0.5)
            yield
            for half in range(2):
                hs = slice(half * 512, (half + 1) * 512)
                fw.stt(tmp[:, hs], bk[half][:, :], rstd[:, 0:1], gpost[:, hs], ALU.mult, ALU.mult)
            fw.tt("pool", tmp[:], tmp[:], xt[:], ALU.add)
            fw.st(C.y[si][r0:r0 + 128, :], tmp[:])

        run_pipe([down_gen(j) for j in range(TB // 128)], 3)
    ph.close()


def phase_rwkv(fw, C, K, l, si):
    T_ = C.seqs[si]
    d = C.d
    n = T_ // 128
    CW = math.exp(-0.5)
    ph = fw.phase()
    mu = ph.sb([64, 3, 15], F32, "mu")
    fw.ldn(mu[:, 1, :], d["d_mu_prev"][l, 0:960].rearrange("(j p) -> p j", p=64))
    fw.ldn(mu[:, 2, :], d["d_mu_next"][l, 0:960].rearrange("(j p) -> p j", p=64))
    fw.tt("dve", mu[:, 0, :], mu[:, 1, :], mu[:, 2, :], ALU.add)
    fw.ts("dve", mu[:, 0, :], mu[:, 0, :], -1.0, 1.0, ALU.mult, ALU.add)
    mug = ph.sb([128, 3], F32, "mug")
    fw.ld(mug[:, 1:2], d["d_mu_prev"][l, 960:1088].rearrange("(p o) -> p o", o=1))
    fw.ld(mug[:, 2:3], d["d_mu_next"][l, 960:1088].rearrange("(p o) -> p o", o=1))
    fw.tt("dve", mug[:, 0:1], mug[:, 1:2], mug[:, 2:3], ALU.add)
    fw.ts("dve", mug[:, 0:1], mug[:, 0:1], -1.0, 1.0, ALU.mult, ALU.add)
    prm = ph.sb([64, 9, 4], F32, "prm")
    fw.ldn(prm[:, 0, :], d["d_w0"][l, 0].rearrange("(h p) -> p h", p=64))
    fw.ldn(prm[:, 1, :], d["d_w0"][l, 1].rearrange("(h p) -> p h", p=64))
    fw.ldn(prm[:, 2, :], d["d_a0"][l].rearrange("(h p) -> p h", p=64))
    fw.ldn(prm[:, 3, :], d["d_k_k"][l].rearrange("(h p) -> p h", p=64))
    fw.ldn(prm[:, 4, :], d["d_k_a"][l].rearrange("(h p) -> p h", p=64))
    fw.ldn(prm[:, 6, :], d["d_r_k"][l].rearrange("h p -> p h"))
    fw.ldn(prm[:, 7, :], d["d_gn_w"][l].rearrange("(h p) -> p h", p=64))
    fw.ldn(prm[:, 8, :], d["d_gn_b"][l].rearrange("(h p) -> p h", p=64))
    fw.ts("dve", prm[:, 5, :], prm[:, 4, :], -1.0, 1.0, ALU.mult, ALU.add)
    wup = ph.sb([64, 2, 256], BF16, "wup")
    fw.ld(wup[:, 0, :], d["d_w_up"][l, 0], q="pool")
    fw.ld(wup[:, 1, :], d["d_w_up"][l, 1], q="pool")
    aup = ph.sb([64, 256], BF16, "aup")
    fw.ld(aup[:], d["d_a_up"][l], q="pool")
    gup = ph.sb([128, 256], BF16, "gup")
    fw.ld(gup[:], d["d_g_up"][l], q="pool")
    dmask = ph.sb([128, 2, 5, 128], BF16, "dmask")
    fw.ld(dmask[:], d["c_dmask"], q="pool")
    seg = ph.sb([64, 512], F32, "seg")
    fw.ld(seg[:], d["c_seg"])

    def pbc(i):
        return prm.bc((slice(None), i, slice(None)), 2, [64, 4, 128])

    ytok = [T(None, "ytok%d" % c) for c in range(n)]
    H = [ph.sb([64, 4, 64], F32, "Hf"), ph.sb([64, 4, 64], F32, "Hb")]
    Hb = [ph.sb([64, 4, 64], BF16, "Hfb"), ph.sb([64, 4, 64], BF16, "Hbb")]
    for dr in range(2):
        fw.memset("dve", H[dr][:], 0.0)
        fw.memset("pool", Hb[dr][:], 0.0)
    PST = ph.pspool(1, [128, 4, 4, 64], BF16, "pst")
    PP = []
    for dr in range(2):
        Q = Ctx()
        Q.PS = ph.pspool(4 if dr == 0 else 3, [128, 512], F32, "ps")
        Q.zts = ph.sbpool(1, [64, 15, 130], F32, "zt")
        Q.zgs = ph.sbpool(2, [128, 130], F32, "zg")
        Q.us = ph.sbpool(1, [64, 15, 128], F32, "u")
        Q.u2s = ph.sbpool(1, [64, 15, 128], F32, "u2")
        Q.ugs = ph.sbpool(1, [128, 128], F32, "ug")
        Q.F4 = ph.sbpool(10, [64, 4, 128], F32, "f4")
        Q.FL = ph.sbpool(3, [64, 4, 128], F32, "fl")
        Q.B4 = ph.sbpool(12, [64, 4, 128], BF16, "b4")
        Q.M4 = ph.sbpool(9, [128, 4, 128], BF16, "m4")
        Q.MK = ph.sbpool(4, [128, 4, 128], BF16, "mk")
        Q.TM = ph.sbpool(2, [128, 4, 4, 64], BF16, "tm")
        Q.SM = ph.sbpool(4, [128, 128], BF16, "sm")
        Q.S4 = ph.sbpool(6, [64, 4], F32, "s4")
        Q.UV = ph.sbpool(2, [128, 4, 64], F32, "uv")
        Q.UB = ph.sbpool(3, [128, 4, 64], BF16, "ub")
        PP.append(Q)

    def v3(t, npart=64):
        return View(t, t.ap[0:npart, :].rearrange("p (h i) -> p h i", h=4))

    def v3s(t):
        return View(t, t.ap[:, 0:256].rearrange("p (h i) -> p h i", h=4))

    def flat(t):
        return View(t, t.ap[:].rearrange("p h i -> p (h i)"))

    def unit(c, dr, first):
        Q = PP[dr]
        PS, F4, FL, B4, M4, MK, SM, S4 = Q.PS, Q.F4, Q.FL, Q.B4, Q.M4, Q.MK, Q.SM, Q.S4
        need_post = not first
        c0 = c * 128
        zt = Q.zts.next()
        zg = Q.zgs.next()
        lo = 1 if c == 0 else 0
        hi = 129 if c == n - 1 else 130
        if c == 0:
            fw.memset("pool", zt[:, :, 0:1], 0.0)
            fw.memset("pool", zg[:, 0:1], 0.0)
        if c == n - 1:
            fw.memset("pool", zt[:, :, 129:130], 0.0)
            fw.memset("pool", zg[:, 129:130], 0.0)
        fw.ld(zt[:, :, lo:hi], C.ZD[:, :, c0 - 1 + lo:c0 - 1 + hi])
        fw.ld(zg[:, lo:hi], C.ZG[:, c0 - 1 + lo:c0 - 1 + hi])
        if need_post:
            yprev = FL.next()
            fw.ld(yprev[:], View(ytok[c], C.YD[c]))
        yield
        u = Q.us.next()
        u2 = Q.u2s.next()
        fw.tt("dve", u[:], zt[:, :, 1:129], mu.bc((slice(None), 0, slice(None)), 2, [64, 15, 128]), ALU.mult)
        fw.tt("pool", u2[:], zt[:, :, 0:128], mu.bc((slice(None), 1, slice(None)), 2, [64, 15, 128]), ALU.mult)
        yield
        fw.tt("dve", u[:], u[:], u2[:], ALU.add)
        fw.tt("pool", u2[:], zt[:, :, 2:130], mu.bc((slice(None), 2, slice(None)), 2, [64, 15, 128]), ALU.mult)
        yield
        fw.tt("dve", u[:], u[:], u2[:], ALU.add)
        ug = Q.ugs.next()
        fw.ts("dve", ug[:], zg[:, 1:129], mug[:, 0:1], None, ALU.mult)
        fw.stt(ug[:], zg[:, 0:128], mug[:, 1:2], ug[:], ALU.mult, ALU.add)
        fw.stt(ug[:], zg[:, 2:130], mug[:, 2:3], ug[:], ALU.mult, ALU.add)
        yield
        tw = SM.next()
        fw.act(tw[0:64, :], u[:, 12 + dr, :], AF.Tanh)
        adb = SM.next()
        fw.copy("act", adb[0:64, :], u[:, 14, :])
        yw = PS.next()
        ya = PS.next()
        for h in range(4):
            fw.mm(yw[0:64, h * 128:(h + 1) * 128], wup[:, dr, h * 64:(h + 1) * 64], tw[0:64, :], sig=(h == 3))
        for h in range(4):
            fw.mm(ya[0:64, h * 128:(h + 1) * 128], aup[:, h * 64:(h + 1) * 64], adb[0:64, :], sig=(h == 3))
        yield
        sg = F4.next()
        fw.tt("dve", sg[:], v3(yw), pbc(dr), ALU.add)
        fw.act(sg[:], sg[:], AF.Tanh, scale=0.5)
        fw.ts("pool", sg[:], sg[:], 0.5, 0.5, ALU.mult, ALU.add)
        a = F4.next()
        fw.tt("dve", a[:], v3(ya), pbc(2), ALU.add)
        fw.act(a[:], a[:], AF.Tanh, scale=0.5)
        fw.ts("pool", a[:], a[:], 0.5, 0.5, ALU.mult, ALU.add)
        yield
        if need_post:
            sgd = SM.next()
            fw.act(sgd[:], ug[:], AF.Tanh, scale=0.5)
            fw.ts("pool", sgd[:], sgd[:], 0.5, 0.5, ALU.mult, ALU.add)
            gps_ = PS.next()
            for h in range(4):
                fw.mm(gps_[0:64, h * 128:(h + 1) * 128], gup[:, h * 64:(h + 1) * 64], sgd[:], sig=(h == 3))
            g_ = FL.next()
            fw.copy("act", g_[:], v3(gps_))
            yield
        r = u[:, 0:4, :]
        k = u[:, 4:8, :]
        v = u[:, 8:12, :]
        kk = F4.next()
        fw.tt("dve", kk[:], k, pbc(3), ALU.mult)
        sq = B4.next()
        fw.tt("pool", sq[:], kk[:], kk[:], ALU.mult)
        ssp = PS.next()
        fw.mm(ssp[0:64, :], K.ones_bf[0:64, 0:64], flat(sq))
        yield
        nr = F4.next()
        fw.ts("dve", nr[:], v3(ssp), 1e-24, None, ALU.max)
        fw.act(nr[:], nr[:], AF.Ln)
        fw.act(nr[:], nr[:], AF.Exp, scale=-0.5)
        kap = F4.next()
        fw.tt("dve", kap[:], kk[:], nr[:], ALU.mult)
        yield
        tk = F4.next()
        fw.tt("pool", tk[:], a[:], pbc(4), ALU.mult)
        fw.tt("pool", tk[:], tk[:], pbc(5), ALU.add)
        kh = FL.next()
        fw.tt("dve", kh[:], k, tk[:], ALU.mult)
        bb = F4.next()
        fw.tt("pool", bb[:], kap[:], a[:], ALU.mult)
        yield
        Pp = F4.next()
        fw.scan(flat(Pp), seg[:], flat(sg), 0.0, ALU.mult, ALU.add)
        Qp = F4.next()
        fw.tt("pool", Qp[:], Pp[:], sg[:], ALU.subtract)
        nT = S4.next()
        pT = S4.next()
        fw.ts("dve", nT[:], Pp[:, :, 127], -CW, None, ALU.mult)
        fw.ts("dve", pT[:], Pp[:, :, 127], CW, None, ALU.mult)
        gC = S4.next()
        fw.act(gC[:], nT[:], AF.Exp)
        yield
        rt = B4.next()
        kp = B4.next()
        kt = B4.next()
        bt = B4.next()
        Khf = B4.next()
        Bhf = B4.next()
        E1 = F4.next()
        E2 = F4.next()
        if dr == 0:
            fw.act(E1[:], Pp[:], AF.Exp, scale=-CW)
            fw.tt("dve", rt[:], r, E1[:], ALU.mult)
            fw.act(E2[:], Qp[:], AF.Exp, scale=-CW)
            fw.tt("pool", kp[:], kap[:], E2[:], ALU.mult)
            yield
            E3 = F4.next()
            fw.act(E3[:], Pp[:], AF.Exp, scale=CW)
            fw.tt("dve", kt[:], kh[:], E3[:], ALU.mult)
            fw.tt("pool", bt[:], bb[:], E3[:], ALU.mult)
            E4 = F4.next()
            for h in range(4):
                fw.act(E4[:, h, :], Pp[:, h, :], AF.Exp, scale=CW, bias=nT[:, h:h + 1])
        else:
            for h in range(4):
                fw.act(E1[:, h, :], Qp[:, h, :], AF.Exp, scale=CW, bias=nT[:, h:h + 1])
            fw.tt("dve", rt[:], r, E1[:], ALU.mult)
            for h in range(4):
                fw.act(E2[:, h, :], Pp[:, h, :], AF.Exp, scale=CW, bias=nT[:, h:h + 1])
            fw.tt("pool", kp[:], kap[:], E2[:], ALU.mult)
            yield
            E3 = F4.next()
            for h in range(4):
                fw.act(E3[:, h, :], Qp[:, h, :], AF.Exp, scale=-CW, bias=pT[:, h:h + 1])
            fw.tt("dve", kt[:], kh[:], E3[:], ALU.mult)
            fw.tt("pool", bt[:], bb[:], E3[:], ALU.mult)
            E4 = F4.next()
            fw.act(E4[:], Qp[:], AF.Exp, scale=-CW)
        yield
        fw.tt("dve", Khf[:], kh[:], E4[:], ALU.mult)
        fw.stt(Bhf[:], bb[:], -1.0, E4[:], ALU.mult, ALU.mult)
        vb = B4.next()
        fw.copy("act", vb[:], v)
        pst = PST.next()
        for ki, src_ in enumerate((kp, Khf, Bhf, vb)):
            for h in range(4):
                fw.tr(pst[:, ki, h, :], src_[:, h, :], K.ident[0:64, 0:64], sig=(ki == 3 and h == 3))
        tm = Q.TM.next()
        fw.copy("dve", tm[:], pst[:])
        yield
        mats = []
        pairs = [(kp, bt), (bt, kp), (kt, kp), (kt, rt), (bt, rt)]
        for ki, (lt, rt_) in enumerate(pairs):
            p_ = PS.next()
            for h in range(4):
                fw.mm(p_[:, h * 128:(h + 1) * 128], lt[:, h, :], rt_[:, h, :], sig=(h == 3))
            m_ = M4.next() if ki < 2 else MK.next()
            fw.tt("dve", m_[:], v3(p_, 128), dmask.bc((slice(None), dr, ki, slice(None)), 1, [128, 4, 128]), ALU.mult)
            mats.append(m_)
            yield
        X, XT, AkkT, ArkT, ArbT = mats
        PT = M4.next()
        fw.tt("pool", PT[:], XT[:], K.ident.bc((slice(None), slice(None)), 1, [128, 4, 128]), ALU.add)
        for lvl in range(6):
            p1 = PS.next()
            for h in range(4):
                fw.mm(p1[:, h * 128:(h + 1) * 128], XT[:, h, :], X[:, h, :], sig=(h == 3))
            if lvl < 5:
                p2 = PS.next()
                for h in range(4):
                    fw.mm(p2[:, h * 128:(h + 1) * 128], X[:, h, :], XT[:, h, :], sig=(h == 3))
            yield
            X2 = M4.next()
            fw.copy("act", X2[:], v3(p1, 128))
            if lvl < 5:
                XT2 = M4.next()
                fw.copy("dve", XT2[:], v3(p2, 128))
            p3 = PS.next()
            for h in range(4):
                fw.mm(p3[:, h * 128:(h + 1) * 128], X2[:, h, :], PT[:, h, :], sig=(h == 3))
            yield
            PT2 = M4.next()
            fw.tt("dve", PT2[:], v3(p3, 128), PT[:], ALU.add)
            PT = PT2
            X = X2
            if lvl < 5:
                XT = XT2
        pw = PS.next()
        for h in range(4):
            fw.mm(pw[0:64, h * 128:(h + 1) * 128], tm[:, 0, h, :], PT[:, h, :], sig=(h == 3))
        pa = PS.next()
        for h in range(4):
            fw.mm(pa[:, h * 64:(h + 1) * 64], AkkT[:, h, :], tm[:, 3, h, :], sig=(h == 3))
        yield
        WkT = B4.next()
        fw.copy("act", WkT[:], v3(pw))
        AV = Q.UB.next()
        fw.copy("dve", AV[:], v3s(pa))
        pu = PS.next()
        for h in range(4):
            fw.mm(pu[:, h * 64:(h + 1) * 64], PT[:, h, :], AV[:, h, :], sig=(h == 3))
        yield
        Uv = Q.UV.next()
        fw.copy("act", Uv[:], v3s(pu))
        pU = PS.next()
        for h in range(4):
            fw.mm(pU[:, h * 64:(h + 1) * 64], WkT[:, h, :], Hb[dr][:, h, :], sig=(h == 3))
        yield
        U = Q.UB.next()
        fw.tt("dve", U[:], v3s(pU), Uv[:], ALU.add)
        pY = PS.next()
        for h in range(4):
            yh = pY[0:64, h * 128:(h + 1) * 128]
            fw.mm(yh, Hb[dr][:, h, :], rt[:, h, :], start=True, stop=False, sig=False)
            fw.mm(yh, tm[:, 3, h, :], ArkT[:, h, :], start=False, stop=False, sig=False)
            fw.mm(yh, U[:, h, :], ArbT[:, h, :], start=False, stop=True, sig=(h == 3))
        pH = PS.next()
        for h in range(4):
            hh = pH[0:64, h * 64:(h + 1) * 64]
            fw.mm(hh, tm[:, 1, h, :], tm[:, 3, h, :], start=True, stop=False, sig=False)
            fw.mm(hh, tm[:, 2, h, :], U[:, h, :], start=False, stop=True, sig=(h == 3))
        fw.tt("pool", H[dr][:], H[dr][:], gC.bc((slice(None), slice(None)), 2, [64, 4, 64]), ALU.mult)
        yield
        fw.tt("dve", H[dr][:], H[dr][:], View(pH, pH.ap[0:64, 0:256].rearrange("p (h i) -> p h i", h=4)), ALU.add)
        fw.copy("act", Hb[dr][:], H[dr][:])
        if first:
            ysb = F4.next()
            fw.copy("act", ysb[:], v3(pY))
            fw.st(View(ytok[c], C.YD[c]), ysb[:])
            return
        y = F4.next()
        fw.tt("dve", y[:], v3(pY), yprev[:], ALU.add)
        pm_ = PS.next()
        fw.mm(pm_[0:64, :], K.ones_f[0:64, 0:64], flat(y))
        rk = F4.next()
        fw.tt("pool", rk[:], r, kh[:], ALU.mult)
        fw.tt("pool", rk[:], rk[:], pbc(6), ALU.mult)
        yield
        yc = F4.next()
        fw.stt(yc[:], v3(pm_), -1.0 / 64, y[:], ALU.mult, ALU.add)
        sq2 = F4.next()
        fw.act(sq2[:], yc[:], AF.Square)
        pv = PS.next()
        fw.mm(pv[0:64, :], K.ones_f[0:64, 0:64], flat(sq2))
        pr_ = PS.next()
        fw.mm(pr_[0:64, :], K.ones_f[0:64, 0:64], flat(rk))
        yield
        rs = F4.next()
        fw.ts("dve", rs[:], v3(pv), 1.0 / 64, GN_EPS, ALU.mult, ALU.add)
        fw.act(rs[:], rs[:], AF.Ln)
        fw.act(rs[:], rs[:], AF.Exp, scale=-0.5)
        fw.tt("dve", yc[:], yc[:], rs[:], ALU.mult)
        fw.tt("pool", yc[:], yc[:], pbc(7), ALU.mult)
        yield
        fw.tt("pool", yc[:], yc[:], pbc(8), ALU.add)
        t_ = F4.next()
        fw.tt("dve", t_[:], v3(pr_), v, ALU.mult)
        fw.tt("pool", yc[:], yc[:], t_[:], ALU.add)
        ob = B4.next()
        fw.tt("dve", ob[:], yc[:], g_[:], ALU.mult)
        for h in range(4):
            fw.st(C.OT[12 + h, :, c * 128:(c + 1) * 128], ob[:, h, :])

    for i in range(n):
        first = i < n // 2
        gens = [unit(i, 0, first), unit(n - 1 - i, 1, first)]
        alive = [True, True]
        while any(alive):
            for gi in range(2):
                if alive[gi]:
                    try:
                        next(gens[gi])
                    except StopIteration:
                        alive[gi] = False
    ph.close()


def build(seqs=(4096, 2048), depth=2, phases=None, debug=False):
    nc = bass.Bass("TRN2", target_bir_lowering=False)
    C = Ctx()
    C.seqs = list(seqs)
    C.d = {}
    C.xin = [nc.dram_tensor("x%d" % i, [t, D_MODEL], F32, kind="ExternalInput").ap() for i, t in enumerate(seqs)]
    C.y = [nc.dram_tensor("y%d" % i, [t, D_MODEL], F32, kind="ExternalOutput").ap() for i, t in enumerate(seqs)]
    for nm, shp in WEIGHT_SPECS:
        C.d[nm] = nc.dram_tensor(nm, list(shp), F32, kind="ExternalInput").ap()
    for nm, arr in make_consts().items():
        C.d[nm] = nc.dram_tensor(nm, list(arr.shape), F32, kind="ExternalInput").ap()
    TM = max(seqs)
    sk = "ExternalOutput" if debug else "Internal"

    def scratch(nm, shp, dt):
        return nc.dram_tensor(nm, list(shp), dt, kind=sk).ap()
    C.QA = scratch("QA", [64, 4, TM], BF16)
    C.KA = scratch("KA", [64, 2, TM], BF16)
    C.VA = scratch("VA", [TM, 128], BF16)
    C.QB = scratch("QB", [64, 4, TM], BF16)
    C.KB = scratch("KB", [64, 4, TM], BF16)
    C.VB = scratch("VB", [TM, 256], BF16)
    C.QC = scratch("QC", [64, 4, TM], BF16)
    C.KC = scratch("KC", [64, 4, TM], BF16)
    C.GC = scratch("GC", [64, 4, TM], BF16)
    C.VC = scratch("VC", [TM, 256], BF16)
    C.ZD = scratch("ZD", [64, 15, TM], F32)
    C.ZG = scratch("ZG", [128, TM], F32)
    C.OT = scratch("OT", [16, 64, TM], BF16)
    C.YD = scratch("YD", [TM // 128, 64, 4, 128], F32)
    allph = ["inproj", "A", "B", "C", "D", "outproj", "ffn"]
    phases = allph if phases is None else phases
    with ExitStack() as es:
        fw = FW(nc, es)
        ph0 = fw.phase()
        K = load_consts(fw, C, ph0)
        fw.barrier()
        for l in range(depth):
            for si in range(len(seqs)):
                if "inproj" in phases:
                    phase_inproj(fw, C, K, l, si)
                if "A" in phases:
                    phase_attnA(fw, C, K, l, si)
                if "B" in phases:
                    phase_attnB(fw, C, K, l, si)
                if "C" in phases:
                    phase_ret(fw, C, K, l, si)
                if "D" in phases:
                    phase_rwkv(fw, C, K, l, si)
                if "outproj" in phases:
                    phase_outproj(fw, C, K, l, si)
                if "ffn" in phases:
                    phase_ffn(fw, C, K, l, si)
        fw.barrier()
        ph0.es.close()
        C.nins = fw.nins
    return nc, C


_CACHE = {}


def kernel(**inputs):
    xp = np.ascontiguousarray(inputs["x_prompt"], dtype=np.float32)
    xs = np.ascontiguousarray(inputs["x_sample"], dtype=np.float32)
    nc, C = build()
    consts = make_consts()
    in_maps = []
    for i in range(8):
        m = {"x0": xp[i], "x1": xs[i]}
        for nm, _ in WEIGHT_SPECS:
            m[nm] = np.ascontiguousarray(inputs[nm], dtype=np.float32)
        m.update(consts)
        in_maps.append(m)
    res = run_bass_kernel_spmd(nc, in_maps, core_ids=list(range(8)))
    yp = np.stack([r["y0"] for r in res.results], 0).astype(np.float32)
    ys = np.stack([r["y1"] for r in res.results], 0).astype(np.float32)
    return (yp, ys)
```
